# Optimizing a Trainium2 kernel written in Bass

```python
import math
import jax, jax.numpy as jnp
from jax import lax
import numpy as np

D_MODEL = 2048
BATCH = 4
SEQ = 2048
DEPTH = 2
DEC_BATCH = 128
DEC_SEQ = 1
PAST_LEN = 8192
PAGE_SIZE = 128

D_MIX = D_MODEL
EPS = 1e-6
ATT_HEAD_DIM = 64
ATT_HEADS = D_MIX // (4 * ATT_HEAD_DIM)
ATT_KV_HEADS = max(1, ATT_HEADS // 4)
ATT_GROUP = ATT_HEADS // ATT_KV_HEADS
D_ATT = ATT_HEADS * ATT_HEAD_DIM
WINDOW = 128
ROPE_THETA = 500000.0
ROPE_DIM = ATT_HEAD_DIM // 4
SSM_HEAD_DIM = 64
SSM_HEADS = D_MIX // (2 * SSM_HEAD_DIM)
D_SSM = SSM_HEADS * SSM_HEAD_DIM
SSM_GROUPS = 2
D_STATE = 128
CONV_WIDTH = 4
CONV_DIM = D_SSM + 2 * SSM_GROUPS * D_STATE
SSM_CHUNK = 128
MLSTM_HEADS = 4
MLSTM_HEAD_DIM = D_MIX // (4 * MLSTM_HEADS)
D_MLSTM = MLSTM_HEADS * MLSTM_HEAD_DIM
MLSTM_CHUNK = 128
D_FF = 4 * D_MODEL
IN_WIDTH = D_ATT + 2 * ATT_KV_HEADS * ATT_HEAD_DIM + D_SSM + CONV_DIM + SSM_HEADS + 4 * D_MLSTM + 2 * MLSTM_HEADS

kernel_name = 'hybrid_swa_ssd_mlstm_step'


def rms_norm(x, w):
    xf = x.astype(jnp.float32)
    y = xf * lax.rsqrt(jnp.mean(xf * xf, axis=-1, keepdims=True) + EPS)
    return (y * w.astype(jnp.float32)).astype(x.dtype)


def _split_points():
    widths = (D_ATT, ATT_KV_HEADS * ATT_HEAD_DIM, ATT_KV_HEADS * ATT_HEAD_DIM, D_SSM, CONV_DIM, SSM_HEADS,
              D_MLSTM, D_MLSTM, D_MLSTM, D_MLSTM, MLSTM_HEADS, MLSTM_HEADS)
    pts, acc = [], 0
    for w in widths[:-1]:
        acc += w
        pts.append(acc)
    return pts


def partial_rope(x, pos):
    half = ROPE_DIM // 2
    inv = jnp.power(jnp.float32(ROPE_THETA), -jnp.arange(half, dtype=jnp.float32) / half)
    ang = pos.astype(jnp.float32)[:, None] * inv[None, :]
    cos = jnp.cos(ang)[None, :, None, :]
    sin = jnp.sin(ang)[None, :, None, :]
    xr = x[..., :ROPE_DIM].astype(jnp.float32)
    x1, x2 = xr[..., :half], xr[..., half:]
    rot = jnp.concatenate([x1 * cos - x2 * sin, x2 * cos + x1 * sin], axis=-1).astype(x.dtype)
    return jnp.concatenate([rot, x[..., ROPE_DIM:]], axis=-1)


def sink_attention(q, k, v, mask, sinks):
    s = jnp.einsum('...qhgd,...khd->...hgqk', q, k).astype(jnp.float32) * (ATT_HEAD_DIM ** -0.5)
    s = jnp.where(mask, s, -jnp.inf)
    sink = jnp.broadcast_to(sinks.astype(jnp.float32)[:, :, None, None], s.shape[:-1] + (1,))
    p = jax.nn.softmax(jnp.concatenate([s, sink], axis=-1), axis=-1)[..., :-1]
    return jnp.einsum('...hgqk,...khd->...qhgd', p.astype(v.dtype), v)


def swa_prompt(q, k, v, sinks):
    B, S = q.shape[0], q.shape[1]
    nb = S // WINDOW
    qb = q.reshape(B, nb, WINDOW, ATT_KV_HEADS, ATT_GROUP, ATT_HEAD_DIM)
    kb = k.reshape(B, nb, WINDOW, ATT_KV_HEADS, ATT_HEAD_DIM)
    vb = v.reshape(B, nb, WINDOW, ATT_KV_HEADS, ATT_HEAD_DIM)
    pad = ((0, 0), (1, 0), (0, 0), (0, 0), (0, 0))
    kk = jnp.concatenate([jnp.pad(kb, pad)[:, :-1], kb], axis=2)
    vv = jnp.concatenate([jnp.pad(vb, pad)[:, :-1], vb], axis=2)
    blk = jnp.arange(nb)[:, None]
    qpos = blk * WINDOW + jnp.arange(WINDOW)[None, :]
    kpos = (blk - 1) * WINDOW + jnp.arange(2 * WINDOW)[None, :]
    diff = qpos[:, :, None] - kpos[:, None, :]
    mask = (diff >= 0) & (diff <= WINDOW) & (kpos[:, None, :] >= 0)
    o = sink_attention(qb, kk, vv, mask[None, :, None, None], sinks)
    return o.reshape(B, S, D_ATT)


def swa_sample(q, k, v, k_buf, v_buf, sinks):
    Bd, T = q.shape[0], q.shape[1]
    wb = k_buf.shape[1]
    kk = jnp.concatenate([k_buf, k], axis=1)
    vv = jnp.concatenate([v_buf, v], axis=1)
    qpos = PAST_LEN + jnp.arange(T)
    kpos = PAST_LEN - wb + jnp.arange(wb + T)
    diff = qpos[:, None] - kpos[None, :]
    mask = (diff >= 0) & (diff <= WINDOW)
    qg = q.reshape(Bd, T, ATT_KV_HEADS, ATT_GROUP, ATT_HEAD_DIM)
    o = sink_attention(qg, kk, vv, mask[None, None, None], sinks)
    return o.reshape(Bd, T, D_ATT), kk[:, -wb:], vv[:, -wb:]


def causal_conv(xbc, buf, w, b):
    L = xbc.shape[1]
    xp = jnp.concatenate([buf, xbc], axis=1)
    y = sum(xp[:, j:j + L] * w[j] for j in range(CONV_WIDTH)) + b
    return jax.nn.silu(y), xp[:, -(CONV_WIDTH - 1):]


def ssd_scan(x, dt, A, Bm, Cm, S0):
    f32 = jnp.float32
    Bsz, L, H, P = x.shape
    G, N = Bm.shape[2], Bm.shape[3]
    R = H // G
    Q = SSM_CHUNK if L % SSM_CHUNK == 0 else L
    nc = L // Q

    def chunks(t):
        t = t.astype(f32).reshape((Bsz, nc, Q) + t.shape[2:])
        return jnp.moveaxis(t, 1, 0)

    xs = (chunks(x.reshape(Bsz, L, G, R, P)), chunks(dt.reshape(Bsz, L, G, R)), chunks(Bm), chunks(Cm))
    A_gr = A.astype(f32).reshape(G, R)
    tril = jnp.tril(jnp.ones((Q, Q), dtype=bool))

    def step(S, inp):
        xq, dq, bq, cq = inp
        a = jnp.cumsum(dq * A_gr, axis=1)
        at = jnp.moveaxis(a, 1, -1)
        dk = jnp.moveaxis(dq, 1, -1)
        seg = jnp.where(tril, at[..., :, None] - at[..., None, :], -jnp.inf)
        cb = jnp.einsum('bqgn,bkgn->bgqk', cq, bq)
        w = jnp.exp(seg) * cb[:, :, None] * dk[..., None, :]
        y = jnp.einsum('bgrqk,bkgrp->bqgrp', w, xq)
        y = y + jnp.einsum('bqgn,bgrpn->bqgrp', cq, S) * jnp.exp(a)[..., None]
        wk = jnp.exp(at[..., -1:] - at) * dk
        S = S * jnp.exp(at[..., -1])[..., None, None] + jnp.einsum('bgrk,bkgrp,bkgn->bgrpn', wk, xq, bq)
        return S, y

    S, ys = lax.scan(step, S0.astype(f32).reshape(Bsz, G, R, P, N), xs)
    y = jnp.moveaxis(ys, 0, 1).reshape(Bsz, L, H, P)
    return y, S.reshape(Bsz, H, P, N)


def mlstm_scan(q, k, v, ig, fg, C0, n0, m0):
    f32 = jnp.float32
    Bsz, L, H, _ = q.shape
    Q = MLSTM_CHUNK if L % MLSTM_CHUNK == 0 else L
    nc = L // Q

    def chunks(t):
        t = t.astype(f32).reshape((Bsz, nc, Q) + t.shape[2:])
        return jnp.moveaxis(t, 1, 0)

    xs = (chunks(q), chunks(k), chunks(v), chunks(ig), chunks(fg))
    tril = jnp.tril(jnp.ones((Q, Q), dtype=bool))

    def step(carry, inp):
        C, n, m = carry
        qc, kc, vc, ic, fc = inp
        b = jnp.moveaxis(jnp.cumsum(jax.nn.log_sigmoid(fc), axis=1), 1, -1)
        it = jnp.moveaxis(ic, 1, -1)
        logw = jnp.where(tril, b[..., :, None] - b[..., None, :] + it[..., None, :], -jnp.inf)
        log_inter = b + m[..., None]
        mt = jnp.maximum(log_inter, jnp.max(logw, axis=-1))
        sw = jnp.exp(logw - mt[..., None]) * jnp.einsum('bqhd,bkhd->bhqk', qc, kc)
        g = jnp.exp(log_inter - mt)
        gq = jnp.moveaxis(g, -1, 1)[..., None]
        num = jnp.einsum('bhqk,bkhe->bqhe', sw, vc) + jnp.einsum('bqhd,bhde->bqhe', qc, C) * gq
        den = jnp.sum(sw, axis=-1) + jnp.einsum('bqhd,bhd->bhq', qc, n) * g
        h = num / jnp.moveaxis(jnp.maximum(jnp.abs(den), jnp.exp(-mt)), -1, 1)[..., None]
        m_new = mt[..., -1]
        wk = jnp.exp(b[..., -1:] - b + it - m_new[..., None])
        g_end = jnp.exp(b[..., -1] + m - m_new)
        C = C * g_end[..., None, None] + jnp.einsum('bhk,bkhd,bkhe->bhde', wk, kc, vc)
        n = n * g_end[..., None] + jnp.einsum('bhk,bkhd->bhd', wk, kc)
        return (C, n, m_new), h

    (C, n, m), hs = lax.scan(step, (C0.astype(f32), n0.astype(f32), m0.astype(f32)), xs)
    return jnp.moveaxis(hs, 0, 1).reshape(Bsz, L, H, v.shape[-1]), C, n, m


def token_mixers(u, lp, kv_buf, conv_buf, ssm0, C0, n0, m0):
    f32 = jnp.float32
    Bsz, L, _ = u.shape
    prompt = kv_buf is None
    proj = u @ lp['w_in']
    (q, k, v, z, xbc, dt, mq, mk, mv, mo, mi, mf) = jnp.split(proj, _split_points(), axis=-1)

    pos = jnp.arange(L, dtype=jnp.int32) + (0 if prompt else PAST_LEN)
    q = partial_rope(q.reshape(Bsz, L, ATT_HEADS, ATT_HEAD_DIM), pos)
    k = partial_rope(k.reshape(Bsz, L, ATT_KV_HEADS, ATT_HEAD_DIM), pos)
    v = v.reshape(Bsz, L, ATT_KV_HEADS, ATT_HEAD_DIM)
    if prompt:
        att = swa_prompt(q, k, v, lp['sinks'])
        keep = min(WINDOW, L)
        k_new, v_new = k[:, L - keep:], v[:, L - keep:]
    else:
        att, k_new, v_new = swa_sample(q, k, v, kv_buf[0], kv_buf[1], lp['sinks'])

    if prompt:
        conv_buf = jnp.zeros((Bsz, CONV_WIDTH - 1, CONV_DIM), xbc.dtype)
        ssm0 = jnp.zeros((Bsz, SSM_HEADS, SSM_HEAD_DIM, D_STATE), f32)
    xbc, conv_new = causal_conv(xbc, conv_buf, lp['conv_w'], lp['conv_b'])
    xs, bm, cm = jnp.split(xbc, [D_SSM, D_SSM + SSM_GROUPS * D_STATE], axis=-1)
    xs = xs.reshape(Bsz, L, SSM_HEADS, SSM_HEAD_DIM)
    dt = jax.nn.softplus(dt.astype(f32) + lp['dt_bias'].astype(f32))
    A = -jnp.exp(lp['a_log'].astype(f32))
    y, ssm_new = ssd_scan(xs, dt, A, bm.reshape(Bsz, L, SSM_GROUPS, D_STATE),
                          cm.reshape(Bsz, L, SSM_GROUPS, D_STATE), ssm0)
    y = y + lp['d_skip'].astype(f32)[:, None] * xs.astype(f32)
    y = y.reshape(Bsz, L, D_SSM) * jax.nn.silu(z.astype(f32))
    y = rms_norm(y.reshape(Bsz, L, SSM_GROUPS, D_SSM // SSM_GROUPS),
                 lp['ssm_norm'].reshape(SSM_GROUPS, D_SSM // SSM_GROUPS)).reshape(Bsz, L, D_SSM)

    if prompt:
        C0 = jnp.zeros((Bsz, MLSTM_HEADS, MLSTM_HEAD_DIM, MLSTM_HEAD_DIM), f32)
        n0 = jnp.zeros((Bsz, MLSTM_HEADS, MLSTM_HEAD_DIM), f32)
        m0 = jnp.zeros((Bsz, MLSTM_HEADS), f32)
    hd = (Bsz, L, MLSTM_HEADS, MLSTM_HEAD_DIM)
    h, C_new, n_new, m_new = mlstm_scan(
        mq.reshape(hd), mk.reshape(hd) * (MLSTM_HEAD_DIM ** -0.5), mv.reshape(hd),
        mi.astype(f32) + lp['igate_b'].astype(f32), mf.astype(f32) + lp['fgate_b'].astype(f32), C0, n0, m0)
    h = rms_norm(h, lp['mlstm_norm'].reshape(MLSTM_HEADS, MLSTM_HEAD_DIM)).reshape(Bsz, L, D_MLSTM)
    h = h * jax.nn.sigmoid(mo.astype(f32))

    mix = jnp.concatenate([att, y.astype(u.dtype), h.astype(u.dtype)], axis=-1) @ lp['w_out']
    return mix, (k_new, v_new, conv_new, ssm_new, C_new, n_new, m_new)


def decoder_layer(x, lp, kv_buf, conv_buf, ssm0, C0, n0, m0):
    mix, new_state = token_mixers(rms_norm(x, lp['norm_mix']), lp, kv_buf, conv_buf, ssm0, C0, n0, m0)
    x = x + mix
    u = rms_norm(x, lp['norm_mlp'])
    x = x + jnp.square(jax.nn.relu(u @ lp['w_up'])) @ lp['w_down']
    return x, new_state


def setup_inputs(seed: int = 0) -> dict:
    key = jax.random.key(seed)
    ks = jax.random.split(key, 32)
    f32 = jnp.float32
    win = min(WINDOW, PAST_LEN)

    def nrm(k, shape, s):
        return jax.random.normal(k, shape, f32) * s

    dt0 = jnp.exp(jax.random.uniform(ks[14], (DEPTH, SSM_HEADS), f32, math.log(1e-3), math.log(1e-1)))
    return {
        'x_prompt': nrm(ks[0], (BATCH, SEQ, D_MODEL), 1.0),
        'x_sample': nrm(ks[1], (DEC_BATCH, DEC_SEQ, D_MODEL), 1.0),
        'cache_swa_k': nrm(ks[2], (DEPTH, DEC_BATCH, win, ATT_KV_HEADS, ATT_HEAD_DIM), 1.0),
        'cache_swa_v': nrm(ks[3], (DEPTH, DEC_BATCH, win, ATT_KV_HEADS, ATT_HEAD_DIM), 1.0),
        'state_conv': nrm(ks[4], (DEPTH, DEC_BATCH, CONV_WIDTH - 1, CONV_DIM), 1.0),
        'state_ssm': nrm(ks[5], (DEPTH, DEC_BATCH, SSM_HEADS, SSM_HEAD_DIM, D_STATE), 0.3),
        'state_mlstm_C': nrm(ks[6], (DEPTH, DEC_BATCH, MLSTM_HEADS, MLSTM_HEAD_DIM, MLSTM_HEAD_DIM), 0.3),
        'state_mlstm_n': nrm(ks[7], (DEPTH, DEC_BATCH, MLSTM_HEADS, MLSTM_HEAD_DIM), 0.3),
        'state_mlstm_m': nrm(ks[8], (DEPTH, DEC_BATCH, MLSTM_HEADS), 1.0),
        'w_norm_mix': 1.0 + nrm(ks[9], (DEPTH, D_MODEL), 0.02),
        'w_in': nrm(ks[10], (DEPTH, D_MODEL, IN_WIDTH), D_MODEL ** -0.5),
        'attn_sinks': nrm(ks[11], (DEPTH, ATT_KV_HEADS, ATT_GROUP), 0.5),
        'conv_w': nrm(ks[12], (DEPTH, CONV_WIDTH, CONV_DIM), CONV_WIDTH ** -0.5),
        'conv_b': nrm(ks[13], (DEPTH, CONV_DIM), 0.02),
        'dt_bias': dt0 + jnp.log(-jnp.expm1(-dt0)),
        'a_log': jnp.log(jax.random.uniform(ks[15], (DEPTH, SSM_HEADS), f32, 1.0, 16.0)),
        'd_skip': 1.0 + nrm(ks[16], (DEPTH, SSM_HEADS), 0.1),
        'w_norm_ssm': 1.0 + nrm(ks[17], (DEPTH, D_SSM), 0.02),
        'igate_b': nrm(ks[18], (DEPTH, MLSTM_HEADS), 0.1),
        'fgate_b': jnp.linspace(3.0, 6.0, MLSTM_HEADS, dtype=f32)[None, :] + nrm(ks[19], (DEPTH, MLSTM_HEADS), 0.1),
        'w_norm_mlstm': 1.0 + nrm(ks[20], (DEPTH, D_MLSTM), 0.02),
        'w_out': nrm(ks[21], (DEPTH, D_MIX, D_MODEL), D_MIX ** -0.5),
        'w_norm_mlp': 1.0 + nrm(ks[22], (DEPTH, D_MODEL), 0.02),
        'w_up': nrm(ks[23], (DEPTH, D_MODEL, D_FF), D_MODEL ** -0.5),
        'w_down': nrm(ks[24], (DEPTH, D_FF, D_MODEL), D_FF ** -0.5),
        'w_norm_final': 1.0 + nrm(ks[25], (D_MODEL,), 0.02),
    }


def reference(x_prompt, x_sample, cache_swa_k, cache_swa_v, state_conv, state_ssm, state_mlstm_C,
              state_mlstm_n, state_mlstm_m, w_norm_mix, w_in, attn_sinks, conv_w, conv_b, dt_bias, a_log,
              d_skip, w_norm_ssm, igate_b, fgate_b, w_norm_mlstm, w_out, w_norm_mlp, w_up, w_down,
              w_norm_final):
    hp, hs = x_prompt, x_sample
    st_p, st_s = [], []
    for l in range(DEPTH):
        lp = {'norm_mix': w_norm_mix[l], 'w_in': w_in[l], 'sinks': attn_sinks[l], 'conv_w': conv_w[l],
              'conv_b': conv_b[l], 'dt_bias': dt_bias[l], 'a_log': a_log[l], 'd_skip': d_skip[l],
              'ssm_norm': w_norm_ssm[l], 'igate_b': igate_b[l], 'fgate_b': fgate_b[l],
              'mlstm_norm': w_norm_mlstm[l], 'w_out': w_out[l], 'norm_mlp': w_norm_mlp[l],
              'w_up': w_up[l], 'w_down': w_down[l]}
        hp, sp = decoder_layer(hp, lp, None, None, None, None, None, None)
        hs, ss = decoder_layer(hs, lp, (cache_swa_k[l], cache_swa_v[l]), state_conv[l], state_ssm[l],
                               state_mlstm_C[l], state_mlstm_n[l], state_mlstm_m[l])
        st_p.append(sp)
        st_s.append(ss)
    y_prompt = rms_norm(hp, w_norm_final)
    y_sample = rms_norm(hs, w_norm_final)
    p_k, p_v, p_conv, p_ssm, p_C, p_n, p_m = [jnp.stack([s[i] for s in st_p]) for i in range(7)]
    s_k, s_v, s_conv, s_ssm, s_C, s_n, s_m = [jnp.stack([s[i] for s in st_s]) for i in range(7)]
    return (y_prompt, y_sample, p_k, p_v, p_conv, p_ssm, p_C, p_n, p_m,
            s_k, s_v, s_conv, s_ssm, s_C, s_n, s_m)
```

```python
import numpy as np
import concourse.bass as bass
import concourse.mybir as mybir
from concourse.bass_utils import run_bass_kernel_spmd
from contextlib import ExitStack

F32 = mybir.dt.float32
BF16 = mybir.dt.bfloat16
AF = mybir.ActivationFunctionType
ALU = mybir.AluOpType
AX = mybir.AxisListType

NCORES = 4
D = 2048
SEQ = 2048
ST = 256
NT = ST // 128
NST = SEQ // ST
RS = 32
L = 2
INW = 5400
O_Q, O_K, O_V, O_Z, O_XBC, O_DT, O_MQ, O_MK, O_MV, O_MO, O_MI, O_MF = (
    0, 512, 640, 768, 1792, 3328, 3344, 3856, 4368, 4880, 5392, 5396)
EPS = 1e-6
PAST = 8192
WB = 256
SEM_LIMIT = 8000


class Buf:
    def __init__(self, name, t=None, parent=None):
        self.name = name
        self.t = t
        self.parent = parent
        self.w = None
        self.r = {}

    def root(self):
        return self.parent.root() if self.parent is not None else self

    def __getitem__(self, idx):
        return self.t[idx]


class Eng:
    def __init__(self, fw, name, h):
        self.fw = fw
        self.name = name
        self.h = h
        self.sem = fw.new_sem(name)
        self.own = {id(self.sem)}
        self.cnt = 0
        self.seen = {}

    def _wait(self, ev):
        if ev is None:
            return
        sem, val = ev
        key = id(sem)
        if self.name == "pe" and key in self.own:
            return
        if key in self.fw.dma_sems:
            val = self.fw.dma_sems[key]
        if self.seen.get(key, 0) >= val:
            return
        self.h.wait_ge(sem, val)
        self.seen[key] = val


def _rnd(n):
    return 32 if n <= 32 else (64 if n <= 64 else 128)


class PEProxy:
    def __init__(self, fw):
        self.fw = fw
        self.last = None

    def _mode(self, st_ap, kind):
        shp = tuple(st_ap.shape)
        m = 1
        for v in shp[1:]:
            m *= v
        mode = (_rnd(shp[0]), _rnd(m), str(st_ap.dtype), kind)
        pe = self.fw.pe
        tiled = mode[0] < 128 or mode[1] < 128
        if self.last is not None and (mode != self.last or tiled) and pe.cnt > 0:
            pe.h.wait_ge(pe.sem, pe.cnt)
        self.last = mode

    def matmul(self, out, lhsT=None, rhs=None, **kw):
        self._mode(lhsT, "m")
        return self.fw.pe.h.matmul(out, lhsT=lhsT, rhs=rhs, **kw)

    def transpose(self, out=None, in_=None, identity=None):
        self._mode(in_, "t")
        return self.fw.pe.h.transpose(out=out, in_=in_, identity=identity)


class FW:
    def __init__(self, nc):
        self.nc = nc
        self.es = ExitStack()
        self.nsem = 0
        self.dma_sems = {}
        self.dma_sem_objs = {}
        self.qpool = {}
        self.pe = Eng(self, "pe", nc.tensor)
        self.act = Eng(self, "act", nc.scalar)
        self.dve = Eng(self, "dve", nc.vector)
        self.pool = Eng(self, "pool", nc.gpsimd)
        self.sp = Eng(self, "sp", nc.sync)
        self.engs = [self.pe, self.act, self.dve, self.pool, self.sp]
        self.pe_proxy = PEProxy(self)
        self.out_events = []
        self.all_sems_used = []

    def new_sem(self, name):
        self.nsem += 1
        return self.es.enter_context(self.nc.semaphore(f"s_{name}_{self.nsem}"))

    def sb(self, name, shape, dt, es=None):
        t = (es or self.es).enter_context(self.nc.sbuf_tensor(name, list(shape), dt))
        return Buf(name, t)

    def ps(self, name, shape, dt=F32):
        t = self.es.enter_context(self.nc.psum_tensor(name, list(shape), dt))
        return Buf(name, t)

    def _deps(self, eng, reads, writes):
        reads = [b.root() for b in reads]
        writes = [b.root() for b in writes]
        for b in reads:
            eng._wait(b.w)
        for b in writes:
            eng._wait(b.w)
            for ev in list(b.r.values()):
                eng._wait(ev)

    def op(self, eng, fn, reads=(), writes=()):
        reads = [b.root() for b in reads]
        writes = [b.root() for b in writes]
        self._deps(eng, reads, writes)
        inst = fn(self.pe_proxy if eng is self.pe else eng.h)
        if eng.cnt >= SEM_LIMIT:
            eng.sem = self.new_sem(eng.name)
            eng.own.add(id(eng.sem))
            eng.cnt = 0
        eng.cnt += 1
        inst.then_inc(eng.sem, 1)
        ev = (eng.sem, eng.cnt)
        for b in writes:
            b.w = ev
            b.r = {}
        for b in reads:
            if b not in writes:
                b.r[id(ev[0])] = ev
        return ev

    def dsem(self, name):
        return [None, name]

    def dma(self, q, out, in_, sem=None, reads=(), writes=(), is_out=False):
        pool = self.qpool.setdefault(q.name, {"sems": [None] * (12 if q.name == "sp" else 4), "i": 0})
        slot = pool["i"] % len(pool["sems"])
        pool["i"] += 1
        sem = pool["sems"][slot]
        if sem is not None:
            prev = self.dma_sems.get(id(sem), 0)
            q._wait((sem, prev))
            if prev >= SEM_LIMIT:
                sem = None
        if sem is None:
            sem = self.new_sem("d" + q.name)
            pool["sems"][slot] = sem
        reads = [b.root() for b in reads]
        writes = [b.root() for b in writes]
        self._deps(q, reads, writes)
        inst = q.h.dma_start(out=out, in_=in_)
        k = id(sem)
        self.dma_sems[k] = self.dma_sems.get(k, 0) + 16
        self.dma_sem_objs[k] = sem
        inst.then_inc(sem, 16)
        ev = (sem, self.dma_sems[k])
        for b in writes:
            b.w = ev
            b.r = {}
        for b in reads:
            b.r[id(ev[0])] = ev
        if is_out:
            self.out_events.append(ev)
        return ev

    def barrier(self):
        evs = [(e.sem, e.cnt) for e in self.engs if e.cnt > 0]
        evs += [(self.dma_sem_objs[k], v) for k, v in self.dma_sems.items()]
        for e in [self.pe, self.act, self.dve, self.pool, self.sp]:
            for ev in evs:
                e._wait(ev)

    def finish(self, close=True):
        for k, v in self.dma_sems.items():
            self.sp._wait((self.dma_sem_objs[k], v))
        if close:
            self.es.close()


def bc(ap, shape):
    return ap.broadcast_to(list(shape))


class _Stop(Exception):
    pass


DEBUG_STOP = [None]


def build():
    nc = bass.Bass("TRN2", target_bir_lowering=False)
    fw = FW(nc)
    stopped = False
    try:
        _build(nc, fw)
    except _Stop:
        stopped = True
    fw.finish(close=not stopped)
    return nc


def _build(nc, fw):
    def ck(name):
        if DEBUG_STOP[0] == name:
            raise _Stop()
    PE, ACT, DVE, POOL, SP = fw.pe, fw.act, fw.dve, fw.pool, fw.sp
    dbg_sem = [None]

    def dump(name, buf, ap, shape):
        if DEBUG_STOP[0] is None:
            return
        if dbg_sem[0] is None:
            dbg_sem[0] = fw.dsem("dbg")
        o = nc.dram_tensor("dbg_" + name, list(shape), F32, kind="ExternalOutput").ap()
        fw.dma(POOL, o, ap, dbg_sem[0], reads=[buf], is_out=True)

    def din(name, shape):
        return nc.dram_tensor(name, list(shape), F32, kind="ExternalInput").ap()

    def dout(name, shape):
        return nc.dram_tensor(name, list(shape), F32, kind="ExternalOutput").ap()

    xp = din("xp", [SEQ, D]); xsm = din("xsm", [RS, D])
    cache_k = din("cache_k", [L, RS, 128, 128]); cache_v = din("cache_v", [L, RS, 128, 128])
    st_conv = din("st_conv", [L, RS, 3, 1536]); st_ssm = din("st_ssm", [L, RS, 1024, 128])
    st_C = din("st_C", [L, RS, 4, 128, 128]); st_n = din("st_n", [L, RS, 4, 128]); st_m = din("st_m", [L, RS, 4])
    w_norm_mix = din("w_norm_mix", [L, D]); w_in = din("w_in", [L, D, INW]); sinks = din("sinks", [L, 8])
    conv_w = din("conv_w", [L, 4, 1536]); conv_b = din("conv_b", [L, 1536]); dt_bias = din("dt_bias", [L, 16])
    a_log = din("a_log", [L, 16]); d_skip = din("d_skip", [L, 16]); w_norm_ssm = din("w_norm_ssm", [L, 1024])
    igb = din("igb", [L, 4]); fgb = din("fgb", [L, 4]); w_norm_ml = din("w_norm_ml", [L, 512])
    w_out = din("w_out", [L, D, D]); w_norm_mlp = din("w_norm_mlp", [L, D]); w_up = din("w_up", [L, D, 4 * D])
    w_down = din("w_down", [L, 4 * D, D]); w_norm_final = din("w_norm_final", [D])
    c_ident = din("c_ident", [128, 128]); c_tri = din("c_tri", [128, 128]); c_triT = din("c_triT", [128, 128])
    c_U = din("c_U", [128, 128]); c_negqk = din("c_negqk", [128, 128]); c_negkq = din("c_negkq", [128, 128])
    c_e127 = din("c_e127", [128, 128]); c_cos = din("c_cos", [128, 17, 8]); c_sin = din("c_sin", [128, 17, 8])
    c_oh = din("c_oh", [RS, RS, 128]); c_pad = din("c_pad", [128, 8])

    y_p = dout("y_p", [SEQ, D]); y_s = dout("y_s", [RS, D])
    p_k = dout("p_k", [L, 128, 128]); p_v = dout("p_v", [L, 128, 128]); p_conv = dout("p_conv", [L, 3, 1536])
    p_ssm = dout("p_ssm", [L, 1024, 128]); p_C = dout("p_C", [L, 4, 128, 128]); p_n = dout("p_n", [L, 4, 128])
    p_m = dout("p_m", [L, 4])
    s_k = dout("s_k", [L, RS, 128, 128]); s_v = dout("s_v", [L, RS, 128, 128]); s_conv = dout("s_conv", [L, RS, 3, 1536])
    s_ssm = dout("s_ssm", [L, RS, 1024, 128]); s_C = dout("s_C", [L, RS, 4, 128, 128]); s_n = dout("s_n", [L, RS, 4, 128])
    s_m = dout("s_m", [L, RS, 4])

    sem_c = fw.dsem("dc")
    sem_o = fw.dsem("do")
    sem_x = fw.dsem("dx")
    sem_g = fw.dsem("dg")
    sem_st = fw.dsem("dst")

    pbig = fw.ps("pbig", [128, 2048], F32)
    pd = [fw.ps(f"pd{i}", [128, 512], F32) for i in range(2)]
    ptb = fw.ps("ptb", [128, 1024], BF16)
    pm = fw.ps("pm", [128, 512], F32)
    pq = [pbig]

    def cload(name, src, shape, dt=F32, cast=None):
        b = fw.sb(name, shape, F32)
        fw.dma(SP, b[:], src, sem_c, writes=[b])
        if cast is not None:
            b2 = fw.sb(name + "_b", shape, cast)
            fw.op(DVE, lambda h: h.tensor_copy(out=b2[:], in_=b[:]), reads=[b], writes=[b2])
            return b, b2
        return b

    ident_f, ident_b = cload("ident", c_ident, [128, 128], cast=BF16)
    tri_f, tri_b = cload("tri", c_tri, [128, 128], cast=BF16)
    triT_f, triT_b = cload("triT", c_triT, [128, 128], cast=BF16)
    U_f = cload("U", c_U, [128, 128])
    negqk = cload("negqk", c_negqk, [128, 128])
    negkq = cload("negkq", c_negkq, [128, 128])
    e127 = cload("e127", c_e127, [128, 128])
    ones_f = fw.sb("ones_f", [128, 128], F32)
    fw.op(DVE, lambda h: h.memset(ones_f[:], 1.0), writes=[ones_f])
    cosT = cload("cosT", c_cos, [128, 17, 8]); sinT = cload("sinT", c_sin, [128, 17, 8])
    padt = cload("padt", c_pad, [128, 8])

    gbc = fw.sb("gbc", [128, 2048], F32)
    lp = {}

    def bload(name, src_row, n):
        b = fw.sb(name, [128, n], F32)
        fw.dma(SP, b[:], src_row.partition_broadcast(128), sem_c, writes=[b])
        return b

    for l in range(L):
        d = {}
        d["dtb"] = bload(f"dtb{l}", dt_bias[l, :], 16)
        al = bload(f"al{l}", a_log[l, :], 16)
        A = fw.sb(f"A{l}", [128, 16], F32)
        fw.op(ACT, lambda h: h.activation(out=A[:], in_=al[:], func=AF.Exp), reads=[al], writes=[A])
        fw.op(DVE, lambda h: h.tensor_scalar(out=A[:], in0=A[:], scalar1=-1.0, scalar2=None, op0=ALU.mult), reads=[A], writes=[A])
        d["A"] = A
        dsk = bload(f"dsk{l}", d_skip[l, :], 16)
        dD = fw.sb(f"dD{l}", [128, 16, 128], BF16)
        fw.op(DVE, lambda h: h.tensor_tensor(out=dD[:], in0=bc(ident_f[:].unsqueeze(1), [128, 16, 128]),
                                             in1=bc(dsk[:].unsqueeze(2), [128, 16, 128]), op=ALU.mult),
              reads=[ident_f, dsk], writes=[dD])
        d["dD"] = dD
        sk = bload(f"sk{l}", sinks[l, :], 8)
        esk = fw.sb(f"esk{l}", [128, 8], F32)
        fw.op(ACT, lambda h: h.activation(out=esk[:], in_=sk[:], func=AF.Exp), reads=[sk], writes=[esk])
        d["esk"] = esk
        gb = fw.sb(f"gb{l}", [128, 8], F32)
        fw.dma(SP, gb[:, 0:4], igb[l, :].partition_broadcast(128), sem_c, writes=[gb])
        fw.dma(SP, gb[:, 4:8], fgb[l, :].partition_broadcast(128), sem_c, writes=[gb])
        d["gb"] = gb
        cw = fw.sb(f"cw{l}", [128, 12, 4], F32)
        cb = fw.sb(f"cb{l}", [128, 12], F32)
        with nc.allow_non_contiguous_dma(reason="small conv params"):
            for j in range(4):
                fw.dma(SP, cw[:, :, j], conv_w[l, j, :].rearrange("(b p) -> p b", p=128), sem_c, writes=[cw])
            fw.dma(SP, cb[:], conv_b[l, :].rearrange("(b p) -> p b", p=128), sem_c, writes=[cb])
        d["cw"] = cw; d["cb"] = cb
        lp[l] = d

    class S:
        pass
    carry = []
    for l in range(L):
        s = S()
        s.ST = fw.sb(f"ST{l}", [128, 1024], F32)
        s.STb = fw.sb(f"STb{l}", [128, 1024], BF16)
        s.Cn = fw.sb(f"Cn{l}", [128, 4, 129], F32)
        s.Cnb = fw.sb(f"Cnb{l}", [128, 4, 129], BF16)
        s.mrow = fw.sb(f"mrow{l}", [128, 4], F32)
        s.kprev = fw.sb(f"kprev{l}", [128, 2, 128], BF16)
        s.vprev = fw.sb(f"vprev{l}", [128, 2, 65], BF16)
        s.xcarry = fw.sb(f"xcar{l}", [128, 12, 3], F32)
        carry.append(s)

    NWB = 2
    wbuf = [fw.sb(f"wb{i}", [128, 16, WB], BF16) for i in range(NWB)]
    wsem = [fw.dsem(f"w{i}") for i in range(NWB)]
    wctr = [0]

    def wload(src2d, ncols):
        i = wctr[0] % NWB
        wctr[0] += 1
        b = wbuf[i]
        fw.dma(POOL, b[:, :, 0:ncols], src2d.rearrange("(k p) n -> p k n", p=128), wsem[i], writes=[b])
        return b

    pdc = [0]

    def next_pd():
        p = pd[pdc[0] % 2]
        pdc[0] += 1
        return p

    def dense_tok(actT, tts, W2d, ncols, consume):
        for c0 in range(0, ncols, WB):
            nb = min(WB, ncols - c0)
            wb = wload(W2d[:, c0:c0 + nb], nb)
            for ti, (t0, tsz) in enumerate(tts):
                p = next_pd()
                for k in range(16):
                    fw.op(PE, lambda h: h.matmul(p[0:tsz, 0:nb], lhsT=actT[:, k, t0:t0 + tsz], rhs=wb[:, k, 0:nb],
                                                 start=(k == 0), stop=(k == 15)), reads=[actT, wb], writes=[p])
                consume(ti, c0, nb, p)

    def dense_feat(actT, ntok, W2d, ncols, consume):
        for c0 in range(0, ncols, WB):
            nb = min(WB, ncols - c0)
            wb = wload(W2d[:, c0:c0 + nb], nb)
            for s0 in range(0, nb, 128):
                p = next_pd()
                for k in range(16):
                    fw.op(PE, lambda h: h.matmul(p[:, 0:ntok], lhsT=wb[:, k, s0:s0 + 128], rhs=actT[:, k, 0:ntok],
                                                 start=(k == 0), stop=(k == 15)), reads=[actT, wb], writes=[p])
                consume((c0 + s0) // 128, p)

    small = fw.sb("small", [128, 64], F32)

    def rmsnorm_to(xt_ap, xbuf, np_, gain_ap, out_ap, outbuf, tmpbuf, nfeat, col=0):
        ss = small[0:np_, col:col + 1]
        fw.op(ACT, lambda h: h.activation(out=tmpbuf[0:np_, 0:nfeat], in_=xt_ap, func=AF.Square, accum_out=ss),
              reads=[xbuf], writes=[tmpbuf, small])
        fw.op(DVE, lambda h: h.tensor_scalar(out=ss, in0=ss, scalar1=1.0 / nfeat, scalar2=EPS, op0=ALU.mult, op1=ALU.add),
              reads=[small], writes=[small])
        fw.op(ACT, lambda h: h.activation(out=ss, in_=ss, func=AF.Sqrt), reads=[small], writes=[small])
        fw.op(DVE, lambda h: h.reciprocal(out=ss, in_=ss), reads=[small], writes=[small])
        fw.op(DVE, lambda h: h.scalar_tensor_tensor(out=out_ap, in0=xt_ap, scalar=ss, in1=gain_ap, op0=ALU.mult, op1=ALU.mult),
              reads=[xbuf, small, gbc], writes=[outbuf])

    def transpose_to(src_ap, srcbuf, np_, ncols, dst_fn, dstbuf):
        nblk = ncols // 128
        for g0 in range(0, nblk, 8):
            g1 = min(nblk, g0 + 8)
            for j in range(g0, g1):
                fw.op(PE, lambda h: h.transpose(out=ptb[:, (j - g0) * 128:(j - g0) * 128 + np_],
                                                in_=src_ap[:, j * 128:(j + 1) * 128], identity=ident_b[0:np_, 0:np_]),
                      reads=[srcbuf, ident_b], writes=[ptb])
            for j in range(g0, g1):
                fw.op(ACT, lambda h: h.copy(out=dst_fn(j), in_=ptb[:, (j - g0) * 128:(j - g0) * 128 + np_]),
                      reads=[ptb], writes=[dstbuf])

    cs = ExitStack()
    o_f = fw.sb("o_f", [128, 8, 65], F32)
    pT = fw.sb("pT", [128, 2, 512], BF16)
    rden = fw.sb("rden", [128, 8], F32)

    def swa_block(l, qT, qTbuf, kcur, kcurbuf, vcur, vcurbuf, kprev, kprevbuf, vprev, vprevbuf, has_prev, out_ap, outbuf):
        for kv in range(2):
            blocks = ([("p", kprev, kprevbuf, vprev, vprevbuf, triT_b)] if has_prev else []) + [("c", kcur, kcurbuf, vcur, vcurbuf, tri_b)]
            for bi, (nm, kf, kb, vf, vb, msk) in enumerate(blocks):
                for hh in range(4):
                    h_ = kv * 4 + hh
                    half = (h_ % 2) * 64
                    fw.op(PE, lambda h: h.matmul(pbig[:, (bi * 4 + hh) * 128:(bi * 4 + hh + 1) * 128],
                                                 lhsT=kf(kv)[half:half + 64, :], rhs=qT(h_ // 2)[half:half + 64, :],
                                                 start=True, stop=True), reads=[kb, qTbuf], writes=[pbig])
                fw.op(ACT, lambda h: h.activation(out=pT[:, bi, :], in_=pbig[:, bi * 512:(bi + 1) * 512], func=AF.Exp, scale=0.125),
                      reads=[pbig], writes=[pT])
                fw.op(DVE, lambda h: h.tensor_tensor(out=pT[:, bi, :].rearrange("p (a b) -> p a b", a=4),
                                                     in0=pT[:, bi, :].rearrange("p (a b) -> p a b", a=4),
                                                     in1=bc(msk[:].unsqueeze(1), [128, 4, 128]), op=ALU.mult),
                      reads=[pT, msk], writes=[pT])
            for hh in range(4):
                for bi, (nm, kf, kb, vf, vb, msk) in enumerate(blocks):
                    fw.op(PE, lambda h: h.matmul(pm[:, hh * 65:(hh + 1) * 65], lhsT=pT[:, bi, hh * 128:(hh + 1) * 128], rhs=vf(kv),
                                                 start=(bi == 0), stop=(bi == len(blocks) - 1)), reads=[pT, vb], writes=[pm])
            fw.op(ACT, lambda h: h.copy(out=o_f[:, kv * 4:(kv + 1) * 4, :], in_=pm[:, 0:260].rearrange("p (a b) -> p a b", a=4)),
                  reads=[pm], writes=[o_f])
        esk = lp[l]["esk"]
        fw.op(DVE, lambda h: h.tensor_tensor(out=rden[:], in0=o_f[:, :, 64], in1=esk[:], op=ALU.add), reads=[o_f, esk], writes=[rden])
        fw.op(DVE, lambda h: h.reciprocal(out=rden[:], in_=rden[:]), reads=[rden], writes=[rden])
        fw.op(DVE, lambda h: h.tensor_tensor(out=out_ap.rearrange("p (a b) -> p a b", a=8), in0=o_f[:, :, 0:64],
                                             in1=bc(rden[:].unsqueeze(2), [128, 8, 64]), op=ALU.mult),
              reads=[o_f, rden], writes=[outbuf])

    dtA = fw.sb("dtA", [128, 16], F32)
    a_sb = fw.sb("a_sb", [128, 16], F32)
    ea = fw.sb("ea", [128, 16], F32)
    eal = fw.sb("eal", [128, 16], F32)
    wk = fw.sb("wk", [128, 16], F32)
    rseg = fw.sb("rseg", [128, 4, 128], F32)
    LT = fw.sb("LT", [128, 16, 128], BF16)
    cbm = fw.sb("cbm", [128, 2, 128], BF16)
    x_dt = fw.sb("x_dt", [128, 1024], BF16)
    xw = fw.sb("xw", [128, 1024], BF16)
    ytmp = fw.sb("ytmp", [128, 1024], F32)
    yy = fw.sb("yy", [128, 1024], F32)
    zs = ytmp

    def ssd_chunk(l, st, x_tok, x_tokbuf, B_tok, B_tokbuf, BT, CT, BCbuf, dt, dtbuf, z_tok, zbuf, out_ap, outbuf):
        A = lp[l]["A"]; dD = lp[l]["dD"]
        fw.op(DVE, lambda h: h.tensor_tensor(out=dtA[:], in0=dt, in1=A[:], op=ALU.mult), reads=[dtbuf, A], writes=[dtA])
        fw.op(PE, lambda h: h.matmul(pm[:, 0:16], lhsT=tri_f[:], rhs=dtA[:], start=True, stop=True), reads=[tri_f, dtA], writes=[pm])
        fw.op(PE, lambda h: h.matmul(pm[:, 16:32], lhsT=ones_f[:], rhs=dtA[:], start=True, stop=True), reads=[ones_f, dtA], writes=[pm])
        fw.op(ACT, lambda h: h.copy(out=a_sb[:], in_=pm[:, 0:16]), reads=[pm], writes=[a_sb])
        fw.op(ACT, lambda h: h.activation(out=ea[:], in_=pm[:, 0:16], func=AF.Exp), reads=[pm], writes=[ea])
        fw.op(ACT, lambda h: h.activation(out=eal[:], in_=pm[:, 16:32], func=AF.Exp), reads=[pm], writes=[eal])
        fw.op(DVE, lambda h: h.tensor_tensor(out=wk[:], in0=pm[:, 16:32], in1=a_sb[:], op=ALU.subtract), reads=[pm, a_sb], writes=[wk])
        fw.op(ACT, lambda h: h.activation(out=wk[:], in_=wk[:], func=AF.Exp), reads=[wk], writes=[wk])
        fw.op(DVE, lambda h: h.tensor_tensor(out=wk[:], in0=wk[:], in1=dt, op=ALU.mult), reads=[wk, dtbuf], writes=[wk])
        for i in range(4):
            fw.op(DVE, lambda h: h.tensor_tensor(out=rseg[:], in0=bc(tri_f[:].unsqueeze(1), [128, 4, 128]),
                                                 in1=bc(dtA[:, i * 4:(i + 1) * 4].unsqueeze(2), [128, 4, 128]), op=ALU.mult),
                  reads=[tri_f, dtA], writes=[rseg])
            fw.op(PE, lambda h: h.matmul(pbig[:, i * 512:(i + 1) * 512], lhsT=U_f[:],
                                         rhs=rseg[:].rearrange("p a b -> p (a b)"), start=True, stop=True),
                  reads=[U_f, rseg], writes=[pbig])
        fw.op(ACT, lambda h: h.activation(out=LT[:].rearrange("p a b -> p (a b)"), in_=pbig[:], func=AF.Exp), reads=[pbig], writes=[LT])
        for g in range(2):
            fw.op(PE, lambda h: h.matmul(pm[:, 64 + g * 128:64 + (g + 1) * 128], lhsT=BT(g), rhs=CT(g), start=True, stop=True),
                  reads=[BCbuf], writes=[pm])
        fw.op(DVE, lambda h: h.tensor_tensor(out=cbm[:], in0=pm[:, 64:320].rearrange("p (a b) -> p a b", a=2),
                                             in1=bc(tri_f[:].unsqueeze(1), [128, 2, 128]), op=ALU.mult),
              reads=[pm, tri_f], writes=[cbm])
        for g in range(2):
            fw.op(DVE, lambda h: h.tensor_tensor(out=LT[:, g * 8:(g + 1) * 8, :], in0=LT[:, g * 8:(g + 1) * 8, :],
                                                 in1=bc(cbm[:, g:g + 1, :], [128, 8, 128]), op=ALU.mult),
                  reads=[LT, cbm], writes=[LT])
        fw.op(DVE, lambda h: h.tensor_tensor(out=x_dt[:].rearrange("p (a b) -> p a b", a=16), in0=x_tok.rearrange("p (a b) -> p a b", a=16),
                                             in1=bc(dt.unsqueeze(2), [128, 16, 64]), op=ALU.mult),
              reads=[x_tokbuf, dtbuf], writes=[x_dt])
        fw.op(DVE, lambda h: h.tensor_tensor(out=xw[:].rearrange("p (a b) -> p a b", a=16), in0=x_tok.rearrange("p (a b) -> p a b", a=16),
                                             in1=bc(wk[:].unsqueeze(2), [128, 16, 64]), op=ALU.mult),
              reads=[x_tokbuf, wk], writes=[xw])
        for g in range(2):
            fw.op(PE, lambda h: h.matmul(pbig[:, g * 512:(g + 1) * 512], lhsT=CT(g), rhs=st.STb[:, g * 512:(g + 1) * 512], start=True, stop=True),
                  reads=[BCbuf, st.STb], writes=[pbig])
        for hh in range(16):
            fw.op(PE, lambda h: h.matmul(pbig[:, 1024 + hh * 64:1024 + (hh + 1) * 64], lhsT=LT[:, hh, :], rhs=x_dt[:, hh * 64:(hh + 1) * 64],
                                         start=True, stop=False), reads=[LT, x_dt], writes=[pbig])
            fw.op(PE, lambda h: h.matmul(pbig[:, 1024 + hh * 64:1024 + (hh + 1) * 64], lhsT=dD[:, hh, :], rhs=x_tok[:, hh * 64:(hh + 1) * 64],
                                         start=False, stop=True), reads=[dD, x_tokbuf], writes=[pbig])
        fw.op(DVE, lambda h: h.tensor_tensor(out=ytmp[:].rearrange("p (a b) -> p a b", a=16), in0=pbig[:, 0:1024].rearrange("p (a b) -> p a b", a=16),
                                             in1=bc(ea[:].unsqueeze(2), [128, 16, 64]), op=ALU.mult),
              reads=[pbig, ea], writes=[ytmp])
        fw.op(DVE, lambda h: h.tensor_tensor(out=yy[:], in0=pbig[:, 1024:2048], in1=ytmp[:], op=ALU.add), reads=[pbig, ytmp], writes=[yy])
        for g in range(2):
            fw.op(PE, lambda h: h.matmul(pd[g][:, :], lhsT=B_tok[:, g * 128:(g + 1) * 128], rhs=xw[:, g * 512:(g + 1) * 512], start=True, stop=True),
                  reads=[B_tokbuf, xw], writes=[pd[g]])
        fw.op(DVE, lambda h: h.tensor_tensor(out=st.ST[:].rearrange("p (a b) -> p a b", a=16), in0=st.ST[:].rearrange("p (a b) -> p a b", a=16),
                                             in1=bc(eal[:].unsqueeze(2), [128, 16, 64]), op=ALU.mult), reads=[st.ST, eal], writes=[st.ST])
        for g in range(2):
            fw.op(DVE, lambda h: h.tensor_tensor(out=st.ST[:, g * 512:(g + 1) * 512], in0=pd[g][:, :], in1=st.ST[:, g * 512:(g + 1) * 512], op=ALU.add),
                  reads=[pd[g], st.ST], writes=[st.ST])
        fw.op(ACT, lambda h: h.copy(out=st.STb[:], in_=st.ST[:]), reads=[st.ST], writes=[st.STb])
        fw.op(ACT, lambda h: h.activation(out=zs[:], in_=z_tok, func=AF.Silu), reads=[zbuf], writes=[zs])
        fw.op(DVE, lambda h: h.tensor_tensor(out=yy[:], in0=yy[:], in1=zs[:], op=ALU.mult), reads=[yy, zs], writes=[yy])
        for g in range(2):
            rmsnorm_to(yy[:, g * 512:(g + 1) * 512], yy, 128, gbc[:, g * 512:(g + 1) * 512], out_ap[:, g * 512:(g + 1) * 512], outbuf, ytmp, 512, col=8 + g)

    lfn = fw.sb("lfn", [128, 4], F32)
    nb_ = fw.sb("nb_", [128, 4], F32)
    nbl = fw.sb("nbl", [128, 4], F32)
    cc = fw.sb("cc", [128, 4], F32)
    Dc = fw.sb("Dc", [128, 4, 128], F32)
    cmx = fw.sb("cmx", [128, 4, 128], F32)
    cm = fw.sb("cm", [128, 4], F32)
    Mq = fw.sb("Mq", [128, 4], F32)
    negM = fw.sb("negM", [128, 4], F32)
    mt = fw.sb("mt", [128, 4], F32)
    gq = fw.sb("gq", [128, 4], F32)
    emt = fw.sb("emt", [128, 4], F32)
    mnew = fw.sb("mnew", [128, 4], F32)
    gend = fw.sb("gend", [128, 4], F32)
    wkm = fw.sb("wkm", [128, 4], F32)
    swe = fw.sb("swe", [128, 4, 128], F32)
    swT = fw.sb("swT", [128, 4, 128], BF16)
    tot = fw.sb("tot", [128, 4, 129], F32)
    ints = fw.sb("ints", [128, 4, 129], F32)
    hh_ = fw.sb("hh_", [128, 4, 128], F32)
    kwm = fw.sb("kwm", [128, 512], BF16)
    sg = fw.sb("sg", [128, 512], F32)
    negkq4 = fw.sb("negkq4", [128, 4, 128], F32)
    fw.op(DVE, lambda h: h.tensor_copy(out=negkq4[:], in_=bc(negkq[:].unsqueeze(1), [128, 4, 128])), reads=[negkq], writes=[negkq4])

    def mlstm_chunk(l, st, qT, kT, qkbuf, k_tok, v_aug, kvbuf, ig, fg, gbuf, mo, mobuf, out_ap, outbuf):
        fw.op(ACT, lambda h: h.activation(out=lfn[:], in_=fg, func=AF.Exp, scale=-1.0), reads=[gbuf], writes=[lfn])
        fw.op(ACT, lambda h: h.activation(out=lfn[:], in_=lfn[:], func=AF.Ln, bias=1.0), reads=[lfn], writes=[lfn])
        fw.op(PE, lambda h: h.matmul(pm[:, 0:4], lhsT=tri_f[:], rhs=lfn[:], start=True, stop=True), reads=[tri_f, lfn], writes=[pm])
        fw.op(PE, lambda h: h.matmul(pm[:, 4:8], lhsT=ones_f[:], rhs=lfn[:], start=True, stop=True), reads=[ones_f, lfn], writes=[pm])
        fw.op(ACT, lambda h: h.copy(out=nb_[:], in_=pm[:, 0:4]), reads=[pm], writes=[nb_])
        fw.op(ACT, lambda h: h.copy(out=nbl[:], in_=pm[:, 4:8]), reads=[pm], writes=[nbl])
        fw.op(DVE, lambda h: h.tensor_tensor(out=cc[:], in0=ig, in1=nb_[:], op=ALU.add), reads=[gbuf, nb_], writes=[cc])
        fw.op(DVE, lambda h: h.tensor_tensor(out=Dc[:], in0=bc(ident_f[:].unsqueeze(1), [128, 4, 128]), in1=bc(cc[:].unsqueeze(2), [128, 4, 128]), op=ALU.mult),
              reads=[ident_f, cc], writes=[Dc])
        fw.op(PE, lambda h: h.matmul(pd[0][:, :], lhsT=ones_f[:], rhs=Dc[:].rearrange("p a b -> p (a b)"), start=True, stop=True),
              reads=[ones_f, Dc], writes=[pd[0]])
        fw.op(DVE, lambda h: h.tensor_tensor(out=cmx[:], in0=pd[0][:, :].rearrange("p (a b) -> p a b", a=4), in1=bc(negqk[:].unsqueeze(1), [128, 4, 128]), op=ALU.add),
              reads=[pd[0], negqk], writes=[cmx])
        fw.op(DVE, lambda h: h.tensor_reduce(out=cm[:], in_=cmx[:], op=ALU.max, axis=AX.X), reads=[cmx], writes=[cm])
        fw.op(DVE, lambda h: h.tensor_tensor(out=Mq[:], in0=cm[:], in1=st.mrow[:], op=ALU.max), reads=[cm, st.mrow], writes=[Mq])
        fw.op(DVE, lambda h: h.tensor_tensor(out=mt[:], in0=Mq[:], in1=nb_[:], op=ALU.subtract), reads=[Mq, nb_], writes=[mt])
        fw.op(DVE, lambda h: h.tensor_scalar(out=negM[:], in0=Mq[:], scalar1=-1.0, scalar2=None, op0=ALU.mult), reads=[Mq], writes=[negM])
        fw.op(DVE, lambda h: h.tensor_tensor(out=Dc[:], in0=bc(ident_f[:].unsqueeze(1), [128, 4, 128]), in1=bc(negM[:].unsqueeze(2), [128, 4, 128]), op=ALU.mult),
              reads=[ident_f, negM], writes=[Dc])
        fw.op(PE, lambda h: h.matmul(pd[1][:, :], lhsT=ones_f[:], rhs=Dc[:].rearrange("p a b -> p (a b)"), start=True, stop=False),
              reads=[ones_f, Dc], writes=[pd[1]])
        fw.op(PE, lambda h: h.matmul(pd[1][:, :], lhsT=ident_f[:], rhs=negkq4[:].rearrange("p a b -> p (a b)"), start=False, stop=True),
              reads=[ident_f, negkq4], writes=[pd[1]])
        for hd in range(4):
            fw.op(ACT, lambda h: h.activation(out=swe[:, hd, :], in_=pd[1][:, hd * 128:(hd + 1) * 128], func=AF.Exp, bias=cc[:, hd:hd + 1]),
                  reads=[pd[1], cc], writes=[swe])
        for hd in range(4):
            fw.op(PE, lambda h: h.matmul(pd[0][:, hd * 128:(hd + 1) * 128], lhsT=kT(hd), rhs=qT(hd), start=True, stop=True), reads=qkbuf, writes=[pd[0]])
        fw.op(DVE, lambda h: h.tensor_tensor(out=swT[:], in0=pd[0][:, :].rearrange("p (a b) -> p a b", a=4), in1=swe[:], op=ALU.mult),
              reads=[pd[0], swe], writes=[swT])
        for hd in range(4):
            o = (hd // 2) * 512 + (hd % 2) * 129
            fw.op(PE, lambda h: h.matmul(pbig[:, o:o + 129], lhsT=swT[:, hd, :], rhs=v_aug(hd), start=True, stop=True), reads=[swT] + kvbuf, writes=[pbig])
            fw.op(PE, lambda h: h.matmul(pbig[:, 1024 + o:1024 + o + 129], lhsT=qT(hd), rhs=st.Cnb[:, hd, :], start=True, stop=True), reads=qkbuf + [st.Cnb], writes=[pbig])
        fw.op(DVE, lambda h: h.tensor_tensor(out=gq[:], in0=st.mrow[:], in1=Mq[:], op=ALU.subtract), reads=[st.mrow, Mq], writes=[gq])
        fw.op(ACT, lambda h: h.activation(out=gq[:], in_=gq[:], func=AF.Exp), reads=[gq], writes=[gq])
        fw.op(ACT, lambda h: h.activation(out=emt[:], in_=mt[:], func=AF.Exp, scale=-1.0), reads=[mt], writes=[emt])
        for hf in range(2):
            fw.op(DVE, lambda h: h.tensor_tensor(out=ints[:, hf * 2:hf * 2 + 2, :], in0=pbig[:, 1024 + hf * 512:1024 + hf * 512 + 258].rearrange("p (a b) -> p a b", a=2),
                                                 in1=bc(gq[:, hf * 2:hf * 2 + 2].unsqueeze(2), [128, 2, 129]), op=ALU.mult), reads=[pbig, gq], writes=[ints])
            fw.op(DVE, lambda h: h.tensor_tensor(out=tot[:, hf * 2:hf * 2 + 2, :], in0=pbig[:, hf * 512:hf * 512 + 258].rearrange("p (a b) -> p a b", a=2),
                                                 in1=ints[:, hf * 2:hf * 2 + 2, :], op=ALU.add), reads=[pbig, ints], writes=[tot])
        fw.op(DVE, lambda h: h.tensor_scalar(out=cm[:], in0=tot[:, :, 128], scalar1=-1.0, scalar2=None, op0=ALU.mult), reads=[tot], writes=[cm])
        fw.op(DVE, lambda h: h.tensor_tensor(out=cm[:], in0=cm[:], in1=tot[:, :, 128], op=ALU.max), reads=[tot, cm], writes=[cm])
        fw.op(DVE, lambda h: h.tensor_tensor(out=cm[:], in0=cm[:], in1=emt[:], op=ALU.max), reads=[cm, emt], writes=[cm])
        fw.op(DVE, lambda h: h.reciprocal(out=cm[:], in_=cm[:]), reads=[cm], writes=[cm])
        fw.op(DVE, lambda h: h.tensor_tensor(out=hh_[:], in0=tot[:, :, 0:128], in1=bc(cm[:].unsqueeze(2), [128, 4, 128]), op=ALU.mult), reads=[tot, cm], writes=[hh_])
        fw.op(DVE, lambda h: h.tensor_tensor(out=cmx[:], in0=hh_[:], in1=hh_[:], op=ALU.mult), reads=[hh_], writes=[cmx])
        fw.op(DVE, lambda h: h.tensor_reduce(out=cm[:], in_=cmx[:], op=ALU.add, axis=AX.X), reads=[cmx], writes=[cm])
        fw.op(DVE, lambda h: h.tensor_scalar(out=cm[:], in0=cm[:], scalar1=1.0 / 128, scalar2=EPS, op0=ALU.mult, op1=ALU.add), reads=[cm], writes=[cm])
        fw.op(ACT, lambda h: h.activation(out=cm[:], in_=cm[:], func=AF.Sqrt), reads=[cm], writes=[cm])
        fw.op(DVE, lambda h: h.reciprocal(out=cm[:], in_=cm[:]), reads=[cm], writes=[cm])
        fw.op(DVE, lambda h: h.tensor_tensor(out=hh_[:], in0=hh_[:], in1=bc(cm[:].unsqueeze(2), [128, 4, 128]), op=ALU.mult), reads=[hh_, cm], writes=[hh_])
        fw.op(DVE, lambda h: h.tensor_tensor(out=hh_[:].rearrange("p a b -> p (a b)"), in0=hh_[:].rearrange("p a b -> p (a b)"), in1=gbc[:, 1024:1536], op=ALU.mult),
              reads=[hh_, gbc], writes=[hh_])
        fw.op(ACT, lambda h: h.activation(out=sg[:], in_=mo, func=AF.Sigmoid), reads=[mobuf], writes=[sg])
        fw.op(DVE, lambda h: h.tensor_tensor(out=out_ap, in0=hh_[:].rearrange("p a b -> p (a b)"), in1=sg[:], op=ALU.mult), reads=[hh_, sg], writes=[outbuf])
        fw.op(PE, lambda h: h.matmul(pm[:, 8:12], lhsT=e127[:], rhs=mt[:], start=True, stop=True), reads=[e127, mt], writes=[pm])
        fw.op(ACT, lambda h: h.copy(out=mnew[:], in_=pm[:, 8:12]), reads=[pm], writes=[mnew])
        fw.op(DVE, lambda h: h.tensor_tensor(out=wkm[:], in0=cc[:], in1=nbl[:], op=ALU.subtract), reads=[cc, nbl], writes=[wkm])
        fw.op(DVE, lambda h: h.tensor_tensor(out=wkm[:], in0=wkm[:], in1=mnew[:], op=ALU.subtract), reads=[wkm, mnew], writes=[wkm])
        fw.op(ACT, lambda h: h.activation(out=wkm[:], in_=wkm[:], func=AF.Exp), reads=[wkm], writes=[wkm])
        fw.op(DVE, lambda h: h.tensor_tensor(out=gend[:], in0=st.mrow[:], in1=nbl[:], op=ALU.subtract), reads=[st.mrow, nbl], writes=[gend])
        fw.op(DVE, lambda h: h.tensor_tensor(out=gend[:], in0=gend[:], in1=mnew[:], op=ALU.subtract), reads=[gend, mnew], writes=[gend])
        fw.op(ACT, lambda h: h.activation(out=gend[:], in_=gend[:], func=AF.Exp), reads=[gend], writes=[gend])
        fw.op(DVE, lambda h: h.tensor_tensor(out=kwm[:].rearrange("p (a b) -> p a b", a=4), in0=k_tok.rearrange("p (a b) -> p a b", a=4),
                                             in1=bc(wkm[:].unsqueeze(2), [128, 4, 128]), op=ALU.mult), reads=kvbuf + [wkm], writes=[kwm])
        for hd in range(4):
            o = (hd // 2) * 512 + (hd % 2) * 129
            fw.op(PE, lambda h: h.matmul(pbig[:, o:o + 129], lhsT=kwm[:, hd * 128:(hd + 1) * 128], rhs=v_aug(hd), start=True, stop=True), reads=[kwm] + kvbuf, writes=[pbig])
        fw.op(DVE, lambda h: h.tensor_tensor(out=st.Cn[:], in0=st.Cn[:], in1=bc(gend[:].unsqueeze(2), [128, 4, 129]), op=ALU.mult), reads=[st.Cn, gend], writes=[st.Cn])
        for hf in range(2):
            fw.op(DVE, lambda h: h.tensor_tensor(out=st.Cn[:, hf * 2:hf * 2 + 2, :], in0=pbig[:, hf * 512:hf * 512 + 258].rearrange("p (a b) -> p a b", a=2),
                                                 in1=st.Cn[:, hf * 2:hf * 2 + 2, :], op=ALU.add), reads=[pbig, st.Cn], writes=[st.Cn])
        fw.op(ACT, lambda h: h.copy(out=st.Cnb[:], in_=st.Cn[:]), reads=[st.Cn], writes=[st.Cnb])
        fw.op(ACT, lambda h: h.copy(out=st.mrow[:], in_=mnew[:]), reads=[mnew], writes=[st.mrow])

    rtmp = fw.sb("rtmp", [128, 10, 16], F32)

    def rope(buf, ap3, np_, nh, ti):
        x1 = ap3[:, :, 0:8]; x2 = ap3[:, :, 8:16]
        cs_ = bc(cosT[0:np_, ti:ti + 1, :], [np_, nh, 8]); sn_ = bc(sinT[0:np_, ti:ti + 1, :], [np_, nh, 8])
        t = rtmp[0:np_, 0:nh, :]
        fw.op(DVE, lambda h: h.tensor_tensor(out=t[:, :, 0:8], in0=x2, in1=sn_, op=ALU.mult), reads=[buf, sinT], writes=[rtmp])
        fw.op(DVE, lambda h: h.tensor_tensor(out=t[:, :, 8:16], in0=x1, in1=sn_, op=ALU.mult), reads=[buf, sinT], writes=[rtmp])
        fw.op(DVE, lambda h: h.tensor_tensor(out=ap3[:, :, 0:16].rearrange("p a (c d) -> p a c d", c=2),
                                             in0=ap3[:, :, 0:16].rearrange("p a (c d) -> p a c d", c=2),
                                             in1=bc(cosT[0:np_, ti:ti + 1, :].unsqueeze(2), [np_, nh, 2, 8]), op=ALU.mult), reads=[buf, cosT], writes=[buf])
        fw.op(DVE, lambda h: h.tensor_tensor(out=x1, in0=x1, in1=t[:, :, 0:8], op=ALU.subtract), reads=[buf, rtmp], writes=[buf])
        fw.op(DVE, lambda h: h.tensor_tensor(out=x2, in0=x2, in1=t[:, :, 8:16], op=ALU.add), reads=[buf, rtmp], writes=[buf])

    def softplus(buf, ap):
        fw.op(ACT, lambda h: h.activation(out=ap, in_=ap, func=AF.Exp), reads=[buf], writes=[buf])
        fw.op(ACT, lambda h: h.activation(out=ap, in_=ap, func=AF.Ln, bias=1.0), reads=[buf], writes=[buf])

    def load_gain(row_ap, c0, n):
        fw.dma(SP, gbc[:, c0:c0 + n], row_ap.partition_broadcast(128), sem_g, writes=[gbc])


    sst = fw.sb("sst", [128, 8, 128], F32)

    def emit_state_out(l, st, d_ssm, d_C, d_n, d_m):
        for g0 in range(0, 8, 4):
            for j in range(g0, g0 + 4):
                fw.op(PE, lambda h: h.transpose(out=pd[0][:, (j - g0) * 128:(j - g0 + 1) * 128], in_=st.ST[:, j * 128:(j + 1) * 128], identity=ident_f[:]),
                      reads=[st.ST, ident_f], writes=[pd[0]])
            fw.op(ACT, lambda h: h.copy(out=sst[:, g0:g0 + 4, :], in_=pd[0][:, :].rearrange("p (a b) -> p a b", a=4)), reads=[pd[0]], writes=[sst])
        fw.dma(SP, d_ssm.rearrange("(j p) n -> p j n", p=128), sst[:], sem_o, reads=[sst], is_out=True)
        fw.dma(SP, d_C.rearrange("h d e -> d h e"), st.Cn[:, :, 0:128], sem_o, reads=[st.Cn], is_out=True)
        with nc.allow_non_contiguous_dma(reason="tiny state out"):
            fw.dma(SP, d_n.rearrange("h d -> d h"), st.Cn[:, :, 128], sem_o, reads=[st.Cn], is_out=True)
        fw.dma(SP, d_m.unsqueeze(0), st.mrow[0:1, :], sem_o, reads=[st.mrow], is_out=True)

    def load_state(l, st, r):
        fw.dma(SP, sst[:], st_ssm[l, r].rearrange("(j p) n -> p j n", p=128), sem_st, writes=[sst])
        for g0 in range(0, 8, 4):
            for j in range(g0, g0 + 4):
                fw.op(PE, lambda h: h.transpose(out=pd[0][:, (j - g0) * 128:(j - g0 + 1) * 128], in_=sst[:, j, :], identity=ident_f[:]),
                      reads=[sst, ident_f], writes=[pd[0]])
            fw.op(ACT, lambda h: h.copy(out=st.ST[:, g0 * 128:(g0 + 4) * 128], in_=pd[0][:, :]), reads=[pd[0]], writes=[st.ST])
        fw.op(ACT, lambda h: h.copy(out=st.STb[:], in_=st.ST[:]), reads=[st.ST], writes=[st.STb])
        fw.dma(SP, st.Cn[:, :, 0:128], st_C[l, r].rearrange("h d e -> d h e"), sem_st, writes=[st.Cn])
        with nc.allow_non_contiguous_dma(reason="tiny state in"):
            fw.dma(SP, st.Cn[:, :, 128], st_n[l, r].rearrange("h d -> d h"), sem_st, writes=[st.Cn])
        fw.dma(SP, st.mrow[:], st_m[l, r, :].partition_broadcast(128), sem_st, writes=[st.mrow])
        fw.op(ACT, lambda h: h.copy(out=st.Cnb[:], in_=st.Cn[:]), reads=[st.Cn], writes=[st.Cnb])

    pes = ExitStack()
    xres = fw.sb("xres", [128, NT, D], F32, pes)
    actT = fw.sb("actT", [128, 16, ST], BF16, pes)
    big16 = fw.sb("big16", [128, NT * 2048], BF16, pes)
    utok = fw.sb("utok", [128, D], BF16, pes)
    sq = fw.sb("sq", [128, D], F32, pes)
    qkf = fw.sb("qkf", [128, NT, 640], F32, pes)
    qkb = fw.sb("qkb", [128, 768], BF16, pes)
    qT = fw.sb("qT", [128, 4, ST], BF16, pes)
    kTd = fw.sb("kTd", [128, 2, ST], BF16, pes)
    vaug = fw.sb("vaug", [128, NT, 2, 65], BF16, pes)
    ztok = fw.sb("ztok", [128, NT, 1024], BF16, pes)
    dtt = fw.sb("dtt", [128, NT, 16], F32, pes)
    mktok = fw.sb("mktok", [128, NT, 512], BF16, pes)
    mvaug = fw.sb("mvaug", [128, NT, 4, 129], BF16, pes)
    motok = fw.sb("motok", [128, NT, 512], BF16, pes)
    gates = fw.sb("gates", [128, NT, 8], F32, pes)
    xraw = fw.sb("xraw", [128, ST + 3], F32, pes)
    cacc = fw.sb("cacc", [128, ST], F32, pes)
    xcT = fw.sb("xcT", [128, 12, ST], BF16, pes)
    mqT = fw.sb("mqT", [128, 4, ST], BF16, pes)
    mkT = fw.sb("mkT", [128, 4, ST], BF16, pes)
    xtok = fw.sb("xtok", [128, 1024], BF16, pes)
    btok = fw.sb("btok", [128, 256], BF16, pes)
    ostage = sq

    fw.op(DVE, lambda h: h.memset(vaug[:], 1.0), writes=[vaug])
    fw.op(DVE, lambda h: h.memset(mvaug[:], 1.0), writes=[mvaug])
    for l in range(L):
        s = carry[l]
        fw.op(DVE, lambda h: h.memset(s.ST[:], 0.0), writes=[s.ST])
        fw.op(DVE, lambda h: h.memset(s.STb[:], 0.0), writes=[s.STb])
        fw.op(DVE, lambda h: h.memset(s.Cn[:], 0.0), writes=[s.Cn])
        fw.op(DVE, lambda h: h.memset(s.Cnb[:], 0.0), writes=[s.Cnb])
        fw.op(DVE, lambda h: h.memset(s.mrow[:], 0.0), writes=[s.mrow])
        fw.op(DVE, lambda h: h.memset(s.xcarry[:], 0.0), writes=[s.xcarry])

    mix_tok = big16[:].rearrange("p (a b) -> p a b", a=NT)
    ck("consts")
    hTg = big16[:].rearrange("p (a b) -> p a b", a=16)
    HT = Buf("HT", hTg, parent=big16)

    def norm_to_actT(nt, tsz, gain_row, xfn):
        load_gain(gain_row, 0, D)
        for tt in range(nt):
            rmsnorm_to(xfn(tt), xres, tsz, gbc[0:tsz, :], utok[0:tsz, :], utok, sq, D, col=tt)
            transpose_to(utok[0:tsz, :], utok, tsz, D, lambda j: actT[:, j, tt * 128:tt * 128 + tsz], actT)

    for stn in range(NST):
        t0g = stn * ST
        for tt in range(NT):
            fw.dma(SP, xres[:, tt, :], xp[t0g + tt * 128:t0g + (tt + 1) * 128, :], sem_x, writes=[xres])
        tts = [(tt * 128, 128) for tt in range(NT)]
        for l in range(L):
            st = carry[l]
            P = lp[l]
            W = w_in[l]
            norm_to_actT(NT, 128, w_norm_mix[l, :], lambda tt: xres[:, tt, :])
            ck("norm")
            load_gain(w_norm_ssm[l, :], 0, 1024)
            load_gain(w_norm_ml[l, :], 1024, 512)

            def c_qkv(ti, c0, nb, p):
                if c0 < 512:
                    fw.op(ACT, lambda h: h.copy(out=qkf[:, ti, c0:c0 + nb], in_=p[:, 0:nb]), reads=[p], writes=[qkf])
                else:
                    fw.op(ACT, lambda h: h.copy(out=qkf[:, ti, 512:640], in_=p[:, 0:128]), reads=[p], writes=[qkf])
                    fw.op(ACT, lambda h: h.copy(out=vaug[:, ti, :, 0:64], in_=p[:, 128:256].rearrange("p (a b) -> p a b", a=2)), reads=[p], writes=[vaug])
                    rope(qkf, qkf[:, ti, :].rearrange("p (a b) -> p a b", b=64), 128, 10, stn * NT + ti)
                    fw.op(DVE, lambda h: h.tensor_copy(out=qkb[:, 0:512], in_=qkf[:, ti, 0:512]), reads=[qkf], writes=[qkb])
                    fw.op(DVE, lambda h: h.tensor_copy(out=qkb[:, 512:768].rearrange("p (a c b) -> p a c b", a=2, c=2),
                                                       in_=bc(qkf[:, ti, 512:640].rearrange("p (a b) -> p a b", a=2).unsqueeze(2), [128, 2, 2, 64])),
                          reads=[qkf], writes=[qkb])
                    transpose_to(qkb[:, 0:512], qkb, 128, 512, lambda j: qT[:, j, ti * 128:(ti + 1) * 128], qT)
                    transpose_to(qkb[:, 512:768], qkb, 128, 256, lambda j: kTd[:, j, ti * 128:(ti + 1) * 128], kTd)
                    if stn == NST - 1 and ti == NT - 1:
                        fw.op(ACT, lambda h: h.copy(out=ostage[:, 0:128], in_=qkf[:, ti, 512:640]), reads=[qkf], writes=[ostage])
                        fw.op(ACT, lambda h: h.copy(out=ostage[:, 128:256], in_=p[:, 128:256]), reads=[p], writes=[ostage])
                        fw.dma(SP, p_k[l], ostage[:, 0:128], sem_o, reads=[ostage], is_out=True)
                        fw.dma(SP, p_v[l], ostage[:, 128:256], sem_o, reads=[ostage], is_out=True)
            dense_tok(actT, tts, W[:, O_Q:O_Q + 768], 768, c_qkv)
            ck("qkv")

            def c_z(ti, c0, nb, p):
                fw.op(ACT, lambda h: h.copy(out=ztok[:, ti, c0:c0 + nb], in_=p[:, 0:nb]), reads=[p], writes=[ztok])
            dense_tok(actT, tts, W[:, O_Z:O_Z + 1024], 1024, c_z)

            def c_dt(ti, c0, nb, p):
                fw.op(DVE, lambda h: h.tensor_tensor(out=dtt[:, ti, :], in0=p[:, 0:16], in1=P["dtb"][:], op=ALU.add), reads=[p, P["dtb"]], writes=[dtt])
                softplus(dtt, dtt[:, ti, :])
            dense_tok(actT, tts, W[:, O_DT:O_DT + 16], 16, c_dt)

            def c_mk(ti, c0, nb, p):
                fw.op(ACT, lambda h: h.activation(out=mktok[:, ti, c0:c0 + nb], in_=p[:, 0:nb], func=AF.Copy, scale=float(128 ** -0.5)), reads=[p], writes=[mktok])
                if c0 + nb == 512:
                    transpose_to(mktok[:, ti, :], mktok, 128, 512, lambda j: mkT[:, j, ti * 128:(ti + 1) * 128], mkT)
            dense_tok(actT, tts, W[:, O_MK:O_MK + 512], 512, c_mk)

            def c_mv(ti, c0, nb, p):
                h0 = c0 // 128
                fw.op(ACT, lambda h: h.copy(out=mvaug[:, ti, h0:h0 + nb // 128, 0:128], in_=p[:, 0:nb].rearrange("p (a b) -> p a b", b=128)), reads=[p], writes=[mvaug])
            dense_tok(actT, tts, W[:, O_MV:O_MV + 512], 512, c_mv)

            def c_mo(ti, c0, nb, p):
                fw.op(ACT, lambda h: h.copy(out=motok[:, ti, c0:c0 + nb], in_=p[:, 0:nb]), reads=[p], writes=[motok])
            dense_tok(actT, tts, W[:, O_MO:O_MO + 512], 512, c_mo)

            def c_g(ti, c0, nb, p):
                fw.op(DVE, lambda h: h.tensor_tensor(out=gates[:, ti, :], in0=p[:, 0:8], in1=P["gb"][:], op=ALU.add), reads=[p, P["gb"]], writes=[gates])
            dense_tok(actT, tts, W[:, O_MI:O_MI + 8], 8, c_g)
            ck("tokproj")

            def c_mq(cb_, p):
                fw.op(ACT, lambda h: h.copy(out=mqT[:, cb_, :], in_=p[:, 0:ST]), reads=[p], writes=[mqT])
            dense_feat(actT, ST, W[:, O_MQ:O_MQ + 512], 512, c_mq)

            def c_xbc(cb_, p):
                fw.op(ACT, lambda h: h.copy(out=xraw[:, 0:3], in_=st.xcarry[:, cb_, :]), reads=[st.xcarry], writes=[xraw])
                fw.op(ACT, lambda h: h.copy(out=xraw[:, 3:ST + 3], in_=p[:, 0:ST]), reads=[p], writes=[xraw])
                fw.op(ACT, lambda h: h.copy(out=st.xcarry[:, cb_, :], in_=xraw[:, ST:ST + 3]), reads=[xraw], writes=[st.xcarry])
                cw = P["cw"]
                fw.op(DVE, lambda h: h.tensor_scalar(out=cacc[:], in0=xraw[:, 0:ST], scalar1=cw[:, cb_, 0:1], scalar2=None, op0=ALU.mult), reads=[xraw, cw], writes=[cacc])
                for j in range(1, 4):
                    fw.op(DVE, lambda h: h.scalar_tensor_tensor(out=cacc[:], in0=xraw[:, j:j + ST], scalar=cw[:, cb_, j:j + 1], in1=cacc[:], op0=ALU.mult, op1=ALU.add),
                          reads=[xraw, cw, cacc], writes=[cacc])
                fw.op(ACT, lambda h: h.activation(out=xcT[:, cb_, :], in_=cacc[:], func=AF.Silu, bias=P["cb"][:, cb_:cb_ + 1]), reads=[cacc, P["cb"]], writes=[xcT])
            dense_feat(actT, ST, W[:, O_XBC:O_XBC + 1536], 1536, c_xbc)
            ck("proj")
            if stn == NST - 1:
                with nc.allow_non_contiguous_dma(reason="tiny conv state out"):
                    for j in range(3):
                        fw.dma(SP, p_conv[l, j, :].rearrange("(b p) -> p b", p=128), st.xcarry[:, :, j], sem_o, reads=[st.xcarry], is_out=True)

            for c in range(NT):
                sl = slice(c * 128, (c + 1) * 128)
                has_prev = not (stn == 0 and c == 0)
                if c == 0:
                    kpf = lambda kv: st.kprev[:, kv, :]; kpb = st.kprev
                    vpf = lambda kv: st.vprev[:, kv, :]; vpb = st.vprev
                else:
                    kpf = (lambda cc_: (lambda kv: kTd[:, kv, (cc_ - 1) * 128:cc_ * 128]))(c); kpb = kTd
                    vpf = (lambda cc_: (lambda kv: vaug[:, cc_ - 1, kv, :]))(c); vpb = vaug
                swa_block(l, lambda j: qT[:, j, sl], qT, lambda kv: kTd[:, kv, sl], kTd, lambda kv: vaug[:, c, kv, :], vaug,
                          kpf, kpb, vpf, vpb, has_prev, mix_tok[:, c, 0:512], big16)
                ck("swa")
                if c == NT - 1:
                    fw.op(ACT, lambda h: h.copy(out=st.kprev[:], in_=kTd[:, :, sl]), reads=[kTd], writes=[st.kprev])
                    fw.op(ACT, lambda h: h.copy(out=st.vprev[:], in_=vaug[:, NT - 1, :, :]), reads=[vaug], writes=[st.vprev])
                for j in range(8):
                    fw.op(PE, lambda h: h.transpose(out=ptb[:, j * 128:(j + 1) * 128], in_=xcT[:, j, sl], identity=ident_b[:]), reads=[xcT, ident_b], writes=[ptb])
                fw.op(ACT, lambda h: h.copy(out=xtok[:], in_=ptb[:, 0:1024]), reads=[ptb], writes=[xtok])
                for j in range(2):
                    fw.op(PE, lambda h: h.transpose(out=ptb[:, j * 128:(j + 1) * 128], in_=xcT[:, 8 + j, sl], identity=ident_b[:]), reads=[xcT, ident_b], writes=[ptb])
                fw.op(ACT, lambda h: h.copy(out=btok[:], in_=ptb[:, 0:256]), reads=[ptb], writes=[btok])
                ssd_chunk(l, st, xtok[:], xtok, btok[:], btok, lambda g: xcT[:, 8 + g, sl], lambda g: xcT[:, 10 + g, sl], xcT,
                          dtt[:, c, :], dtt, ztok[:, c, :], ztok, mix_tok[:, c, 512:1536], big16)
                ck("ssd")
                mlstm_chunk(l, st, lambda hd: mqT[:, hd, sl], lambda hd: mkT[:, hd, sl], [mqT, mkT], mktok[:, c, :], lambda hd: mvaug[:, c, hd, :], [mktok, mvaug],
                            gates[:, c, 0:4], gates[:, c, 4:8], gates, motok[:, c, :], motok, mix_tok[:, c, 1536:2048], big16)
                if DEBUG_STOP[0] == "mlstm":
                    dump("mix", big16, mix_tok[:, c, :], [128, 2048])
                    dump("dtt", dtt, dtt[:, c, :], [128, 16])
                    dump("xtok", xtok, xtok[:], [128, 1024])
                    dump("ST", st.ST, st.ST[:], [128, 1024])
                    dump("Cn", st.Cn, st.Cn[:], [128, 4, 129])
                    dump("mrow", st.mrow, st.mrow[:], [128, 4])
                ck("mlstm")
            if stn == NST - 1:
                emit_state_out(l, st, p_ssm[l], p_C[l], p_n[l], p_m[l])

            for tt in range(NT):
                transpose_to(mix_tok[:, tt, :], big16, 128, D, lambda j: actT[:, j, tt * 128:(tt + 1) * 128], actT)

            def c_res(ti, c0, nb, p):
                fw.op(DVE, lambda h: h.tensor_tensor(out=xres[:, ti, c0:c0 + nb], in0=p[:, 0:nb], in1=xres[:, ti, c0:c0 + nb], op=ALU.add), reads=[p, xres], writes=[xres])
            dense_tok(actT, tts, w_out[l], D, c_res)
            ck("wout")

            norm_to_actT(NT, 128, w_norm_mlp[l, :], lambda tt: xres[:, tt, :])
            for g in range(4):
                def c_up(cb_, p):
                    fw.op(ACT, lambda h: h.activation(out=sq[:, 0:ST], in_=p[:, 0:ST], func=AF.Relu), reads=[p], writes=[sq])
                    fw.op(DVE, lambda h: h.tensor_tensor(out=hTg[:, cb_, :], in0=sq[:, 0:ST], in1=sq[:, 0:ST], op=ALU.mult), reads=[sq], writes=[big16])
                dense_feat(actT, ST, w_up[l][:, g * 2048:(g + 1) * 2048], 2048, c_up)
                dense_tok(HT, tts, w_down[l][g * 2048:(g + 1) * 2048, :], D, c_res)
            ck("layer")

        load_gain(w_norm_final, 0, D)
        for tt in range(NT):
            rmsnorm_to(xres[:, tt, :], xres, 128, gbc[:, :], ostage[:, :], ostage, sq, D, col=tt)
            fw.dma(SP, y_p[t0g + tt * 128:t0g + (tt + 1) * 128, :], ostage[:, :], sem_o, reads=[ostage], is_out=True)
        ck("st")
        ck("st%d" % stn)


    fw.barrier()
    pes.close()
    ck("prompt")
    ses = ExitStack()
    xrs = fw.sb("xrs", [RS, D], F32, ses)
    actS = fw.sb("actS", [128, 16, RS], BF16, ses)
    utS = fw.sb("utS", [RS, D], BF16, ses)
    sqS = fw.sb("sqS", [RS, D], F32, ses)
    sall = fw.sb("sall", [RS, INW], F32, ses)
    mixs = fw.sb("mixs", [RS, D], BF16, ses)
    hsT = fw.sb("hsT", [128, 16, RS], BF16, ses)
    oh = fw.sb("oh", [RS, RS, 128], F32, ses)
    cj = fw.sb("cj", [RS, 1536], F32, ses)
    wj = fw.sb("wj", [RS, 1536], F32, ses)
    xcs = sqS
    ckf = fw.sb("ckf", [128, 128], F32, ses)
    cvf = fw.sb("cvf", [128, 128], F32, ses)
    ckb = fw.sb("ckb", [128, 256], BF16, ses)
    r_qT = fw.sb("r_qT", [128, 4, 128], BF16, ses)
    r_kTd = fw.sb("r_kTd", [128, 2, 128], BF16, ses)
    r_v = fw.sb("r_v", [128, 2, 65], BF16, ses)
    r_kp = fw.sb("r_kp", [128, 2, 128], BF16, ses)
    r_vp = fw.sb("r_vp", [128, 2, 65], BF16, ses)
    r_x = fw.sb("r_x", [128, 1024], BF16, ses)
    r_b = fw.sb("r_b", [128, 256], BF16, ses)
    r_bc = fw.sb("r_bc", [128, 4, 128], BF16, ses)
    r_dt = fw.sb("r_dt", [128, 16], F32, ses)
    r_z = fw.sb("r_z", [128, 1024], BF16, ses)
    r_mqT = fw.sb("r_mqT", [128, 4, 128], BF16, ses)
    r_mkT = fw.sb("r_mkT", [128, 4, 128], BF16, ses)
    r_mk = fw.sb("r_mk", [128, 512], BF16, ses)
    r_mv = fw.sb("r_mv", [128, 4, 129], BF16, ses)
    r_g = fw.sb("r_g", [128, 8], F32, ses)
    r_mo = fw.sb("r_mo", [128, 512], BF16, ses)
    r_mix = fw.sb("r_mix", [128, D], BF16, ses)
    sem_s = fw.dsem("dss")
    sem_m = fw.dsem("dsm")

    fw.dma(SP, xrs[:], xsm[:, :], sem_s, writes=[xrs])
    fw.dma(SP, oh[:], c_oh[:, :, :], sem_s, writes=[oh])
    fw.op(DVE, lambda h: h.memset(r_v[:], 1.0), writes=[r_v])
    fw.op(DVE, lambda h: h.memset(r_vp[:], 1.0), writes=[r_vp])
    fw.op(DVE, lambda h: h.memset(r_mv[:], 1.0), writes=[r_mv])
    ttS = [(0, RS)]

    def stage_tok(r, c0, n, dst_ap, dstbuf, scale=None):
        for o in range(0, n, 512):
            m = min(512, n - o)
            fw.op(PE, lambda h: h.matmul(pd[1][:, 0:m], lhsT=oh[:, r, :], rhs=sall[:, c0 + o:c0 + o + m], start=True, stop=True), reads=[oh, sall], writes=[pd[1]])
            fw.op(ACT, lambda h: h.copy(out=dst_ap(o, m), in_=pd[1][:, 0:m]), reads=[pd[1]], writes=[dstbuf])

    def stage_feat(r, srcbuf, src_ap, dst_ap, dstbuf):
        fw.op(PE, lambda h: h.matmul(pd[1][:, 0:128], lhsT=src_ap, rhs=oh[:, r, :], start=True, stop=True), reads=[oh, srcbuf], writes=[pd[1]])
        fw.op(ACT, lambda h: h.copy(out=dst_ap, in_=pd[1][:, 0:128]), reads=[pd[1]], writes=[dstbuf])

    def norm_to_actS(gain_row):
        load_gain(gain_row, 0, D)
        rmsnorm_to(xrs[:, :], xrs, RS, gbc[0:RS, :], utS[:, :], utS, sqS, D, col=0)
        transpose_to(utS[:, :], utS, RS, D, lambda j: actS[:, j, :], actS)

    def c_res_s(ti, c0, nb, p):
        fw.op(DVE, lambda h: h.tensor_tensor(out=xrs[:, c0:c0 + nb], in0=p[0:RS, 0:nb], in1=xrs[:, c0:c0 + nb], op=ALU.add), reads=[p, xrs], writes=[xrs])

    for l in range(L):
        st = carry[l]
        P = lp[l]
        norm_to_actS(w_norm_mix[l, :])
        load_gain(w_norm_ssm[l, :], 0, 1024)
        load_gain(w_norm_ml[l, :], 1024, 512)

        def c_all(ti, c0, nb, p):
            fw.op(ACT, lambda h: h.copy(out=sall[:, c0:c0 + nb], in_=p[0:RS, 0:nb]), reads=[p], writes=[sall])
        dense_tok(actS, ttS, w_in[l], INW, c_all)
        rope(sall, sall[:, 0:640].rearrange("p (a b) -> p a b", b=64), RS, 10, 16)
        fw.op(DVE, lambda h: h.tensor_scalar(out=sall[:, O_MK:O_MK + 512], in0=sall[:, O_MK:O_MK + 512], scalar1=float(128 ** -0.5), scalar2=None, op0=ALU.mult), reads=[sall], writes=[sall])
        fw.op(DVE, lambda h: h.tensor_tensor(out=sall[:, O_DT:O_DT + 16], in0=sall[:, O_DT:O_DT + 16], in1=P["dtb"][0:RS, :], op=ALU.add), reads=[sall, P["dtb"]], writes=[sall])
        softplus(sall, sall[:, O_DT:O_DT + 16])
        fw.op(DVE, lambda h: h.tensor_tensor(out=sall[:, O_MI:O_MI + 8], in0=sall[:, O_MI:O_MI + 8], in1=P["gb"][0:RS, :], op=ALU.add), reads=[sall, P["gb"]], writes=[sall])
        fw.dma(SP, s_k[l, :, 0:127, :], cache_k[l, :, 1:128, :], sem_o, is_out=True)
        fw.dma(SP, s_v[l, :, 0:127, :], cache_v[l, :, 1:128, :], sem_o, is_out=True)
        fw.dma(SP, s_k[l, :, 127, :], sall[:, O_K:O_K + 128], sem_o, reads=[sall], is_out=True)
        fw.dma(SP, s_v[l, :, 127, :], sall[:, O_V:O_V + 128], sem_o, reads=[sall], is_out=True)
        fw.dma(SP, s_conv[l, :, 0:2, :], st_conv[l, :, 1:3, :], sem_o, is_out=True)
        fw.dma(SP, s_conv[l, :, 2, :], sall[:, O_XBC:O_XBC + 1536], sem_o, reads=[sall], is_out=True)
        fw.dma(SP, wj[:], conv_w[l, 3, :].partition_broadcast(RS), sem_s, writes=[wj])
        fw.op(DVE, lambda h: h.tensor_tensor(out=xcs[:, 0:1536], in0=sall[:, O_XBC:O_XBC + 1536], in1=wj[:], op=ALU.mult), reads=[sall, wj], writes=[xcs])
        for j in range(3):
            fw.dma(SP, cj[:], st_conv[l, :, j, :], sem_s, writes=[cj])
            fw.dma(SP, wj[:], conv_w[l, j, :].partition_broadcast(RS), sem_s, writes=[wj])
            fw.op(DVE, lambda h: h.tensor_tensor(out=cj[:], in0=cj[:], in1=wj[:], op=ALU.mult), reads=[cj, wj], writes=[cj])
            fw.op(DVE, lambda h: h.tensor_tensor(out=xcs[:, 0:1536], in0=xcs[:, 0:1536], in1=cj[:], op=ALU.add), reads=[xcs, cj], writes=[xcs])
        fw.dma(SP, wj[:], conv_b[l, :].partition_broadcast(RS), sem_s, writes=[wj])
        fw.op(DVE, lambda h: h.tensor_tensor(out=xcs[:, 0:1536], in0=xcs[:, 0:1536], in1=wj[:], op=ALU.add), reads=[xcs, wj], writes=[xcs])
        fw.op(ACT, lambda h: h.activation(out=sall[:, O_XBC:O_XBC + 1536], in_=xcs[:, 0:1536], func=AF.Silu), reads=[xcs, sall], writes=[sall])

        for r in range(RS):
            fw.dma(SP, ckf[:], cache_k[l, r], sem_s, writes=[ckf])
            fw.dma(SP, cvf[:], cache_v[l, r], sem_s, writes=[cvf])
            fw.op(DVE, lambda h: h.tensor_copy(out=ckb[:].rearrange("p (a c b) -> p a c b", a=2, c=2),
                                               in_=bc(ckf[:].rearrange("p (a b) -> p a b", a=2).unsqueeze(2), [128, 2, 2, 64])), reads=[ckf], writes=[ckb])
            transpose_to(ckb[:], ckb, 128, 256, lambda j: r_kp[:, j, :], r_kp)
            fw.op(ACT, lambda h: h.copy(out=r_vp[:, :, 0:64], in_=cvf[:].rearrange("p (a b) -> p a b", a=2)), reads=[cvf], writes=[r_vp])
            for j in range(4):
                stage_feat(r, sall, sall[:, O_Q + j * 128:O_Q + (j + 1) * 128], r_qT[:, j, :], r_qT)
            for kv in range(2):
                fw.op(PE, lambda h: h.matmul(pd[1][0:64, 0:128], lhsT=sall[:, O_K + kv * 64:O_K + (kv + 1) * 64], rhs=oh[:, r, :], start=True, stop=True), reads=[oh, sall], writes=[pd[1]])
                fw.op(PE, lambda h: h.matmul(pd[1][64:128, 0:128], lhsT=sall[:, O_K + kv * 64:O_K + (kv + 1) * 64], rhs=oh[:, r, :], start=True, stop=True), reads=[oh, sall], writes=[pd[1]])
                fw.op(ACT, lambda h: h.copy(out=r_kTd[:, kv, :], in_=pd[1][:, 0:128]), reads=[pd[1]], writes=[r_kTd])
            stage_tok(r, O_V, 128, lambda o, m: r_v[:, :, 0:64], r_v)
            swa_block(l, lambda j: r_qT[:, j, :], r_qT, lambda kv: r_kTd[:, kv, :], r_kTd, lambda kv: r_v[:, kv, :], r_v,
                      lambda kv: r_kp[:, kv, :], r_kp, lambda kv: r_vp[:, kv, :], r_vp, True, r_mix[:, 0:512], r_mix)
            load_state(l, st, r)
            stage_tok(r, O_XBC, 1024, lambda o, m: r_x[:, o:o + m], r_x)
            stage_tok(r, O_XBC + 1024, 256, lambda o, m: r_b[:, o:o + m], r_b)
            for j in range(4):
                stage_feat(r, sall, sall[:, O_XBC + 1024 + j * 128:O_XBC + 1024 + (j + 1) * 128], r_bc[:, j, :], r_bc)
            stage_tok(r, O_DT, 16, lambda o, m: r_dt[:, o:o + m], r_dt)
            stage_tok(r, O_Z, 1024, lambda o, m: r_z[:, o:o + m], r_z)
            ssd_chunk(l, st, r_x[:], r_x, r_b[:], r_b, lambda g: r_bc[:, g, :], lambda g: r_bc[:, 2 + g, :], r_bc,
                      r_dt[:], r_dt, r_z[:], r_z, r_mix[:, 512:1536], r_mix)
            for j in range(4):
                stage_feat(r, sall, sall[:, O_MQ + j * 128:O_MQ + (j + 1) * 128], r_mqT[:, j, :], r_mqT)
                stage_feat(r, sall, sall[:, O_MK + j * 128:O_MK + (j + 1) * 128], r_mkT[:, j, :], r_mkT)
            stage_tok(r, O_MK, 512, lambda o, m: r_mk[:, o:o + m], r_mk)
            stage_tok(r, O_MV, 512, lambda o, m: r_mv[:, :, 0:128], r_mv)
            stage_tok(r, O_MO, 512, lambda o, m: r_mo[:, o:o + m], r_mo)
            fw.op(PE, lambda h: h.matmul(pd[1][:, 0:8], lhsT=oh[:, r, :], rhs=sall[:, O_MI:O_MI + 8], start=True, stop=True), reads=[oh, sall], writes=[pd[1]])
            fw.op(DVE, lambda h: h.tensor_tensor(out=r_g[:], in0=pd[1][:, 0:8], in1=padt[:], op=ALU.add), reads=[pd[1], padt], writes=[r_g])
            mlstm_chunk(l, st, lambda hd: r_mqT[:, hd, :], lambda hd: r_mkT[:, hd, :], [r_mqT, r_mkT], r_mk[:], lambda hd: r_mv[:, hd, :], [r_mk, r_mv],
                        r_g[:, 0:4], r_g[:, 4:8], r_g, r_mo[:], r_mo, r_mix[:, 1536:2048], r_mix)
            emit_state_out(l, st, s_ssm[l, r], s_C[l, r], s_n[l, r], s_m[l, r])
            fw.dma(SP, mixs[r:r + 1, :], r_mix[0:1, :], sem_m, reads=[r_mix], writes=[mixs])

        transpose_to(mixs[:, :], mixs, RS, D, lambda j: actS[:, j, :], actS)
        dense_tok(actS, ttS, w_out[l], D, c_res_s)
        norm_to_actS(w_norm_mlp[l, :])
        for g in range(4):
            def c_up_s(ti, c0, nb, p):
                fw.op(ACT, lambda h: h.activation(out=sqS[:, c0:c0 + nb], in_=p[0:RS, 0:nb], func=AF.Relu), reads=[p], writes=[sqS])
                fw.op(DVE, lambda h: h.tensor_tensor(out=utS[:, c0:c0 + nb], in0=sqS[:, c0:c0 + nb], in1=sqS[:, c0:c0 + nb], op=ALU.mult), reads=[sqS], writes=[utS])
            dense_tok(actS, ttS, w_up[l][:, g * 2048:(g + 1) * 2048], 2048, c_up_s)
            transpose_to(utS[:, :], utS, RS, D, lambda j: hsT[:, j, :], hsT)
            dense_tok(hsT, ttS, w_down[l][g * 2048:(g + 1) * 2048, :], D, c_res_s)

    load_gain(w_norm_final, 0, D)
    rmsnorm_to(xrs[:, :], xrs, RS, gbc[0:RS, :], sqS[:, :], sqS, sall, D, col=0)
    fw.dma(SP, y_s[:, :], sqS[:, :], sem_o, reads=[sqS], is_out=True)
    ses.close()


_NC = [None]


def _consts():
    i = np.arange(128)
    c = {}
    c["c_ident"] = np.eye(128, dtype=np.float32)
    c["c_tri"] = (i[:, None] <= i[None, :]).astype(np.float32)
    c["c_triT"] = (i[:, None] >= i[None, :]).astype(np.float32)
    c["c_U"] = (i[:, None] > i[None, :]).astype(np.float32)
    c["c_negqk"] = np.where(i[None, :] > i[:, None], -1e30, 0.0).astype(np.float32)
    c["c_negkq"] = np.where(i[:, None] > i[None, :], -30000.0, 0.0).astype(np.float32)
    e = np.zeros((128, 128), np.float32); e[127, :] = 1.0
    c["c_e127"] = e
    half = 8
    inv = np.power(np.float32(500000.0), -np.arange(half, dtype=np.float32) / half).astype(np.float32)
    pos = np.zeros((128, 17), np.float32)
    for t in range(16):
        pos[:, t] = t * 128 + i
    pos[:, 16] = PAST
    ang = pos[:, :, None].astype(np.float32) * inv[None, None, :]
    c["c_cos"] = np.cos(ang).astype(np.float32)
    c["c_sin"] = np.sin(ang).astype(np.float32)
    oh = np.zeros((RS, RS, 128), np.float32)
    for r in range(RS):
        oh[r, r, 0] = 1.0
    c["c_oh"] = oh
    pad = np.zeros((128, 8), np.float32)
    pad[1:, 0:4] = -1.0e4
    pad[1:, 4:8] = 1.0e4
    c["c_pad"] = pad
    return c


def kernel(x_prompt, x_sample, cache_swa_k, cache_swa_v, state_conv, state_ssm, state_mlstm_C,
           state_mlstm_n, state_mlstm_m, w_norm_mix, w_in, attn_sinks, conv_w, conv_b, dt_bias, a_log,
           d_skip, w_norm_ssm, igate_b, fgate_b, w_norm_mlstm, w_out, w_norm_mlp, w_up, w_down,
           w_norm_final):
    f = lambda a: np.ascontiguousarray(np.asarray(a, dtype=np.float32))
    if _NC[0] is None:
        _NC[0] = build()
    nc = _NC[0]
    cst = _consts()
    shared = {
        "w_norm_mix": f(w_norm_mix), "w_in": f(w_in), "sinks": f(attn_sinks).reshape(L, 8), "conv_w": f(conv_w),
        "conv_b": f(conv_b), "dt_bias": f(dt_bias), "a_log": f(a_log), "d_skip": f(d_skip), "w_norm_ssm": f(w_norm_ssm),
        "igb": f(igate_b), "fgb": f(fgate_b), "w_norm_ml": f(w_norm_mlstm), "w_out": f(w_out), "w_norm_mlp": f(w_norm_mlp),
        "w_up": f(w_up), "w_down": f(w_down), "w_norm_final": f(w_norm_final),
    }
    shared.update(cst)
    in_maps = []
    for c in range(NCORES):
        rs = slice(c * RS, (c + 1) * RS)
        m = dict(shared)
        m["xp"] = f(x_prompt[c])
        m["xsm"] = f(x_sample[rs, 0, :])
        m["cache_k"] = f(np.asarray(cache_swa_k)[:, rs].reshape(L, RS, 128, 128))
        m["cache_v"] = f(np.asarray(cache_swa_v)[:, rs].reshape(L, RS, 128, 128))
        m["st_conv"] = f(np.asarray(state_conv)[:, rs])
        m["st_ssm"] = f(np.asarray(state_ssm)[:, rs].reshape(L, RS, 1024, 128))
        m["st_C"] = f(np.asarray(state_mlstm_C)[:, rs])
        m["st_n"] = f(np.asarray(state_mlstm_n)[:, rs])
        m["st_m"] = f(np.asarray(state_mlstm_m)[:, rs])
        in_maps.append(m)
    res = run_bass_kernel_spmd(nc, in_maps, core_ids=list(range(NCORES)))
    R = res.results
    cat = lambda k, ax: np.concatenate([np.asarray(R[c][k]) for c in range(NCORES)], axis=ax)
    stk = lambda k: np.stack([np.asarray(R[c][k]) for c in range(NCORES)], axis=1)
    y_prompt = np.stack([np.asarray(R[c]["y_p"]) for c in range(NCORES)], axis=0)
    y_sample = cat("y_s", 0).reshape(NCORES * RS, 1, D)
    p_k = stk("p_k").reshape(L, NCORES, 128, 2, 64)
    p_v = stk("p_v").reshape(L, NCORES, 128, 2, 64)
    p_conv = stk("p_conv")
    p_ssm = stk("p_ssm").reshape(L, NCORES, 16, 64, 128)
    p_C = stk("p_C"); p_n = stk("p_n"); p_m = stk("p_m")
    s_k = cat("s_k", 1).reshape(L, NCORES * RS, 128, 2, 64)
    s_v = cat("s_v", 1).reshape(L, NCORES * RS, 128, 2, 64)
    s_conv = cat("s_conv", 1)
    s_ssm = cat("s_ssm", 1).reshape(L, NCORES * RS, 16, 64, 128)
    s_C = cat("s_C", 1); s_n = cat("s_n", 1); s_m = cat("s_m", 1)
    outs = (y_prompt, y_sample, p_k, p_v, p_conv, p_ssm, p_C, p_n, p_m, s_k, s_v, s_conv, s_ssm, s_C, s_n, s_m)
    return tuple(np.ascontiguousarray(o, dtype=np.float32) for o in outs)
```

```python
import numpy as np
import concourse.bass as bass
import concourse.mybir as mybir
from concourse.bass_utils import run_bass_kernel_spmd
from contextlib import ExitStack

F32 = mybir.dt.float32
BF16 = mybir.dt.bfloat16
AF = mybir.ActivationFunctionType
ALU = mybir.AluOpType
AX = mybir.AxisListType

NCORES = 4
D = 2048
SEQ = 2048
ST = 256
NT = ST // 128
NST = SEQ // ST
RS = 32
L = 2
INW = 5400
O_Q, O_K, O_V, O_Z, O_XBC, O_DT, O_MQ, O_MK, O_MV, O_MO, O_MI, O_MF = (
    0, 512, 640, 768, 1792, 3328, 3344, 3856, 4368, 4880, 5392, 5396)
EPS = 1e-6
PAST = 8192
WB = 256
SEM_LIMIT = 8000


class Buf:
    def __init__(self, name, t=None, parent=None):
        self.name = name
        self.t = t
        self.parent = parent
        self.w = None
        self.r = {}

    def root(self):
        return self.parent.root() if self.parent is not None else self

    def __getitem__(self, idx):
        return self.t[idx]


class Eng:
    def __init__(self, fw, name, h):
        self.fw = fw
        self.name = name
        self.h = h
        self.sem = fw.new_sem(name)
        self.own = {id(self.sem)}
        self.cnt = 0
        self.seen = {}

    def _wait(self, ev):
        if ev is None:
            return
        sem, val = ev
        key = id(sem)
        if self.name == "pe" and key in self.own:
            return
        if key in self.fw.dma_sems:
            val = self.fw.dma_sems[key]
        if self.seen.get(key, 0) >= val:
            return
        self.h.wait_ge(sem, val)
        self.seen[key] = val


def _rnd(n):
    return 32 if n <= 32 else (64 if n <= 64 else 128)


class PEProxy:
    def __init__(self, fw):
        self.fw = fw
        self.last = None

    def _mode(self, st_ap, kind):
        shp = tuple(st_ap.shape)
        m = 1
        for v in shp[1:]:
            m *= v
        mode = (_rnd(shp[0]), _rnd(m), str(st_ap.dtype), kind)
        pe = self.fw.pe
        tiled = mode[0] < 128 or mode[1] < 128
        if self.last is not None and (mode != self.last or tiled) and pe.cnt > 0:
            pe.h.wait_ge(pe.sem, pe.cnt)
        self.last = mode

    def matmul(self, out, lhsT=None, rhs=None, **kw):
        self._mode(lhsT, "m")
        return self.fw.pe.h.matmul(out, lhsT=lhsT, rhs=rhs, **kw)

    def transpose(self, out=None, in_=None, identity=None):
        self._mode(in_, "t")
        return self.fw.pe.h.transpose(out=out, in_=in_, identity=identity)


class FW:
    def __init__(self, nc):
        self.nc = nc
        self.es = ExitStack()
        self.nsem = 0
        self.dma_sems = {}
        self.dma_sem_objs = {}
        self.qpool = {}
        self.pe = Eng(self, "pe", nc.tensor)
        self.act = Eng(self, "act", nc.scalar)
        self.dve = Eng(self, "dve", nc.vector)
        self.pool = Eng(self, "pool", nc.gpsimd)
        self.sp = Eng(self, "sp", nc.sync)
        self.engs = [self.pe, self.act, self.dve, self.pool, self.sp]
        self.pe_proxy = PEProxy(self)
        self.out_events = []
        self.all_sems_used = []

    def new_sem(self, name):
        self.nsem += 1
        return self.es.enter_context(self.nc.semaphore(f"s_{name}_{self.nsem}"))

    def sb(self, name, shape, dt, es=None):
        t = (es or getattr(self, "es_alloc", None) or self.es).enter_context(self.nc.sbuf_tensor(name, list(shape), dt))
        return Buf(name, t)

    def ps(self, name, shape, dt=F32):
        t = self.es.enter_context(self.nc.psum_tensor(name, list(shape), dt))
        return Buf(name, t)

    def _deps(self, eng, reads, writes):
        reads = [b.root() for b in reads]
        writes = [b.root() for b in writes]
        for b in reads:
            eng._wait(b.w)
        for b in writes:
            eng._wait(b.w)
            for ev in list(b.r.values()):
                eng._wait(ev)

    def op(self, eng, fn, reads=(), writes=()):
        reads = [b.root() for b in reads]
        writes = [b.root() for b in writes]
        self._deps(eng, reads, writes)
        inst = fn(self.pe_proxy if eng is self.pe else eng.h)
        if eng.cnt >= SEM_LIMIT:
            eng.sem = self.new_sem(eng.name)
            eng.own.add(id(eng.sem))
            eng.cnt = 0
        eng.cnt += 1
        inst.then_inc(eng.sem, 1)
        ev = (eng.sem, eng.cnt)
        for b in writes:
            b.w = ev
            b.r = {}
        for b in reads:
            if b not in writes:
                b.r[id(ev[0])] = ev
        return ev

    def dsem(self, name):
        return [None, name]

    def dma(self, q, out, in_, sem=None, reads=(), writes=(), is_out=False):
        pool = self.qpool.setdefault(q.name, {"sems": [None] * (12 if q.name == "sp" else 4), "i": 0})
        slot = pool["i"] % len(pool["sems"])
        pool["i"] += 1
        sem = pool["sems"][slot]
        if sem is not None:
            prev = self.dma_sems.get(id(sem), 0)
            q._wait((sem, prev))
            if prev >= SEM_LIMIT:
                sem = None
        if sem is None:
            sem = self.new_sem("d" + q.name)
            pool["sems"][slot] = sem
        reads = [b.root() for b in reads]
        writes = [b.root() for b in writes]
        self._deps(q, reads, writes)
        inst = q.h.dma_start(out=out, in_=in_)
        k = id(sem)
        self.dma_sems[k] = self.dma_sems.get(k, 0) + 16
        self.dma_sem_objs[k] = sem
        inst.then_inc(sem, 16)
        ev = (sem, self.dma_sems[k])
        for b in writes:
            b.w = ev
            b.r = {}
        for b in reads:
            b.r[id(ev[0])] = ev
        if is_out:
            self.out_events.append(ev)
        return ev

    def barrier(self):
        evs = [(e.sem, e.cnt) for e in self.engs if e.cnt > 0]
        evs += [(self.dma_sem_objs[k], v) for k, v in self.dma_sems.items()]
        for e in [self.pe, self.act, self.dve, self.pool, self.sp]:
            for ev in evs:
                e._wait(ev)

    def finish(self, close=True):
        for k, v in self.dma_sems.items():
            self.sp._wait((self.dma_sem_objs[k], v))
        if close:
            self.es.close()


def bc(ap, shape):
    return ap.broadcast_to(list(shape))


class _Stop(Exception):
    pass


DEBUG_STOP = [None]


def build():
    nc = bass.Bass("TRN2", target_bir_lowering=False)
    fw = FW(nc)
    stopped = False
    try:
        _build(nc, fw)
    except _Stop:
        stopped = True
    fw.finish(close=not stopped)
    return nc


def _build(nc, fw):
    def ck(name):
        if DEBUG_STOP[0] == name:
            raise _Stop()
    PE, ACT, DVE, POOL, SP = fw.pe, fw.act, fw.dve, fw.pool, fw.sp
    dbg_sem = [None]

    def dump(name, buf, ap, shape):
        if DEBUG_STOP[0] is None:
            return
        if dbg_sem[0] is None:
            dbg_sem[0] = fw.dsem("dbg")
        o = nc.dram_tensor("dbg_" + name, list(shape), F32, kind="ExternalOutput").ap()
        fw.dma(POOL, o, ap, dbg_sem[0], reads=[buf], is_out=True)

    def din(name, shape):
        return nc.dram_tensor(name, list(shape), F32, kind="ExternalInput").ap()

    def dout(name, shape):
        return nc.dram_tensor(name, list(shape), F32, kind="ExternalOutput").ap()

    xp = din("xp", [SEQ, D]); xsm = din("xsm", [RS, D])
    cache_k = din("cache_k", [L, RS, 128, 128]); cache_v = din("cache_v", [L, RS, 128, 128])
    st_conv = din("st_conv", [L, RS, 3, 1536]); st_ssm = din("st_ssm", [L, RS, 1024, 128])
    st_C = din("st_C", [L, RS, 4, 128, 128]); st_n = din("st_n", [L, RS, 4, 128]); st_m = din("st_m", [L, RS, 4])
    w_norm_mix = din("w_norm_mix", [L, D]); w_in = din("w_in", [L, D, INW]); sinks = din("sinks", [L, 8])
    conv_w = din("conv_w", [L, 4, 1536]); conv_b = din("conv_b", [L, 1536]); dt_bias = din("dt_bias", [L, 16])
    a_log = din("a_log", [L, 16]); d_skip = din("d_skip", [L, 16]); w_norm_ssm = din("w_norm_ssm", [L, 1024])
    igb = din("igb", [L, 4]); fgb = din("fgb", [L, 4]); w_norm_ml = din("w_norm_ml", [L, 512])
    w_out = din("w_out", [L, D, D]); w_norm_mlp = din("w_norm_mlp", [L, D]); w_up = din("w_up", [L, D, 4 * D])
    w_down = din("w_down", [L, 4 * D, D]); w_norm_final = din("w_norm_final", [D])
    c_ident = din("c_ident", [128, 128]); c_tri = din("c_tri", [128, 128]); c_triT = din("c_triT", [128, 128])
    c_U = din("c_U", [128, 128]); c_negqk = din("c_negqk", [128, 128]); c_negkq = din("c_negkq", [128, 128])
    c_e127 = din("c_e127", [128, 128]); c_cos = din("c_cos", [128, 17, 8]); c_sin = din("c_sin", [128, 17, 8])
    c_oh = din("c_oh", [RS, RS, 128]); c_pad = din("c_pad", [128, 8])

    y_p = dout("y_p", [SEQ, D]); y_s = dout("y_s", [RS, D])
    p_k = dout("p_k", [L, 128, 128]); p_v = dout("p_v", [L, 128, 128]); p_conv = dout("p_conv", [L, 3, 1536])
    p_ssm = dout("p_ssm", [L, 1024, 128]); p_C = dout("p_C", [L, 4, 128, 128]); p_n = dout("p_n", [L, 4, 128])
    p_m = dout("p_m", [L, 4])
    s_k = dout("s_k", [L, RS, 128, 128]); s_v = dout("s_v", [L, RS, 128, 128]); s_conv = dout("s_conv", [L, RS, 3, 1536])
    s_ssm = dout("s_ssm", [L, RS, 1024, 128]); s_C = dout("s_C", [L, RS, 4, 128, 128]); s_n = dout("s_n", [L, RS, 4, 128])
    s_m = dout("s_m", [L, RS, 4])

    sem_c = fw.dsem("dc")
    sem_o = fw.dsem("do")
    sem_x = fw.dsem("dx")
    sem_g = fw.dsem("dg")
    sem_st = fw.dsem("dst")

    pbig = fw.ps("pbig", [128, 2048], F32)
    pd = [fw.ps(f"pd{i}", [128, 512], F32) for i in range(2)]
    ptb = fw.ps("ptb", [128, 1024], BF16)
    pm = fw.ps("pm", [128, 512], F32)
    pq = [pbig]

    def cload(name, src, shape, dt=F32, cast=None):
        b = fw.sb(name, shape, F32)
        fw.dma(SP, b[:], src, sem_c, writes=[b])
        if cast is not None:
            b2 = fw.sb(name + "_b", shape, cast)
            fw.op(DVE, lambda h: h.tensor_copy(out=b2[:], in_=b[:]), reads=[b], writes=[b2])
            return b, b2
        return b

    ident_f, ident_b = cload("ident", c_ident, [128, 128], cast=BF16)
    tri_f, tri_b = cload("tri", c_tri, [128, 128], cast=BF16)
    triT_f, triT_b = cload("triT", c_triT, [128, 128], cast=BF16)
    U_f = cload("U", c_U, [128, 128])
    negqk = cload("negqk", c_negqk, [128, 128])
    negkq = cload("negkq", c_negkq, [128, 128])
    e127 = cload("e127", c_e127, [128, 128])
    ones_f = fw.sb("ones_f", [128, 128], F32)
    fw.op(DVE, lambda h: h.memset(ones_f[:], 1.0), writes=[ones_f])
    cosT = cload("cosT", c_cos, [128, 17, 8]); sinT = cload("sinT", c_sin, [128, 17, 8])
    padt = cload("padt", c_pad, [128, 8])

    gbc = fw.sb("gbc", [128, 2048], F32)
    lp = {}

    def bload(name, src_row, n):
        b = fw.sb(name, [128, n], F32)
        fw.dma(SP, b[:], src_row.partition_broadcast(128), sem_c, writes=[b])
        return b

    for l in range(L):
        d = {}
        d["dtb"] = bload(f"dtb{l}", dt_bias[l, :], 16)
        al = bload(f"al{l}", a_log[l, :], 16)
        A = fw.sb(f"A{l}", [128, 16], F32)
        fw.op(ACT, lambda h: h.activation(out=A[:], in_=al[:], func=AF.Exp), reads=[al], writes=[A])
        fw.op(DVE, lambda h: h.tensor_scalar(out=A[:], in0=A[:], scalar1=-1.0, scalar2=None, op0=ALU.mult), reads=[A], writes=[A])
        d["A"] = A
        dsk = bload(f"dsk{l}", d_skip[l, :], 16)
        dD = fw.sb(f"dD{l}", [128, 16, 128], BF16)
        fw.op(DVE, lambda h: h.tensor_tensor(out=dD[:], in0=bc(ident_f[:].unsqueeze(1), [128, 16, 128]),
                                             in1=bc(dsk[:].unsqueeze(2), [128, 16, 128]), op=ALU.mult),
              reads=[ident_f, dsk], writes=[dD])
        d["dD"] = dD
        d["dsk"] = dsk
        sk = bload(f"sk{l}", sinks[l, :], 8)
        esk = fw.sb(f"esk{l}", [128, 8], F32)
        fw.op(ACT, lambda h: h.activation(out=esk[:], in_=sk[:], func=AF.Exp), reads=[sk], writes=[esk])
        d["esk"] = esk
        gb = fw.sb(f"gb{l}", [128, 8], F32)
        fw.dma(SP, gb[:, 0:4], igb[l, :].partition_broadcast(128), sem_c, writes=[gb])
        fw.dma(SP, gb[:, 4:8], fgb[l, :].partition_broadcast(128), sem_c, writes=[gb])
        d["gb"] = gb
        cw = fw.sb(f"cw{l}", [128, 12, 4], F32)
        cb = fw.sb(f"cb{l}", [128, 12], F32)
        with nc.allow_non_contiguous_dma(reason="small conv params"):
            for j in range(4):
                fw.dma(SP, cw[:, :, j], conv_w[l, j, :].rearrange("(b p) -> p b", p=128), sem_c, writes=[cw])
            fw.dma(SP, cb[:], conv_b[l, :].rearrange("(b p) -> p b", p=128), sem_c, writes=[cb])
        d["cw"] = cw; d["cb"] = cb
        lp[l] = d

    class S:
        pass
    small = fw.sb("small", [128, 64], F32)
    rtmp = fw.sb("rtmp", [128, 10, 16], F32)
    NWB_ = 2
    wbuf = [fw.sb(f"wb{i}", [128, 16, WB], BF16) for i in range(NWB_)]
    pes = ExitStack()
    _es_orig = fw.es
    carry = []
    fw.es_alloc = pes
    for l in range(L):
        s = S()
        s.ST = fw.sb(f"ST{l}", [128, 1024], F32)
        s.STb = fw.sb(f"STb{l}", [128, 1024], BF16)
        s.Cn = fw.sb(f"Cn{l}", [128, 4, 129], F32)
        s.Cnb = fw.sb(f"Cnb{l}", [128, 4, 129], BF16)
        s.mrow = fw.sb(f"mrow{l}", [128, 4], F32)
        s.kprev = fw.sb(f"kprev{l}", [128, 2, 128], BF16)
        s.vprev = fw.sb(f"vprev{l}", [128, 2, 65], BF16)
        s.xcarry = fw.sb(f"xcar{l}", [128, 12, 3], F32)
        carry.append(s)

    NWB = 2
    wsem = [fw.dsem(f"w{i}") for i in range(NWB)]
    wctr = [0]

    NBLK = 2 * (3 + 4 + 1 + 2 + 2 + 2 + 1 + 2 + 6 + 8 + 4 * 16)
    wcache_t = nc.dram_tensor("wcache", [NBLK, 128, 16, WB], BF16, kind="Internal").ap()
    wcache = Buf("wcache")
    wpass = [0, 0]

    def new_pass():
        assert wpass[0] == 0 or wpass[1] == NBLK, wpass
        wpass[0] += 1
        wpass[1] = 0

    def wload(src2d, ncols):
        i = wctr[0] % NWB
        wctr[0] += 1
        b = wbuf[i]
        blk = wpass[1]
        wpass[1] += 1
        if wpass[0] == 1:
            fw.dma(POOL, b[:, :, 0:ncols], src2d.rearrange("(k p) n -> p k n", p=128), wsem[i], writes=[b])
            fw.dma(SP, wcache_t[blk, :, :, 0:ncols], b[:, :, 0:ncols], None, reads=[b], writes=[wcache])
        else:
            fw.dma(POOL, b[:, :, 0:ncols], wcache_t[blk, :, :, 0:ncols], wsem[i], reads=[wcache], writes=[b])
        return b

    pdc = [0]

    def next_pd():
        p = pd[pdc[0] % 2]
        pdc[0] += 1
        return p

    def dense_tok(actT, tts, W2d, ncols, consume):
        for c0 in range(0, ncols, WB):
            nb = min(WB, ncols - c0)
            wb = wload(W2d[:, c0:c0 + nb], nb)
            for ti, (t0, tsz) in enumerate(tts):
                p = next_pd()
                for k in range(16):
                    fw.op(PE, lambda h: h.matmul(p[0:tsz, 0:nb], lhsT=actT[:, k, t0:t0 + tsz], rhs=wb[:, k, 0:nb],
                                                 start=(k == 0), stop=(k == 15)), reads=[actT, wb], writes=[p])
                consume(ti, c0, nb, p)

    def dense_feat(actT, ntok, W2d, ncols, consume):
        for c0 in range(0, ncols, WB):
            nb = min(WB, ncols - c0)
            wb = wload(W2d[:, c0:c0 + nb], nb)
            for s0 in range(0, nb, 128):
                p = next_pd()
                for k in range(16):
                    fw.op(PE, lambda h: h.matmul(p[:, 0:ntok], lhsT=wb[:, k, s0:s0 + 128], rhs=actT[:, k, 0:ntok],
                                                 start=(k == 0), stop=(k == 15)), reads=[actT, wb], writes=[p])
                consume((c0 + s0) // 128, p)


    def rmsnorm_to(xt_ap, xbuf, np_, gain_ap, out_ap, outbuf, tmpbuf, nfeat, col=0):
        ss = small[0:np_, col:col + 1]
        fw.op(ACT, lambda h: h.activation(out=tmpbuf[0:np_, 0:nfeat], in_=xt_ap, func=AF.Square, accum_out=ss),
              reads=[xbuf], writes=[tmpbuf, small])
        fw.op(DVE, lambda h: h.tensor_scalar(out=ss, in0=ss, scalar1=1.0 / nfeat, scalar2=EPS, op0=ALU.mult, op1=ALU.add),
              reads=[small], writes=[small])
        fw.op(ACT, lambda h: h.activation(out=ss, in_=ss, func=AF.Sqrt), reads=[small], writes=[small])
        fw.op(DVE, lambda h: h.reciprocal(out=ss, in_=ss), reads=[small], writes=[small])
        fw.op(DVE, lambda h: h.scalar_tensor_tensor(out=out_ap, in0=xt_ap, scalar=ss, in1=gain_ap, op0=ALU.mult, op1=ALU.mult),
              reads=[xbuf, small, gbc], writes=[outbuf])

    def transpose_to(src_ap, srcbuf, np_, ncols, dst_fn, dstbuf):
        nblk = ncols // 128
        for g0 in range(0, nblk, 8):
            g1 = min(nblk, g0 + 8)
            for j in range(g0, g1):
                fw.op(PE, lambda h: h.transpose(out=ptb[:, (j - g0) * 128:(j - g0) * 128 + np_],
                                                in_=src_ap[:, j * 128:(j + 1) * 128], identity=ident_b[0:np_, 0:np_]),
                      reads=[srcbuf, ident_b], writes=[ptb])
            for j in range(g0, g1):
                fw.op(ACT, lambda h: h.copy(out=dst_fn(j), in_=ptb[:, (j - g0) * 128:(j - g0) * 128 + np_]),
                      reads=[ptb], writes=[dstbuf])

    cs = ExitStack()
    o_f = fw.sb("o_f", [128, 8, 65], F32)
    pT = fw.sb("pT", [128, 2, 512], BF16)
    rden = fw.sb("rden", [128, 8], F32)

    def swa_block(l, qT, qTbuf, kcur, kcurbuf, vcur, vcurbuf, kprev, kprevbuf, vprev, vprevbuf, has_prev, out_ap, outbuf):
        for kv in range(2):
            blocks = ([("p", kprev, kprevbuf, vprev, vprevbuf, triT_b)] if has_prev else []) + [("c", kcur, kcurbuf, vcur, vcurbuf, tri_b)]
            for bi, (nm, kf, kb, vf, vb, msk) in enumerate(blocks):
                for hh in range(4):
                    h_ = kv * 4 + hh
                    half = (h_ % 2) * 64
                    fw.op(PE, lambda h: h.matmul(pbig[:, (bi * 4 + hh) * 128:(bi * 4 + hh + 1) * 128],
                                                 lhsT=kf(kv)[half:half + 64, :], rhs=qT(h_ // 2)[half:half + 64, :],
                                                 start=True, stop=True), reads=[kb, qTbuf], writes=[pbig])
                fw.op(ACT, lambda h: h.activation(out=pT[:, bi, :], in_=pbig[:, bi * 512:(bi + 1) * 512], func=AF.Exp, scale=0.125),
                      reads=[pbig], writes=[pT])
                fw.op(DVE, lambda h: h.tensor_tensor(out=pT[:, bi, :].rearrange("p (a b) -> p a b", a=4),
                                                     in0=pT[:, bi, :].rearrange("p (a b) -> p a b", a=4),
                                                     in1=bc(msk[:].unsqueeze(1), [128, 4, 128]), op=ALU.mult),
                      reads=[pT, msk], writes=[pT])
            for hh in range(4):
                for bi, (nm, kf, kb, vf, vb, msk) in enumerate(blocks):
                    fw.op(PE, lambda h: h.matmul(pm[:, hh * 65:(hh + 1) * 65], lhsT=pT[:, bi, hh * 128:(hh + 1) * 128], rhs=vf(kv),
                                                 start=(bi == 0), stop=(bi == len(blocks) - 1)), reads=[pT, vb], writes=[pm])
            fw.op(ACT, lambda h: h.copy(out=o_f[:, kv * 4:(kv + 1) * 4, :], in_=pm[:, 0:260].rearrange("p (a b) -> p a b", a=4)),
                  reads=[pm], writes=[o_f])
        esk = lp[l]["esk"]
        fw.op(DVE, lambda h: h.tensor_tensor(out=rden[:], in0=o_f[:, :, 64], in1=esk[:], op=ALU.add), reads=[o_f, esk], writes=[rden])
        fw.op(DVE, lambda h: h.reciprocal(out=rden[:], in_=rden[:]), reads=[rden], writes=[rden])
        fw.op(DVE, lambda h: h.tensor_tensor(out=out_ap.rearrange("p (a b) -> p a b", a=8), in0=o_f[:, :, 0:64],
                                             in1=bc(rden[:].unsqueeze(2), [128, 8, 64]), op=ALU.mult),
              reads=[o_f, rden], writes=[outbuf])

    dtA = fw.sb("dtA", [128, 16], F32)
    a_sb = fw.sb("a_sb", [128, 16], F32)
    ea = fw.sb("ea", [128, 16], F32)
    eal = fw.sb("eal", [128, 16], F32)
    wk = fw.sb("wk", [128, 16], F32)
    rseg = fw.sb("rseg", [128, 4, 128], F32)
    LT = fw.sb("LT", [128, 16, 128], BF16)
    cbm = fw.sb("cbm", [128, 2, 128], BF16)
    x_dt = fw.sb("x_dt", [128, 1024], BF16)
    xw = fw.sb("xw", [128, 1024], BF16)
    ytmp = fw.sb("ytmp", [128, 1024], F32)
    yy = fw.sb("yy", [128, 1024], F32)
    zs = ytmp

    def ssd_chunk(l, st, x_tok, x_tokbuf, B_tok, B_tokbuf, BT, CT, BCbuf, dt, dtbuf, z_tok, zbuf, out_ap, outbuf):
        A = lp[l]["A"]; dD = lp[l]["dD"]
        fw.op(DVE, lambda h: h.tensor_tensor(out=dtA[:], in0=dt, in1=A[:], op=ALU.mult), reads=[dtbuf, A], writes=[dtA])
        fw.op(PE, lambda h: h.matmul(pm[:, 0:16], lhsT=tri_f[:], rhs=dtA[:], start=True, stop=True), reads=[tri_f, dtA], writes=[pm])
        fw.op(PE, lambda h: h.matmul(pm[:, 16:32], lhsT=ones_f[:], rhs=dtA[:], start=True, stop=True), reads=[ones_f, dtA], writes=[pm])
        fw.op(ACT, lambda h: h.copy(out=a_sb[:], in_=pm[:, 0:16]), reads=[pm], writes=[a_sb])
        fw.op(ACT, lambda h: h.activation(out=ea[:], in_=pm[:, 0:16], func=AF.Exp), reads=[pm], writes=[ea])
        fw.op(ACT, lambda h: h.activation(out=eal[:], in_=pm[:, 16:32], func=AF.Exp), reads=[pm], writes=[eal])
        fw.op(DVE, lambda h: h.tensor_tensor(out=wk[:], in0=pm[:, 16:32], in1=a_sb[:], op=ALU.subtract), reads=[pm, a_sb], writes=[wk])
        fw.op(ACT, lambda h: h.activation(out=wk[:], in_=wk[:], func=AF.Exp), reads=[wk], writes=[wk])
        fw.op(DVE, lambda h: h.tensor_tensor(out=wk[:], in0=wk[:], in1=dt, op=ALU.mult), reads=[wk, dtbuf], writes=[wk])
        for i in range(4):
            fw.op(DVE, lambda h: h.tensor_tensor(out=rseg[:], in0=bc(tri_f[:].unsqueeze(1), [128, 4, 128]),
                                                 in1=bc(dtA[:, i * 4:(i + 1) * 4].unsqueeze(2), [128, 4, 128]), op=ALU.mult),
                  reads=[tri_f, dtA], writes=[rseg])
            fw.op(PE, lambda h: h.matmul(pbig[:, i * 512:(i + 1) * 512], lhsT=U_f[:],
                                         rhs=rseg[:].rearrange("p a b -> p (a b)"), start=True, stop=True),
                  reads=[U_f, rseg], writes=[pbig])
        fw.op(ACT, lambda h: h.activation(out=LT[:].rearrange("p a b -> p (a b)"), in_=pbig[:], func=AF.Exp), reads=[pbig], writes=[LT])
        for g in range(2):
            fw.op(PE, lambda h: h.matmul(pm[:, 64 + g * 128:64 + (g + 1) * 128], lhsT=BT(g), rhs=CT(g), start=True, stop=True),
                  reads=[BCbuf], writes=[pm])
        fw.op(DVE, lambda h: h.tensor_tensor(out=cbm[:], in0=pm[:, 64:320].rearrange("p (a b) -> p a b", a=2),
                                             in1=bc(tri_f[:].unsqueeze(1), [128, 2, 128]), op=ALU.mult),
              reads=[pm, tri_f], writes=[cbm])
        for g in range(2):
            fw.op(DVE, lambda h: h.tensor_tensor(out=LT[:, g * 8:(g + 1) * 8, :], in0=LT[:, g * 8:(g + 1) * 8, :],
                                                 in1=bc(cbm[:, g:g + 1, :], [128, 8, 128]), op=ALU.mult),
                  reads=[LT, cbm], writes=[LT])
        fw.op(DVE, lambda h: h.tensor_tensor(out=x_dt[:].rearrange("p (a b) -> p a b", a=16), in0=x_tok.rearrange("p (a b) -> p a b", a=16),
                                             in1=bc(dt.unsqueeze(2), [128, 16, 64]), op=ALU.mult),
              reads=[x_tokbuf, dtbuf], writes=[x_dt])
        fw.op(DVE, lambda h: h.tensor_tensor(out=xw[:].rearrange("p (a b) -> p a b", a=16), in0=x_tok.rearrange("p (a b) -> p a b", a=16),
                                             in1=bc(wk[:].unsqueeze(2), [128, 16, 64]), op=ALU.mult),
              reads=[x_tokbuf, wk], writes=[xw])
        for g in range(2):
            fw.op(PE, lambda h: h.matmul(pbig[:, g * 512:(g + 1) * 512], lhsT=CT(g), rhs=st.STb[:, g * 512:(g + 1) * 512], start=True, stop=True),
                  reads=[BCbuf, st.STb], writes=[pbig])
        for hh in range(16):
            fw.op(PE, lambda h: h.matmul(pbig[:, 1024 + hh * 64:1024 + (hh + 1) * 64], lhsT=LT[:, hh, :], rhs=x_dt[:, hh * 64:(hh + 1) * 64],
                                         start=True, stop=False), reads=[LT, x_dt], writes=[pbig])
            fw.op(PE, lambda h: h.matmul(pbig[:, 1024 + hh * 64:1024 + (hh + 1) * 64], lhsT=dD[:, hh, :], rhs=x_tok[:, hh * 64:(hh + 1) * 64],
                                         start=False, stop=True), reads=[dD, x_tokbuf], writes=[pbig])
        fw.op(DVE, lambda h: h.tensor_tensor(out=ytmp[:].rearrange("p (a b) -> p a b", a=16), in0=pbig[:, 0:1024].rearrange("p (a b) -> p a b", a=16),
                                             in1=bc(ea[:].unsqueeze(2), [128, 16, 64]), op=ALU.mult),
              reads=[pbig, ea], writes=[ytmp])
        fw.op(DVE, lambda h: h.tensor_tensor(out=yy[:], in0=pbig[:, 1024:2048], in1=ytmp[:], op=ALU.add), reads=[pbig, ytmp], writes=[yy])
        for g in range(2):
            fw.op(PE, lambda h: h.matmul(pd[g][:, :], lhsT=B_tok[:, g * 128:(g + 1) * 128], rhs=xw[:, g * 512:(g + 1) * 512], start=True, stop=True),
                  reads=[B_tokbuf, xw], writes=[pd[g]])
        fw.op(DVE, lambda h: h.tensor_tensor(out=st.ST[:].rearrange("p (a b) -> p a b", a=16), in0=st.ST[:].rearrange("p (a b) -> p a b", a=16),
                                             in1=bc(eal[:].unsqueeze(2), [128, 16, 64]), op=ALU.mult), reads=[st.ST, eal], writes=[st.ST])
        for g in range(2):
            fw.op(DVE, lambda h: h.tensor_tensor(out=st.ST[:, g * 512:(g + 1) * 512], in0=pd[g][:, :], in1=st.ST[:, g * 512:(g + 1) * 512], op=ALU.add),
                  reads=[pd[g], st.ST], writes=[st.ST])
        fw.op(ACT, lambda h: h.copy(out=st.STb[:], in_=st.ST[:]), reads=[st.ST], writes=[st.STb])
        fw.op(ACT, lambda h: h.activation(out=zs[:], in_=z_tok, func=AF.Silu), reads=[zbuf], writes=[zs])
        fw.op(DVE, lambda h: h.tensor_tensor(out=yy[:], in0=yy[:], in1=zs[:], op=ALU.mult), reads=[yy, zs], writes=[yy])
        for g in range(2):
            rmsnorm_to(yy[:, g * 512:(g + 1) * 512], yy, 128, gbc[:, g * 512:(g + 1) * 512], out_ap[:, g * 512:(g + 1) * 512], outbuf, ytmp, 512, col=8 + g)

    lfn = fw.sb("lfn", [128, 4], F32)
    nb_ = fw.sb("nb_", [128, 4], F32)
    nbl = fw.sb("nbl", [128, 4], F32)
    cc = fw.sb("cc", [128, 4], F32)
    Dc = fw.sb("Dc", [128, 4, 128], F32)
    cmx = fw.sb("cmx", [128, 4, 128], F32)
    cm = fw.sb("cm", [128, 4], F32)
    Mq = fw.sb("Mq", [128, 4], F32)
    negM = fw.sb("negM", [128, 4], F32)
    mt = fw.sb("mt", [128, 4], F32)
    gq = fw.sb("gq", [128, 4], F32)
    emt = fw.sb("emt", [128, 4], F32)
    mnew = fw.sb("mnew", [128, 4], F32)
    gend = fw.sb("gend", [128, 4], F32)
    wkm = fw.sb("wkm", [128, 4], F32)
    swe = fw.sb("swe", [128, 4, 128], F32)
    swT = fw.sb("swT", [128, 4, 128], BF16)
    tot = fw.sb("tot", [128, 4, 129], F32)
    ints = fw.sb("ints", [128, 4, 129], F32)
    hh_ = fw.sb("hh_", [128, 4, 128], F32)
    kwm = fw.sb("kwm", [128, 512], BF16)
    sg = fw.sb("sg", [128, 512], F32)
    negkq4 = fw.sb("negkq4", [128, 4, 128], F32)
    fw.op(DVE, lambda h: h.tensor_copy(out=negkq4[:], in_=bc(negkq[:].unsqueeze(1), [128, 4, 128])), reads=[negkq], writes=[negkq4])

    def mlstm_chunk(l, st, qT, kT, qkbuf, k_tok, v_aug, kvbuf, ig, fg, gbuf, mo, mobuf, out_ap, outbuf):
        fw.op(ACT, lambda h: h.activation(out=lfn[:], in_=fg, func=AF.Exp, scale=-1.0), reads=[gbuf], writes=[lfn])
        fw.op(ACT, lambda h: h.activation(out=lfn[:], in_=lfn[:], func=AF.Ln, bias=1.0), reads=[lfn], writes=[lfn])
        fw.op(PE, lambda h: h.matmul(pm[:, 0:4], lhsT=tri_f[:], rhs=lfn[:], start=True, stop=True), reads=[tri_f, lfn], writes=[pm])
        fw.op(PE, lambda h: h.matmul(pm[:, 4:8], lhsT=ones_f[:], rhs=lfn[:], start=True, stop=True), reads=[ones_f, lfn], writes=[pm])
        fw.op(ACT, lambda h: h.copy(out=nb_[:], in_=pm[:, 0:4]), reads=[pm], writes=[nb_])
        fw.op(ACT, lambda h: h.copy(out=nbl[:], in_=pm[:, 4:8]), reads=[pm], writes=[nbl])
        fw.op(DVE, lambda h: h.tensor_tensor(out=cc[:], in0=ig, in1=nb_[:], op=ALU.add), reads=[gbuf, nb_], writes=[cc])
        fw.op(DVE, lambda h: h.tensor_tensor(out=Dc[:], in0=bc(ident_f[:].unsqueeze(1), [128, 4, 128]), in1=bc(cc[:].unsqueeze(2), [128, 4, 128]), op=ALU.mult),
              reads=[ident_f, cc], writes=[Dc])
        fw.op(PE, lambda h: h.matmul(pd[0][:, :], lhsT=ones_f[:], rhs=Dc[:].rearrange("p a b -> p (a b)"), start=True, stop=True),
              reads=[ones_f, Dc], writes=[pd[0]])
        fw.op(DVE, lambda h: h.tensor_tensor(out=cmx[:], in0=pd[0][:, :].rearrange("p (a b) -> p a b", a=4), in1=bc(negqk[:].unsqueeze(1), [128, 4, 128]), op=ALU.add),
              reads=[pd[0], negqk], writes=[cmx])
        fw.op(DVE, lambda h: h.tensor_reduce(out=cm[:], in_=cmx[:], op=ALU.max, axis=AX.X), reads=[cmx], writes=[cm])
        fw.op(DVE, lambda h: h.tensor_tensor(out=Mq[:], in0=cm[:], in1=st.mrow[:], op=ALU.max), reads=[cm, st.mrow], writes=[Mq])
        fw.op(DVE, lambda h: h.tensor_tensor(out=mt[:], in0=Mq[:], in1=nb_[:], op=ALU.subtract), reads=[Mq, nb_], writes=[mt])
        fw.op(DVE, lambda h: h.tensor_scalar(out=negM[:], in0=Mq[:], scalar1=-1.0, scalar2=None, op0=ALU.mult), reads=[Mq], writes=[negM])
        fw.op(DVE, lambda h: h.tensor_tensor(out=Dc[:], in0=bc(ident_f[:].unsqueeze(1), [128, 4, 128]), in1=bc(negM[:].unsqueeze(2), [128, 4, 128]), op=ALU.mult),
              reads=[ident_f, negM], writes=[Dc])
        fw.op(PE, lambda h: h.matmul(pd[1][:, :], lhsT=ones_f[:], rhs=Dc[:].rearrange("p a b -> p (a b)"), start=True, stop=False),
              reads=[ones_f, Dc], writes=[pd[1]])
        fw.op(PE, lambda h: h.matmul(pd[1][:, :], lhsT=ident_f[:], rhs=negkq4[:].rearrange("p a b -> p (a b)"), start=False, stop=True),
              reads=[ident_f, negkq4], writes=[pd[1]])
        for hd in range(4):
            fw.op(ACT, lambda h: h.activation(out=swe[:, hd, :], in_=pd[1][:, hd * 128:(hd + 1) * 128], func=AF.Exp, bias=cc[:, hd:hd + 1]),
                  reads=[pd[1], cc], writes=[swe])
        for hd in range(4):
            fw.op(PE, lambda h: h.matmul(pd[0][:, hd * 128:(hd + 1) * 128], lhsT=kT(hd), rhs=qT(hd), start=True, stop=True), reads=qkbuf, writes=[pd[0]])
        fw.op(DVE, lambda h: h.tensor_tensor(out=swT[:], in0=pd[0][:, :].rearrange("p (a b) -> p a b", a=4), in1=swe[:], op=ALU.mult),
              reads=[pd[0], swe], writes=[swT])
        for hd in range(4):
            o = (hd // 2) * 512 + (hd % 2) * 129
            fw.op(PE, lambda h: h.matmul(pbig[:, o:o + 129], lhsT=swT[:, hd, :], rhs=v_aug(hd), start=True, stop=True), reads=[swT] + kvbuf, writes=[pbig])
            fw.op(PE, lambda h: h.matmul(pbig[:, 1024 + o:1024 + o + 129], lhsT=qT(hd), rhs=st.Cnb[:, hd, :], start=True, stop=True), reads=qkbuf + [st.Cnb], writes=[pbig])
        fw.op(DVE, lambda h: h.tensor_tensor(out=gq[:], in0=st.mrow[:], in1=Mq[:], op=ALU.subtract), reads=[st.mrow, Mq], writes=[gq])
        fw.op(ACT, lambda h: h.activation(out=gq[:], in_=gq[:], func=AF.Exp), reads=[gq], writes=[gq])
        fw.op(ACT, lambda h: h.activation(out=emt[:], in_=mt[:], func=AF.Exp, scale=-1.0), reads=[mt], writes=[emt])
        for hf in range(2):
            fw.op(DVE, lambda h: h.tensor_tensor(out=ints[:, hf * 2:hf * 2 + 2, :], in0=pbig[:, 1024 + hf * 512:1024 + hf * 512 + 258].rearrange("p (a b) -> p a b", a=2),
                                                 in1=bc(gq[:, hf * 2:hf * 2 + 2].unsqueeze(2), [128, 2, 129]), op=ALU.mult), reads=[pbig, gq], writes=[ints])
            fw.op(DVE, lambda h: h.tensor_tensor(out=tot[:, hf * 2:hf * 2 + 2, :], in0=pbig[:, hf * 512:hf * 512 + 258].rearrange("p (a b) -> p a b", a=2),
                                                 in1=ints[:, hf * 2:hf * 2 + 2, :], op=ALU.add), reads=[pbig, ints], writes=[tot])
        fw.op(DVE, lambda h: h.tensor_scalar(out=cm[:], in0=tot[:, :, 128], scalar1=-1.0, scalar2=None, op0=ALU.mult), reads=[tot], writes=[cm])
        fw.op(DVE, lambda h: h.tensor_tensor(out=cm[:], in0=cm[:], in1=tot[:, :, 128], op=ALU.max), reads=[tot, cm], writes=[cm])
        fw.op(DVE, lambda h: h.tensor_tensor(out=cm[:], in0=cm[:], in1=emt[:], op=ALU.max), reads=[cm, emt], writes=[cm])
        fw.op(DVE, lambda h: h.reciprocal(out=cm[:], in_=cm[:]), reads=[cm], writes=[cm])
        fw.op(DVE, lambda h: h.tensor_tensor(out=hh_[:], in0=tot[:, :, 0:128], in1=bc(cm[:].unsqueeze(2), [128, 4, 128]), op=ALU.mult), reads=[tot, cm], writes=[hh_])
        fw.op(DVE, lambda h: h.tensor_tensor(out=cmx[:], in0=hh_[:], in1=hh_[:], op=ALU.mult), reads=[hh_], writes=[cmx])
        fw.op(DVE, lambda h: h.tensor_reduce(out=cm[:], in_=cmx[:], op=ALU.add, axis=AX.X), reads=[cmx], writes=[cm])
        fw.op(DVE, lambda h: h.tensor_scalar(out=cm[:], in0=cm[:], scalar1=1.0 / 128, scalar2=EPS, op0=ALU.mult, op1=ALU.add), reads=[cm], writes=[cm])
        fw.op(ACT, lambda h: h.activation(out=cm[:], in_=cm[:], func=AF.Sqrt), reads=[cm], writes=[cm])
        fw.op(DVE, lambda h: h.reciprocal(out=cm[:], in_=cm[:]), reads=[cm], writes=[cm])
        fw.op(DVE, lambda h: h.tensor_tensor(out=hh_[:], in0=hh_[:], in1=bc(cm[:].unsqueeze(2), [128, 4, 128]), op=ALU.mult), reads=[hh_, cm], writes=[hh_])
        fw.op(DVE, lambda h: h.tensor_tensor(out=hh_[:].rearrange("p a b -> p (a b)"), in0=hh_[:].rearrange("p a b -> p (a b)"), in1=gbc[:, 1024:1536], op=ALU.mult),
              reads=[hh_, gbc], writes=[hh_])
        fw.op(ACT, lambda h: h.activation(out=sg[:], in_=mo, func=AF.Sigmoid), reads=[mobuf], writes=[sg])
        fw.op(DVE, lambda h: h.tensor_tensor(out=out_ap, in0=hh_[:].rearrange("p a b -> p (a b)"), in1=sg[:], op=ALU.mult), reads=[hh_, sg], writes=[outbuf])
        fw.op(PE, lambda h: h.matmul(pm[:, 8:12], lhsT=e127[:], rhs=mt[:], start=True, stop=True), reads=[e127, mt], writes=[pm])
        fw.op(ACT, lambda h: h.copy(out=mnew[:], in_=pm[:, 8:12]), reads=[pm], writes=[mnew])
        fw.op(DVE, lambda h: h.tensor_tensor(out=wkm[:], in0=cc[:], in1=nbl[:], op=ALU.subtract), reads=[cc, nbl], writes=[wkm])
        fw.op(DVE, lambda h: h.tensor_tensor(out=wkm[:], in0=wkm[:], in1=mnew[:], op=ALU.subtract), reads=[wkm, mnew], writes=[wkm])
        fw.op(ACT, lambda h: h.activation(out=wkm[:], in_=wkm[:], func=AF.Exp), reads=[wkm], writes=[wkm])
        fw.op(DVE, lambda h: h.tensor_tensor(out=gend[:], in0=st.mrow[:], in1=nbl[:], op=ALU.subtract), reads=[st.mrow, nbl], writes=[gend])
        fw.op(DVE, lambda h: h.tensor_tensor(out=gend[:], in0=gend[:], in1=mnew[:], op=ALU.subtract), reads=[gend, mnew], writes=[gend])
        fw.op(ACT, lambda h: h.activation(out=gend[:], in_=gend[:], func=AF.Exp), reads=[gend], writes=[gend])
        fw.op(DVE, lambda h: h.tensor_tensor(out=kwm[:].rearrange("p (a b) -> p a b", a=4), in0=k_tok.rearrange("p (a b) -> p a b", a=4),
                                             in1=bc(wkm[:].unsqueeze(2), [128, 4, 128]), op=ALU.mult), reads=kvbuf + [wkm], writes=[kwm])
        for hd in range(4):
            o = (hd // 2) * 512 + (hd % 2) * 129
            fw.op(PE, lambda h: h.matmul(pbig[:, o:o + 129], lhsT=kwm[:, hd * 128:(hd + 1) * 128], rhs=v_aug(hd), start=True, stop=True), reads=[kwm] + kvbuf, writes=[pbig])
        fw.op(DVE, lambda h: h.tensor_tensor(out=st.Cn[:], in0=st.Cn[:], in1=bc(gend[:].unsqueeze(2), [128, 4, 129]), op=ALU.mult), reads=[st.Cn, gend], writes=[st.Cn])
        for hf in range(2):
            fw.op(DVE, lambda h: h.tensor_tensor(out=st.Cn[:, hf * 2:hf * 2 + 2, :], in0=pbig[:, hf * 512:hf * 512 + 258].rearrange("p (a b) -> p a b", a=2),
                                                 in1=st.Cn[:, hf * 2:hf * 2 + 2, :], op=ALU.add), reads=[pbig, st.Cn], writes=[st.Cn])
        fw.op(ACT, lambda h: h.copy(out=st.Cnb[:], in_=st.Cn[:]), reads=[st.Cn], writes=[st.Cnb])
        fw.op(ACT, lambda h: h.copy(out=st.mrow[:], in_=mnew[:]), reads=[mnew], writes=[st.mrow])


    def rope(buf, ap3, np_, nh, ti):
        x1 = ap3[:, :, 0:8]; x2 = ap3[:, :, 8:16]
        cs_ = bc(cosT[0:np_, ti:ti + 1, :], [np_, nh, 8]); sn_ = bc(sinT[0:np_, ti:ti + 1, :], [np_, nh, 8])
        t = rtmp[0:np_, 0:nh, :]
        fw.op(DVE, lambda h: h.tensor_tensor(out=t[:, :, 0:8], in0=x2, in1=sn_, op=ALU.mult), reads=[buf, sinT], writes=[rtmp])
        fw.op(DVE, lambda h: h.tensor_tensor(out=t[:, :, 8:16], in0=x1, in1=sn_, op=ALU.mult), reads=[buf, sinT], writes=[rtmp])
        fw.op(DVE, lambda h: h.tensor_tensor(out=ap3[:, :, 0:16].rearrange("p a (c d) -> p a c d", c=2),
                                             in0=ap3[:, :, 0:16].rearrange("p a (c d) -> p a c d", c=2),
                                             in1=bc(cosT[0:np_, ti:ti + 1, :].unsqueeze(2), [np_, nh, 2, 8]), op=ALU.mult), reads=[buf, cosT], writes=[buf])
        fw.op(DVE, lambda h: h.tensor_tensor(out=x1, in0=x1, in1=t[:, :, 0:8], op=ALU.subtract), reads=[buf, rtmp], writes=[buf])
        fw.op(DVE, lambda h: h.tensor_tensor(out=x2, in0=x2, in1=t[:, :, 8:16], op=ALU.add), reads=[buf, rtmp], writes=[buf])

    def softplus(buf, ap):
        fw.op(ACT, lambda h: h.activation(out=ap, in_=ap, func=AF.Exp), reads=[buf], writes=[buf])
        fw.op(ACT, lambda h: h.activation(out=ap, in_=ap, func=AF.Ln, bias=1.0), reads=[buf], writes=[buf])

    def load_gain(row_ap, c0, n):
        fw.dma(SP, gbc[:, c0:c0 + n], row_ap.partition_broadcast(128), sem_g, writes=[gbc])


    sst = fw.sb("sst", [128, 8, 128], F32)

    def emit_state_out(l, st, d_ssm, d_C, d_n, d_m):
        for g0 in range(0, 8, 4):
            for j in range(g0, g0 + 4):
                fw.op(PE, lambda h: h.transpose(out=pd[0][:, (j - g0) * 128:(j - g0 + 1) * 128], in_=st.ST[:, j * 128:(j + 1) * 128], identity=ident_f[:]),
                      reads=[st.ST, ident_f], writes=[pd[0]])
            fw.op(ACT, lambda h: h.copy(out=sst[:, g0:g0 + 4, :], in_=pd[0][:, :].rearrange("p (a b) -> p a b", a=4)), reads=[pd[0]], writes=[sst])
        fw.dma(SP, d_ssm.rearrange("(j p) n -> p j n", p=128), sst[:], sem_o, reads=[sst], is_out=True)
        fw.dma(SP, d_C.rearrange("h d e -> d h e"), st.Cn[:, :, 0:128], sem_o, reads=[st.Cn], is_out=True)
        with nc.allow_non_contiguous_dma(reason="tiny state out"):
            fw.dma(SP, d_n.rearrange("h d -> d h"), st.Cn[:, :, 128], sem_o, reads=[st.Cn], is_out=True)
        fw.dma(SP, d_m.unsqueeze(0), st.mrow[0:1, :], sem_o, reads=[st.mrow], is_out=True)

    def load_state(l, st, r):
        fw.dma(SP, sst[:], st_ssm[l, r].rearrange("(j p) n -> p j n", p=128), sem_st, writes=[sst])
        for g0 in range(0, 8, 4):
            for j in range(g0, g0 + 4):
                fw.op(PE, lambda h: h.transpose(out=pd[0][:, (j - g0) * 128:(j - g0 + 1) * 128], in_=sst[:, j, :], identity=ident_f[:]),
                      reads=[sst, ident_f], writes=[pd[0]])
            fw.op(ACT, lambda h: h.copy(out=st.ST[:, g0 * 128:(g0 + 4) * 128], in_=pd[0][:, :]), reads=[pd[0]], writes=[st.ST])
        fw.op(ACT, lambda h: h.copy(out=st.STb[:], in_=st.ST[:]), reads=[st.ST], writes=[st.STb])
        fw.dma(SP, st.Cn[:, :, 0:128], st_C[l, r].rearrange("h d e -> d h e"), sem_st, writes=[st.Cn])
        with nc.allow_non_contiguous_dma(reason="tiny state in"):
            fw.dma(SP, st.Cn[:, :, 128], st_n[l, r].rearrange("h d -> d h"), sem_st, writes=[st.Cn])
        fw.dma(SP, st.mrow[:], st_m[l, r, :].partition_broadcast(128), sem_st, writes=[st.mrow])
        fw.op(ACT, lambda h: h.copy(out=st.Cnb[:], in_=st.Cn[:]), reads=[st.Cn], writes=[st.Cnb])

    xres = fw.sb("xres", [128, NT, D], F32, pes)
    actT = fw.sb("actT", [128, 16, ST], BF16, pes)
    big16 = fw.sb("big16", [128, NT * 2048], BF16, pes)
    utok = fw.sb("utok", [128, D], BF16, pes)
    sq = fw.sb("sq", [128, D], F32, pes)
    qkf = fw.sb("qkf", [128, NT, 640], F32, pes)
    qkb = fw.sb("qkb", [128, 768], BF16, pes)
    qT = fw.sb("qT", [128, 4, ST], BF16, pes)
    kTd = fw.sb("kTd", [128, 2, ST], BF16, pes)
    vaug = fw.sb("vaug", [128, NT, 2, 65], BF16, pes)
    ztok = fw.sb("ztok", [128, NT, 1024], BF16, pes)
    dtt = fw.sb("dtt", [128, NT, 16], F32, pes)
    mktok = fw.sb("mktok", [128, NT, 512], BF16, pes)
    mvaug = fw.sb("mvaug", [128, NT, 4, 129], BF16, pes)
    motok = fw.sb("motok", [128, NT, 512], BF16, pes)
    gates = fw.sb("gates", [128, NT, 8], F32, pes)
    xraw = fw.sb("xraw", [128, ST + 3], F32, pes)
    cacc = fw.sb("cacc", [128, ST], F32, pes)
    xcT = fw.sb("xcT", [128, 12, ST], BF16, pes)
    mqT = fw.sb("mqT", [128, 4, ST], BF16, pes)
    mkT = fw.sb("mkT", [128, 4, ST], BF16, pes)
    xtok = fw.sb("xtok", [128, 1024], BF16, pes)
    btok = fw.sb("btok", [128, 256], BF16, pes)
    ostage = sq

    fw.op(DVE, lambda h: h.memset(vaug[:], 1.0), writes=[vaug])
    fw.op(DVE, lambda h: h.memset(mvaug[:], 1.0), writes=[mvaug])
    for l in range(L):
        s = carry[l]
        fw.op(DVE, lambda h: h.memset(s.ST[:], 0.0), writes=[s.ST])
        fw.op(DVE, lambda h: h.memset(s.STb[:], 0.0), writes=[s.STb])
        fw.op(DVE, lambda h: h.memset(s.Cn[:], 0.0), writes=[s.Cn])
        fw.op(DVE, lambda h: h.memset(s.Cnb[:], 0.0), writes=[s.Cnb])
        fw.op(DVE, lambda h: h.memset(s.mrow[:], 0.0), writes=[s.mrow])
        fw.op(DVE, lambda h: h.memset(s.xcarry[:], 0.0), writes=[s.xcarry])

    mix_tok = big16[:].rearrange("p (a b) -> p a b", a=NT)
    ck("consts")
    hTg = big16[:].rearrange("p (a b) -> p a b", a=16)
    HT = Buf("HT", hTg, parent=big16)

    def norm_to_actT(nt, tsz, gain_row, xfn):
        load_gain(gain_row, 0, D)
        for tt in range(nt):
            rmsnorm_to(xfn(tt), xres, tsz, gbc[0:tsz, :], utok[0:tsz, :], utok, sq, D, col=tt)
            transpose_to(utok[0:tsz, :], utok, tsz, D, lambda j: actT[:, j, tt * 128:tt * 128 + tsz], actT)

    for stn in range(NST):
        new_pass()
        t0g = stn * ST
        for tt in range(NT):
            fw.dma(SP, xres[:, tt, :], xp[t0g + tt * 128:t0g + (tt + 1) * 128, :], sem_x, writes=[xres])
        tts = [(tt * 128, 128) for tt in range(NT)]
        for l in range(L):
            st = carry[l]
            P = lp[l]
            W = w_in[l]
            norm_to_actT(NT, 128, w_norm_mix[l, :], lambda tt: xres[:, tt, :])
            ck("norm")
            load_gain(w_norm_ssm[l, :], 0, 1024)
            load_gain(w_norm_ml[l, :], 1024, 512)

            def c_qkv(ti, c0, nb, p):
                if c0 < 512:
                    fw.op(ACT, lambda h: h.copy(out=qkf[:, ti, c0:c0 + nb], in_=p[:, 0:nb]), reads=[p], writes=[qkf])
                else:
                    fw.op(ACT, lambda h: h.copy(out=qkf[:, ti, 512:640], in_=p[:, 0:128]), reads=[p], writes=[qkf])
                    fw.op(ACT, lambda h: h.copy(out=vaug[:, ti, :, 0:64], in_=p[:, 128:256].rearrange("p (a b) -> p a b", a=2)), reads=[p], writes=[vaug])
                    rope(qkf, qkf[:, ti, :].rearrange("p (a b) -> p a b", b=64), 128, 10, stn * NT + ti)
                    fw.op(DVE, lambda h: h.tensor_copy(out=qkb[:, 0:512], in_=qkf[:, ti, 0:512]), reads=[qkf], writes=[qkb])
                    fw.op(DVE, lambda h: h.tensor_copy(out=qkb[:, 512:768].rearrange("p (a c b) -> p a c b", a=2, c=2),
                                                       in_=bc(qkf[:, ti, 512:640].rearrange("p (a b) -> p a b", a=2).unsqueeze(2), [128, 2, 2, 64])),
                          reads=[qkf], writes=[qkb])
                    transpose_to(qkb[:, 0:512], qkb, 128, 512, lambda j: qT[:, j, ti * 128:(ti + 1) * 128], qT)
                    transpose_to(qkb[:, 512:768], qkb, 128, 256, lambda j: kTd[:, j, ti * 128:(ti + 1) * 128], kTd)
                    if stn == NST - 1 and ti == NT - 1:
                        fw.op(ACT, lambda h: h.copy(out=ostage[:, 0:128], in_=qkf[:, ti, 512:640]), reads=[qkf], writes=[ostage])
                        fw.op(ACT, lambda h: h.copy(out=ostage[:, 128:256], in_=p[:, 128:256]), reads=[p], writes=[ostage])
                        fw.dma(SP, p_k[l], ostage[:, 0:128], sem_o, reads=[ostage], is_out=True)
                        fw.dma(SP, p_v[l], ostage[:, 128:256], sem_o, reads=[ostage], is_out=True)
            dense_tok(actT, tts, W[:, O_Q:O_Q + 768], 768, c_qkv)
            ck("qkv")

            def c_z(ti, c0, nb, p):
                fw.op(ACT, lambda h: h.copy(out=ztok[:, ti, c0:c0 + nb], in_=p[:, 0:nb]), reads=[p], writes=[ztok])
            dense_tok(actT, tts, W[:, O_Z:O_Z + 1024], 1024, c_z)

            def c_dt(ti, c0, nb, p):
                fw.op(DVE, lambda h: h.tensor_tensor(out=dtt[:, ti, :], in0=p[:, 0:16], in1=P["dtb"][:], op=ALU.add), reads=[p, P["dtb"]], writes=[dtt])
                softplus(dtt, dtt[:, ti, :])
            dense_tok(actT, tts, W[:, O_DT:O_DT + 16], 16, c_dt)

            def c_mk(ti, c0, nb, p):
                fw.op(ACT, lambda h: h.activation(out=mktok[:, ti, c0:c0 + nb], in_=p[:, 0:nb], func=AF.Copy, scale=float(128 ** -0.5)), reads=[p], writes=[mktok])
                if c0 + nb == 512:
                    transpose_to(mktok[:, ti, :], mktok, 128, 512, lambda j: mkT[:, j, ti * 128:(ti + 1) * 128], mkT)
            dense_tok(actT, tts, W[:, O_MK:O_MK + 512], 512, c_mk)

            def c_mv(ti, c0, nb, p):
                h0 = c0 // 128
                fw.op(ACT, lambda h: h.copy(out=mvaug[:, ti, h0:h0 + nb // 128, 0:128], in_=p[:, 0:nb].rearrange("p (a b) -> p a b", b=128)), reads=[p], writes=[mvaug])
            dense_tok(actT, tts, W[:, O_MV:O_MV + 512], 512, c_mv)

            def c_mo(ti, c0, nb, p):
                fw.op(ACT, lambda h: h.copy(out=motok[:, ti, c0:c0 + nb], in_=p[:, 0:nb]), reads=[p], writes=[motok])
            dense_tok(actT, tts, W[:, O_MO:O_MO + 512], 512, c_mo)

            def c_g(ti, c0, nb, p):
                fw.op(DVE, lambda h: h.tensor_tensor(out=gates[:, ti, :], in0=p[:, 0:8], in1=P["gb"][:], op=ALU.add), reads=[p, P["gb"]], writes=[gates])
            dense_tok(actT, tts, W[:, O_MI:O_MI + 8], 8, c_g)
            ck("tokproj")

            def c_mq(cb_, p):
                fw.op(ACT, lambda h: h.copy(out=mqT[:, cb_, :], in_=p[:, 0:ST]), reads=[p], writes=[mqT])
            dense_feat(actT, ST, W[:, O_MQ:O_MQ + 512], 512, c_mq)

            def c_xbc(cb_, p):
                fw.op(ACT, lambda h: h.copy(out=xraw[:, 0:3], in_=st.xcarry[:, cb_, :]), reads=[st.xcarry], writes=[xraw])
                fw.op(ACT, lambda h: h.copy(out=xraw[:, 3:ST + 3], in_=p[:, 0:ST]), reads=[p], writes=[xraw])
                fw.op(ACT, lambda h: h.copy(out=st.xcarry[:, cb_, :], in_=xraw[:, ST:ST + 3]), reads=[xraw], writes=[st.xcarry])
                cw = P["cw"]
                fw.op(DVE, lambda h: h.tensor_scalar(out=cacc[:], in0=xraw[:, 0:ST], scalar1=cw[:, cb_, 0:1], scalar2=None, op0=ALU.mult), reads=[xraw, cw], writes=[cacc])
                for j in range(1, 4):
                    fw.op(DVE, lambda h: h.scalar_tensor_tensor(out=cacc[:], in0=xraw[:, j:j + ST], scalar=cw[:, cb_, j:j + 1], in1=cacc[:], op0=ALU.mult, op1=ALU.add),
                          reads=[xraw, cw, cacc], writes=[cacc])
                fw.op(ACT, lambda h: h.activation(out=xcT[:, cb_, :], in_=cacc[:], func=AF.Silu, bias=P["cb"][:, cb_:cb_ + 1]), reads=[cacc, P["cb"]], writes=[xcT])
            dense_feat(actT, ST, W[:, O_XBC:O_XBC + 1536], 1536, c_xbc)
            ck("proj")
            if stn == NST - 1:
                with nc.allow_non_contiguous_dma(reason="tiny conv state out"):
                    for j in range(3):
                        fw.dma(SP, p_conv[l, j, :].rearrange("(b p) -> p b", p=128), st.xcarry[:, :, j], sem_o, reads=[st.xcarry], is_out=True)

            for c in range(NT):
                sl = slice(c * 128, (c + 1) * 128)
                has_prev = not (stn == 0 and c == 0)
                if c == 0:
                    kpf = lambda kv: st.kprev[:, kv, :]; kpb = st.kprev
                    vpf = lambda kv: st.vprev[:, kv, :]; vpb = st.vprev
                else:
                    kpf = (lambda cc_: (lambda kv: kTd[:, kv, (cc_ - 1) * 128:cc_ * 128]))(c); kpb = kTd
                    vpf = (lambda cc_: (lambda kv: vaug[:, cc_ - 1, kv, :]))(c); vpb = vaug
                swa_block(l, lambda j: qT[:, j, sl], qT, lambda kv: kTd[:, kv, sl], kTd, lambda kv: vaug[:, c, kv, :], vaug,
                          kpf, kpb, vpf, vpb, has_prev, mix_tok[:, c, 0:512], big16)
                ck("swa")
                if c == NT - 1:
                    fw.op(ACT, lambda h: h.copy(out=st.kprev[:], in_=kTd[:, :, sl]), reads=[kTd], writes=[st.kprev])
                    fw.op(ACT, lambda h: h.copy(out=st.vprev[:], in_=vaug[:, NT - 1, :, :]), reads=[vaug], writes=[st.vprev])
                for j in range(8):
                    fw.op(PE, lambda h: h.transpose(out=ptb[:, j * 128:(j + 1) * 128], in_=xcT[:, j, sl], identity=ident_b[:]), reads=[xcT, ident_b], writes=[ptb])
                fw.op(ACT, lambda h: h.copy(out=xtok[:], in_=ptb[:, 0:1024]), reads=[ptb], writes=[xtok])
                for j in range(2):
                    fw.op(PE, lambda h: h.transpose(out=ptb[:, j * 128:(j + 1) * 128], in_=xcT[:, 8 + j, sl], identity=ident_b[:]), reads=[xcT, ident_b], writes=[ptb])
                fw.op(ACT, lambda h: h.copy(out=btok[:], in_=ptb[:, 0:256]), reads=[ptb], writes=[btok])
                ssd_chunk(l, st, xtok[:], xtok, btok[:], btok, lambda g: xcT[:, 8 + g, sl], lambda g: xcT[:, 10 + g, sl], xcT,
                          dtt[:, c, :], dtt, ztok[:, c, :], ztok, mix_tok[:, c, 512:1536], big16)
                ck("ssd")
                mlstm_chunk(l, st, lambda hd: mqT[:, hd, sl], lambda hd: mkT[:, hd, sl], [mqT, mkT], mktok[:, c, :], lambda hd: mvaug[:, c, hd, :], [mktok, mvaug],
                            gates[:, c, 0:4], gates[:, c, 4:8], gates, motok[:, c, :], motok, mix_tok[:, c, 1536:2048], big16)
                if DEBUG_STOP[0] == "mlstm":
                    dump("mix", big16, mix_tok[:, c, :], [128, 2048])
                    dump("dtt", dtt, dtt[:, c, :], [128, 16])
                    dump("xtok", xtok, xtok[:], [128, 1024])
                    dump("ST", st.ST, st.ST[:], [128, 1024])
                    dump("Cn", st.Cn, st.Cn[:], [128, 4, 129])
                    dump("mrow", st.mrow, st.mrow[:], [128, 4])
                ck("mlstm")
            if stn == NST - 1:
                emit_state_out(l, st, p_ssm[l], p_C[l], p_n[l], p_m[l])

            for tt in range(NT):
                transpose_to(mix_tok[:, tt, :], big16, 128, D, lambda j: actT[:, j, tt * 128:(tt + 1) * 128], actT)

            def c_res(ti, c0, nb, p):
                fw.op(DVE, lambda h: h.tensor_tensor(out=xres[:, ti, c0:c0 + nb], in0=p[:, 0:nb], in1=xres[:, ti, c0:c0 + nb], op=ALU.add), reads=[p, xres], writes=[xres])
            dense_tok(actT, tts, w_out[l], D, c_res)
            ck("wout")

            norm_to_actT(NT, 128, w_norm_mlp[l, :], lambda tt: xres[:, tt, :])
            for g in range(4):
                def c_up(cb_, p):
                    fw.op(ACT, lambda h: h.activation(out=sq[:, 0:ST], in_=p[:, 0:ST], func=AF.Relu), reads=[p], writes=[sq])
                    fw.op(DVE, lambda h: h.tensor_tensor(out=hTg[:, cb_, :], in0=sq[:, 0:ST], in1=sq[:, 0:ST], op=ALU.mult), reads=[sq], writes=[big16])
                dense_feat(actT, ST, w_up[l][:, g * 2048:(g + 1) * 2048], 2048, c_up)
                dense_tok(HT, tts, w_down[l][g * 2048:(g + 1) * 2048, :], D, c_res)
            ck("layer")

        load_gain(w_norm_final, 0, D)
        for tt in range(NT):
            rmsnorm_to(xres[:, tt, :], xres, 128, gbc[:, :], ostage[:, :], ostage, sq, D, col=tt)
            fw.dma(SP, y_p[t0g + tt * 128:t0g + (tt + 1) * 128, :], ostage[:, :], sem_o, reads=[ostage], is_out=True)
        ck("st")
        ck("st%d" % stn)


    fw.barrier()
    pes.close()
    fw.es_alloc = None
    ck("prompt")
    ses = ExitStack()
    xrs = fw.sb("xrs", [RS, D], F32, ses)
    actS = fw.sb("actS", [128, 16, RS], BF16, ses)
    utS = fw.sb("utS", [RS, D], BF16, ses)
    sqS = fw.sb("sqS", [RS, D], F32, ses)
    sall = fw.sb("sall", [RS, INW], F32, ses)
    mixs = fw.sb("mixs", [RS, D], BF16, ses)
    hsT = fw.sb("hsT", [128, 16, RS], BF16, ses)
    xcs = sqS
    PB = [fw.sb(f"pb{i}", [RS, 8192], F32, ses) for i in range(3)]
    cj = PB[0]
    wj = PB[1]
    pbc = [0]

    def nextpb():
        b_ = PB[pbc[0] % 3]
        pbc[0] += 1
        return b_
    scs = fw.sb("scs", [RS, 8, 129], F32, ses)
    sden = fw.sb("sden", [RS, 8], F32, ses)
    so = fw.sb("so", [RS, 2, 8, 64], F32, ses)
    dec = fw.sb("dec", [RS, 16], F32, ses)
    xdt = fw.sb("xdt", [RS, 16, 64], F32, ses)
    yv = fw.sb("yv", [RS, 16, 64], F32, ses)
    nst = fw.sb("nst", [RS, 4, 128], F32, ses)
    ms = fw.sb("ms", [RS, 48], F32, ses)
    mtmp = fw.sb("mtmp", [RS, 4, 128], F32, ses)
    kws = fw.sb("kws", [RS, 4, 128], F32, ses)
    qcs = Buf("qcs", so[:].rearrange("p a h d -> p (a h d)").rearrange("p (a h d) -> p a h d", a=2, h=4), parent=so)
    sem_s = None
    sem_m = None

    fw.dma(SP, xrs[:], xsm[:, :], sem_s, writes=[xrs])
    ttS = [(0, RS)]
    new_pass()

    def stage_tok(r, c0, n, dst_ap, dstbuf, scale=None):
        for o in range(0, n, 512):
            m = min(512, n - o)
            fw.op(PE, lambda h: h.matmul(pd[1][:, 0:m], lhsT=oh[:, r, :], rhs=sall[:, c0 + o:c0 + o + m], start=True, stop=True), reads=[oh, sall], writes=[pd[1]])
            fw.op(ACT, lambda h: h.copy(out=dst_ap(o, m), in_=pd[1][:, 0:m]), reads=[pd[1]], writes=[dstbuf])

    def stage_feat(r, srcbuf, src_ap, dst_ap, dstbuf):
        fw.op(PE, lambda h: h.matmul(pd[1][:, 0:128], lhsT=src_ap, rhs=oh[:, r, :], start=True, stop=True), reads=[oh, srcbuf], writes=[pd[1]])
        fw.op(ACT, lambda h: h.copy(out=dst_ap, in_=pd[1][:, 0:128]), reads=[pd[1]], writes=[dstbuf])

    def norm_to_actS(gain_row):
        load_gain(gain_row, 0, D)
        rmsnorm_to(xrs[:, :], xrs, RS, gbc[0:RS, :], utS[:, :], utS, sqS, D, col=0)
        transpose_to(utS[:, :], utS, RS, D, lambda j: actS[:, j, :], actS)

    def c_res_s(ti, c0, nb, p):
        fw.op(DVE, lambda h: h.tensor_tensor(out=xrs[:, c0:c0 + nb], in0=p[0:RS, 0:nb], in1=xrs[:, c0:c0 + nb], op=ALU.add), reads=[p, xrs], writes=[xrs])

    for l in range(L):
        st = carry[l]
        P = lp[l]
        norm_to_actS(w_norm_mix[l, :])
        load_gain(w_norm_ssm[l, :], 0, 1024)
        load_gain(w_norm_ml[l, :], 1024, 512)

        def c_all(ti, c0, nb, p):
            fw.op(ACT, lambda h: h.copy(out=sall[:, c0:c0 + nb], in_=p[0:RS, 0:nb]), reads=[p], writes=[sall])
        for (o_, n_) in [(O_Q, 768), (O_Z, 1024), (O_DT, 16), (O_MK, 512), (O_MV, 512), (O_MO, 512), (O_MI, 8), (O_MQ, 512), (O_XBC, 1536)]:
            def c_seg(ti, c0, nb, p, o_=o_):
                c_all(ti, o_ + c0, nb, p)
            dense_tok(actS, ttS, w_in[l][:, o_:o_ + n_], n_, c_seg)
        rope(sall, sall[:, 0:640].rearrange("p (a b) -> p a b", b=64), RS, 10, 16)
        fw.op(DVE, lambda h: h.tensor_scalar(out=sall[:, O_MK:O_MK + 512], in0=sall[:, O_MK:O_MK + 512], scalar1=float(128 ** -0.5), scalar2=None, op0=ALU.mult), reads=[sall], writes=[sall])
        fw.op(DVE, lambda h: h.tensor_tensor(out=sall[:, O_DT:O_DT + 16], in0=sall[:, O_DT:O_DT + 16], in1=P["dtb"][0:RS, :], op=ALU.add), reads=[sall, P["dtb"]], writes=[sall])
        softplus(sall, sall[:, O_DT:O_DT + 16])
        fw.op(DVE, lambda h: h.tensor_tensor(out=sall[:, O_MI:O_MI + 8], in0=sall[:, O_MI:O_MI + 8], in1=P["gb"][0:RS, :], op=ALU.add), reads=[sall, P["gb"]], writes=[sall])
        fw.dma(SP, s_k[l, :, 0:127, :], cache_k[l, :, 1:128, :], sem_o, is_out=True)
        fw.dma(SP, s_v[l, :, 0:127, :], cache_v[l, :, 1:128, :], sem_o, is_out=True)
        fw.dma(SP, s_k[l, :, 127, :], sall[:, O_K:O_K + 128], sem_o, reads=[sall], is_out=True)
        fw.dma(SP, s_v[l, :, 127, :], sall[:, O_V:O_V + 128], sem_o, reads=[sall], is_out=True)
        fw.dma(SP, s_conv[l, :, 0:2, :], st_conv[l, :, 1:3, :], sem_o, is_out=True)
        fw.dma(SP, s_conv[l, :, 2, :], sall[:, O_XBC:O_XBC + 1536], sem_o, reads=[sall], is_out=True)
        fw.dma(SP, wj[:, 0:1536], conv_w[l, 3, :].partition_broadcast(RS), sem_s, writes=[wj])
        fw.op(DVE, lambda h: h.tensor_tensor(out=xcs[:, 0:1536], in0=sall[:, O_XBC:O_XBC + 1536], in1=wj[:, 0:1536], op=ALU.mult), reads=[sall, wj], writes=[xcs])
        for j in range(3):
            fw.dma(SP, cj[:, 0:1536], st_conv[l, :, j, :], sem_s, writes=[cj])
            fw.dma(SP, wj[:, 0:1536], conv_w[l, j, :].partition_broadcast(RS), sem_s, writes=[wj])
            fw.op(DVE, lambda h: h.tensor_tensor(out=cj[:, 0:1536], in0=cj[:, 0:1536], in1=wj[:, 0:1536], op=ALU.mult), reads=[cj, wj], writes=[cj])
            fw.op(DVE, lambda h: h.tensor_tensor(out=xcs[:, 0:1536], in0=xcs[:, 0:1536], in1=cj[:, 0:1536], op=ALU.add), reads=[xcs, cj], writes=[xcs])
        fw.dma(SP, wj[:, 0:1536], conv_b[l, :].partition_broadcast(RS), sem_s, writes=[wj])
        fw.op(DVE, lambda h: h.tensor_tensor(out=xcs[:, 0:1536], in0=xcs[:, 0:1536], in1=wj[:, 0:1536], op=ALU.add), reads=[xcs, wj], writes=[xcs])
        fw.op(ACT, lambda h: h.activation(out=sall[:, O_XBC:O_XBC + 1536], in_=xcs[:, 0:1536], func=AF.Silu), reads=[xcs, sall], writes=[sall])

        A_ = P["A"]
        qv = sall[:, 0:512].rearrange("p (h d) -> p h d", h=8)
        knew = sall[:, O_K:O_K + 128].rearrange("p (a d) -> p a d", a=2)
        vnew = sall[:, O_V:O_V + 128].rearrange("p (a d) -> p a d", a=2)
        for kv in range(2):
            Kc, Vc, T = nextpb(), nextpb(), nextpb()
            Kc3 = Kc[:, :].rearrange("p (s d) -> p s d", d=64)
            Vc3 = Vc[:, :].rearrange("p (s d) -> p s d", d=64)
            T3 = T[:, 0:4096].rearrange("p (a b) -> p a b", a=64)
            with nc.allow_non_contiguous_dma(reason="kv cache head slice (256B runs)"):
                fw.dma(SP, Kc3, cache_k[l, :, :, kv * 64:(kv + 1) * 64], None, writes=[Kc])
                fw.dma(SP, Vc3, cache_v[l, :, :, kv * 64:(kv + 1) * 64], None, writes=[Vc])
            hs = slice(kv * 4, kv * 4 + 4)
            for h4 in range(4):
                h_ = kv * 4 + h4
                for ch in range(2):
                    ps_ = slice(ch * 64, (ch + 1) * 64)
                    fw.op(DVE, lambda h: h.tensor_tensor(out=T3, in0=Kc3[:, ps_, :], in1=bc(qv[:, h_:h_ + 1, :], [RS, 64, 64]), op=ALU.mult), reads=[Kc, sall], writes=[T])
                    fw.op(DVE, lambda h: h.tensor_reduce(out=scs[:, h_, ps_], in_=T3, op=ALU.add, axis=AX.X), reads=[T], writes=[scs])
            fw.op(DVE, lambda h: h.tensor_tensor(out=T[:, 0:256].rearrange("p (a b) -> p a b", a=4), in0=qv[:, hs, :], in1=bc(knew[:, kv:kv + 1, :], [RS, 4, 64]), op=ALU.mult), reads=[sall], writes=[T])
            fw.op(DVE, lambda h: h.tensor_reduce(out=scs[:, hs, 128], in_=T[:, 0:256].rearrange("p (a b) -> p a b", a=4), op=ALU.add, axis=AX.X), reads=[T], writes=[scs])
            fw.op(ACT, lambda h: h.activation(out=scs[:, hs, :], in_=scs[:, hs, :], func=AF.Exp, scale=0.125), reads=[scs], writes=[scs])
            fw.op(DVE, lambda h: h.tensor_reduce(out=sden[:, hs], in_=scs[:, hs, :], op=ALU.add, axis=AX.X), reads=[scs], writes=[sden])
            for h4 in range(4):
                h_ = kv * 4 + h4
                for ch in range(2):
                    ps_ = slice(ch * 64, (ch + 1) * 64)
                    fw.op(DVE, lambda h: h.tensor_tensor(out=T3, in0=Vc3[:, ps_, :].rearrange("p s d -> p d s"), in1=bc(scs[:, h_:h_ + 1, ps_], [RS, 64, 64]), op=ALU.mult), reads=[Vc, scs], writes=[T])
                    fw.op(DVE, lambda h: h.tensor_reduce(out=so[:, ch, h_, :], in_=T3, op=ALU.add, axis=AX.X), reads=[T], writes=[so])
                fw.op(DVE, lambda h: h.scalar_tensor_tensor(out=so[:, 0, h_, :], in0=vnew[:, kv, :], scalar=scs[:, h_, 128:129], in1=so[:, 0, h_, :], op0=ALU.mult, op1=ALU.add), reads=[sall, scs, so], writes=[so])
        fw.op(DVE, lambda h: h.tensor_tensor(out=so[:, 0, :, :], in0=so[:, 0, :, :], in1=so[:, 1, :, :], op=ALU.add), reads=[so], writes=[so])
        fw.op(DVE, lambda h: h.tensor_tensor(out=sden[:], in0=sden[:], in1=P["esk"][0:RS, :], op=ALU.add), reads=[sden, P["esk"]], writes=[sden])
        fw.op(DVE, lambda h: h.reciprocal(out=sden[:], in_=sden[:]), reads=[sden], writes=[sden])
        fw.op(DVE, lambda h: h.tensor_tensor(out=mixs[:, 0:512].rearrange("p (a b) -> p a b", a=8), in0=so[:, 0, :, :], in1=bc(sden[:].unsqueeze(2), [RS, 8, 64]), op=ALU.mult), reads=[so, sden], writes=[mixs])

        x16 = sall[:, O_XBC:O_XBC + 1024].rearrange("p (a b) -> p a b", a=16)
        Bm = sall[:, O_XBC + 1024:O_XBC + 1280].rearrange("p (a b) -> p a b", a=2)
        Cm = sall[:, O_XBC + 1280:O_XBC + 1536].rearrange("p (a b) -> p a b", a=2)
        dts = sall[:, O_DT:O_DT + 16]
        fw.op(DVE, lambda h: h.tensor_tensor(out=dec[:], in0=dts, in1=A_[0:RS, :], op=ALU.mult), reads=[sall, A_], writes=[dec])
        fw.op(ACT, lambda h: h.activation(out=dec[:], in_=dec[:], func=AF.Exp), reads=[dec], writes=[dec])
        fw.op(DVE, lambda h: h.tensor_tensor(out=xdt[:], in0=x16, in1=bc(dts.unsqueeze(2), [RS, 16, 64]), op=ALU.mult), reads=[sall], writes=[xdt])
        for hh in range(16):
            g = hh // 8
            Sp, T = nextpb(), nextpb()
            Sp3 = Sp[:, :].rearrange("p (a b) -> p a b", a=64)
            T3 = T[:, :].rearrange("p (a b) -> p a b", a=64)
            fw.dma(SP, Sp3, st_ssm[l, :, hh * 64:(hh + 1) * 64, :], None, writes=[Sp])
            fw.op(POOL, lambda h: h.tensor_tensor(out=T3, in0=bc(xdt[:, hh, :].unsqueeze(2), [RS, 64, 128]), in1=bc(Bm[:, g:g + 1, :], [RS, 64, 128]), op=ALU.mult), reads=[xdt, sall], writes=[T])
            fw.op(DVE, lambda h: h.scalar_tensor_tensor(out=Sp[:, :], in0=Sp[:, :], scalar=dec[:, hh:hh + 1], in1=T[:, :], op0=ALU.mult, op1=ALU.add), reads=[Sp, dec, T], writes=[Sp])
            fw.dma(SP, s_ssm[l, :, hh * 64:(hh + 1) * 64, :], Sp3, None, reads=[Sp], is_out=True)
            fw.op(DVE, lambda h: h.tensor_tensor(out=T3, in0=Sp3, in1=bc(Cm[:, g:g + 1, :], [RS, 64, 128]), op=ALU.mult), reads=[Sp, sall], writes=[T])
            fw.op(DVE, lambda h: h.tensor_reduce(out=yv[:, hh, :], in_=T3, op=ALU.add, axis=AX.X), reads=[T], writes=[yv])
        fw.op(DVE, lambda h: h.tensor_tensor(out=xdt[:], in0=x16, in1=bc(P["dsk"][0:RS, :].unsqueeze(2), [RS, 16, 64]), op=ALU.mult), reads=[sall, P["dsk"]], writes=[xdt])
        fw.op(DVE, lambda h: h.tensor_tensor(out=yv[:], in0=yv[:], in1=xdt[:], op=ALU.add), reads=[yv, xdt], writes=[yv])
        fw.op(ACT, lambda h: h.activation(out=xdt[:].rearrange("p a b -> p (a b)"), in_=sall[:, O_Z:O_Z + 1024], func=AF.Silu), reads=[sall], writes=[xdt])
        fw.op(DVE, lambda h: h.tensor_tensor(out=yv[:], in0=yv[:], in1=xdt[:], op=ALU.mult), reads=[yv, xdt], writes=[yv])
        yvf = yv[:].rearrange("p a b -> p (a b)")
        for g in range(2):
            rmsnorm_to(yvf[:, g * 512:(g + 1) * 512], yv, RS, gbc[0:RS, g * 512:(g + 1) * 512], mixs[:, 512 + g * 512:512 + (g + 1) * 512], mixs, sqS, 512, col=8 + g)

        q4 = sall[:, O_MQ:O_MQ + 512].rearrange("p (a b) -> p a b", a=4)
        k4 = sall[:, O_MK:O_MK + 512].rearrange("p (a b) -> p a b", a=4)
        v4 = sall[:, O_MV:O_MV + 512].rearrange("p (a b) -> p a b", a=4)
        igs = sall[:, O_MI:O_MI + 4]
        fgs = sall[:, O_MF:O_MF + 4]
        fw.dma(SP, nst[:], st_n[l], None, writes=[nst])
        fw.dma(SP, ms[:, 0:4], st_m[l], None, writes=[ms])
        M_, LF, BM, MT, SWS, G_, EMT, QK, QN, SW, DEN, RD = [ms[:, 4 * i:4 * i + 4] for i in range(12)]
        def dv(out, in0, in1, op):
            fw.op(DVE, lambda h: h.tensor_tensor(out=out, in0=in0, in1=in1, op=op), reads=[ms, sall], writes=[ms])
        fw.op(ACT, lambda h: h.activation(out=LF, in_=fgs, func=AF.Exp, scale=-1.0), reads=[sall], writes=[ms])
        fw.op(ACT, lambda h: h.activation(out=LF, in_=LF, func=AF.Ln, bias=1.0), reads=[ms], writes=[ms])
        dv(BM, M_, LF, ALU.subtract)
        dv(MT, BM, igs, ALU.max)
        dv(SWS, igs, MT, ALU.subtract)
        fw.op(ACT, lambda h: h.activation(out=SWS, in_=SWS, func=AF.Exp), reads=[ms], writes=[ms])
        dv(G_, BM, MT, ALU.subtract)
        fw.op(ACT, lambda h: h.activation(out=G_, in_=G_, func=AF.Exp), reads=[ms], writes=[ms])
        fw.op(ACT, lambda h: h.activation(out=EMT, in_=MT, func=AF.Exp, scale=-1.0), reads=[ms], writes=[ms])
        fw.op(DVE, lambda h: h.tensor_tensor(out=mtmp[:], in0=q4, in1=k4, op=ALU.mult), reads=[sall], writes=[mtmp])
        fw.op(DVE, lambda h: h.tensor_reduce(out=QK, in_=mtmp[:], op=ALU.add, axis=AX.X), reads=[mtmp], writes=[ms])
        fw.op(DVE, lambda h: h.tensor_tensor(out=mtmp[:], in0=q4, in1=nst[:], op=ALU.mult), reads=[sall, nst], writes=[mtmp])
        fw.op(DVE, lambda h: h.tensor_reduce(out=QN, in_=mtmp[:], op=ALU.add, axis=AX.X), reads=[mtmp], writes=[ms])
        dv(SW, SWS, QK, ALU.mult)
        dv(DEN, G_, QN, ALU.mult)
        dv(DEN, DEN, SW, ALU.add)
        fw.op(DVE, lambda h: h.tensor_scalar(out=RD, in0=DEN, scalar1=-1.0, scalar2=None, op0=ALU.mult), reads=[ms], writes=[ms])
        dv(RD, RD, DEN, ALU.max)
        dv(RD, RD, EMT, ALU.max)
        fw.op(DVE, lambda h: h.reciprocal(out=RD, in_=RD), reads=[ms], writes=[ms])
        fw.op(DVE, lambda h: h.tensor_tensor(out=kws[:], in0=k4, in1=bc(SWS.unsqueeze(2), [RS, 4, 128]), op=ALU.mult), reads=[sall, ms], writes=[kws])
        fw.op(DVE, lambda h: h.tensor_tensor(out=nst[:], in0=nst[:], in1=bc(G_.unsqueeze(2), [RS, 4, 128]), op=ALU.mult), reads=[nst, ms], writes=[nst])
        fw.op(DVE, lambda h: h.tensor_tensor(out=nst[:], in0=nst[:], in1=kws[:], op=ALU.add), reads=[nst, kws], writes=[nst])
        fw.dma(SP, s_n[l], nst[:], None, reads=[nst], is_out=True)
        fw.dma(SP, s_m[l], MT, None, reads=[ms], is_out=True)
        for hd in range(4):
            for hf in range(2):
                dsl = slice(hf * 64, (hf + 1) * 64)
                Ct, T = nextpb(), nextpb()
                Ct3 = Ct[:, :].rearrange("p (a b) -> p a b", a=64)
                T3 = T[:, :].rearrange("p (a b) -> p a b", a=64)
                Te = T[:, :].rearrange("p (e d) -> p e d", e=128)
                fw.dma(SP, Ct3, st_C[l, :, hd, dsl, :], None, writes=[Ct])
                fw.op(DVE, lambda h: h.tensor_tensor(out=Te, in0=Ct3.rearrange("p d e -> p e d"), in1=bc(q4[:, hd:hd + 1, dsl], [RS, 128, 64]), op=ALU.mult), reads=[Ct, sall], writes=[T])
                fw.op(DVE, lambda h: h.tensor_reduce(out=qcs[:, hf, hd, :], in_=Te, op=ALU.add, axis=AX.X), reads=[T], writes=[qcs])
                fw.op(POOL, lambda h: h.tensor_tensor(out=T3, in0=bc(kws[:, hd, dsl].unsqueeze(2), [RS, 64, 128]), in1=bc(v4[:, hd:hd + 1, :], [RS, 64, 128]), op=ALU.mult), reads=[kws, sall], writes=[T])
                fw.op(DVE, lambda h: h.scalar_tensor_tensor(out=Ct[:, :], in0=Ct[:, :], scalar=G_[:, hd:hd + 1], in1=T[:, :], op0=ALU.mult, op1=ALU.add), reads=[Ct, ms, T], writes=[Ct])
                fw.dma(SP, s_C[l, :, hd, dsl, :], Ct3, None, reads=[Ct], is_out=True)
        fw.op(DVE, lambda h: h.tensor_tensor(out=qcs[:, 0, :, :], in0=qcs[:, 0, :, :], in1=qcs[:, 1, :, :], op=ALU.add), reads=[qcs], writes=[qcs])
        fw.op(DVE, lambda h: h.tensor_tensor(out=mtmp[:], in0=v4, in1=bc(SW.unsqueeze(2), [RS, 4, 128]), op=ALU.mult), reads=[sall, ms], writes=[mtmp])
        fw.op(DVE, lambda h: h.tensor_tensor(out=qcs[:, 0, :, :], in0=qcs[:, 0, :, :], in1=bc(G_.unsqueeze(2), [RS, 4, 128]), op=ALU.mult), reads=[qcs, ms], writes=[qcs])
        fw.op(DVE, lambda h: h.tensor_tensor(out=mtmp[:], in0=mtmp[:], in1=qcs[:, 0, :, :], op=ALU.add), reads=[mtmp, qcs], writes=[mtmp])
        fw.op(DVE, lambda h: h.tensor_tensor(out=mtmp[:], in0=mtmp[:], in1=bc(RD.unsqueeze(2), [RS, 4, 128]), op=ALU.mult), reads=[mtmp, ms], writes=[mtmp])
        fw.op(DVE, lambda h: h.tensor_tensor(out=kws[:], in0=mtmp[:], in1=mtmp[:], op=ALU.mult), reads=[mtmp], writes=[kws])
        fw.op(DVE, lambda h: h.tensor_reduce(out=QK, in_=kws[:], op=ALU.add, axis=AX.X), reads=[kws], writes=[ms])
        fw.op(DVE, lambda h: h.tensor_scalar(out=QK, in0=QK, scalar1=1.0 / 128, scalar2=EPS, op0=ALU.mult, op1=ALU.add), reads=[ms], writes=[ms])
        fw.op(ACT, lambda h: h.activation(out=QK, in_=QK, func=AF.Sqrt), reads=[ms], writes=[ms])
        fw.op(DVE, lambda h: h.reciprocal(out=QK, in_=QK), reads=[ms], writes=[ms])
        fw.op(DVE, lambda h: h.tensor_tensor(out=mtmp[:], in0=mtmp[:], in1=bc(QK.unsqueeze(2), [RS, 4, 128]), op=ALU.mult), reads=[mtmp, ms], writes=[mtmp])
        mtf = mtmp[:].rearrange("p a b -> p (a b)")
        fw.op(DVE, lambda h: h.tensor_tensor(out=mtf, in0=mtf, in1=gbc[0:RS, 1024:1536], op=ALU.mult), reads=[mtmp, gbc], writes=[mtmp])
        fw.op(ACT, lambda h: h.activation(out=kws[:].rearrange("p a b -> p (a b)"), in_=sall[:, O_MO:O_MO + 512], func=AF.Sigmoid), reads=[sall], writes=[kws])
        fw.op(DVE, lambda h: h.tensor_tensor(out=mixs[:, 1536:2048], in0=mtf, in1=kws[:].rearrange("p a b -> p (a b)"), op=ALU.mult), reads=[mtmp, kws], writes=[mixs])

        transpose_to(mixs[:, :], mixs, RS, D, lambda j: actS[:, j, :], actS)
        dense_tok(actS, ttS, w_out[l], D, c_res_s)
        norm_to_actS(w_norm_mlp[l, :])
        for g in range(4):
            def c_up_s(ti, c0, nb, p):
                fw.op(ACT, lambda h: h.activation(out=sqS[:, c0:c0 + nb], in_=p[0:RS, 0:nb], func=AF.Relu), reads=[p], writes=[sqS])
                fw.op(DVE, lambda h: h.tensor_tensor(out=utS[:, c0:c0 + nb], in0=sqS[:, c0:c0 + nb], in1=sqS[:, c0:c0 + nb], op=ALU.mult), reads=[sqS], writes=[utS])
            dense_tok(actS, ttS, w_up[l][:, g * 2048:(g + 1) * 2048], 2048, c_up_s)
            transpose_to(utS[:, :], utS, RS, D, lambda j: hsT[:, j, :], hsT)
            dense_tok(hsT, ttS, w_down[l][g * 2048:(g + 1) * 2048, :], D, c_res_s)

    load_gain(w_norm_final, 0, D)
    rmsnorm_to(xrs[:, :], xrs, RS, gbc[0:RS, :], sqS[:, :], sqS, sall, D, col=0)
    fw.dma(SP, y_s[:, :], sqS[:, :], sem_o, reads=[sqS], is_out=True)
    ses.close()


_NC = [None]


def _consts():
    i = np.arange(128)
    c = {}
    c["c_ident"] = np.eye(128, dtype=np.float32)
    c["c_tri"] = (i[:, None] <= i[None, :]).astype(np.float32)
    c["c_triT"] = (i[:, None] >= i[None, :]).astype(np.float32)
    c["c_U"] = (i[:, None] > i[None, :]).astype(np.float32)
    c["c_negqk"] = np.where(i[None, :] > i[:, None], -1e30, 0.0).astype(np.float32)
    c["c_negkq"] = np.where(i[:, None] > i[None, :], -30000.0, 0.0).astype(np.float32)
    e = np.zeros((128, 128), np.float32); e[127, :] = 1.0
    c["c_e127"] = e
    half = 8
    inv = np.power(np.float32(500000.0), -np.arange(half, dtype=np.float32) / half).astype(np.float32)
    pos = np.zeros((128, 17), np.float32)
    for t in range(16):
        pos[:, t] = t * 128 + i
    pos[:, 16] = PAST
    ang = pos[:, :, None].astype(np.float32) * inv[None, None, :]
    c["c_cos"] = np.cos(ang).astype(np.float32)
    c["c_sin"] = np.sin(ang).astype(np.float32)
    oh = np.zeros((RS, RS, 128), np.float32)
    for r in range(RS):
        oh[r, r, 0] = 1.0
    c["c_oh"] = oh
    pad = np.zeros((128, 8), np.float32)
    pad[1:, 0:4] = -1.0e4
    pad[1:, 4:8] = 1.0e4
    c["c_pad"] = pad
    return c


def kernel(x_prompt, x_sample, cache_swa_k, cache_swa_v, state_conv, state_ssm, state_mlstm_C,
           state_mlstm_n, state_mlstm_m, w_norm_mix, w_in, attn_sinks, conv_w, conv_b, dt_bias, a_log,
           d_skip, w_norm_ssm, igate_b, fgate_b, w_norm_mlstm, w_out, w_norm_mlp, w_up, w_down,
           w_norm_final):
    f = lambda a: np.ascontiguousarray(np.asarray(a, dtype=np.float32))
    if _NC[0] is None:
        _NC[0] = build()
    nc = _NC[0]
    cst = _consts()
    shared = {
        "w_norm_mix": f(w_norm_mix), "w_in": f(w_in), "sinks": f(attn_sinks).reshape(L, 8), "conv_w": f(conv_w),
        "conv_b": f(conv_b), "dt_bias": f(dt_bias), "a_log": f(a_log), "d_skip": f(d_skip), "w_norm_ssm": f(w_norm_ssm),
        "igb": f(igate_b), "fgb": f(fgate_b), "w_norm_ml": f(w_norm_mlstm), "w_out": f(w_out), "w_norm_mlp": f(w_norm_mlp),
        "w_up": f(w_up), "w_down": f(w_down), "w_norm_final": f(w_norm_final),
    }
    shared.update(cst)
    in_maps = []
    for c in range(NCORES):
        rs = slice(c * RS, (c + 1) * RS)
        m = dict(shared)
        m["xp"] = f(x_prompt[c])
        m["xsm"] = f(x_sample[rs, 0, :])
        m["cache_k"] = f(np.asarray(cache_swa_k)[:, rs].reshape(L, RS, 128, 128))
        m["cache_v"] = f(np.asarray(cache_swa_v)[:, rs].reshape(L, RS, 128, 128))
        m["st_conv"] = f(np.asarray(state_conv)[:, rs])
        m["st_ssm"] = f(np.asarray(state_ssm)[:, rs].reshape(L, RS, 1024, 128))
        m["st_C"] = f(np.asarray(state_mlstm_C)[:, rs])
        m["st_n"] = f(np.asarray(state_mlstm_n)[:, rs])
        m["st_m"] = f(np.asarray(state_mlstm_m)[:, rs])
        in_maps.append(m)
    res = run_bass_kernel_spmd(nc, in_maps, core_ids=list(range(NCORES)))
    R = res.results
    cat = lambda k, ax: np.concatenate([np.asarray(R[c][k]) for c in range(NCORES)], axis=ax)
    stk = lambda k: np.stack([np.asarray(R[c][k]) for c in range(NCORES)], axis=1)
    y_prompt = np.stack([np.asarray(R[c]["y_p"]) for c in range(NCORES)], axis=0)
    y_sample = cat("y_s", 0).reshape(NCORES * RS, 1, D)
    p_k = stk("p_k").reshape(L, NCORES, 128, 2, 64)
    p_v = stk("p_v").reshape(L, NCORES, 128, 2, 64)
    p_conv = stk("p_conv")
    p_ssm = stk("p_ssm").reshape(L, NCORES, 16, 64, 128)
    p_C = stk("p_C"); p_n = stk("p_n"); p_m = stk("p_m")
    s_k = cat("s_k", 1).reshape(L, NCORES * RS, 128, 2, 64)
    s_v = cat("s_v", 1).reshape(L, NCORES * RS, 128, 2, 64)
    s_conv = cat("s_conv", 1)
    s_ssm = cat("s_ssm", 1).reshape(L, NCORES * RS, 16, 64, 128)
    s_C = cat("s_C", 1); s_n = cat("s_n", 1); s_m = cat("s_m", 1)
    outs = (y_prompt, y_sample, p_k, p_v, p_conv, p_ssm, p_C, p_n, p_m, s_k, s_v, s_conv, s_ssm, s_C, s_n, s_m)
    return tuple(np.ascontiguousarray(o, dtype=np.float32) for o in outs)
```

```python
import numpy as np
import concourse.bass as bass
import concourse.mybir as mybir
from concourse.bass_utils import run_bass_kernel_spmd
from contextlib import ExitStack

F32 = mybir.dt.float32
BF16 = mybir.dt.bfloat16
AF = mybir.ActivationFunctionType
ALU = mybir.AluOpType
AX = mybir.AxisListType

NCORES = 4
D = 2048
SEQ = 2048
ST = 256
NT = ST // 128
NST = SEQ // ST
RS = 32
L = 2
INW = 5400
O_Q, O_K, O_V, O_Z, O_XBC, O_DT, O_MQ, O_MK, O_MV, O_MO, O_MI, O_MF = (
    0, 512, 640, 768, 1792, 3328, 3344, 3856, 4368, 4880, 5392, 5396)
EPS = 1e-6
PAST = 8192
WB = 256
SEM_LIMIT = 8000


class Buf:
    def __init__(self, name, t=None, parent=None):
        self.name = name
        self.t = t
        self.parent = parent
        self.w = None
        self.r = {}

    def root(self):
        return self.parent.root() if self.parent is not None else self

    def __getitem__(self, idx):
        return self.t[idx]


class Eng:
    def __init__(self, fw, name, h):
        self.fw = fw
        self.name = name
        self.h = h
        self.sem = fw.new_sem(name)
        self.own = {id(self.sem)}
        self.cnt = 0
        self.seen = {}

    def _wait(self, ev):
        if ev is None:
            return
        sem, val = ev
        key = id(sem)
        if self.name == "pe" and key in self.own:
            return
        if key in self.fw.dma_sems:
            val = self.fw.dma_sems[key]
        if self.seen.get(key, 0) >= val:
            return
        self.h.wait_ge(sem, val)
        self.seen[key] = val


def _rnd(n):
    return 32 if n <= 32 else (64 if n <= 64 else 128)


class PEProxy:
    def __init__(self, fw):
        self.fw = fw
        self.last = None

    def _mode(self, st_ap, kind):
        shp = tuple(st_ap.shape)
        m = 1
        for v in shp[1:]:
            m *= v
        mode = (_rnd(shp[0]), _rnd(m), str(st_ap.dtype), kind)
        pe = self.fw.pe
        tiled = mode[0] < 128 or mode[1] < 128
        if self.last is not None and (mode != self.last or tiled) and pe.cnt > 0:
            pe.h.wait_ge(pe.sem, pe.cnt)
        self.last = mode

    def matmul(self, out, lhsT=None, rhs=None, **kw):
        self._mode(lhsT, "m")
        return self.fw.pe.h.matmul(out, lhsT=lhsT, rhs=rhs, **kw)

    def transpose(self, out=None, in_=None, identity=None):
        self._mode(in_, "t")
        return self.fw.pe.h.transpose(out=out, in_=in_, identity=identity)


class FW:
    def __init__(self, nc):
        self.nc = nc
        self.es = ExitStack()
        self.nsem = 0
        self.dma_sems = {}
        self.dma_sem_objs = {}
        self.qpool = {}
        self.pe = Eng(self, "pe", nc.tensor)
        self.act = Eng(self, "act", nc.scalar)
        self.dve = Eng(self, "dve", nc.vector)
        self.pool = Eng(self, "pool", nc.gpsimd)
        self.sp = Eng(self, "sp", nc.sync)
        self.engs = [self.pe, self.act, self.dve, self.pool, self.sp]
        self.pe_proxy = PEProxy(self)
        self.out_events = []
        self.all_sems_used = []

    def new_sem(self, name):
        self.nsem += 1
        return self.es.enter_context(self.nc.semaphore(f"s_{name}_{self.nsem}"))

    def sb(self, name, shape, dt, es=None):
        t = (es or getattr(self, "es_alloc", None) or self.es).enter_context(self.nc.sbuf_tensor(name, list(shape), dt))
        return Buf(name, t)

    def ps(self, name, shape, dt=F32):
        t = self.es.enter_context(self.nc.psum_tensor(name, list(shape), dt))
        return Buf(name, t)

    def _deps(self, eng, reads, writes):
        reads = [b.root() for b in reads]
        writes = [b.root() for b in writes]
        for b in reads:
            eng._wait(b.w)
        for b in writes:
            eng._wait(b.w)
            for ev in list(b.r.values()):
                eng._wait(ev)

    def op(self, eng, fn, reads=(), writes=()):
        reads = [b.root() for b in reads]
        writes = [b.root() for b in writes]
        self._deps(eng, reads, writes)
        inst = fn(self.pe_proxy if eng is self.pe else eng.h)
        if eng.cnt >= SEM_LIMIT:
            eng.sem = self.new_sem(eng.name)
            eng.own.add(id(eng.sem))
            eng.cnt = 0
        eng.cnt += 1
        inst.then_inc(eng.sem, 1)
        ev = (eng.sem, eng.cnt)
        for b in writes:
            b.w = ev
            b.r = {}
        for b in reads:
            if b not in writes:
                b.r[id(ev[0])] = ev
        return ev

    def dsem(self, name):
        return [None, name]

    def dma(self, q, out, in_, sem=None, reads=(), writes=(), is_out=False):
        pool = self.qpool.setdefault(q.name, {"sems": [None] * (12 if q.name == "sp" else 4), "i": 0})
        slot = pool["i"] % len(pool["sems"])
        pool["i"] += 1
        sem = pool["sems"][slot]
        if sem is not None:
            prev = self.dma_sems.get(id(sem), 0)
            q._wait((sem, prev))
            if prev >= SEM_LIMIT:
                sem = None
        if sem is None:
            sem = self.new_sem("d" + q.name)
            pool["sems"][slot] = sem
        reads = [b.root() for b in reads]
        writes = [b.root() for b in writes]
        self._deps(q, reads, writes)
        inst = q.h.dma_start(out=out, in_=in_)
        k = id(sem)
        self.dma_sems[k] = self.dma_sems.get(k, 0) + 16
        self.dma_sem_objs[k] = sem
        inst.then_inc(sem, 16)
        ev = (sem, self.dma_sems[k])
        for b in writes:
            b.w = ev
            b.r = {}
        for b in reads:
            b.r[id(ev[0])] = ev
        if is_out:
            self.out_events.append(ev)
        return ev

    def barrier(self):
        evs = [(e.sem, e.cnt) for e in self.engs if e.cnt > 0]
        evs += [(self.dma_sem_objs[k], v) for k, v in self.dma_sems.items()]
        for e in [self.pe, self.act, self.dve, self.pool, self.sp]:
            for ev in evs:
                e._wait(ev)

    def finish(self, close=True):
        for k, v in self.dma_sems.items():
            self.sp._wait((self.dma_sem_objs[k], v))
        if close:
            self.es.close()


def bc(ap, shape):
    return ap.broadcast_to(list(shape))


class _Stop(Exception):
    pass


DEBUG_STOP = [None]


def build():
    nc = bass.Bass("TRN2", target_bir_lowering=False)
    fw = FW(nc)
    stopped = False
    try:
        _build(nc, fw)
    except _Stop:
        stopped = True
    fw.finish(close=not stopped)
    return nc


def _build(nc, fw):
    def ck(name):
        if DEBUG_STOP[0] == name:
            raise _Stop()
    PE, ACT, DVE, POOL, SP = fw.pe, fw.act, fw.dve, fw.pool, fw.sp
    dbg_sem = [None]

    def dump(name, buf, ap, shape):
        if DEBUG_STOP[0] is None:
            return
        if dbg_sem[0] is None:
            dbg_sem[0] = fw.dsem("dbg")
        o = nc.dram_tensor("dbg_" + name, list(shape), F32, kind="ExternalOutput").ap()
        fw.dma(POOL, o, ap, dbg_sem[0], reads=[buf], is_out=True)

    def din(name, shape):
        return nc.dram_tensor(name, list(shape), F32, kind="ExternalInput").ap()

    def dout(name, shape):
        return nc.dram_tensor(name, list(shape), F32, kind="ExternalOutput").ap()

    xp = din("xp", [SEQ, D]); xsm = din("xsm", [RS, D])
    cache_k = din("cache_k", [L, RS, 128, 128]); cache_v = din("cache_v", [L, RS, 128, 128])
    st_conv = din("st_conv", [L, RS, 3, 1536]); st_ssm = din("st_ssm", [L, RS, 1024, 128])
    st_C = din("st_C", [L, RS, 4, 128, 128]); st_n = din("st_n", [L, RS, 4, 128]); st_m = din("st_m", [L, RS, 4])
    w_norm_mix = din("w_norm_mix", [L, D]); w_in = din("w_in", [L, D, INW]); sinks = din("sinks", [L, 8])
    conv_w = din("conv_w", [L, 4, 1536]); conv_b = din("conv_b", [L, 1536]); dt_bias = din("dt_bias", [L, 16])
    a_log = din("a_log", [L, 16]); d_skip = din("d_skip", [L, 16]); w_norm_ssm = din("w_norm_ssm", [L, 1024])
    igb = din("igb", [L, 4]); fgb = din("fgb", [L, 4]); w_norm_ml = din("w_norm_ml", [L, 512])
    w_out = din("w_out", [L, D, D]); w_norm_mlp = din("w_norm_mlp", [L, D]); w_up = din("w_up", [L, D, 4 * D])
    w_down = din("w_down", [L, 4 * D, D]); w_norm_final = din("w_norm_final", [D])
    c_ident = din("c_ident", [128, 128]); c_tri = din("c_tri", [128, 128]); c_triT = din("c_triT", [128, 128])
    c_U = din("c_U", [128, 128]); c_negqk = din("c_negqk", [128, 128]); c_negkq = din("c_negkq", [128, 128])
    c_e127 = din("c_e127", [128, 128]); c_cos = din("c_cos", [128, 17, 8]); c_sin = din("c_sin", [128, 17, 8])
    c_oh = din("c_oh", [RS, RS, 128]); c_pad = din("c_pad", [128, 8])

    y_p = dout("y_p", [SEQ, D]); y_s = dout("y_s", [RS, D])
    p_k = dout("p_k", [L, 128, 128]); p_v = dout("p_v", [L, 128, 128]); p_conv = dout("p_conv", [L, 3, 1536])
    p_ssm = dout("p_ssm", [L, 1024, 128]); p_C = dout("p_C", [L, 4, 128, 128]); p_n = dout("p_n", [L, 4, 128])
    p_m = dout("p_m", [L, 4])
    s_k = dout("s_k", [L, RS, 128, 128]); s_v = dout("s_v", [L, RS, 128, 128]); s_conv = dout("s_conv", [L, RS, 3, 1536])
    s_ssm = dout("s_ssm", [L, RS, 1024, 128]); s_C = dout("s_C", [L, RS, 4, 128, 128]); s_n = dout("s_n", [L, RS, 4, 128])
    s_m = dout("s_m", [L, RS, 4])

    sem_c = fw.dsem("dc")
    sem_o = fw.dsem("do")
    sem_x = fw.dsem("dx")
    sem_g = fw.dsem("dg")
    sem_st = fw.dsem("dst")

    pbig = fw.ps("pbig", [128, 2048], F32)
    pd = [fw.ps(f"pd{i}", [128, 512], F32) for i in range(2)]
    ptb = fw.ps("ptb", [128, 1024], BF16)
    pm = fw.ps("pm", [128, 512], F32)
    pq = [pbig]

    def cload(name, src, shape, dt=F32, cast=None):
        b = fw.sb(name, shape, F32)
        fw.dma(SP, b[:], src, sem_c, writes=[b])
        if cast is not None:
            b2 = fw.sb(name + "_b", shape, cast)
            fw.op(DVE, lambda h: h.tensor_copy(out=b2[:], in_=b[:]), reads=[b], writes=[b2])
            return b, b2
        return b

    ident_f, ident_b = cload("ident", c_ident, [128, 128], cast=BF16)
    tri_f, tri_b = cload("tri", c_tri, [128, 128], cast=BF16)
    triT_f, triT_b = cload("triT", c_triT, [128, 128], cast=BF16)
    U_f = cload("U", c_U, [128, 128])
    negqk = cload("negqk", c_negqk, [128, 128])
    negkq = cload("negkq", c_negkq, [128, 128])
    e127 = cload("e127", c_e127, [128, 128])
    ones_f = fw.sb("ones_f", [128, 128], F32)
    fw.op(DVE, lambda h: h.memset(ones_f[:], 1.0), writes=[ones_f])
    cosT = cload("cosT", c_cos, [128, 17, 8]); sinT = cload("sinT", c_sin, [128, 17, 8])
    padt = cload("padt", c_pad, [128, 8])

    gbc = fw.sb("gbc", [128, 2048], F32)
    lp = {}

    def bload(name, src_row, n):
        b = fw.sb(name, [128, n], F32)
        fw.dma(SP, b[:], src_row.partition_broadcast(128), sem_c, writes=[b])
        return b

    for l in range(L):
        d = {}
        d["dtb"] = bload(f"dtb{l}", dt_bias[l, :], 16)
        al = bload(f"al{l}", a_log[l, :], 16)
        A = fw.sb(f"A{l}", [128, 16], F32)
        fw.op(ACT, lambda h: h.activation(out=A[:], in_=al[:], func=AF.Exp), reads=[al], writes=[A])
        fw.op(DVE, lambda h: h.tensor_scalar(out=A[:], in0=A[:], scalar1=-1.0, scalar2=None, op0=ALU.mult), reads=[A], writes=[A])
        d["A"] = A
        dsk = bload(f"dsk{l}", d_skip[l, :], 16)
        d["dsk"] = dsk
        sk = bload(f"sk{l}", sinks[l, :], 8)
        esk = fw.sb(f"esk{l}", [128, 8], F32)
        fw.op(ACT, lambda h: h.activation(out=esk[:], in_=sk[:], func=AF.Exp), reads=[sk], writes=[esk])
        d["esk"] = esk
        gb = fw.sb(f"gb{l}", [128, 8], F32)
        fw.dma(SP, gb[:, 0:4], igb[l, :].partition_broadcast(128), sem_c, writes=[gb])
        fw.dma(SP, gb[:, 4:8], fgb[l, :].partition_broadcast(128), sem_c, writes=[gb])
        d["gb"] = gb
        cw = fw.sb(f"cw{l}", [128, 12, 4], F32)
        cb = fw.sb(f"cb{l}", [128, 12], F32)
        with nc.allow_non_contiguous_dma(reason="small conv params"):
            for j in range(4):
                fw.dma(SP, cw[:, :, j], conv_w[l, j, :].rearrange("(b p) -> p b", p=128), sem_c, writes=[cw])
            fw.dma(SP, cb[:], conv_b[l, :].rearrange("(b p) -> p b", p=128), sem_c, writes=[cb])
        d["cw"] = cw; d["cb"] = cb
        lp[l] = d

    class S:
        pass
    small = fw.sb("small", [128, 64], F32)
    rtmp = fw.sb("rtmp", [128, 10, 16], F32)
    NWB_ = 4
    wbuf = [fw.sb(f"wb{i}", [128, 16, WB], BF16) for i in range(NWB_)]
    pes = ExitStack()
    _es_orig = fw.es
    carry = []
    fw.es_alloc = pes
    for l in range(L):
        dD = fw.sb(f"dD{l}", [128, 16, 128], BF16)
        dsk = lp[l]["dsk"]
        fw.op(DVE, lambda h: h.tensor_tensor(out=dD[:], in0=bc(ident_f[:].unsqueeze(1), [128, 16, 128]),
                                             in1=bc(dsk[:].unsqueeze(2), [128, 16, 128]), op=ALU.mult),
              reads=[ident_f, dsk], writes=[dD])
        lp[l]["dD"] = dD
    for l in range(L):
        s = S()
        s.ST = fw.sb(f"ST{l}", [128, 1024], F32)
        s.STb = fw.sb(f"STb{l}", [128, 1024], BF16)
        s.Cn = fw.sb(f"Cn{l}", [128, 4, 129], F32)
        s.Cnb = fw.sb(f"Cnb{l}", [128, 4, 129], BF16)
        s.mrow = fw.sb(f"mrow{l}", [128, 4], F32)
        s.kprev = fw.sb(f"kprev{l}", [128, 2, 128], BF16)
        s.vprev = fw.sb(f"vprev{l}", [128, 2, 65], BF16)
        s.xcarry = fw.sb(f"xcar{l}", [128, 12, 3], F32)
        carry.append(s)

    NWB = 4
    wsem = [fw.dsem(f"w{i}") for i in range(NWB)]
    wctr = [0]

    NBLK = 2 * (3 + 4 + 1 + 2 + 2 + 2 + 1 + 2 + 6 + 8 + 4 * 16)
    wcache_t = nc.dram_tensor("wcache", [NBLK, 128, 16, WB], BF16, kind="Internal").ap()
    wcache = Buf("wcache")
    wpass = [0, 0]

    def new_pass():
        assert wpass[0] == 0 or wpass[1] == NBLK, wpass
        wpass[0] += 1
        wpass[1] = 0

    def wload(src2d, ncols):
        i = wctr[0] % NWB
        wctr[0] += 1
        b = wbuf[i]
        blk = wpass[1]
        wpass[1] += 1
        if wpass[0] == 1:
            fw.dma(POOL, b[:, :, 0:ncols], src2d.rearrange("(k p) n -> p k n", p=128), wsem[i], writes=[b])
            fw.dma(SP, wcache_t[blk, :, :, 0:ncols], b[:, :, 0:ncols], None, reads=[b], writes=[wcache])
        else:
            fw.dma(POOL, b[:, :, 0:ncols], wcache_t[blk, :, :, 0:ncols], wsem[i], reads=[wcache], writes=[b])
        return b

    pdc = [0]

    def next_pd():
        p = pd[pdc[0] % 2]
        pdc[0] += 1
        return p

    def dense_tok(actT, tts, W2d, ncols, consume):
        for c0 in range(0, ncols, WB):
            nb = min(WB, ncols - c0)
            wb = wload(W2d[:, c0:c0 + nb], nb)
            for ti, (t0, tsz) in enumerate(tts):
                p = next_pd()
                for k in range(16):
                    fw.op(PE, lambda h: h.matmul(p[0:tsz, 0:nb], lhsT=actT[:, k, t0:t0 + tsz], rhs=wb[:, k, 0:nb],
                                                 start=(k == 0), stop=(k == 15)), reads=[actT, wb], writes=[p])
                consume(ti, c0, nb, p)

    def dense_feat(actT, ntok, W2d, ncols, consume):
        for c0 in range(0, ncols, WB):
            nb = min(WB, ncols - c0)
            wb = wload(W2d[:, c0:c0 + nb], nb)
            for s0 in range(0, nb, 128):
                p = next_pd()
                for k in range(16):
                    fw.op(PE, lambda h: h.matmul(p[:, 0:ntok], lhsT=wb[:, k, s0:s0 + 128], rhs=actT[:, k, 0:ntok],
                                                 start=(k == 0), stop=(k == 15)), reads=[actT, wb], writes=[p])
                consume((c0 + s0) // 128, p)


    def rmsnorm_to(xt_ap, xbuf, np_, gain_ap, out_ap, outbuf, tmpbuf, nfeat, col=0):
        ss = small[0:np_, col:col + 1]
        fw.op(ACT, lambda h: h.activation(out=tmpbuf[0:np_, 0:nfeat], in_=xt_ap, func=AF.Square, accum_out=ss),
              reads=[xbuf], writes=[tmpbuf, small])
        fw.op(DVE, lambda h: h.tensor_scalar(out=ss, in0=ss, scalar1=1.0 / nfeat, scalar2=EPS, op0=ALU.mult, op1=ALU.add),
              reads=[small], writes=[small])
        fw.op(ACT, lambda h: h.activation(out=ss, in_=ss, func=AF.Sqrt), reads=[small], writes=[small])
        fw.op(DVE, lambda h: h.reciprocal(out=ss, in_=ss), reads=[small], writes=[small])
        fw.op(DVE, lambda h: h.scalar_tensor_tensor(out=out_ap, in0=xt_ap, scalar=ss, in1=gain_ap, op0=ALU.mult, op1=ALU.mult),
              reads=[xbuf, small, gbc], writes=[outbuf])

    def transpose_to(src_ap, srcbuf, np_, ncols, dst_fn, dstbuf):
        nblk = ncols // 128
        for g0 in range(0, nblk, 8):
            g1 = min(nblk, g0 + 8)
            for j in range(g0, g1):
                fw.op(PE, lambda h: h.transpose(out=ptb[:, (j - g0) * 128:(j - g0) * 128 + np_],
                                                in_=src_ap[:, j * 128:(j + 1) * 128], identity=ident_b[0:np_, 0:np_]),
                      reads=[srcbuf, ident_b], writes=[ptb])
            for j in range(g0, g1):
                fw.op(ACT, lambda h: h.copy(out=dst_fn(j), in_=ptb[:, (j - g0) * 128:(j - g0) * 128 + np_]),
                      reads=[ptb], writes=[dstbuf])

    cs = ExitStack()
    o_f = fw.sb("o_f", [128, 8, 65], F32)
    pT = fw.sb("pT", [128, 2, 512], BF16)
    rden = fw.sb("rden", [128, 8], F32)

    def swa_block(l, qT, qTbuf, kcur, kcurbuf, vcur, vcurbuf, kprev, kprevbuf, vprev, vprevbuf, has_prev, out_ap, outbuf):
        for kv in range(2):
            blocks = ([("p", kprev, kprevbuf, vprev, vprevbuf, triT_b)] if has_prev else []) + [("c", kcur, kcurbuf, vcur, vcurbuf, tri_b)]
            for bi, (nm, kf, kb, vf, vb, msk) in enumerate(blocks):
                for hh in range(4):
                    h_ = kv * 4 + hh
                    half = (h_ % 2) * 64
                    fw.op(PE, lambda h: h.matmul(pbig[:, (bi * 4 + hh) * 128:(bi * 4 + hh + 1) * 128],
                                                 lhsT=kf(kv)[half:half + 64, :], rhs=qT(h_ // 2)[half:half + 64, :],
                                                 start=True, stop=True), reads=[kb, qTbuf], writes=[pbig])
                fw.op(ACT, lambda h: h.activation(out=pT[:, bi, :], in_=pbig[:, bi * 512:(bi + 1) * 512], func=AF.Exp, scale=0.125),
                      reads=[pbig], writes=[pT])
                fw.op(DVE, lambda h: h.tensor_tensor(out=pT[:, bi, :].rearrange("p (a b) -> p a b", a=4),
                                                     in0=pT[:, bi, :].rearrange("p (a b) -> p a b", a=4),
                                                     in1=bc(msk[:].unsqueeze(1), [128, 4, 128]), op=ALU.mult),
                      reads=[pT, msk], writes=[pT])
            for hh in range(4):
                for bi, (nm, kf, kb, vf, vb, msk) in enumerate(blocks):
                    fw.op(PE, lambda h: h.matmul(pm[:, hh * 65:(hh + 1) * 65], lhsT=pT[:, bi, hh * 128:(hh + 1) * 128], rhs=vf(kv),
                                                 start=(bi == 0), stop=(bi == len(blocks) - 1)), reads=[pT, vb], writes=[pm])
            fw.op(ACT, lambda h: h.copy(out=o_f[:, kv * 4:(kv + 1) * 4, :], in_=pm[:, 0:260].rearrange("p (a b) -> p a b", a=4)),
                  reads=[pm], writes=[o_f])
        esk = lp[l]["esk"]
        fw.op(DVE, lambda h: h.tensor_tensor(out=rden[:], in0=o_f[:, :, 64], in1=esk[:], op=ALU.add), reads=[o_f, esk], writes=[rden])
        fw.op(DVE, lambda h: h.reciprocal(out=rden[:], in_=rden[:]), reads=[rden], writes=[rden])
        fw.op(DVE, lambda h: h.tensor_tensor(out=out_ap.rearrange("p (a b) -> p a b", a=8), in0=o_f[:, :, 0:64],
                                             in1=bc(rden[:].unsqueeze(2), [128, 8, 64]), op=ALU.mult),
              reads=[o_f, rden], writes=[outbuf])

    dtA = fw.sb("dtA", [128, 16], F32)
    a_sb = fw.sb("a_sb", [128, 16], F32)
    ea = fw.sb("ea", [128, 16], F32)
    eal = fw.sb("eal", [128, 16], F32)
    wk = fw.sb("wk", [128, 16], F32)
    rseg = fw.sb("rseg", [128, 4, 128], F32)
    LT = fw.sb("LT", [128, 16, 128], BF16)
    cbm = fw.sb("cbm", [128, 2, 128], BF16)
    x_dt = fw.sb("x_dt", [128, 1024], BF16)
    xw = fw.sb("xw", [128, 1024], BF16)
    ytmp = fw.sb("ytmp", [128, 1024], F32)
    yy = fw.sb("yy", [128, 1024], F32)
    zs = ytmp

    def ssd_chunk(l, st, x_tok, x_tokbuf, B_tok, B_tokbuf, BT, CT, BCbuf, dt, dtbuf, z_tok, zbuf, out_ap, outbuf):
        A = lp[l]["A"]; dD = lp[l]["dD"]
        fw.op(DVE, lambda h: h.tensor_tensor(out=dtA[:], in0=dt, in1=A[:], op=ALU.mult), reads=[dtbuf, A], writes=[dtA])
        fw.op(PE, lambda h: h.matmul(pm[:, 0:16], lhsT=tri_f[:], rhs=dtA[:], start=True, stop=True), reads=[tri_f, dtA], writes=[pm])
        fw.op(PE, lambda h: h.matmul(pm[:, 16:32], lhsT=ones_f[:], rhs=dtA[:], start=True, stop=True), reads=[ones_f, dtA], writes=[pm])
        fw.op(ACT, lambda h: h.copy(out=a_sb[:], in_=pm[:, 0:16]), reads=[pm], writes=[a_sb])
        fw.op(ACT, lambda h: h.activation(out=ea[:], in_=pm[:, 0:16], func=AF.Exp), reads=[pm], writes=[ea])
        fw.op(ACT, lambda h: h.activation(out=eal[:], in_=pm[:, 16:32], func=AF.Exp), reads=[pm], writes=[eal])
        fw.op(DVE, lambda h: h.tensor_tensor(out=wk[:], in0=pm[:, 16:32], in1=a_sb[:], op=ALU.subtract), reads=[pm, a_sb], writes=[wk])
        fw.op(ACT, lambda h: h.activation(out=wk[:], in_=wk[:], func=AF.Exp), reads=[wk], writes=[wk])
        fw.op(DVE, lambda h: h.tensor_tensor(out=wk[:], in0=wk[:], in1=dt, op=ALU.mult), reads=[wk, dtbuf], writes=[wk])
        for i in range(4):
            fw.op(DVE, lambda h: h.tensor_tensor(out=rseg[:], in0=bc(tri_f[:].unsqueeze(1), [128, 4, 128]),
                                                 in1=bc(dtA[:, i * 4:(i + 1) * 4].unsqueeze(2), [128, 4, 128]), op=ALU.mult),
                  reads=[tri_f, dtA], writes=[rseg])
            fw.op(PE, lambda h: h.matmul(pbig[:, i * 512:(i + 1) * 512], lhsT=U_f[:],
                                         rhs=rseg[:].rearrange("p a b -> p (a b)"), start=True, stop=True),
                  reads=[U_f, rseg], writes=[pbig])
        fw.op(ACT, lambda h: h.activation(out=LT[:].rearrange("p a b -> p (a b)"), in_=pbig[:], func=AF.Exp), reads=[pbig], writes=[LT])
        for g in range(2):
            fw.op(PE, lambda h: h.matmul(pm[:, 64 + g * 128:64 + (g + 1) * 128], lhsT=BT(g), rhs=CT(g), start=True, stop=True),
                  reads=[BCbuf], writes=[pm])
        fw.op(DVE, lambda h: h.tensor_tensor(out=cbm[:], in0=pm[:, 64:320].rearrange("p (a b) -> p a b", a=2),
                                             in1=bc(tri_f[:].unsqueeze(1), [128, 2, 128]), op=ALU.mult),
              reads=[pm, tri_f], writes=[cbm])
        for g in range(2):
            fw.op(DVE, lambda h: h.tensor_tensor(out=LT[:, g * 8:(g + 1) * 8, :], in0=LT[:, g * 8:(g + 1) * 8, :],
                                                 in1=bc(cbm[:, g:g + 1, :], [128, 8, 128]), op=ALU.mult),
                  reads=[LT, cbm], writes=[LT])
        fw.op(DVE, lambda h: h.tensor_tensor(out=x_dt[:].rearrange("p (a b) -> p a b", a=16), in0=x_tok.rearrange("p (a b) -> p a b", a=16),
                                             in1=bc(dt.unsqueeze(2), [128, 16, 64]), op=ALU.mult),
              reads=[x_tokbuf, dtbuf], writes=[x_dt])
        fw.op(DVE, lambda h: h.tensor_tensor(out=xw[:].rearrange("p (a b) -> p a b", a=16), in0=x_tok.rearrange("p (a b) -> p a b", a=16),
                                             in1=bc(wk[:].unsqueeze(2), [128, 16, 64]), op=ALU.mult),
              reads=[x_tokbuf, wk], writes=[xw])
        for g in range(2):
            fw.op(PE, lambda h: h.matmul(pbig[:, g * 512:(g + 1) * 512], lhsT=CT(g), rhs=st.STb[:, g * 512:(g + 1) * 512], start=True, stop=True),
                  reads=[BCbuf, st.STb], writes=[pbig])
        for hh in range(16):
            fw.op(PE, lambda h: h.matmul(pbig[:, 1024 + hh * 64:1024 + (hh + 1) * 64], lhsT=LT[:, hh, :], rhs=x_dt[:, hh * 64:(hh + 1) * 64],
                                         start=True, stop=False), reads=[LT, x_dt], writes=[pbig])
            fw.op(PE, lambda h: h.matmul(pbig[:, 1024 + hh * 64:1024 + (hh + 1) * 64], lhsT=dD[:, hh, :], rhs=x_tok[:, hh * 64:(hh + 1) * 64],
                                         start=False, stop=True), reads=[dD, x_tokbuf], writes=[pbig])
        fw.op(DVE, lambda h: h.tensor_tensor(out=ytmp[:].rearrange("p (a b) -> p a b", a=16), in0=pbig[:, 0:1024].rearrange("p (a b) -> p a b", a=16),
                                             in1=bc(ea[:].unsqueeze(2), [128, 16, 64]), op=ALU.mult),
              reads=[pbig, ea], writes=[ytmp])
        fw.op(DVE, lambda h: h.tensor_tensor(out=yy[:], in0=pbig[:, 1024:2048], in1=ytmp[:], op=ALU.add), reads=[pbig, ytmp], writes=[yy])
        for g in range(2):
            fw.op(PE, lambda h: h.matmul(pd[g][:, :], lhsT=B_tok[:, g * 128:(g + 1) * 128], rhs=xw[:, g * 512:(g + 1) * 512], start=True, stop=True),
                  reads=[B_tokbuf, xw], writes=[pd[g]])
        fw.op(DVE, lambda h: h.tensor_tensor(out=st.ST[:].rearrange("p (a b) -> p a b", a=16), in0=st.ST[:].rearrange("p (a b) -> p a b", a=16),
                                             in1=bc(eal[:].unsqueeze(2), [128, 16, 64]), op=ALU.mult), reads=[st.ST, eal], writes=[st.ST])
        for g in range(2):
            fw.op(DVE, lambda h: h.tensor_tensor(out=st.ST[:, g * 512:(g + 1) * 512], in0=pd[g][:, :], in1=st.ST[:, g * 512:(g + 1) * 512], op=ALU.add),
                  reads=[pd[g], st.ST], writes=[st.ST])
        fw.op(ACT, lambda h: h.copy(out=st.STb[:], in_=st.ST[:]), reads=[st.ST], writes=[st.STb])
        fw.op(ACT, lambda h: h.activation(out=zs[:], in_=z_tok, func=AF.Silu), reads=[zbuf], writes=[zs])
        fw.op(DVE, lambda h: h.tensor_tensor(out=yy[:], in0=yy[:], in1=zs[:], op=ALU.mult), reads=[yy, zs], writes=[yy])
        for g in range(2):
            rmsnorm_to(yy[:, g * 512:(g + 1) * 512], yy, 128, gbc[:, g * 512:(g + 1) * 512], out_ap[:, g * 512:(g + 1) * 512], outbuf, ytmp, 512, col=8 + g)

    lfn = fw.sb("lfn", [128, 4], F32)
    nb_ = fw.sb("nb_", [128, 4], F32)
    nbl = fw.sb("nbl", [128, 4], F32)
    cc = fw.sb("cc", [128, 4], F32)
    Dc = fw.sb("Dc", [128, 4, 128], F32)
    cmx = fw.sb("cmx", [128, 4, 128], F32)
    cm = fw.sb("cm", [128, 4], F32)
    Mq = fw.sb("Mq", [128, 4], F32)
    negM = fw.sb("negM", [128, 4], F32)
    mt = fw.sb("mt", [128, 4], F32)
    gq = fw.sb("gq", [128, 4], F32)
    emt = fw.sb("emt", [128, 4], F32)
    mnew = fw.sb("mnew", [128, 4], F32)
    gend = fw.sb("gend", [128, 4], F32)
    wkm = fw.sb("wkm", [128, 4], F32)
    swe = fw.sb("swe", [128, 4, 128], F32)
    swT = fw.sb("swT", [128, 4, 128], BF16)
    tot = fw.sb("tot", [128, 4, 129], F32)
    ints = fw.sb("ints", [128, 4, 129], F32)
    hh_ = fw.sb("hh_", [128, 4, 128], F32)
    kwm = fw.sb("kwm", [128, 512], BF16)
    sg = fw.sb("sg", [128, 512], F32)
    negkq4 = fw.sb("negkq4", [128, 4, 128], F32)
    fw.op(DVE, lambda h: h.tensor_copy(out=negkq4[:], in_=bc(negkq[:].unsqueeze(1), [128, 4, 128])), reads=[negkq], writes=[negkq4])

    def mlstm_chunk(l, st, qT, kT, qkbuf, k_tok, v_aug, kvbuf, ig, fg, gbuf, mo, mobuf, out_ap, outbuf):
        fw.op(ACT, lambda h: h.activation(out=lfn[:], in_=fg, func=AF.Exp, scale=-1.0), reads=[gbuf], writes=[lfn])
        fw.op(ACT, lambda h: h.activation(out=lfn[:], in_=lfn[:], func=AF.Ln, bias=1.0), reads=[lfn], writes=[lfn])
        fw.op(PE, lambda h: h.matmul(pm[:, 0:4], lhsT=tri_f[:], rhs=lfn[:], start=True, stop=True), reads=[tri_f, lfn], writes=[pm])
        fw.op(PE, lambda h: h.matmul(pm[:, 4:8], lhsT=ones_f[:], rhs=lfn[:], start=True, stop=True), reads=[ones_f, lfn], writes=[pm])
        fw.op(ACT, lambda h: h.copy(out=nb_[:], in_=pm[:, 0:4]), reads=[pm], writes=[nb_])
        fw.op(ACT, lambda h: h.copy(out=nbl[:], in_=pm[:, 4:8]), reads=[pm], writes=[nbl])
        fw.op(DVE, lambda h: h.tensor_tensor(out=cc[:], in0=ig, in1=nb_[:], op=ALU.add), reads=[gbuf, nb_], writes=[cc])
        fw.op(DVE, lambda h: h.tensor_tensor(out=Dc[:], in0=bc(ident_f[:].unsqueeze(1), [128, 4, 128]), in1=bc(cc[:].unsqueeze(2), [128, 4, 128]), op=ALU.mult),
              reads=[ident_f, cc], writes=[Dc])
        fw.op(PE, lambda h: h.matmul(pd[0][:, :], lhsT=ones_f[:], rhs=Dc[:].rearrange("p a b -> p (a b)"), start=True, stop=True),
              reads=[ones_f, Dc], writes=[pd[0]])
        fw.op(DVE, lambda h: h.tensor_tensor(out=cmx[:], in0=pd[0][:, :].rearrange("p (a b) -> p a b", a=4), in1=bc(negqk[:].unsqueeze(1), [128, 4, 128]), op=ALU.add),
              reads=[pd[0], negqk], writes=[cmx])
        fw.op(DVE, lambda h: h.tensor_reduce(out=cm[:], in_=cmx[:], op=ALU.max, axis=AX.X), reads=[cmx], writes=[cm])
        fw.op(DVE, lambda h: h.tensor_tensor(out=Mq[:], in0=cm[:], in1=st.mrow[:], op=ALU.max), reads=[cm, st.mrow], writes=[Mq])
        fw.op(DVE, lambda h: h.tensor_tensor(out=mt[:], in0=Mq[:], in1=nb_[:], op=ALU.subtract), reads=[Mq, nb_], writes=[mt])
        fw.op(DVE, lambda h: h.tensor_scalar(out=negM[:], in0=Mq[:], scalar1=-1.0, scalar2=None, op0=ALU.mult), reads=[Mq], writes=[negM])
        fw.op(DVE, lambda h: h.tensor_tensor(out=Dc[:], in0=bc(ident_f[:].unsqueeze(1), [128, 4, 128]), in1=bc(negM[:].unsqueeze(2), [128, 4, 128]), op=ALU.mult),
              reads=[ident_f, negM], writes=[Dc])
        fw.op(PE, lambda h: h.matmul(pd[1][:, :], lhsT=ones_f[:], rhs=Dc[:].rearrange("p a b -> p (a b)"), start=True, stop=False),
              reads=[ones_f, Dc], writes=[pd[1]])
        fw.op(PE, lambda h: h.matmul(pd[1][:, :], lhsT=ident_f[:], rhs=negkq4[:].rearrange("p a b -> p (a b)"), start=False, stop=True),
              reads=[ident_f, negkq4], writes=[pd[1]])
        for hd in range(4):
            fw.op(ACT, lambda h: h.activation(out=swe[:, hd, :], in_=pd[1][:, hd * 128:(hd + 1) * 128], func=AF.Exp, bias=cc[:, hd:hd + 1]),
                  reads=[pd[1], cc], writes=[swe])
        for hd in range(4):
            fw.op(PE, lambda h: h.matmul(pd[0][:, hd * 128:(hd + 1) * 128], lhsT=kT(hd), rhs=qT(hd), start=True, stop=True), reads=qkbuf, writes=[pd[0]])
        fw.op(DVE, lambda h: h.tensor_tensor(out=swT[:], in0=pd[0][:, :].rearrange("p (a b) -> p a b", a=4), in1=swe[:], op=ALU.mult),
              reads=[pd[0], swe], writes=[swT])
        for hd in range(4):
            o = (hd // 2) * 512 + (hd % 2) * 129
            fw.op(PE, lambda h: h.matmul(pbig[:, o:o + 129], lhsT=swT[:, hd, :], rhs=v_aug(hd), start=True, stop=True), reads=[swT] + kvbuf, writes=[pbig])
            fw.op(PE, lambda h: h.matmul(pbig[:, 1024 + o:1024 + o + 129], lhsT=qT(hd), rhs=st.Cnb[:, hd, :], start=True, stop=True), reads=qkbuf + [st.Cnb], writes=[pbig])
        fw.op(DVE, lambda h: h.tensor_tensor(out=gq[:], in0=st.mrow[:], in1=Mq[:], op=ALU.subtract), reads=[st.mrow, Mq], writes=[gq])
        fw.op(ACT, lambda h: h.activation(out=gq[:], in_=gq[:], func=AF.Exp), reads=[gq], writes=[gq])
        fw.op(ACT, lambda h: h.activation(out=emt[:], in_=mt[:], func=AF.Exp, scale=-1.0), reads=[mt], writes=[emt])
        for hf in range(2):
            fw.op(DVE, lambda h: h.tensor_tensor(out=ints[:, hf * 2:hf * 2 + 2, :], in0=pbig[:, 1024 + hf * 512:1024 + hf * 512 + 258].rearrange("p (a b) -> p a b", a=2),
                                                 in1=bc(gq[:, hf * 2:hf * 2 + 2].unsqueeze(2), [128, 2, 129]), op=ALU.mult), reads=[pbig, gq], writes=[ints])
            fw.op(DVE, lambda h: h.tensor_tensor(out=tot[:, hf * 2:hf * 2 + 2, :], in0=pbig[:, hf * 512:hf * 512 + 258].rearrange("p (a b) -> p a b", a=2),
                                                 in1=ints[:, hf * 2:hf * 2 + 2, :], op=ALU.add), reads=[pbig, ints], writes=[tot])
        fw.op(DVE, lambda h: h.tensor_scalar(out=cm[:], in0=tot[:, :, 128], scalar1=-1.0, scalar2=None, op0=ALU.mult), reads=[tot], writes=[cm])
        fw.op(DVE, lambda h: h.tensor_tensor(out=cm[:], in0=cm[:], in1=tot[:, :, 128], op=ALU.max), reads=[tot, cm], writes=[cm])
        fw.op(DVE, lambda h: h.tensor_tensor(out=cm[:], in0=cm[:], in1=emt[:], op=ALU.max), reads=[cm, emt], writes=[cm])
        fw.op(DVE, lambda h: h.reciprocal(out=cm[:], in_=cm[:]), reads=[cm], writes=[cm])
        fw.op(DVE, lambda h: h.tensor_tensor(out=hh_[:], in0=tot[:, :, 0:128], in1=bc(cm[:].unsqueeze(2), [128, 4, 128]), op=ALU.mult), reads=[tot, cm], writes=[hh_])
        fw.op(DVE, lambda h: h.tensor_tensor(out=cmx[:], in0=hh_[:], in1=hh_[:], op=ALU.mult), reads=[hh_], writes=[cmx])
        fw.op(DVE, lambda h: h.tensor_reduce(out=cm[:], in_=cmx[:], op=ALU.add, axis=AX.X), reads=[cmx], writes=[cm])
        fw.op(DVE, lambda h: h.tensor_scalar(out=cm[:], in0=cm[:], scalar1=1.0 / 128, scalar2=EPS, op0=ALU.mult, op1=ALU.add), reads=[cm], writes=[cm])
        fw.op(ACT, lambda h: h.activation(out=cm[:], in_=cm[:], func=AF.Sqrt), reads=[cm], writes=[cm])
        fw.op(DVE, lambda h: h.reciprocal(out=cm[:], in_=cm[:]), reads=[cm], writes=[cm])
        fw.op(DVE, lambda h: h.tensor_tensor(out=hh_[:], in0=hh_[:], in1=bc(cm[:].unsqueeze(2), [128, 4, 128]), op=ALU.mult), reads=[hh_, cm], writes=[hh_])
        fw.op(DVE, lambda h: h.tensor_tensor(out=hh_[:].rearrange("p a b -> p (a b)"), in0=hh_[:].rearrange("p a b -> p (a b)"), in1=gbc[:, 1024:1536], op=ALU.mult),
              reads=[hh_, gbc], writes=[hh_])
        fw.op(ACT, lambda h: h.activation(out=sg[:], in_=mo, func=AF.Sigmoid), reads=[mobuf], writes=[sg])
        fw.op(DVE, lambda h: h.tensor_tensor(out=out_ap, in0=hh_[:].rearrange("p a b -> p (a b)"), in1=sg[:], op=ALU.mult), reads=[hh_, sg], writes=[outbuf])
        fw.op(PE, lambda h: h.matmul(pm[:, 8:12], lhsT=e127[:], rhs=mt[:], start=True, stop=True), reads=[e127, mt], writes=[pm])
        fw.op(ACT, lambda h: h.copy(out=mnew[:], in_=pm[:, 8:12]), reads=[pm], writes=[mnew])
        fw.op(DVE, lambda h: h.tensor_tensor(out=wkm[:], in0=cc[:], in1=nbl[:], op=ALU.subtract), reads=[cc, nbl], writes=[wkm])
        fw.op(DVE, lambda h: h.tensor_tensor(out=wkm[:], in0=wkm[:], in1=mnew[:], op=ALU.subtract), reads=[wkm, mnew], writes=[wkm])
        fw.op(ACT, lambda h: h.activation(out=wkm[:], in_=wkm[:], func=AF.Exp), reads=[wkm], writes=[wkm])
        fw.op(DVE, lambda h: h.tensor_tensor(out=gend[:], in0=st.mrow[:], in1=nbl[:], op=ALU.subtract), reads=[st.mrow, nbl], writes=[gend])
        fw.op(DVE, lambda h: h.tensor_tensor(out=gend[:], in0=gend[:], in1=mnew[:], op=ALU.subtract), reads=[gend, mnew], writes=[gend])
        fw.op(ACT, lambda h: h.activation(out=gend[:], in_=gend[:], func=AF.Exp), reads=[gend], writes=[gend])
        fw.op(DVE, lambda h: h.tensor_tensor(out=kwm[:].rearrange("p (a b) -> p a b", a=4), in0=k_tok.rearrange("p (a b) -> p a b", a=4),
                                             in1=bc(wkm[:].unsqueeze(2), [128, 4, 128]), op=ALU.mult), reads=kvbuf + [wkm], writes=[kwm])
        for hd in range(4):
            o = (hd // 2) * 512 + (hd % 2) * 129
            fw.op(PE, lambda h: h.matmul(pbig[:, o:o + 129], lhsT=kwm[:, hd * 128:(hd + 1) * 128], rhs=v_aug(hd), start=True, stop=True), reads=[kwm] + kvbuf, writes=[pbig])
        fw.op(DVE, lambda h: h.tensor_tensor(out=st.Cn[:], in0=st.Cn[:], in1=bc(gend[:].unsqueeze(2), [128, 4, 129]), op=ALU.mult), reads=[st.Cn, gend], writes=[st.Cn])
        for hf in range(2):
            fw.op(DVE, lambda h: h.tensor_tensor(out=st.Cn[:, hf * 2:hf * 2 + 2, :], in0=pbig[:, hf * 512:hf * 512 + 258].rearrange("p (a b) -> p a b", a=2),
                                                 in1=st.Cn[:, hf * 2:hf * 2 + 2, :], op=ALU.add), reads=[pbig, st.Cn], writes=[st.Cn])
        fw.op(ACT, lambda h: h.copy(out=st.Cnb[:], in_=st.Cn[:]), reads=[st.Cn], writes=[st.Cnb])
        fw.op(ACT, lambda h: h.copy(out=st.mrow[:], in_=mnew[:]), reads=[mnew], writes=[st.mrow])


    def rope(buf, ap3, np_, nh, ti):
        x1 = ap3[:, :, 0:8]; x2 = ap3[:, :, 8:16]
        cs_ = bc(cosT[0:np_, ti:ti + 1, :], [np_, nh, 8]); sn_ = bc(sinT[0:np_, ti:ti + 1, :], [np_, nh, 8])
        t = rtmp[0:np_, 0:nh, :]
        fw.op(DVE, lambda h: h.tensor_tensor(out=t[:, :, 0:8], in0=x2, in1=sn_, op=ALU.mult), reads=[buf, sinT], writes=[rtmp])
        fw.op(DVE, lambda h: h.tensor_tensor(out=t[:, :, 8:16], in0=x1, in1=sn_, op=ALU.mult), reads=[buf, sinT], writes=[rtmp])
        fw.op(DVE, lambda h: h.tensor_tensor(out=ap3[:, :, 0:16].rearrange("p a (c d) -> p a c d", c=2),
                                             in0=ap3[:, :, 0:16].rearrange("p a (c d) -> p a c d", c=2),
                                             in1=bc(cosT[0:np_, ti:ti + 1, :].unsqueeze(2), [np_, nh, 2, 8]), op=ALU.mult), reads=[buf, cosT], writes=[buf])
        fw.op(DVE, lambda h: h.tensor_tensor(out=x1, in0=x1, in1=t[:, :, 0:8], op=ALU.subtract), reads=[buf, rtmp], writes=[buf])
        fw.op(DVE, lambda h: h.tensor_tensor(out=x2, in0=x2, in1=t[:, :, 8:16], op=ALU.add), reads=[buf, rtmp], writes=[buf])

    def softplus(buf, ap):
        fw.op(ACT, lambda h: h.activation(out=ap, in_=ap, func=AF.Exp), reads=[buf], writes=[buf])
        fw.op(ACT, lambda h: h.activation(out=ap, in_=ap, func=AF.Ln, bias=1.0), reads=[buf], writes=[buf])

    def load_gain(row_ap, c0, n):
        fw.dma(SP, gbc[:, c0:c0 + n], row_ap.partition_broadcast(128), sem_g, writes=[gbc])


    sst = fw.sb("sst", [128, 8, 128], F32)

    def emit_state_out(l, st, d_ssm, d_C, d_n, d_m):
        for g0 in range(0, 8, 4):
            for j in range(g0, g0 + 4):
                fw.op(PE, lambda h: h.transpose(out=pd[0][:, (j - g0) * 128:(j - g0 + 1) * 128], in_=st.ST[:, j * 128:(j + 1) * 128], identity=ident_f[:]),
                      reads=[st.ST, ident_f], writes=[pd[0]])
            fw.op(ACT, lambda h: h.copy(out=sst[:, g0:g0 + 4, :], in_=pd[0][:, :].rearrange("p (a b) -> p a b", a=4)), reads=[pd[0]], writes=[sst])
        fw.dma(SP, d_ssm.rearrange("(j p) n -> p j n", p=128), sst[:], sem_o, reads=[sst], is_out=True)
        fw.dma(SP, d_C.rearrange("h d e -> d h e"), st.Cn[:, :, 0:128], sem_o, reads=[st.Cn], is_out=True)
        with nc.allow_non_contiguous_dma(reason="tiny state out"):
            fw.dma(SP, d_n.rearrange("h d -> d h"), st.Cn[:, :, 128], sem_o, reads=[st.Cn], is_out=True)
        fw.dma(SP, d_m.unsqueeze(0), st.mrow[0:1, :], sem_o, reads=[st.mrow], is_out=True)

    def load_state(l, st, r):
        fw.dma(SP, sst[:], st_ssm[l, r].rearrange("(j p) n -> p j n", p=128), sem_st, writes=[sst])
        for g0 in range(0, 8, 4):
            for j in range(g0, g0 + 4):
                fw.op(PE, lambda h: h.transpose(out=pd[0][:, (j - g0) * 128:(j - g0 + 1) * 128], in_=sst[:, j, :], identity=ident_f[:]),
                      reads=[sst, ident_f], writes=[pd[0]])
            fw.op(ACT, lambda h: h.copy(out=st.ST[:, g0 * 128:(g0 + 4) * 128], in_=pd[0][:, :]), reads=[pd[0]], writes=[st.ST])
        fw.op(ACT, lambda h: h.copy(out=st.STb[:], in_=st.ST[:]), reads=[st.ST], writes=[st.STb])
        fw.dma(SP, st.Cn[:, :, 0:128], st_C[l, r].rearrange("h d e -> d h e"), sem_st, writes=[st.Cn])
        with nc.allow_non_contiguous_dma(reason="tiny state in"):
            fw.dma(SP, st.Cn[:, :, 128], st_n[l, r].rearrange("h d -> d h"), sem_st, writes=[st.Cn])
        fw.dma(SP, st.mrow[:], st_m[l, r, :].partition_broadcast(128), sem_st, writes=[st.mrow])
        fw.op(ACT, lambda h: h.copy(out=st.Cnb[:], in_=st.Cn[:]), reads=[st.Cn], writes=[st.Cnb])

    xres = fw.sb("xres", [128, NT, D], F32, pes)
    actT = fw.sb("actT", [128, 16, ST], BF16, pes)
    big16 = fw.sb("big16", [128, NT * 2048], BF16, pes)
    utok = fw.sb("utok", [128, D], BF16, pes)
    sq = fw.sb("sq", [128, D], F32, pes)
    qkf = fw.sb("qkf", [128, NT, 640], F32, pes)
    qkb = fw.sb("qkb", [128, 768], BF16, pes)
    qT = fw.sb("qT", [128, 4, ST], BF16, pes)
    kTd = fw.sb("kTd", [128, 2, ST], BF16, pes)
    vaug = fw.sb("vaug", [128, NT, 2, 65], BF16, pes)
    ztok = fw.sb("ztok", [128, NT, 1024], BF16, pes)
    dtt = fw.sb("dtt", [128, NT, 16], F32, pes)
    mktok = fw.sb("mktok", [128, NT, 512], BF16, pes)
    mvaug = fw.sb("mvaug", [128, NT, 4, 129], BF16, pes)
    motok = fw.sb("motok", [128, NT, 512], BF16, pes)
    gates = fw.sb("gates", [128, NT, 8], F32, pes)
    xraw = fw.sb("xraw", [128, ST + 3], F32, pes)
    cacc = fw.sb("cacc", [128, ST], F32, pes)
    xcT = fw.sb("xcT", [128, 12, ST], BF16, pes)
    mqT = fw.sb("mqT", [128, 4, ST], BF16, pes)
    mkT = fw.sb("mkT", [128, 4, ST], BF16, pes)
    xtok = fw.sb("xtok", [128, 1024], BF16, pes)
    btok = fw.sb("btok", [128, 256], BF16, pes)
    ostage = sq

    fw.op(DVE, lambda h: h.memset(vaug[:], 1.0), writes=[vaug])
    fw.op(DVE, lambda h: h.memset(mvaug[:], 1.0), writes=[mvaug])
    for l in range(L):
        s = carry[l]
        fw.op(DVE, lambda h: h.memset(s.ST[:], 0.0), writes=[s.ST])
        fw.op(DVE, lambda h: h.memset(s.STb[:], 0.0), writes=[s.STb])
        fw.op(DVE, lambda h: h.memset(s.Cn[:], 0.0), writes=[s.Cn])
        fw.op(DVE, lambda h: h.memset(s.Cnb[:], 0.0), writes=[s.Cnb])
        fw.op(DVE, lambda h: h.memset(s.mrow[:], 0.0), writes=[s.mrow])
        fw.op(DVE, lambda h: h.memset(s.xcarry[:], 0.0), writes=[s.xcarry])

    mix_tok = big16[:].rearrange("p (a b) -> p a b", a=NT)
    ck("consts")
    hTg = big16[:].rearrange("p (a b) -> p a b", a=16)
    HT = Buf("HT", hTg, parent=big16)

    def norm_to_actT(nt, tsz, gain_row, xfn):
        load_gain(gain_row, 0, D)
        for tt in range(nt):
            rmsnorm_to(xfn(tt), xres, tsz, gbc[0:tsz, :], utok[0:tsz, :], utok, sq, D, col=tt)
            transpose_to(utok[0:tsz, :], utok, tsz, D, lambda j: actT[:, j, tt * 128:tt * 128 + tsz], actT)

    for stn in range(NST):
        new_pass()
        t0g = stn * ST
        for tt in range(NT):
            fw.dma(SP, xres[:, tt, :], xp[t0g + tt * 128:t0g + (tt + 1) * 128, :], sem_x, writes=[xres])
        tts = [(tt * 128, 128) for tt in range(NT)]
        for l in range(L):
            st = carry[l]
            P = lp[l]
            W = w_in[l]
            norm_to_actT(NT, 128, w_norm_mix[l, :], lambda tt: xres[:, tt, :])
            ck("norm")
            load_gain(w_norm_ssm[l, :], 0, 1024)
            load_gain(w_norm_ml[l, :], 1024, 512)

            def c_qkv(ti, c0, nb, p):
                if c0 < 512:
                    fw.op(ACT, lambda h: h.copy(out=qkf[:, ti, c0:c0 + nb], in_=p[:, 0:nb]), reads=[p], writes=[qkf])
                else:
                    fw.op(ACT, lambda h: h.copy(out=qkf[:, ti, 512:640], in_=p[:, 0:128]), reads=[p], writes=[qkf])
                    fw.op(ACT, lambda h: h.copy(out=vaug[:, ti, :, 0:64], in_=p[:, 128:256].rearrange("p (a b) -> p a b", a=2)), reads=[p], writes=[vaug])
                    rope(qkf, qkf[:, ti, :].rearrange("p (a b) -> p a b", b=64), 128, 10, stn * NT + ti)
                    fw.op(DVE, lambda h: h.tensor_copy(out=qkb[:, 0:512], in_=qkf[:, ti, 0:512]), reads=[qkf], writes=[qkb])
                    fw.op(DVE, lambda h: h.tensor_copy(out=qkb[:, 512:768].rearrange("p (a c b) -> p a c b", a=2, c=2),
                                                       in_=bc(qkf[:, ti, 512:640].rearrange("p (a b) -> p a b", a=2).unsqueeze(2), [128, 2, 2, 64])),
                          reads=[qkf], writes=[qkb])
                    transpose_to(qkb[:, 0:512], qkb, 128, 512, lambda j: qT[:, j, ti * 128:(ti + 1) * 128], qT)
                    transpose_to(qkb[:, 512:768], qkb, 128, 256, lambda j: kTd[:, j, ti * 128:(ti + 1) * 128], kTd)
                    if stn == NST - 1 and ti == NT - 1:
                        fw.op(ACT, lambda h: h.copy(out=ostage[:, 0:128], in_=qkf[:, ti, 512:640]), reads=[qkf], writes=[ostage])
                        fw.op(ACT, lambda h: h.copy(out=ostage[:, 128:256], in_=p[:, 128:256]), reads=[p], writes=[ostage])
                        fw.dma(SP, p_k[l], ostage[:, 0:128], sem_o, reads=[ostage], is_out=True)
                        fw.dma(SP, p_v[l], ostage[:, 128:256], sem_o, reads=[ostage], is_out=True)
            dense_tok(actT, tts, W[:, O_Q:O_Q + 768], 768, c_qkv)
            ck("qkv")

            def c_z(ti, c0, nb, p):
                fw.op(ACT, lambda h: h.copy(out=ztok[:, ti, c0:c0 + nb], in_=p[:, 0:nb]), reads=[p], writes=[ztok])
            dense_tok(actT, tts, W[:, O_Z:O_Z + 1024], 1024, c_z)

            def c_dt(ti, c0, nb, p):
                fw.op(DVE, lambda h: h.tensor_tensor(out=dtt[:, ti, :], in0=p[:, 0:16], in1=P["dtb"][:], op=ALU.add), reads=[p, P["dtb"]], writes=[dtt])
                softplus(dtt, dtt[:, ti, :])
            dense_tok(actT, tts, W[:, O_DT:O_DT + 16], 16, c_dt)

            def c_mk(ti, c0, nb, p):
                fw.op(ACT, lambda h: h.activation(out=mktok[:, ti, c0:c0 + nb], in_=p[:, 0:nb], func=AF.Copy, scale=float(128 ** -0.5)), reads=[p], writes=[mktok])
                if c0 + nb == 512:
                    transpose_to(mktok[:, ti, :], mktok, 128, 512, lambda j: mkT[:, j, ti * 128:(ti + 1) * 128], mkT)
            dense_tok(actT, tts, W[:, O_MK:O_MK + 512], 512, c_mk)

            def c_mv(ti, c0, nb, p):
                h0 = c0 // 128
                fw.op(ACT, lambda h: h.copy(out=mvaug[:, ti, h0:h0 + nb // 128, 0:128], in_=p[:, 0:nb].rearrange("p (a b) -> p a b", b=128)), reads=[p], writes=[mvaug])
            dense_tok(actT, tts, W[:, O_MV:O_MV + 512], 512, c_mv)

            def c_mo(ti, c0, nb, p):
                fw.op(ACT, lambda h: h.copy(out=motok[:, ti, c0:c0 + nb], in_=p[:, 0:nb]), reads=[p], writes=[motok])
            dense_tok(actT, tts, W[:, O_MO:O_MO + 512], 512, c_mo)

            def c_g(ti, c0, nb, p):
                fw.op(DVE, lambda h: h.tensor_tensor(out=gates[:, ti, :], in0=p[:, 0:8], in1=P["gb"][:], op=ALU.add), reads=[p, P["gb"]], writes=[gates])
            dense_tok(actT, tts, W[:, O_MI:O_MI + 8], 8, c_g)
            ck("tokproj")

            def c_mq(cb_, p):
                fw.op(ACT, lambda h: h.copy(out=mqT[:, cb_, :], in_=p[:, 0:ST]), reads=[p], writes=[mqT])
            dense_feat(actT, ST, W[:, O_MQ:O_MQ + 512], 512, c_mq)

            def c_xbc(cb_, p):
                fw.op(ACT, lambda h: h.copy(out=xraw[:, 0:3], in_=st.xcarry[:, cb_, :]), reads=[st.xcarry], writes=[xraw])
                fw.op(ACT, lambda h: h.copy(out=xraw[:, 3:ST + 3], in_=p[:, 0:ST]), reads=[p], writes=[xraw])
                fw.op(ACT, lambda h: h.copy(out=st.xcarry[:, cb_, :], in_=xraw[:, ST:ST + 3]), reads=[xraw], writes=[st.xcarry])
                cw = P["cw"]
                fw.op(DVE, lambda h: h.tensor_scalar(out=cacc[:], in0=xraw[:, 0:ST], scalar1=cw[:, cb_, 0:1], scalar2=None, op0=ALU.mult), reads=[xraw, cw], writes=[cacc])
                for j in range(1, 4):
                    fw.op(DVE, lambda h: h.scalar_tensor_tensor(out=cacc[:], in0=xraw[:, j:j + ST], scalar=cw[:, cb_, j:j + 1], in1=cacc[:], op0=ALU.mult, op1=ALU.add),
                          reads=[xraw, cw, cacc], writes=[cacc])
                fw.op(ACT, lambda h: h.activation(out=xcT[:, cb_, :], in_=cacc[:], func=AF.Silu, bias=P["cb"][:, cb_:cb_ + 1]), reads=[cacc, P["cb"]], writes=[xcT])
            dense_feat(actT, ST, W[:, O_XBC:O_XBC + 1536], 1536, c_xbc)
            ck("proj")
            if stn == NST - 1:
                with nc.allow_non_contiguous_dma(reason="tiny conv state out"):
                    for j in range(3):
                        fw.dma(SP, p_conv[l, j, :].rearrange("(b p) -> p b", p=128), st.xcarry[:, :, j], sem_o, reads=[st.xcarry], is_out=True)

            for c in range(NT):
                sl = slice(c * 128, (c + 1) * 128)
                has_prev = not (stn == 0 and c == 0)
                if c == 0:
                    kpf = lambda kv: st.kprev[:, kv, :]; kpb = st.kprev
                    vpf = lambda kv: st.vprev[:, kv, :]; vpb = st.vprev
                else:
                    kpf = (lambda cc_: (lambda kv: kTd[:, kv, (cc_ - 1) * 128:cc_ * 128]))(c); kpb = kTd
                    vpf = (lambda cc_: (lambda kv: vaug[:, cc_ - 1, kv, :]))(c); vpb = vaug
                swa_block(l, lambda j: qT[:, j, sl], qT, lambda kv: kTd[:, kv, sl], kTd, lambda kv: vaug[:, c, kv, :], vaug,
                          kpf, kpb, vpf, vpb, has_prev, mix_tok[:, c, 0:512], big16)
                ck("swa")
                if c == NT - 1:
                    fw.op(ACT, lambda h: h.copy(out=st.kprev[:], in_=kTd[:, :, sl]), reads=[kTd], writes=[st.kprev])
                    fw.op(ACT, lambda h: h.copy(out=st.vprev[:], in_=vaug[:, NT - 1, :, :]), reads=[vaug], writes=[st.vprev])
                for j in range(8):
                    fw.op(PE, lambda h: h.transpose(out=ptb[:, j * 128:(j + 1) * 128], in_=xcT[:, j, sl], identity=ident_b[:]), reads=[xcT, ident_b], writes=[ptb])
                fw.op(ACT, lambda h: h.copy(out=xtok[:], in_=ptb[:, 0:1024]), reads=[ptb], writes=[xtok])
                for j in range(2):
                    fw.op(PE, lambda h: h.transpose(out=ptb[:, j * 128:(j + 1) * 128], in_=xcT[:, 8 + j, sl], identity=ident_b[:]), reads=[xcT, ident_b], writes=[ptb])
                fw.op(ACT, lambda h: h.copy(out=btok[:], in_=ptb[:, 0:256]), reads=[ptb], writes=[btok])
                ssd_chunk(l, st, xtok[:], xtok, btok[:], btok, lambda g: xcT[:, 8 + g, sl], lambda g: xcT[:, 10 + g, sl], xcT,
                          dtt[:, c, :], dtt, ztok[:, c, :], ztok, mix_tok[:, c, 512:1536], big16)
                ck("ssd")
                mlstm_chunk(l, st, lambda hd: mqT[:, hd, sl], lambda hd: mkT[:, hd, sl], [mqT, mkT], mktok[:, c, :], lambda hd: mvaug[:, c, hd, :], [mktok, mvaug],
                            gates[:, c, 0:4], gates[:, c, 4:8], gates, motok[:, c, :], motok, mix_tok[:, c, 1536:2048], big16)
                if DEBUG_STOP[0] == "mlstm":
                    dump("mix", big16, mix_tok[:, c, :], [128, 2048])
                    dump("dtt", dtt, dtt[:, c, :], [128, 16])
                    dump("xtok", xtok, xtok[:], [128, 1024])
                    dump("ST", st.ST, st.ST[:], [128, 1024])
                    dump("Cn", st.Cn, st.Cn[:], [128, 4, 129])
                    dump("mrow", st.mrow, st.mrow[:], [128, 4])
                ck("mlstm")
            if stn == NST - 1:
                emit_state_out(l, st, p_ssm[l], p_C[l], p_n[l], p_m[l])

            for tt in range(NT):
                transpose_to(mix_tok[:, tt, :], big16, 128, D, lambda j: actT[:, j, tt * 128:(tt + 1) * 128], actT)

            def c_res(ti, c0, nb, p):
                fw.op(DVE, lambda h: h.tensor_tensor(out=xres[:, ti, c0:c0 + nb], in0=p[:, 0:nb], in1=xres[:, ti, c0:c0 + nb], op=ALU.add), reads=[p, xres], writes=[xres])
            dense_tok(actT, tts, w_out[l], D, c_res)
            ck("wout")

            norm_to_actT(NT, 128, w_norm_mlp[l, :], lambda tt: xres[:, tt, :])
            for g in range(4):
                def c_up(cb_, p):
                    fw.op(ACT, lambda h: h.activation(out=sq[:, 0:ST], in_=p[:, 0:ST], func=AF.Relu), reads=[p], writes=[sq])
                    fw.op(DVE, lambda h: h.tensor_tensor(out=hTg[:, cb_, :], in0=sq[:, 0:ST], in1=sq[:, 0:ST], op=ALU.mult), reads=[sq], writes=[big16])
                dense_feat(actT, ST, w_up[l][:, g * 2048:(g + 1) * 2048], 2048, c_up)
                dense_tok(HT, tts, w_down[l][g * 2048:(g + 1) * 2048, :], D, c_res)
            ck("layer")

        load_gain(w_norm_final, 0, D)
        for tt in range(NT):
            rmsnorm_to(xres[:, tt, :], xres, 128, gbc[:, :], ostage[:, :], ostage, sq, D, col=tt)
            fw.dma(SP, y_p[t0g + tt * 128:t0g + (tt + 1) * 128, :], ostage[:, :], sem_o, reads=[ostage], is_out=True)
        ck("st")
        ck("st%d" % stn)


    fw.barrier()
    pes.close()
    fw.es_alloc = None
    ck("prompt")
    ses = ExitStack()
    xrs = fw.sb("xrs", [RS, D], F32, ses)
    actS = fw.sb("actS", [128, 16, RS], BF16, ses)
    utS = fw.sb("utS", [RS, D], BF16, ses)
    sqS = fw.sb("sqS", [RS, D], F32, ses)
    sall = fw.sb("sall", [RS, INW], F32, ses)
    mixs = fw.sb("mixs", [RS, D], BF16, ses)
    hsT = fw.sb("hsT", [128, 16, RS], BF16, ses)
    xcs = sqS
    PB = [fw.sb(f"pb{i}", [RS, 8192], F32, ses) for i in range(3)]
    cj = PB[0]
    wj = PB[1]
    pbc = [0]

    def nextpb():
        b_ = PB[pbc[0] % 3]
        pbc[0] += 1
        return b_
    scs = fw.sb("scs", [RS, 8, 129], F32, ses)
    sden = fw.sb("sden", [RS, 8], F32, ses)
    so = fw.sb("so", [RS, 2, 8, 64], F32, ses)
    dec = fw.sb("dec", [RS, 16], F32, ses)
    xdt = Buf("xdt", scs[:].rearrange("p a b -> p (a b)")[:, 0:1024].rearrange("p (a b) -> p a b", a=16), parent=scs)
    yv = Buf("yv", so[:].rearrange("p a h d -> p (a h d)").rearrange("p (a b) -> p a b", a=16), parent=so)
    nst = fw.sb("nst", [RS, 4, 128], F32, ses)
    ms = fw.sb("ms", [RS, 48], F32, ses)
    mtmp = fw.sb("mtmp", [RS, 4, 128], F32, ses)
    kws = fw.sb("kws", [RS, 4, 128], F32, ses)
    qcs = Buf("qcs", so[:].rearrange("p a h d -> p (a h d)").rearrange("p (a h d) -> p a h d", a=2, h=4), parent=so)
    sem_s = None
    sem_m = None

    fw.dma(SP, xrs[:], xsm[:, :], sem_s, writes=[xrs])
    ttS = [(0, RS)]
    new_pass()

    def stage_tok(r, c0, n, dst_ap, dstbuf, scale=None):
        for o in range(0, n, 512):
            m = min(512, n - o)
            fw.op(PE, lambda h: h.matmul(pd[1][:, 0:m], lhsT=oh[:, r, :], rhs=sall[:, c0 + o:c0 + o + m], start=True, stop=True), reads=[oh, sall], writes=[pd[1]])
            fw.op(ACT, lambda h: h.copy(out=dst_ap(o, m), in_=pd[1][:, 0:m]), reads=[pd[1]], writes=[dstbuf])

    def stage_feat(r, srcbuf, src_ap, dst_ap, dstbuf):
        fw.op(PE, lambda h: h.matmul(pd[1][:, 0:128], lhsT=src_ap, rhs=oh[:, r, :], start=True, stop=True), reads=[oh, srcbuf], writes=[pd[1]])
        fw.op(ACT, lambda h: h.copy(out=dst_ap, in_=pd[1][:, 0:128]), reads=[pd[1]], writes=[dstbuf])

    def norm_to_actS(gain_row):
        load_gain(gain_row, 0, D)
        rmsnorm_to(xrs[:, :], xrs, RS, gbc[0:RS, :], utS[:, :], utS, sqS, D, col=0)
        transpose_to(utS[:, :], utS, RS, D, lambda j: actS[:, j, :], actS)

    def c_res_s(ti, c0, nb, p):
        fw.op(DVE, lambda h: h.tensor_tensor(out=xrs[:, c0:c0 + nb], in0=p[0:RS, 0:nb], in1=xrs[:, c0:c0 + nb], op=ALU.add), reads=[p, xrs], writes=[xrs])

    for l in range(L):
        st = carry[l]
        P = lp[l]
        norm_to_actS(w_norm_mix[l, :])
        load_gain(w_norm_ssm[l, :], 0, 1024)
        load_gain(w_norm_ml[l, :], 1024, 512)

        def c_all(ti, c0, nb, p):
            fw.op(ACT, lambda h: h.copy(out=sall[:, c0:c0 + nb], in_=p[0:RS, 0:nb]), reads=[p], writes=[sall])
        for (o_, n_) in [(O_Q, 768), (O_Z, 1024), (O_DT, 16), (O_MK, 512), (O_MV, 512), (O_MO, 512), (O_MI, 8), (O_MQ, 512), (O_XBC, 1536)]:
            def c_seg(ti, c0, nb, p, o_=o_):
                c_all(ti, o_ + c0, nb, p)
            dense_tok(actS, ttS, w_in[l][:, o_:o_ + n_], n_, c_seg)
        rope(sall, sall[:, 0:640].rearrange("p (a b) -> p a b", b=64), RS, 10, 16)
        fw.op(DVE, lambda h: h.tensor_scalar(out=sall[:, O_MK:O_MK + 512], in0=sall[:, O_MK:O_MK + 512], scalar1=float(128 ** -0.5), scalar2=None, op0=ALU.mult), reads=[sall], writes=[sall])
        fw.op(DVE, lambda h: h.tensor_tensor(out=sall[:, O_DT:O_DT + 16], in0=sall[:, O_DT:O_DT + 16], in1=P["dtb"][0:RS, :], op=ALU.add), reads=[sall, P["dtb"]], writes=[sall])
        softplus(sall, sall[:, O_DT:O_DT + 16])
        fw.op(DVE, lambda h: h.tensor_tensor(out=sall[:, O_MI:O_MI + 8], in0=sall[:, O_MI:O_MI + 8], in1=P["gb"][0:RS, :], op=ALU.add), reads=[sall, P["gb"]], writes=[sall])
        fw.dma(SP, s_k[l, :, 0:127, :], cache_k[l, :, 1:128, :], sem_o, is_out=True)
        fw.dma(SP, s_v[l, :, 0:127, :], cache_v[l, :, 1:128, :], sem_o, is_out=True)
        fw.dma(SP, s_k[l, :, 127, :], sall[:, O_K:O_K + 128], sem_o, reads=[sall], is_out=True)
        fw.dma(SP, s_v[l, :, 127, :], sall[:, O_V:O_V + 128], sem_o, reads=[sall], is_out=True)
        fw.dma(SP, s_conv[l, :, 0:2, :], st_conv[l, :, 1:3, :], sem_o, is_out=True)
        fw.dma(SP, s_conv[l, :, 2, :], sall[:, O_XBC:O_XBC + 1536], sem_o, reads=[sall], is_out=True)
        fw.dma(SP, wj[:, 0:1536], conv_w[l, 3, :].partition_broadcast(RS), sem_s, writes=[wj])
        fw.op(DVE, lambda h: h.tensor_tensor(out=xcs[:, 0:1536], in0=sall[:, O_XBC:O_XBC + 1536], in1=wj[:, 0:1536], op=ALU.mult), reads=[sall, wj], writes=[xcs])
        for j in range(3):
            fw.dma(SP, cj[:, 0:1536], st_conv[l, :, j, :], sem_s, writes=[cj])
            fw.dma(SP, wj[:, 0:1536], conv_w[l, j, :].partition_broadcast(RS), sem_s, writes=[wj])
            fw.op(DVE, lambda h: h.tensor_tensor(out=cj[:, 0:1536], in0=cj[:, 0:1536], in1=wj[:, 0:1536], op=ALU.mult), reads=[cj, wj], writes=[cj])
            fw.op(DVE, lambda h: h.tensor_tensor(out=xcs[:, 0:1536], in0=xcs[:, 0:1536], in1=cj[:, 0:1536], op=ALU.add), reads=[xcs, cj], writes=[xcs])
        fw.dma(SP, wj[:, 0:1536], conv_b[l, :].partition_broadcast(RS), sem_s, writes=[wj])
        fw.op(DVE, lambda h: h.tensor_tensor(out=xcs[:, 0:1536], in0=xcs[:, 0:1536], in1=wj[:, 0:1536], op=ALU.add), reads=[xcs, wj], writes=[xcs])
        fw.op(ACT, lambda h: h.activation(out=sall[:, O_XBC:O_XBC + 1536], in_=xcs[:, 0:1536], func=AF.Silu), reads=[xcs, sall], writes=[sall])

        A_ = P["A"]
        qv = sall[:, 0:512].rearrange("p (h d) -> p h d", h=8)
        knew = sall[:, O_K:O_K + 128].rearrange("p (a d) -> p a d", a=2)
        vnew = sall[:, O_V:O_V + 128].rearrange("p (a d) -> p a d", a=2)
        for kv in range(2):
            Kc, Vc, T = nextpb(), nextpb(), nextpb()
            Kc3 = Kc[:, :].rearrange("p (s d) -> p s d", d=64)
            Vc3 = Vc[:, :].rearrange("p (s d) -> p s d", d=64)
            T3 = T[:, 0:4096].rearrange("p (a b) -> p a b", a=64)
            with nc.allow_non_contiguous_dma(reason="kv cache head slice (256B runs)"):
                fw.dma(SP, Kc3, cache_k[l, :, :, kv * 64:(kv + 1) * 64], None, writes=[Kc])
                fw.dma(SP, Vc3, cache_v[l, :, :, kv * 64:(kv + 1) * 64], None, writes=[Vc])
            hs = slice(kv * 4, kv * 4 + 4)
            for h4 in range(4):
                h_ = kv * 4 + h4
                for ch in range(2):
                    ps_ = slice(ch * 64, (ch + 1) * 64)
                    fw.op(DVE, lambda h: h.tensor_tensor(out=T3, in0=Kc3[:, ps_, :], in1=bc(qv[:, h_:h_ + 1, :], [RS, 64, 64]), op=ALU.mult), reads=[Kc, sall], writes=[T])
                    fw.op(DVE, lambda h: h.tensor_reduce(out=scs[:, h_, ps_], in_=T3, op=ALU.add, axis=AX.X), reads=[T], writes=[scs])
            fw.op(DVE, lambda h: h.tensor_tensor(out=T[:, 0:256].rearrange("p (a b) -> p a b", a=4), in0=qv[:, hs, :], in1=bc(knew[:, kv:kv + 1, :], [RS, 4, 64]), op=ALU.mult), reads=[sall], writes=[T])
            fw.op(DVE, lambda h: h.tensor_reduce(out=scs[:, hs, 128], in_=T[:, 0:256].rearrange("p (a b) -> p a b", a=4), op=ALU.add, axis=AX.X), reads=[T], writes=[scs])
            fw.op(ACT, lambda h: h.activation(out=scs[:, hs, :], in_=scs[:, hs, :], func=AF.Exp, scale=0.125), reads=[scs], writes=[scs])
            fw.op(DVE, lambda h: h.tensor_reduce(out=sden[:, hs], in_=scs[:, hs, :], op=ALU.add, axis=AX.X), reads=[scs], writes=[sden])
            for h4 in range(4):
                h_ = kv * 4 + h4
                for ch in range(2):
                    ps_ = slice(ch * 64, (ch + 1) * 64)
                    fw.op(DVE, lambda h: h.tensor_tensor(out=T3, in0=Vc3[:, ps_, :].rearrange("p s d -> p d s"), in1=bc(scs[:, h_:h_ + 1, ps_], [RS, 64, 64]), op=ALU.mult), reads=[Vc, scs], writes=[T])
                    fw.op(DVE, lambda h: h.tensor_reduce(out=so[:, ch, h_, :], in_=T3, op=ALU.add, axis=AX.X), reads=[T], writes=[so])
                fw.op(DVE, lambda h: h.scalar_tensor_tensor(out=so[:, 0, h_, :], in0=vnew[:, kv, :], scalar=scs[:, h_, 128:129], in1=so[:, 0, h_, :], op0=ALU.mult, op1=ALU.add), reads=[sall, scs, so], writes=[so])
        fw.op(DVE, lambda h: h.tensor_tensor(out=so[:, 0, :, :], in0=so[:, 0, :, :], in1=so[:, 1, :, :], op=ALU.add), reads=[so], writes=[so])
        fw.op(DVE, lambda h: h.tensor_tensor(out=sden[:], in0=sden[:], in1=P["esk"][0:RS, :], op=ALU.add), reads=[sden, P["esk"]], writes=[sden])
        fw.op(DVE, lambda h: h.reciprocal(out=sden[:], in_=sden[:]), reads=[sden], writes=[sden])
        fw.op(DVE, lambda h: h.tensor_tensor(out=mixs[:, 0:512].rearrange("p (a b) -> p a b", a=8), in0=so[:, 0, :, :], in1=bc(sden[:].unsqueeze(2), [RS, 8, 64]), op=ALU.mult), reads=[so, sden], writes=[mixs])

        x16 = sall[:, O_XBC:O_XBC + 1024].rearrange("p (a b) -> p a b", a=16)
        Bm = sall[:, O_XBC + 1024:O_XBC + 1280].rearrange("p (a b) -> p a b", a=2)
        Cm = sall[:, O_XBC + 1280:O_XBC + 1536].rearrange("p (a b) -> p a b", a=2)
        dts = sall[:, O_DT:O_DT + 16]
        fw.op(DVE, lambda h: h.tensor_tensor(out=dec[:], in0=dts, in1=A_[0:RS, :], op=ALU.mult), reads=[sall, A_], writes=[dec])
        fw.op(ACT, lambda h: h.activation(out=dec[:], in_=dec[:], func=AF.Exp), reads=[dec], writes=[dec])
        fw.op(DVE, lambda h: h.tensor_tensor(out=xdt[:], in0=x16, in1=bc(dts.unsqueeze(2), [RS, 16, 64]), op=ALU.mult), reads=[sall], writes=[xdt])
        for hh in range(16):
            g = hh // 8
            Sp, T = nextpb(), nextpb()
            Sp3 = Sp[:, :].rearrange("p (a b) -> p a b", a=64)
            T3 = T[:, :].rearrange("p (a b) -> p a b", a=64)
            fw.dma(SP, Sp3, st_ssm[l, :, hh * 64:(hh + 1) * 64, :], None, writes=[Sp])
            fw.op(POOL, lambda h: h.tensor_tensor(out=T3, in0=bc(xdt[:, hh, :].unsqueeze(2), [RS, 64, 128]), in1=bc(Bm[:, g:g + 1, :], [RS, 64, 128]), op=ALU.mult), reads=[xdt, sall], writes=[T])
            fw.op(DVE, lambda h: h.scalar_tensor_tensor(out=Sp[:, :], in0=Sp[:, :], scalar=dec[:, hh:hh + 1], in1=T[:, :], op0=ALU.mult, op1=ALU.add), reads=[Sp, dec, T], writes=[Sp])
            fw.dma(SP, s_ssm[l, :, hh * 64:(hh + 1) * 64, :], Sp3, None, reads=[Sp], is_out=True)
            fw.op(DVE, lambda h: h.tensor_tensor(out=T3, in0=Sp3, in1=bc(Cm[:, g:g + 1, :], [RS, 64, 128]), op=ALU.mult), reads=[Sp, sall], writes=[T])
            fw.op(DVE, lambda h: h.tensor_reduce(out=yv[:, hh, :], in_=T3, op=ALU.add, axis=AX.X), reads=[T], writes=[yv])
        fw.op(DVE, lambda h: h.tensor_tensor(out=xdt[:], in0=x16, in1=bc(P["dsk"][0:RS, :].unsqueeze(2), [RS, 16, 64]), op=ALU.mult), reads=[sall, P["dsk"]], writes=[xdt])
        fw.op(DVE, lambda h: h.tensor_tensor(out=yv[:], in0=yv[:], in1=xdt[:], op=ALU.add), reads=[yv, xdt], writes=[yv])
        fw.op(ACT, lambda h: h.activation(out=xdt[:].rearrange("p a b -> p (a b)"), in_=sall[:, O_Z:O_Z + 1024], func=AF.Silu), reads=[sall], writes=[xdt])
        fw.op(DVE, lambda h: h.tensor_tensor(out=yv[:], in0=yv[:], in1=xdt[:], op=ALU.mult), reads=[yv, xdt], writes=[yv])
        yvf = yv[:].rearrange("p a b -> p (a b)")
        for g in range(2):
            rmsnorm_to(yvf[:, g * 512:(g + 1) * 512], yv, RS, gbc[0:RS, g * 512:(g + 1) * 512], mixs[:, 512 + g * 512:512 + (g + 1) * 512], mixs, sqS, 512, col=8 + g)

        q4 = sall[:, O_MQ:O_MQ + 512].rearrange("p (a b) -> p a b", a=4)
        k4 = sall[:, O_MK:O_MK + 512].rearrange("p (a b) -> p a b", a=4)
        v4 = sall[:, O_MV:O_MV + 512].rearrange("p (a b) -> p a b", a=4)
        igs = sall[:, O_MI:O_MI + 4]
        fgs = sall[:, O_MF:O_MF + 4]
        fw.dma(SP, nst[:], st_n[l], None, writes=[nst])
        fw.dma(SP, ms[:, 0:4], st_m[l], None, writes=[ms])
        M_, LF, BM, MT, SWS, G_, EMT, QK, QN, SW, DEN, RD = [ms[:, 4 * i:4 * i + 4] for i in range(12)]
        def dv(out, in0, in1, op):
            fw.op(DVE, lambda h: h.tensor_tensor(out=out, in0=in0, in1=in1, op=op), reads=[ms, sall], writes=[ms])
        fw.op(ACT, lambda h: h.activation(out=LF, in_=fgs, func=AF.Exp, scale=-1.0), reads=[sall], writes=[ms])
        fw.op(ACT, lambda h: h.activation(out=LF, in_=LF, func=AF.Ln, bias=1.0), reads=[ms], writes=[ms])
        dv(BM, M_, LF, ALU.subtract)
        dv(MT, BM, igs, ALU.max)
        dv(SWS, igs, MT, ALU.subtract)
        fw.op(ACT, lambda h: h.activation(out=SWS, in_=SWS, func=AF.Exp), reads=[ms], writes=[ms])
        dv(G_, BM, MT, ALU.subtract)
        fw.op(ACT, lambda h: h.activation(out=G_, in_=G_, func=AF.Exp), reads=[ms], writes=[ms])
        fw.op(ACT, lambda h: h.activation(out=EMT, in_=MT, func=AF.Exp, scale=-1.0), reads=[ms], writes=[ms])
        fw.op(DVE, lambda h: h.tensor_tensor(out=mtmp[:], in0=q4, in1=k4, op=ALU.mult), reads=[sall], writes=[mtmp])
        fw.op(DVE, lambda h: h.tensor_reduce(out=QK, in_=mtmp[:], op=ALU.add, axis=AX.X), reads=[mtmp], writes=[ms])
        fw.op(DVE, lambda h: h.tensor_tensor(out=mtmp[:], in0=q4, in1=nst[:], op=ALU.mult), reads=[sall, nst], writes=[mtmp])
        fw.op(DVE, lambda h: h.tensor_reduce(out=QN, in_=mtmp[:], op=ALU.add, axis=AX.X), reads=[mtmp], writes=[ms])
        dv(SW, SWS, QK, ALU.mult)
        dv(DEN, G_, QN, ALU.mult)
        dv(DEN, DEN, SW, ALU.add)
        fw.op(DVE, lambda h: h.tensor_scalar(out=RD, in0=DEN, scalar1=-1.0, scalar2=None, op0=ALU.mult), reads=[ms], writes=[ms])
        dv(RD, RD, DEN, ALU.max)
        dv(RD, RD, EMT, ALU.max)
        fw.op(DVE, lambda h: h.reciprocal(out=RD, in_=RD), reads=[ms], writes=[ms])
        fw.op(DVE, lambda h: h.tensor_tensor(out=kws[:], in0=k4, in1=bc(SWS.unsqueeze(2), [RS, 4, 128]), op=ALU.mult), reads=[sall, ms], writes=[kws])
        fw.op(DVE, lambda h: h.tensor_tensor(out=nst[:], in0=nst[:], in1=bc(G_.unsqueeze(2), [RS, 4, 128]), op=ALU.mult), reads=[nst, ms], writes=[nst])
        fw.op(DVE, lambda h: h.tensor_tensor(out=nst[:], in0=nst[:], in1=kws[:], op=ALU.add), reads=[nst, kws], writes=[nst])
        fw.dma(SP, s_n[l], nst[:], None, reads=[nst], is_out=True)
        fw.dma(SP, s_m[l], MT, None, reads=[ms], is_out=True)
        for hd in range(4):
            for hf in range(2):
                dsl = slice(hf * 64, (hf + 1) * 64)
                Ct, T = nextpb(), nextpb()
                Ct3 = Ct[:, :].rearrange("p (a b) -> p a b", a=64)
                T3 = T[:, :].rearrange("p (a b) -> p a b", a=64)
                Te = T[:, :].rearrange("p (e d) -> p e d", e=128)
                fw.dma(SP, Ct3, st_C[l, :, hd, dsl, :], None, writes=[Ct])
                fw.op(DVE, lambda h: h.tensor_tensor(out=Te, in0=Ct3.rearrange("p d e -> p e d"), in1=bc(q4[:, hd:hd + 1, dsl], [RS, 128, 64]), op=ALU.mult), reads=[Ct, sall], writes=[T])
                fw.op(DVE, lambda h: h.tensor_reduce(out=qcs[:, hf, hd, :], in_=Te, op=ALU.add, axis=AX.X), reads=[T], writes=[qcs])
                fw.op(POOL, lambda h: h.tensor_tensor(out=T3, in0=bc(kws[:, hd, dsl].unsqueeze(2), [RS, 64, 128]), in1=bc(v4[:, hd:hd + 1, :], [RS, 64, 128]), op=ALU.mult), reads=[kws, sall], writes=[T])
                fw.op(DVE, lambda h: h.scalar_tensor_tensor(out=Ct[:, :], in0=Ct[:, :], scalar=G_[:, hd:hd + 1], in1=T[:, :], op0=ALU.mult, op1=ALU.add), reads=[Ct, ms, T], writes=[Ct])
                fw.dma(SP, s_C[l, :, hd, dsl, :], Ct3, None, reads=[Ct], is_out=True)
        fw.op(DVE, lambda h: h.tensor_tensor(out=qcs[:, 0, :, :], in0=qcs[:, 0, :, :], in1=qcs[:, 1, :, :], op=ALU.add), reads=[qcs], writes=[qcs])
        fw.op(DVE, lambda h: h.tensor_tensor(out=mtmp[:], in0=v4, in1=bc(SW.unsqueeze(2), [RS, 4, 128]), op=ALU.mult), reads=[sall, ms], writes=[mtmp])
        fw.op(DVE, lambda h: h.tensor_tensor(out=qcs[:, 0, :, :], in0=qcs[:, 0, :, :], in1=bc(G_.unsqueeze(2), [RS, 4, 128]), op=ALU.mult), reads=[qcs, ms], writes=[qcs])
        fw.op(DVE, lambda h: h.tensor_tensor(out=mtmp[:], in0=mtmp[:], in1=qcs[:, 0, :, :], op=ALU.add), reads=[mtmp, qcs], writes=[mtmp])
        fw.op(DVE, lambda h: h.tensor_tensor(out=mtmp[:], in0=mtmp[:], in1=bc(RD.unsqueeze(2), [RS, 4, 128]), op=ALU.mult), reads=[mtmp, ms], writes=[mtmp])
        fw.op(DVE, lambda h: h.tensor_tensor(out=kws[:], in0=mtmp[:], in1=mtmp[:], op=ALU.mult), reads=[mtmp], writes=[kws])
        fw.op(DVE, lambda h: h.tensor_reduce(out=QK, in_=kws[:], op=ALU.add, axis=AX.X), reads=[kws], writes=[ms])
        fw.op(DVE, lambda h: h.tensor_scalar(out=QK, in0=QK, scalar1=1.0 / 128, scalar2=EPS, op0=ALU.mult, op1=ALU.add), reads=[ms], writes=[ms])
        fw.op(ACT, lambda h: h.activation(out=QK, in_=QK, func=AF.Sqrt), reads=[ms], writes=[ms])
        fw.op(DVE, lambda h: h.reciprocal(out=QK, in_=QK), reads=[ms], writes=[ms])
        fw.op(DVE, lambda h: h.tensor_tensor(out=mtmp[:], in0=mtmp[:], in1=bc(QK.unsqueeze(2), [RS, 4, 128]), op=ALU.mult), reads=[mtmp, ms], writes=[mtmp])
        mtf = mtmp[:].rearrange("p a b -> p (a b)")
        fw.op(DVE, lambda h: h.tensor_tensor(out=mtf, in0=mtf, in1=gbc[0:RS, 1024:1536], op=ALU.mult), reads=[mtmp, gbc], writes=[mtmp])
        fw.op(ACT, lambda h: h.activation(out=kws[:].rearrange("p a b -> p (a b)"), in_=sall[:, O_MO:O_MO + 512], func=AF.Sigmoid), reads=[sall], writes=[kws])
        fw.op(DVE, lambda h: h.tensor_tensor(out=mixs[:, 1536:2048], in0=mtf, in1=kws[:].rearrange("p a b -> p (a b)"), op=ALU.mult), reads=[mtmp, kws], writes=[mixs])

        transpose_to(mixs[:, :], mixs, RS, D, lambda j: actS[:, j, :], actS)
        dense_tok(actS, ttS, w_out[l], D, c_res_s)
        norm_to_actS(w_norm_mlp[l, :])
        for g in range(4):
            def c_up_s(ti, c0, nb, p):
                fw.op(ACT, lambda h: h.activation(out=sqS[:, c0:c0 + nb], in_=p[0:RS, 0:nb], func=AF.Relu), reads=[p], writes=[sqS])
                fw.op(DVE, lambda h: h.tensor_tensor(out=utS[:, c0:c0 + nb], in0=sqS[:, c0:c0 + nb], in1=sqS[:, c0:c0 + nb], op=ALU.mult), reads=[sqS], writes=[utS])
            dense_tok(actS, ttS, w_up[l][:, g * 2048:(g + 1) * 2048], 2048, c_up_s)
            transpose_to(utS[:, :], utS, RS, D, lambda j: hsT[:, j, :], hsT)
            dense_tok(hsT, ttS, w_down[l][g * 2048:(g + 1) * 2048, :], D, c_res_s)

    load_gain(w_norm_final, 0, D)
    rmsnorm_to(xrs[:, :], xrs, RS, gbc[0:RS, :], sqS[:, :], sqS, sall, D, col=0)
    fw.dma(SP, y_s[:, :], sqS[:, :], sem_o, reads=[sqS], is_out=True)
    ses.close()


_NC = [None]


def _consts():
    i = np.arange(128)
    c = {}
    c["c_ident"] = np.eye(128, dtype=np.float32)
    c["c_tri"] = (i[:, None] <= i[None, :]).astype(np.float32)
    c["c_triT"] = (i[:, None] >= i[None, :]).astype(np.float32)
    c["c_U"] = (i[:, None] > i[None, :]).astype(np.float32)
    c["c_negqk"] = np.where(i[None, :] > i[:, None], -1e30, 0.0).astype(np.float32)
    c["c_negkq"] = np.where(i[:, None] > i[None, :], -30000.0, 0.0).astype(np.float32)
    e = np.zeros((128, 128), np.float32); e[127, :] = 1.0
    c["c_e127"] = e
    half = 8
    inv = np.power(np.float32(500000.0), -np.arange(half, dtype=np.float32) / half).astype(np.float32)
    pos = np.zeros((128, 17), np.float32)
    for t in range(16):
        pos[:, t] = t * 128 + i
    pos[:, 16] = PAST
    ang = pos[:, :, None].astype(np.float32) * inv[None, None, :]
    c["c_cos"] = np.cos(ang).astype(np.float32)
    c["c_sin"] = np.sin(ang).astype(np.float32)
    oh = np.zeros((RS, RS, 128), np.float32)
    for r in range(RS):
        oh[r, r, 0] = 1.0
    c["c_oh"] = oh
    pad = np.zeros((128, 8), np.float32)
    pad[1:, 0:4] = -1.0e4
    pad[1:, 4:8] = 1.0e4
    c["c_pad"] = pad
    return c


def kernel(x_prompt, x_sample, cache_swa_k, cache_swa_v, state_conv, state_ssm, state_mlstm_C,
           state_mlstm_n, state_mlstm_m, w_norm_mix, w_in, attn_sinks, conv_w, conv_b, dt_bias, a_log,
           d_skip, w_norm_ssm, igate_b, fgate_b, w_norm_mlstm, w_out, w_norm_mlp, w_up, w_down,
           w_norm_final):
    f = lambda a: np.ascontiguousarray(np.asarray(a, dtype=np.float32))
    if _NC[0] is None:
        _NC[0] = build()
    nc = _NC[0]
    cst = _consts()
    shared = {
        "w_norm_mix": f(w_norm_mix), "w_in": f(w_in), "sinks": f(attn_sinks).reshape(L, 8), "conv_w": f(conv_w),
        "conv_b": f(conv_b), "dt_bias": f(dt_bias), "a_log": f(a_log), "d_skip": f(d_skip), "w_norm_ssm": f(w_norm_ssm),
        "igb": f(igate_b), "fgb": f(fgate_b), "w_norm_ml": f(w_norm_mlstm), "w_out": f(w_out), "w_norm_mlp": f(w_norm_mlp),
        "w_up": f(w_up), "w_down": f(w_down), "w_norm_final": f(w_norm_final),
    }
    shared.update(cst)
    in_maps = []
    for c in range(NCORES):
        rs = slice(c * RS, (c + 1) * RS)
        m = dict(shared)
        m["xp"] = f(x_prompt[c])
        m["xsm"] = f(x_sample[rs, 0, :])
        m["cache_k"] = f(np.asarray(cache_swa_k)[:, rs].reshape(L, RS, 128, 128))
        m["cache_v"] = f(np.asarray(cache_swa_v)[:, rs].reshape(L, RS, 128, 128))
        m["st_conv"] = f(np.asarray(state_conv)[:, rs])
        m["st_ssm"] = f(np.asarray(state_ssm)[:, rs].reshape(L, RS, 1024, 128))
        m["st_C"] = f(np.asarray(state_mlstm_C)[:, rs])
        m["st_n"] = f(np.asarray(state_mlstm_n)[:, rs])
        m["st_m"] = f(np.asarray(state_mlstm_m)[:, rs])
        in_maps.append(m)
    res = run_bass_kernel_spmd(nc, in_maps, core_ids=list(range(NCORES)))
    R = res.results
    cat = lambda k, ax: np.concatenate([np.asarray(R[c][k]) for c in range(NCORES)], axis=ax)
    stk = lambda k: np.stack([np.asarray(R[c][k]) for c in range(NCORES)], axis=1)
    y_prompt = np.stack([np.asarray(R[c]["y_p"]) for c in range(NCORES)], axis=0)
    y_sample = cat("y_s", 0).reshape(NCORES * RS, 1, D)
    p_k = stk("p_k").reshape(L, NCORES, 128, 2, 64)
    p_v = stk("p_v").reshape(L, NCORES, 128, 2, 64)
    p_conv = stk("p_conv")
    p_ssm = stk("p_ssm").reshape(L, NCORES, 16, 64, 128)
    p_C = stk("p_C"); p_n = stk("p_n"); p_m = stk("p_m")
    s_k = cat("s_k", 1).reshape(L, NCORES * RS, 128, 2, 64)
    s_v = cat("s_v", 1).reshape(L, NCORES * RS, 128, 2, 64)
    s_conv = cat("s_conv", 1)
    s_ssm = cat("s_ssm", 1).reshape(L, NCORES * RS, 16, 64, 128)
    s_C = cat("s_C", 1); s_n = cat("s_n", 1); s_m = cat("s_m", 1)
    outs = (y_prompt, y_sample, p_k, p_v, p_conv, p_ssm, p_C, p_n, p_m, s_k, s_v, s_conv, s_ssm, s_C, s_n, s_m)
    return tuple(np.ascontiguousarray(o, dtype=np.float32) for o in outs)
```

```python
import numpy as np
import concourse.bass as bass
import concourse.mybir as mybir
from concourse.bass_utils import run_bass_kernel_spmd
from contextlib import ExitStack

F32 = mybir.dt.float32
BF16 = mybir.dt.bfloat16
AF = mybir.ActivationFunctionType
ALU = mybir.AluOpType
AX = mybir.AxisListType

NCORES = 4
D = 2048
SEQ = 2048
ST = 256
NT = ST // 128
NST = SEQ // ST
RS = 32
L = 2
INW = 5400
O_Q, O_K, O_V, O_Z, O_XBC, O_DT, O_MQ, O_MK, O_MV, O_MO, O_MI, O_MF = (
    0, 512, 640, 768, 1792, 3328, 3344, 3856, 4368, 4880, 5392, 5396)
EPS = 1e-6
PAST = 8192
WB = 256
SEM_LIMIT = 8000


class Buf:
    def __init__(self, name, t=None, parent=None):
        self.name = name
        self.t = t
        self.parent = parent
        self.w = None
        self.r = {}

    def root(self):
        return self.parent.root() if self.parent is not None else self

    def __getitem__(self, idx):
        return self.t[idx]


class Eng:
    def __init__(self, fw, name, h):
        self.fw = fw
        self.name = name
        self.h = h
        self.sem = fw.new_sem(name)
        self.own = {id(self.sem)}
        self.cnt = 0
        self.seen = {}

    def _wait(self, ev):
        if ev is None:
            return
        sem, val = ev
        key = id(sem)
        if self.name == "pe" and key in self.own:
            return
        if key in self.fw.dma_sems:
            val = self.fw.dma_sems[key]
        if self.seen.get(key, 0) >= val:
            return
        self.h.wait_ge(sem, val)
        self.seen[key] = val


def _rnd(n):
    return 32 if n <= 32 else (64 if n <= 64 else 128)


class PEProxy:
    def __init__(self, fw):
        self.fw = fw
        self.last = None

    def _mode(self, st_ap, kind):
        shp = tuple(st_ap.shape)
        m = 1
        for v in shp[1:]:
            m *= v
        mode = (_rnd(shp[0]), _rnd(m), str(st_ap.dtype), kind)
        pe = self.fw.pe
        tiled = mode[0] < 128 or mode[1] < 128
        if self.last is not None and (mode != self.last or tiled) and pe.cnt > 0:
            pe.h.wait_ge(pe.sem, pe.cnt)
        self.last = mode

    def matmul(self, out, lhsT=None, rhs=None, **kw):
        self._mode(lhsT, "m")
        return self.fw.pe.h.matmul(out, lhsT=lhsT, rhs=rhs, **kw)

    def transpose(self, out=None, in_=None, identity=None):
        self._mode(in_, "t")
        return self.fw.pe.h.transpose(out=out, in_=in_, identity=identity)


class FW:
    def __init__(self, nc):
        self.nc = nc
        self.es = ExitStack()
        self.nsem = 0
        self.dma_sems = {}
        self.dma_sem_objs = {}
        self.qpool = {}
        self.pe = Eng(self, "pe", nc.tensor)
        self.act = Eng(self, "act", nc.scalar)
        self.dve = Eng(self, "dve", nc.vector)
        self.pool = Eng(self, "pool", nc.gpsimd)
        self.sp = Eng(self, "sp", nc.sync)
        self.engs = [self.pe, self.act, self.dve, self.pool, self.sp]
        self.pe_proxy = PEProxy(self)
        self.out_events = []
        self.all_sems_used = []

    def new_sem(self, name):
        self.nsem += 1
        return self.es.enter_context(self.nc.semaphore(f"s_{name}_{self.nsem}"))

    def sb(self, name, shape, dt, es=None):
        t = (es or getattr(self, "es_alloc", None) or self.es).enter_context(self.nc.sbuf_tensor(name, list(shape), dt))
        return Buf(name, t)

    def ps(self, name, shape, dt=F32):
        t = self.es.enter_context(self.nc.psum_tensor(name, list(shape), dt))
        return Buf(name, t)

    def _deps(self, eng, reads, writes):
        reads = [b.root() for b in reads]
        writes = [b.root() for b in writes]
        for b in reads:
            eng._wait(b.w)
        for b in writes:
            eng._wait(b.w)
            for ev in list(b.r.values()):
                eng._wait(ev)

    def op(self, eng, fn, reads=(), writes=()):
        reads = [b.root() for b in reads]
        writes = [b.root() for b in writes]
        self._deps(eng, reads, writes)
        inst = fn(self.pe_proxy if eng is self.pe else eng.h)
        if eng.cnt >= SEM_LIMIT:
            eng.sem = self.new_sem(eng.name)
            eng.own.add(id(eng.sem))
            eng.cnt = 0
        eng.cnt += 1
        inst.then_inc(eng.sem, 1)
        ev = (eng.sem, eng.cnt)
        for b in writes:
            b.w = ev
            b.r = {}
        for b in reads:
            if b not in writes:
                b.r[id(ev[0])] = ev
        return ev

    def dsem(self, name):
        return [None, name]

    def dma(self, q, out, in_, sem=None, reads=(), writes=(), is_out=False):
        pool = self.qpool.setdefault(q.name, {"sems": [None] * (12 if q.name == "sp" else 4), "i": 0})
        slot = pool["i"] % len(pool["sems"])
        pool["i"] += 1
        sem = pool["sems"][slot]
        if sem is not None:
            prev = self.dma_sems.get(id(sem), 0)
            q._wait((sem, prev))
            if prev >= SEM_LIMIT:
                sem = None
        if sem is None:
            sem = self.new_sem("d" + q.name)
            pool["sems"][slot] = sem
        reads = [b.root() for b in reads]
        writes = [b.root() for b in writes]
        self._deps(q, reads, writes)
        inst = q.h.dma_start(out=out, in_=in_)
        k = id(sem)
        self.dma_sems[k] = self.dma_sems.get(k, 0) + 16
        self.dma_sem_objs[k] = sem
        inst.then_inc(sem, 16)
        ev = (sem, self.dma_sems[k])
        for b in writes:
            b.w = ev
            b.r = {}
        for b in reads:
            b.r[id(ev[0])] = ev
        if is_out:
            self.out_events.append(ev)
        return ev

    def barrier(self):
        evs = [(e.sem, e.cnt) for e in self.engs if e.cnt > 0]
        evs += [(self.dma_sem_objs[k], v) for k, v in self.dma_sems.items()]
        for e in [self.pe, self.act, self.dve, self.pool, self.sp]:
            for ev in evs:
                e._wait(ev)

    def finish(self, close=True):
        for k, v in self.dma_sems.items():
            self.sp._wait((self.dma_sem_objs[k], v))
        if close:
            self.es.close()


def bc(ap, shape):
    return ap.broadcast_to(list(shape))


class _Stop(Exception):
    pass


DEBUG_STOP = [None]


def build():
    nc = bass.Bass("TRN2", target_bir_lowering=False)
    fw = FW(nc)
    stopped = False
    try:
        _build(nc, fw)
    except _Stop:
        stopped = True
    fw.finish(close=not stopped)
    return nc


def _build(nc, fw):
    def ck(name):
        if DEBUG_STOP[0] == name:
            raise _Stop()
    PE, ACT, DVE, POOL, SP = fw.pe, fw.act, fw.dve, fw.pool, fw.sp
    dbg_sem = [None]

    def dump(name, buf, ap, shape):
        if DEBUG_STOP[0] is None:
            return
        if dbg_sem[0] is None:
            dbg_sem[0] = fw.dsem("dbg")
        o = nc.dram_tensor("dbg_" + name, list(shape), F32, kind="ExternalOutput").ap()
        fw.dma(POOL, o, ap, dbg_sem[0], reads=[buf], is_out=True)

    def din(name, shape):
        return nc.dram_tensor(name, list(shape), F32, kind="ExternalInput").ap()

    def dout(name, shape):
        return nc.dram_tensor(name, list(shape), F32, kind="ExternalOutput").ap()

    xp = din("xp", [SEQ, D]); xsm = din("xsm", [RS, D])
    cache_k = din("cache_k", [L, RS, 128, 128]); cache_v = din("cache_v", [L, RS, 128, 128])
    st_conv = din("st_conv", [L, RS, 3, 1536]); st_ssm = din("st_ssm", [L, RS, 1024, 128])
    st_C = din("st_C", [L, RS, 4, 128, 128]); st_n = din("st_n", [L, RS, 4, 128]); st_m = din("st_m", [L, RS, 4])
    w_norm_mix = din("w_norm_mix", [L, D]); w_in = din("w_in", [L, D, INW]); sinks = din("sinks", [L, 8])
    conv_w = din("conv_w", [L, 4, 1536]); conv_b = din("conv_b", [L, 1536]); dt_bias = din("dt_bias", [L, 16])
    a_log = din("a_log", [L, 16]); d_skip = din("d_skip", [L, 16]); w_norm_ssm = din("w_norm_ssm", [L, 1024])
    igb = din("igb", [L, 4]); fgb = din("fgb", [L, 4]); w_norm_ml = din("w_norm_ml", [L, 512])
    w_out = din("w_out", [L, D, D]); w_norm_mlp = din("w_norm_mlp", [L, D]); w_up = din("w_up", [L, D, 4 * D])
    w_down = din("w_down", [L, 4 * D, D]); w_norm_final = din("w_norm_final", [D])
    c_ident = din("c_ident", [128, 128]); c_tri = din("c_tri", [128, 128]); c_triT = din("c_triT", [128, 128])
    c_U = din("c_U", [128, 128]); c_negqk = din("c_negqk", [128, 128]); c_negkq = din("c_negkq", [128, 128])
    c_e127 = din("c_e127", [128, 128]); c_cos = din("c_cos", [128, 17, 8]); c_sin = din("c_sin", [128, 17, 8])
    c_oh = din("c_oh", [RS, RS, 128]); c_pad = din("c_pad", [128, 8])

    y_p = dout("y_p", [SEQ, D]); y_s = dout("y_s", [RS, D])
    p_k = dout("p_k", [L, 128, 128]); p_v = dout("p_v", [L, 128, 128]); p_conv = dout("p_conv", [L, 3, 1536])
    p_ssm = dout("p_ssm", [L, 1024, 128]); p_C = dout("p_C", [L, 4, 128, 128]); p_n = dout("p_n", [L, 4, 128])
    p_m = dout("p_m", [L, 4])
    s_k = dout("s_k", [L, RS, 128, 128]); s_v = dout("s_v", [L, RS, 128, 128]); s_conv = dout("s_conv", [L, RS, 3, 1536])
    s_ssm = dout("s_ssm", [L, RS, 1024, 128]); s_C = dout("s_C", [L, RS, 4, 128, 128]); s_n = dout("s_n", [L, RS, 4, 128])
    s_m = dout("s_m", [L, RS, 4])

    sem_c = fw.dsem("dc")
    sem_o = fw.dsem("do")
    sem_x = fw.dsem("dx")
    sem_g = fw.dsem("dg")
    sem_st = fw.dsem("dst")

    pbig = fw.ps("pbig", [128, 2048], F32)
    pd = [fw.ps(f"pd{i}", [128, 512], F32) for i in range(2)]
    ptb = fw.ps("ptb", [128, 1024], BF16)
    pm = fw.ps("pm", [128, 512], F32)
    pq = [pbig]

    def cload(name, src, shape, dt=F32, cast=None):
        b = fw.sb(name, shape, F32)
        fw.dma(SP, b[:], src, sem_c, writes=[b])
        if cast is not None:
            b2 = fw.sb(name + "_b", shape, cast)
            fw.op(DVE, lambda h: h.tensor_copy(out=b2[:], in_=b[:]), reads=[b], writes=[b2])
            return b, b2
        return b

    ident_f, ident_b = cload("ident", c_ident, [128, 128], cast=BF16)
    tri_f, tri_b = cload("tri", c_tri, [128, 128], cast=BF16)
    triT_f, triT_b = cload("triT", c_triT, [128, 128], cast=BF16)
    U_f = cload("U", c_U, [128, 128])
    negqk = cload("negqk", c_negqk, [128, 128])
    negkq = cload("negkq", c_negkq, [128, 128])
    e127 = cload("e127", c_e127, [128, 128])
    ones_f = fw.sb("ones_f", [128, 128], F32)
    fw.op(DVE, lambda h: h.memset(ones_f[:], 1.0), writes=[ones_f])
    cosT = cload("cosT", c_cos, [128, 17, 8]); sinT = cload("sinT", c_sin, [128, 17, 8])
    padt = cload("padt", c_pad, [128, 8])

    gbc = fw.sb("gbc", [128, 2048], F32)
    lp = {}

    def bload(name, src_row, n):
        b = fw.sb(name, [128, n], F32)
        fw.dma(SP, b[:], src_row.partition_broadcast(128), sem_c, writes=[b])
        return b

    for l in range(L):
        d = {}
        d["dtb"] = bload(f"dtb{l}", dt_bias[l, :], 16)
        al = bload(f"al{l}", a_log[l, :], 16)
        A = fw.sb(f"A{l}", [128, 16], F32)
        fw.op(ACT, lambda h: h.activation(out=A[:], in_=al[:], func=AF.Exp), reads=[al], writes=[A])
        fw.op(DVE, lambda h: h.tensor_scalar(out=A[:], in0=A[:], scalar1=-1.0, scalar2=None, op0=ALU.mult), reads=[A], writes=[A])
        d["A"] = A
        dsk = bload(f"dsk{l}", d_skip[l, :], 16)
        d["dsk"] = dsk
        sk = bload(f"sk{l}", sinks[l, :], 8)
        esk = fw.sb(f"esk{l}", [128, 8], F32)
        fw.op(ACT, lambda h: h.activation(out=esk[:], in_=sk[:], func=AF.Exp), reads=[sk], writes=[esk])
        d["esk"] = esk
        gb = fw.sb(f"gb{l}", [128, 8], F32)
        fw.dma(SP, gb[:, 0:4], igb[l, :].partition_broadcast(128), sem_c, writes=[gb])
        fw.dma(SP, gb[:, 4:8], fgb[l, :].partition_broadcast(128), sem_c, writes=[gb])
        d["gb"] = gb
        cw = fw.sb(f"cw{l}", [128, 12, 4], F32)
        cb = fw.sb(f"cb{l}", [128, 12], F32)
        with nc.allow_non_contiguous_dma(reason="small conv params"):
            for j in range(4):
                fw.dma(SP, cw[:, :, j], conv_w[l, j, :].rearrange("(b p) -> p b", p=128), sem_c, writes=[cw])
            fw.dma(SP, cb[:], conv_b[l, :].rearrange("(b p) -> p b", p=128), sem_c, writes=[cb])
        d["cw"] = cw; d["cb"] = cb
        lp[l] = d

    class S:
        pass
    small = fw.sb("small", [128, 64], F32)
    rtmp = fw.sb("rtmp", [128, 10, 16], F32)
    NWB_ = 4
    wbuf = [fw.sb(f"wb{i}", [128, 16, WB], BF16) for i in range(NWB_)]
    pes = ExitStack()
    _es_orig = fw.es
    carry = []
    fw.es_alloc = pes
    for l in range(L):
        dD = fw.sb(f"dD{l}", [128, 16, 128], BF16)
        dsk = lp[l]["dsk"]
        fw.op(DVE, lambda h: h.tensor_tensor(out=dD[:], in0=bc(ident_f[:].unsqueeze(1), [128, 16, 128]),
                                             in1=bc(dsk[:].unsqueeze(2), [128, 16, 128]), op=ALU.mult),
              reads=[ident_f, dsk], writes=[dD])
        lp[l]["dD"] = dD
    for l in range(L):
        s = S()
        s.ST = fw.sb(f"ST{l}", [128, 1024], F32)
        s.STb = fw.sb(f"STb{l}", [128, 1024], BF16)
        s.Cn = fw.sb(f"Cn{l}", [128, 4, 129], F32)
        s.Cnb = fw.sb(f"Cnb{l}", [128, 4, 129], BF16)
        s.mrow = fw.sb(f"mrow{l}", [128, 4], F32)
        s.kprev = fw.sb(f"kprev{l}", [128, 2, 128], BF16)
        s.vprev = fw.sb(f"vprev{l}", [128, 2, 65], BF16)
        s.xcarry = fw.sb(f"xcar{l}", [128, 12, 3], F32)
        carry.append(s)

    NWB = 4
    wsem = [fw.dsem(f"w{i}") for i in range(NWB)]
    wctr = [0]

    NBLK = 2 * (3 + 4 + 1 + 2 + 2 + 2 + 1 + 2 + 6 + 8 + 4 * 16)
    wcache_t = nc.dram_tensor("wcache", [NBLK, 128, 16, WB], BF16, kind="Internal").ap()
    wcache = Buf("wcache")
    wpass = [0, 0]

    def new_pass():
        assert wpass[0] == 0 or wpass[1] == NBLK, wpass
        wpass[0] += 1
        wpass[1] = 0

    def wload(src2d, ncols):
        i = wctr[0] % NWB
        wctr[0] += 1
        b = wbuf[i]
        blk = wpass[1]
        wpass[1] += 1
        if wpass[0] == 1:
            fw.dma(POOL, b[:, :, 0:ncols], src2d.rearrange("(k p) n -> p k n", p=128), wsem[i], writes=[b])
            fw.dma(SP, wcache_t[blk, :, :, 0:ncols], b[:, :, 0:ncols], None, reads=[b], writes=[wcache])
        else:
            fw.dma(POOL, b[:, :, 0:ncols], wcache_t[blk, :, :, 0:ncols], wsem[i], reads=[wcache], writes=[b])
        return b

    pdc = [0]

    def next_pd():
        p = pd[pdc[0] % 2]
        pdc[0] += 1
        return p

    def dense_tok(actT, tts, W2d, ncols, consume):
        for c0 in range(0, ncols, WB):
            nb = min(WB, ncols - c0)
            wb = wload(W2d[:, c0:c0 + nb], nb)
            for ti, (t0, tsz) in enumerate(tts):
                p = next_pd()
                for k in range(16):
                    fw.op(PE, lambda h: h.matmul(p[0:tsz, 0:nb], lhsT=actT[:, k, t0:t0 + tsz], rhs=wb[:, k, 0:nb],
                                                 start=(k == 0), stop=(k == 15)), reads=[actT, wb], writes=[p])
                consume(ti, c0, nb, p)

    def dense_feat(actT, ntok, W2d, ncols, consume):
        for c0 in range(0, ncols, WB):
            nb = min(WB, ncols - c0)
            wb = wload(W2d[:, c0:c0 + nb], nb)
            for s0 in range(0, nb, 128):
                p = next_pd()
                for k in range(16):
                    fw.op(PE, lambda h: h.matmul(p[:, 0:ntok], lhsT=wb[:, k, s0:s0 + 128], rhs=actT[:, k, 0:ntok],
                                                 start=(k == 0), stop=(k == 15)), reads=[actT, wb], writes=[p])
                consume((c0 + s0) // 128, p)


    def rmsnorm_to(xt_ap, xbuf, np_, gain_ap, out_ap, outbuf, tmpbuf, nfeat, col=0):
        ss = small[0:np_, col:col + 1]
        fw.op(ACT, lambda h: h.activation(out=tmpbuf[0:np_, 0:nfeat], in_=xt_ap, func=AF.Square, accum_out=ss),
              reads=[xbuf], writes=[tmpbuf, small])
        fw.op(DVE, lambda h: h.tensor_scalar(out=ss, in0=ss, scalar1=1.0 / nfeat, scalar2=EPS, op0=ALU.mult, op1=ALU.add),
              reads=[small], writes=[small])
        fw.op(ACT, lambda h: h.activation(out=ss, in_=ss, func=AF.Sqrt), reads=[small], writes=[small])
        fw.op(DVE, lambda h: h.reciprocal(out=ss, in_=ss), reads=[small], writes=[small])
        fw.op(DVE, lambda h: h.scalar_tensor_tensor(out=out_ap, in0=xt_ap, scalar=ss, in1=gain_ap, op0=ALU.mult, op1=ALU.mult),
              reads=[xbuf, small, gbc], writes=[outbuf])

    def transpose_to(src_ap, srcbuf, np_, ncols, dst_fn, dstbuf):
        nblk = ncols // 128
        for g0 in range(0, nblk, 8):
            g1 = min(nblk, g0 + 8)
            for j in range(g0, g1):
                fw.op(PE, lambda h: h.transpose(out=ptb[:, (j - g0) * 128:(j - g0) * 128 + np_],
                                                in_=src_ap[:, j * 128:(j + 1) * 128], identity=ident_b[0:np_, 0:np_]),
                      reads=[srcbuf, ident_b], writes=[ptb])
            for j in range(g0, g1):
                fw.op(ACT, lambda h: h.copy(out=dst_fn(j), in_=ptb[:, (j - g0) * 128:(j - g0) * 128 + np_]),
                      reads=[ptb], writes=[dstbuf])

    cs = ExitStack()
    o_f = fw.sb("o_f", [128, 8, 65], F32)
    pT = fw.sb("pT", [128, 2, 512], BF16)
    rden = fw.sb("rden", [128, 8], F32)

    def swa_block(l, qT, qTbuf, kcur, kcurbuf, vcur, vcurbuf, kprev, kprevbuf, vprev, vprevbuf, has_prev, out_ap, outbuf):
        for kv in range(2):
            blocks = ([("p", kprev, kprevbuf, vprev, vprevbuf, triT_b)] if has_prev else []) + [("c", kcur, kcurbuf, vcur, vcurbuf, tri_b)]
            for bi, (nm, kf, kb, vf, vb, msk) in enumerate(blocks):
                for hh in range(4):
                    h_ = kv * 4 + hh
                    half = (h_ % 2) * 64
                    fw.op(PE, lambda h: h.matmul(pbig[:, (bi * 4 + hh) * 128:(bi * 4 + hh + 1) * 128],
                                                 lhsT=kf(kv)[half:half + 64, :], rhs=qT(h_ // 2)[half:half + 64, :],
                                                 start=True, stop=True), reads=[kb, qTbuf], writes=[pbig])
                fw.op(ACT, lambda h: h.activation(out=pT[:, bi, :], in_=pbig[:, bi * 512:(bi + 1) * 512], func=AF.Exp, scale=0.125),
                      reads=[pbig], writes=[pT])
                fw.op(DVE, lambda h: h.tensor_tensor(out=pT[:, bi, :].rearrange("p (a b) -> p a b", a=4),
                                                     in0=pT[:, bi, :].rearrange("p (a b) -> p a b", a=4),
                                                     in1=bc(msk[:].unsqueeze(1), [128, 4, 128]), op=ALU.mult),
                      reads=[pT, msk], writes=[pT])
            for hh in range(4):
                for bi, (nm, kf, kb, vf, vb, msk) in enumerate(blocks):
                    fw.op(PE, lambda h: h.matmul(pm[:, hh * 65:(hh + 1) * 65], lhsT=pT[:, bi, hh * 128:(hh + 1) * 128], rhs=vf(kv),
                                                 start=(bi == 0), stop=(bi == len(blocks) - 1)), reads=[pT, vb], writes=[pm])
            fw.op(ACT, lambda h: h.copy(out=o_f[:, kv * 4:(kv + 1) * 4, :], in_=pm[:, 0:260].rearrange("p (a b) -> p a b", a=4)),
                  reads=[pm], writes=[o_f])
        esk = lp[l]["esk"]
        fw.op(DVE, lambda h: h.tensor_tensor(out=rden[:], in0=o_f[:, :, 64], in1=esk[:], op=ALU.add), reads=[o_f, esk], writes=[rden])
        fw.op(DVE, lambda h: h.reciprocal(out=rden[:], in_=rden[:]), reads=[rden], writes=[rden])
        fw.op(DVE, lambda h: h.tensor_tensor(out=out_ap.rearrange("p (a b) -> p a b", a=8), in0=o_f[:, :, 0:64],
                                             in1=bc(rden[:].unsqueeze(2), [128, 8, 64]), op=ALU.mult),
              reads=[o_f, rden], writes=[outbuf])

    dtA = fw.sb("dtA", [128, 16], F32)
    a_sb = fw.sb("a_sb", [128, 16], F32)
    ea = fw.sb("ea", [128, 16], F32)
    eal = fw.sb("eal", [128, 16], F32)
    wk = fw.sb("wk", [128, 16], F32)
    rseg = fw.sb("rseg", [128, 4, 128], F32)
    LT = fw.sb("LT", [128, 16, 128], BF16)
    cbm = fw.sb("cbm", [128, 2, 128], BF16)
    x_dt = fw.sb("x_dt", [128, 1024], BF16)
    xw = fw.sb("xw", [128, 1024], BF16)
    ytmp = fw.sb("ytmp", [128, 1024], F32)
    yy = fw.sb("yy", [128, 1024], F32)
    zs = ytmp

    def ssd_chunk(l, st, x_tok, x_tokbuf, B_tok, B_tokbuf, BT, CT, BCbuf, dt, dtbuf, z_tok, zbuf, out_ap, outbuf):
        A = lp[l]["A"]; dD = lp[l]["dD"]
        fw.op(DVE, lambda h: h.tensor_tensor(out=dtA[:], in0=dt, in1=A[:], op=ALU.mult), reads=[dtbuf, A], writes=[dtA])
        fw.op(PE, lambda h: h.matmul(pm[:, 0:16], lhsT=tri_f[:], rhs=dtA[:], start=True, stop=True), reads=[tri_f, dtA], writes=[pm])
        fw.op(PE, lambda h: h.matmul(pm[:, 16:32], lhsT=ones_f[:], rhs=dtA[:], start=True, stop=True), reads=[ones_f, dtA], writes=[pm])
        fw.op(ACT, lambda h: h.copy(out=a_sb[:], in_=pm[:, 0:16]), reads=[pm], writes=[a_sb])
        fw.op(ACT, lambda h: h.activation(out=ea[:], in_=pm[:, 0:16], func=AF.Exp), reads=[pm], writes=[ea])
        fw.op(ACT, lambda h: h.activation(out=eal[:], in_=pm[:, 16:32], func=AF.Exp), reads=[pm], writes=[eal])
        fw.op(DVE, lambda h: h.tensor_tensor(out=wk[:], in0=pm[:, 16:32], in1=a_sb[:], op=ALU.subtract), reads=[pm, a_sb], writes=[wk])
        fw.op(ACT, lambda h: h.activation(out=wk[:], in_=wk[:], func=AF.Exp), reads=[wk], writes=[wk])
        fw.op(DVE, lambda h: h.tensor_tensor(out=wk[:], in0=wk[:], in1=dt, op=ALU.mult), reads=[wk, dtbuf], writes=[wk])
        for i in range(4):
            fw.op(DVE, lambda h: h.tensor_tensor(out=rseg[:], in0=bc(tri_f[:].unsqueeze(1), [128, 4, 128]),
                                                 in1=bc(dtA[:, i * 4:(i + 1) * 4].unsqueeze(2), [128, 4, 128]), op=ALU.mult),
                  reads=[tri_f, dtA], writes=[rseg])
            fw.op(PE, lambda h: h.matmul(pbig[:, i * 512:(i + 1) * 512], lhsT=U_f[:],
                                         rhs=rseg[:].rearrange("p a b -> p (a b)"), start=True, stop=True),
                  reads=[U_f, rseg], writes=[pbig])
        fw.op(ACT, lambda h: h.activation(out=LT[:].rearrange("p a b -> p (a b)"), in_=pbig[:], func=AF.Exp), reads=[pbig], writes=[LT])
        for g in range(2):
            fw.op(PE, lambda h: h.matmul(pm[:, 64 + g * 128:64 + (g + 1) * 128], lhsT=BT(g), rhs=CT(g), start=True, stop=True),
                  reads=[BCbuf], writes=[pm])
        fw.op(DVE, lambda h: h.tensor_tensor(out=cbm[:], in0=pm[:, 64:320].rearrange("p (a b) -> p a b", a=2),
                                             in1=bc(tri_f[:].unsqueeze(1), [128, 2, 128]), op=ALU.mult),
              reads=[pm, tri_f], writes=[cbm])
        for g in range(2):
            fw.op(DVE, lambda h: h.tensor_tensor(out=LT[:, g * 8:(g + 1) * 8, :], in0=LT[:, g * 8:(g + 1) * 8, :],
                                                 in1=bc(cbm[:, g:g + 1, :], [128, 8, 128]), op=ALU.mult),
                  reads=[LT, cbm], writes=[LT])
        fw.op(DVE, lambda h: h.tensor_tensor(out=x_dt[:].rearrange("p (a b) -> p a b", a=16), in0=x_tok.rearrange("p (a b) -> p a b", a=16),
                                             in1=bc(dt.unsqueeze(2), [128, 16, 64]), op=ALU.mult),
              reads=[x_tokbuf, dtbuf], writes=[x_dt])
        fw.op(DVE, lambda h: h.tensor_tensor(out=xw[:].rearrange("p (a b) -> p a b", a=16), in0=x_tok.rearrange("p (a b) -> p a b", a=16),
                                             in1=bc(wk[:].unsqueeze(2), [128, 16, 64]), op=ALU.mult),
              reads=[x_tokbuf, wk], writes=[xw])
        for g in range(2):
            fw.op(PE, lambda h: h.matmul(pbig[:, g * 512:(g + 1) * 512], lhsT=CT(g), rhs=st.STb[:, g * 512:(g + 1) * 512], start=True, stop=True),
                  reads=[BCbuf, st.STb], writes=[pbig])
        for hh in range(16):
            fw.op(PE, lambda h: h.matmul(pbig[:, 1024 + hh * 64:1024 + (hh + 1) * 64], lhsT=LT[:, hh, :], rhs=x_dt[:, hh * 64:(hh + 1) * 64],
                                         start=True, stop=False), reads=[LT, x_dt], writes=[pbig])
            fw.op(PE, lambda h: h.matmul(pbig[:, 1024 + hh * 64:1024 + (hh + 1) * 64], lhsT=dD[:, hh, :], rhs=x_tok[:, hh * 64:(hh + 1) * 64],
                                         start=False, stop=True), reads=[dD, x_tokbuf], writes=[pbig])
        fw.op(DVE, lambda h: h.tensor_tensor(out=ytmp[:].rearrange("p (a b) -> p a b", a=16), in0=pbig[:, 0:1024].rearrange("p (a b) -> p a b", a=16),
                                             in1=bc(ea[:].unsqueeze(2), [128, 16, 64]), op=ALU.mult),
              reads=[pbig, ea], writes=[ytmp])
        fw.op(DVE, lambda h: h.tensor_tensor(out=yy[:], in0=pbig[:, 1024:2048], in1=ytmp[:], op=ALU.add), reads=[pbig, ytmp], writes=[yy])
        for g in range(2):
            fw.op(PE, lambda h: h.matmul(pd[g][:, :], lhsT=B_tok[:, g * 128:(g + 1) * 128], rhs=xw[:, g * 512:(g + 1) * 512], start=True, stop=True),
                  reads=[B_tokbuf, xw], writes=[pd[g]])
        fw.op(DVE, lambda h: h.tensor_tensor(out=st.ST[:].rearrange("p (a b) -> p a b", a=16), in0=st.ST[:].rearrange("p (a b) -> p a b", a=16),
                                             in1=bc(eal[:].unsqueeze(2), [128, 16, 64]), op=ALU.mult), reads=[st.ST, eal], writes=[st.ST])
        for g in range(2):
            fw.op(DVE, lambda h: h.tensor_tensor(out=st.ST[:, g * 512:(g + 1) * 512], in0=pd[g][:, :], in1=st.ST[:, g * 512:(g + 1) * 512], op=ALU.add),
                  reads=[pd[g], st.ST], writes=[st.ST])
        fw.op(ACT, lambda h: h.copy(out=st.STb[:], in_=st.ST[:]), reads=[st.ST], writes=[st.STb])
        fw.op(ACT, lambda h: h.activation(out=zs[:], in_=z_tok, func=AF.Silu), reads=[zbuf], writes=[zs])
        fw.op(DVE, lambda h: h.tensor_tensor(out=yy[:], in0=yy[:], in1=zs[:], op=ALU.mult), reads=[yy, zs], writes=[yy])
        for g in range(2):
            rmsnorm_to(yy[:, g * 512:(g + 1) * 512], yy, 128, gbc[:, g * 512:(g + 1) * 512], out_ap[:, g * 512:(g + 1) * 512], outbuf, ytmp, 512, col=8 + g)

    lfn = fw.sb("lfn", [128, 4], F32)
    nb_ = fw.sb("nb_", [128, 4], F32)
    nbl = fw.sb("nbl", [128, 4], F32)
    cc = fw.sb("cc", [128, 4], F32)
    Dc = fw.sb("Dc", [128, 4, 128], F32)
    cmx = fw.sb("cmx", [128, 4, 128], F32)
    cm = fw.sb("cm", [128, 4], F32)
    Mq = fw.sb("Mq", [128, 4], F32)
    negM = fw.sb("negM", [128, 4], F32)
    mt = fw.sb("mt", [128, 4], F32)
    gq = fw.sb("gq", [128, 4], F32)
    emt = fw.sb("emt", [128, 4], F32)
    mnew = fw.sb("mnew", [128, 4], F32)
    gend = fw.sb("gend", [128, 4], F32)
    wkm = fw.sb("wkm", [128, 4], F32)
    swe = fw.sb("swe", [128, 4, 128], F32)
    swT = fw.sb("swT", [128, 4, 128], BF16)
    tot = fw.sb("tot", [128, 4, 129], F32)
    ints = fw.sb("ints", [128, 4, 129], F32)
    hh_ = fw.sb("hh_", [128, 4, 128], F32)
    kwm = fw.sb("kwm", [128, 512], BF16)
    sg = fw.sb("sg", [128, 512], F32)
    negkq4 = fw.sb("negkq4", [128, 4, 128], F32)
    fw.op(DVE, lambda h: h.tensor_copy(out=negkq4[:], in_=bc(negkq[:].unsqueeze(1), [128, 4, 128])), reads=[negkq], writes=[negkq4])

    def mlstm_chunk(l, st, qT, kT, qkbuf, k_tok, v_aug, kvbuf, ig, fg, gbuf, mo, mobuf, out_ap, outbuf):
        fw.op(ACT, lambda h: h.activation(out=lfn[:], in_=fg, func=AF.Exp, scale=-1.0), reads=[gbuf], writes=[lfn])
        fw.op(ACT, lambda h: h.activation(out=lfn[:], in_=lfn[:], func=AF.Ln, bias=1.0), reads=[lfn], writes=[lfn])
        fw.op(PE, lambda h: h.matmul(pm[:, 0:4], lhsT=tri_f[:], rhs=lfn[:], start=True, stop=True), reads=[tri_f, lfn], writes=[pm])
        fw.op(PE, lambda h: h.matmul(pm[:, 4:8], lhsT=ones_f[:], rhs=lfn[:], start=True, stop=True), reads=[ones_f, lfn], writes=[pm])
        fw.op(ACT, lambda h: h.copy(out=nb_[:], in_=pm[:, 0:4]), reads=[pm], writes=[nb_])
        fw.op(ACT, lambda h: h.copy(out=nbl[:], in_=pm[:, 4:8]), reads=[pm], writes=[nbl])
        fw.op(DVE, lambda h: h.tensor_tensor(out=cc[:], in0=ig, in1=nb_[:], op=ALU.add), reads=[gbuf, nb_], writes=[cc])
        fw.op(DVE, lambda h: h.tensor_tensor(out=Dc[:], in0=bc(ident_f[:].unsqueeze(1), [128, 4, 128]), in1=bc(cc[:].unsqueeze(2), [128, 4, 128]), op=ALU.mult),
              reads=[ident_f, cc], writes=[Dc])
        fw.op(PE, lambda h: h.matmul(pd[0][:, :], lhsT=ones_f[:], rhs=Dc[:].rearrange("p a b -> p (a b)"), start=True, stop=True),
              reads=[ones_f, Dc], writes=[pd[0]])
        fw.op(DVE, lambda h: h.tensor_tensor(out=cmx[:], in0=pd[0][:, :].rearrange("p (a b) -> p a b", a=4), in1=bc(negqk[:].unsqueeze(1), [128, 4, 128]), op=ALU.add),
              reads=[pd[0], negqk], writes=[cmx])
        fw.op(DVE, lambda h: h.tensor_reduce(out=cm[:], in_=cmx[:], op=ALU.max, axis=AX.X), reads=[cmx], writes=[cm])
        fw.op(DVE, lambda h: h.tensor_tensor(out=Mq[:], in0=cm[:], in1=st.mrow[:], op=ALU.max), reads=[cm, st.mrow], writes=[Mq])
        fw.op(DVE, lambda h: h.tensor_tensor(out=mt[:], in0=Mq[:], in1=nb_[:], op=ALU.subtract), reads=[Mq, nb_], writes=[mt])
        fw.op(DVE, lambda h: h.tensor_scalar(out=negM[:], in0=Mq[:], scalar1=-1.0, scalar2=None, op0=ALU.mult), reads=[Mq], writes=[negM])
        fw.op(DVE, lambda h: h.tensor_tensor(out=Dc[:], in0=bc(ident_f[:].unsqueeze(1), [128, 4, 128]), in1=bc(negM[:].unsqueeze(2), [128, 4, 128]), op=ALU.mult),
              reads=[ident_f, negM], writes=[Dc])
        fw.op(PE, lambda h: h.matmul(pd[1][:, :], lhsT=ones_f[:], rhs=Dc[:].rearrange("p a b -> p (a b)"), start=True, stop=False),
              reads=[ones_f, Dc], writes=[pd[1]])
        fw.op(PE, lambda h: h.matmul(pd[1][:, :], lhsT=ident_f[:], rhs=negkq4[:].rearrange("p a b -> p (a b)"), start=False, stop=True),
              reads=[ident_f, negkq4], writes=[pd[1]])
        for hd in range(4):
            fw.op(ACT, lambda h: h.activation(out=swe[:, hd, :], in_=pd[1][:, hd * 128:(hd + 1) * 128], func=AF.Exp, bias=cc[:, hd:hd + 1]),
                  reads=[pd[1], cc], writes=[swe])
        for hd in range(4):
            fw.op(PE, lambda h: h.matmul(pd[0][:, hd * 128:(hd + 1) * 128], lhsT=kT(hd), rhs=qT(hd), start=True, stop=True), reads=qkbuf, writes=[pd[0]])
        fw.op(DVE, lambda h: h.tensor_tensor(out=swT[:], in0=pd[0][:, :].rearrange("p (a b) -> p a b", a=4), in1=swe[:], op=ALU.mult),
              reads=[pd[0], swe], writes=[swT])
        for hd in range(4):
            o = (hd // 2) * 512 + (hd % 2) * 129
            fw.op(PE, lambda h: h.matmul(pbig[:, o:o + 129], lhsT=swT[:, hd, :], rhs=v_aug(hd), start=True, stop=True), reads=[swT] + kvbuf, writes=[pbig])
            fw.op(PE, lambda h: h.matmul(pbig[:, 1024 + o:1024 + o + 129], lhsT=qT(hd), rhs=st.Cnb[:, hd, :], start=True, stop=True), reads=qkbuf + [st.Cnb], writes=[pbig])
        fw.op(DVE, lambda h: h.tensor_tensor(out=gq[:], in0=st.mrow[:], in1=Mq[:], op=ALU.subtract), reads=[st.mrow, Mq], writes=[gq])
        fw.op(ACT, lambda h: h.activation(out=gq[:], in_=gq[:], func=AF.Exp), reads=[gq], writes=[gq])
        fw.op(ACT, lambda h: h.activation(out=emt[:], in_=mt[:], func=AF.Exp, scale=-1.0), reads=[mt], writes=[emt])
        for hf in range(2):
            fw.op(DVE, lambda h: h.tensor_tensor(out=ints[:, hf * 2:hf * 2 + 2, :], in0=pbig[:, 1024 + hf * 512:1024 + hf * 512 + 258].rearrange("p (a b) -> p a b", a=2),
                                                 in1=bc(gq[:, hf * 2:hf * 2 + 2].unsqueeze(2), [128, 2, 129]), op=ALU.mult), reads=[pbig, gq], writes=[ints])
            fw.op(DVE, lambda h: h.tensor_tensor(out=tot[:, hf * 2:hf * 2 + 2, :], in0=pbig[:, hf * 512:hf * 512 + 258].rearrange("p (a b) -> p a b", a=2),
                                                 in1=ints[:, hf * 2:hf * 2 + 2, :], op=ALU.add), reads=[pbig, ints], writes=[tot])
        fw.op(DVE, lambda h: h.tensor_scalar(out=cm[:], in0=tot[:, :, 128], scalar1=-1.0, scalar2=None, op0=ALU.mult), reads=[tot], writes=[cm])
        fw.op(DVE, lambda h: h.tensor_tensor(out=cm[:], in0=cm[:], in1=tot[:, :, 128], op=ALU.max), reads=[tot, cm], writes=[cm])
        fw.op(DVE, lambda h: h.tensor_tensor(out=cm[:], in0=cm[:], in1=emt[:], op=ALU.max), reads=[cm, emt], writes=[cm])
        fw.op(DVE, lambda h: h.reciprocal(out=cm[:], in_=cm[:]), reads=[cm], writes=[cm])
        fw.op(DVE, lambda h: h.tensor_tensor(out=hh_[:], in0=tot[:, :, 0:128], in1=bc(cm[:].unsqueeze(2), [128, 4, 128]), op=ALU.mult), reads=[tot, cm], writes=[hh_])
        fw.op(DVE, lambda h: h.tensor_tensor(out=cmx[:], in0=hh_[:], in1=hh_[:], op=ALU.mult), reads=[hh_], writes=[cmx])
        fw.op(DVE, lambda h: h.tensor_reduce(out=cm[:], in_=cmx[:], op=ALU.add, axis=AX.X), reads=[cmx], writes=[cm])
        fw.op(DVE, lambda h: h.tensor_scalar(out=cm[:], in0=cm[:], scalar1=1.0 / 128, scalar2=EPS, op0=ALU.mult, op1=ALU.add), reads=[cm], writes=[cm])
        fw.op(ACT, lambda h: h.activation(out=cm[:], in_=cm[:], func=AF.Sqrt), reads=[cm], writes=[cm])
        fw.op(DVE, lambda h: h.reciprocal(out=cm[:], in_=cm[:]), reads=[cm], writes=[cm])
        fw.op(DVE, lambda h: h.tensor_tensor(out=hh_[:], in0=hh_[:], in1=bc(cm[:].unsqueeze(2), [128, 4, 128]), op=ALU.mult), reads=[hh_, cm], writes=[hh_])
        fw.op(DVE, lambda h: h.tensor_tensor(out=hh_[:].rearrange("p a b -> p (a b)"), in0=hh_[:].rearrange("p a b -> p (a b)"), in1=gbc[:, 1024:1536], op=ALU.mult),
              reads=[hh_, gbc], writes=[hh_])
        fw.op(ACT, lambda h: h.activation(out=sg[:], in_=mo, func=AF.Sigmoid), reads=[mobuf], writes=[sg])
        fw.op(DVE, lambda h: h.tensor_tensor(out=out_ap, in0=hh_[:].rearrange("p a b -> p (a b)"), in1=sg[:], op=ALU.mult), reads=[hh_, sg], writes=[outbuf])
        fw.op(PE, lambda h: h.matmul(pm[:, 8:12], lhsT=e127[:], rhs=mt[:], start=True, stop=True), reads=[e127, mt], writes=[pm])
        fw.op(ACT, lambda h: h.copy(out=mnew[:], in_=pm[:, 8:12]), reads=[pm], writes=[mnew])
        fw.op(DVE, lambda h: h.tensor_tensor(out=wkm[:], in0=cc[:], in1=nbl[:], op=ALU.subtract), reads=[cc, nbl], writes=[wkm])
        fw.op(DVE, lambda h: h.tensor_tensor(out=wkm[:], in0=wkm[:], in1=mnew[:], op=ALU.subtract), reads=[wkm, mnew], writes=[wkm])
        fw.op(ACT, lambda h: h.activation(out=wkm[:], in_=wkm[:], func=AF.Exp), reads=[wkm], writes=[wkm])
        fw.op(DVE, lambda h: h.tensor_tensor(out=gend[:], in0=st.mrow[:], in1=nbl[:], op=ALU.subtract), reads=[st.mrow, nbl], writes=[gend])
        fw.op(DVE, lambda h: h.tensor_tensor(out=gend[:], in0=gend[:], in1=mnew[:], op=ALU.subtract), reads=[gend, mnew], writes=[gend])
        fw.op(ACT, lambda h: h.activation(out=gend[:], in_=gend[:], func=AF.Exp), reads=[gend], writes=[gend])
        fw.op(DVE, lambda h: h.tensor_tensor(out=kwm[:].rearrange("p (a b) -> p a b", a=4), in0=k_tok.rearrange("p (a b) -> p a b", a=4),
                                             in1=bc(wkm[:].unsqueeze(2), [128, 4, 128]), op=ALU.mult), reads=kvbuf + [wkm], writes=[kwm])
        for hd in range(4):
            o = (hd // 2) * 512 + (hd % 2) * 129
            fw.op(PE, lambda h: h.matmul(pbig[:, o:o + 129], lhsT=kwm[:, hd * 128:(hd + 1) * 128], rhs=v_aug(hd), start=True, stop=True), reads=[kwm] + kvbuf, writes=[pbig])
        fw.op(DVE, lambda h: h.tensor_tensor(out=st.Cn[:], in0=st.Cn[:], in1=bc(gend[:].unsqueeze(2), [128, 4, 129]), op=ALU.mult), reads=[st.Cn, gend], writes=[st.Cn])
        for hf in range(2):
            fw.op(DVE, lambda h: h.tensor_tensor(out=st.Cn[:, hf * 2:hf * 2 + 2, :], in0=pbig[:, hf * 512:hf * 512 + 258].rearrange("p (a b) -> p a b", a=2),
                                                 in1=st.Cn[:, hf * 2:hf * 2 + 2, :], op=ALU.add), reads=[pbig, st.Cn], writes=[st.Cn])
        fw.op(ACT, lambda h: h.copy(out=st.Cnb[:], in_=st.Cn[:]), reads=[st.Cn], writes=[st.Cnb])
        fw.op(ACT, lambda h: h.copy(out=st.mrow[:], in_=mnew[:]), reads=[mnew], writes=[st.mrow])


    def rope(buf, ap3, np_, nh, ti):
        x1 = ap3[:, :, 0:8]; x2 = ap3[:, :, 8:16]
        cs_ = bc(cosT[0:np_, ti:ti + 1, :], [np_, nh, 8]); sn_ = bc(sinT[0:np_, ti:ti + 1, :], [np_, nh, 8])
        t = rtmp[0:np_, 0:nh, :]
        fw.op(DVE, lambda h: h.tensor_tensor(out=t[:, :, 0:8], in0=x2, in1=sn_, op=ALU.mult), reads=[buf, sinT], writes=[rtmp])
        fw.op(DVE, lambda h: h.tensor_tensor(out=t[:, :, 8:16], in0=x1, in1=sn_, op=ALU.mult), reads=[buf, sinT], writes=[rtmp])
        fw.op(DVE, lambda h: h.tensor_tensor(out=ap3[:, :, 0:16].rearrange("p a (c d) -> p a c d", c=2),
                                             in0=ap3[:, :, 0:16].rearrange("p a (c d) -> p a c d", c=2),
                                             in1=bc(cosT[0:np_, ti:ti + 1, :].unsqueeze(2), [np_, nh, 2, 8]), op=ALU.mult), reads=[buf, cosT], writes=[buf])
        fw.op(DVE, lambda h: h.tensor_tensor(out=x1, in0=x1, in1=t[:, :, 0:8], op=ALU.subtract), reads=[buf, rtmp], writes=[buf])
        fw.op(DVE, lambda h: h.tensor_tensor(out=x2, in0=x2, in1=t[:, :, 8:16], op=ALU.add), reads=[buf, rtmp], writes=[buf])

    def softplus(buf, ap):
        fw.op(ACT, lambda h: h.activation(out=ap, in_=ap, func=AF.Exp), reads=[buf], writes=[buf])
        fw.op(ACT, lambda h: h.activation(out=ap, in_=ap, func=AF.Ln, bias=1.0), reads=[buf], writes=[buf])

    def load_gain(row_ap, c0, n):
        fw.dma(SP, gbc[:, c0:c0 + n], row_ap.partition_broadcast(128), sem_g, writes=[gbc])


    sst = fw.sb("sst", [128, 8, 128], F32)

    def emit_state_out(l, st, d_ssm, d_C, d_n, d_m):
        for g0 in range(0, 8, 4):
            for j in range(g0, g0 + 4):
                fw.op(PE, lambda h: h.transpose(out=pd[0][:, (j - g0) * 128:(j - g0 + 1) * 128], in_=st.ST[:, j * 128:(j + 1) * 128], identity=ident_f[:]),
                      reads=[st.ST, ident_f], writes=[pd[0]])
            fw.op(ACT, lambda h: h.copy(out=sst[:, g0:g0 + 4, :], in_=pd[0][:, :].rearrange("p (a b) -> p a b", a=4)), reads=[pd[0]], writes=[sst])
        fw.dma(SP, d_ssm.rearrange("(j p) n -> p j n", p=128), sst[:], sem_o, reads=[sst], is_out=True)
        fw.dma(SP, d_C.rearrange("h d e -> d h e"), st.Cn[:, :, 0:128], sem_o, reads=[st.Cn], is_out=True)
        with nc.allow_non_contiguous_dma(reason="tiny state out"):
            fw.dma(SP, d_n.rearrange("h d -> d h"), st.Cn[:, :, 128], sem_o, reads=[st.Cn], is_out=True)
        fw.dma(SP, d_m.unsqueeze(0), st.mrow[0:1, :], sem_o, reads=[st.mrow], is_out=True)

    def load_state(l, st, r):
        fw.dma(SP, sst[:], st_ssm[l, r].rearrange("(j p) n -> p j n", p=128), sem_st, writes=[sst])
        for g0 in range(0, 8, 4):
            for j in range(g0, g0 + 4):
                fw.op(PE, lambda h: h.transpose(out=pd[0][:, (j - g0) * 128:(j - g0 + 1) * 128], in_=sst[:, j, :], identity=ident_f[:]),
                      reads=[sst, ident_f], writes=[pd[0]])
            fw.op(ACT, lambda h: h.copy(out=st.ST[:, g0 * 128:(g0 + 4) * 128], in_=pd[0][:, :]), reads=[pd[0]], writes=[st.ST])
        fw.op(ACT, lambda h: h.copy(out=st.STb[:], in_=st.ST[:]), reads=[st.ST], writes=[st.STb])
        fw.dma(SP, st.Cn[:, :, 0:128], st_C[l, r].rearrange("h d e -> d h e"), sem_st, writes=[st.Cn])
        with nc.allow_non_contiguous_dma(reason="tiny state in"):
            fw.dma(SP, st.Cn[:, :, 128], st_n[l, r].rearrange("h d -> d h"), sem_st, writes=[st.Cn])
        fw.dma(SP, st.mrow[:], st_m[l, r, :].partition_broadcast(128), sem_st, writes=[st.mrow])
        fw.op(ACT, lambda h: h.copy(out=st.Cnb[:], in_=st.Cn[:]), reads=[st.Cn], writes=[st.Cnb])

    xres = fw.sb("xres", [128, NT, D], F32, pes)
    actT = fw.sb("actT", [128, 16, ST], BF16, pes)
    big16 = fw.sb("big16", [128, NT * 2048], BF16, pes)
    utok = fw.sb("utok", [128, D], BF16, pes)
    sq = fw.sb("sq", [128, D], F32, pes)
    qkf = fw.sb("qkf", [128, NT, 640], F32, pes)
    qkb = fw.sb("qkb", [128, 768], BF16, pes)
    qT = fw.sb("qT", [128, 4, ST], BF16, pes)
    kTd = fw.sb("kTd", [128, 2, ST], BF16, pes)
    vaug = fw.sb("vaug", [128, NT, 2, 65], BF16, pes)
    ztok = fw.sb("ztok", [128, NT, 1024], BF16, pes)
    dtt = fw.sb("dtt", [128, NT, 16], F32, pes)
    mktok = fw.sb("mktok", [128, NT, 512], BF16, pes)
    mvaug = fw.sb("mvaug", [128, NT, 4, 129], BF16, pes)
    motok = fw.sb("motok", [128, NT, 512], BF16, pes)
    gates = fw.sb("gates", [128, NT, 8], F32, pes)
    xraw = fw.sb("xraw", [128, ST + 3], F32, pes)
    cacc = fw.sb("cacc", [128, ST], F32, pes)
    xcT = fw.sb("xcT", [128, 12, ST], BF16, pes)
    mqT = fw.sb("mqT", [128, 4, ST], BF16, pes)
    mkT = fw.sb("mkT", [128, 4, ST], BF16, pes)
    xtok = fw.sb("xtok", [128, 1024], BF16, pes)
    btok = fw.sb("btok", [128, 256], BF16, pes)
    ostage = sq

    fw.op(DVE, lambda h: h.memset(vaug[:], 1.0), writes=[vaug])
    fw.op(DVE, lambda h: h.memset(mvaug[:], 1.0), writes=[mvaug])
    for l in range(L):
        s = carry[l]
        fw.op(DVE, lambda h: h.memset(s.ST[:], 0.0), writes=[s.ST])
        fw.op(DVE, lambda h: h.memset(s.STb[:], 0.0), writes=[s.STb])
        fw.op(DVE, lambda h: h.memset(s.Cn[:], 0.0), writes=[s.Cn])
        fw.op(DVE, lambda h: h.memset(s.Cnb[:], 0.0), writes=[s.Cnb])
        fw.op(DVE, lambda h: h.memset(s.mrow[:], 0.0), writes=[s.mrow])
        fw.op(DVE, lambda h: h.memset(s.xcarry[:], 0.0), writes=[s.xcarry])

    mix_tok = big16[:].rearrange("p (a b) -> p a b", a=NT)
    ck("consts")
    hTg = big16[:].rearrange("p (a b) -> p a b", a=16)
    HT = Buf("HT", hTg, parent=big16)

    def norm_to_actT(nt, tsz, gain_row, xfn):
        load_gain(gain_row, 0, D)
        for tt in range(nt):
            rmsnorm_to(xfn(tt), xres, tsz, gbc[0:tsz, :], utok[0:tsz, :], utok, sq, D, col=tt)
            transpose_to(utok[0:tsz, :], utok, tsz, D, lambda j: actT[:, j, tt * 128:tt * 128 + tsz], actT)

    for stn in range(NST):
        new_pass()
        t0g = stn * ST
        for tt in range(NT):
            fw.dma(SP, xres[:, tt, :], xp[t0g + tt * 128:t0g + (tt + 1) * 128, :], sem_x, writes=[xres])
        tts = [(tt * 128, 128) for tt in range(NT)]
        for l in range(L):
            st = carry[l]
            P = lp[l]
            W = w_in[l]
            norm_to_actT(NT, 128, w_norm_mix[l, :], lambda tt: xres[:, tt, :])
            ck("norm")
            load_gain(w_norm_ssm[l, :], 0, 1024)
            load_gain(w_norm_ml[l, :], 1024, 512)

            def c_qkv(ti, c0, nb, p):
                if c0 < 512:
                    fw.op(ACT, lambda h: h.copy(out=qkf[:, ti, c0:c0 + nb], in_=p[:, 0:nb]), reads=[p], writes=[qkf])
                else:
                    fw.op(ACT, lambda h: h.copy(out=qkf[:, ti, 512:640], in_=p[:, 0:128]), reads=[p], writes=[qkf])
                    fw.op(ACT, lambda h: h.copy(out=vaug[:, ti, :, 0:64], in_=p[:, 128:256].rearrange("p (a b) -> p a b", a=2)), reads=[p], writes=[vaug])
                    rope(qkf, qkf[:, ti, :].rearrange("p (a b) -> p a b", b=64), 128, 10, stn * NT + ti)
                    fw.op(DVE, lambda h: h.tensor_copy(out=qkb[:, 0:512], in_=qkf[:, ti, 0:512]), reads=[qkf], writes=[qkb])
                    fw.op(DVE, lambda h: h.tensor_copy(out=qkb[:, 512:768].rearrange("p (a c b) -> p a c b", a=2, c=2),
                                                       in_=bc(qkf[:, ti, 512:640].rearrange("p (a b) -> p a b", a=2).unsqueeze(2), [128, 2, 2, 64])),
                          reads=[qkf], writes=[qkb])
                    transpose_to(qkb[:, 0:512], qkb, 128, 512, lambda j: qT[:, j, ti * 128:(ti + 1) * 128], qT)
                    transpose_to(qkb[:, 512:768], qkb, 128, 256, lambda j: kTd[:, j, ti * 128:(ti + 1) * 128], kTd)
                    if stn == NST - 1 and ti == NT - 1:
                        fw.op(ACT, lambda h: h.copy(out=ostage[:, 0:128], in_=qkf[:, ti, 512:640]), reads=[qkf], writes=[ostage])
                        fw.op(ACT, lambda h: h.copy(out=ostage[:, 128:256], in_=p[:, 128:256]), reads=[p], writes=[ostage])
                        fw.dma(SP, p_k[l], ostage[:, 0:128], sem_o, reads=[ostage], is_out=True)
                        fw.dma(SP, p_v[l], ostage[:, 128:256], sem_o, reads=[ostage], is_out=True)
            dense_tok(actT, tts, W[:, O_Q:O_Q + 768], 768, c_qkv)
            ck("qkv")

            def c_z(ti, c0, nb, p):
                fw.op(ACT, lambda h: h.copy(out=ztok[:, ti, c0:c0 + nb], in_=p[:, 0:nb]), reads=[p], writes=[ztok])
            dense_tok(actT, tts, W[:, O_Z:O_Z + 1024], 1024, c_z)

            def c_dt(ti, c0, nb, p):
                fw.op(DVE, lambda h: h.tensor_tensor(out=dtt[:, ti, :], in0=p[:, 0:16], in1=P["dtb"][:], op=ALU.add), reads=[p, P["dtb"]], writes=[dtt])
                softplus(dtt, dtt[:, ti, :])
            dense_tok(actT, tts, W[:, O_DT:O_DT + 16], 16, c_dt)

            def c_mk(ti, c0, nb, p):
                fw.op(ACT, lambda h: h.activation(out=mktok[:, ti, c0:c0 + nb], in_=p[:, 0:nb], func=AF.Copy, scale=float(128 ** -0.5)), reads=[p], writes=[mktok])
                if c0 + nb == 512:
                    transpose_to(mktok[:, ti, :], mktok, 128, 512, lambda j: mkT[:, j, ti * 128:(ti + 1) * 128], mkT)
            dense_tok(actT, tts, W[:, O_MK:O_MK + 512], 512, c_mk)

            def c_mv(ti, c0, nb, p):
                h0 = c0 // 128
                fw.op(ACT, lambda h: h.copy(out=mvaug[:, ti, h0:h0 + nb // 128, 0:128], in_=p[:, 0:nb].rearrange("p (a b) -> p a b", b=128)), reads=[p], writes=[mvaug])
            dense_tok(actT, tts, W[:, O_MV:O_MV + 512], 512, c_mv)

            def c_mo(ti, c0, nb, p):
                fw.op(ACT, lambda h: h.copy(out=motok[:, ti, c0:c0 + nb], in_=p[:, 0:nb]), reads=[p], writes=[motok])
            dense_tok(actT, tts, W[:, O_MO:O_MO + 512], 512, c_mo)

            def c_g(ti, c0, nb, p):
                fw.op(DVE, lambda h: h.tensor_tensor(out=gates[:, ti, :], in0=p[:, 0:8], in1=P["gb"][:], op=ALU.add), reads=[p, P["gb"]], writes=[gates])
            dense_tok(actT, tts, W[:, O_MI:O_MI + 8], 8, c_g)
            ck("tokproj")

            def c_mq(cb_, p):
                fw.op(ACT, lambda h: h.copy(out=mqT[:, cb_, :], in_=p[:, 0:ST]), reads=[p], writes=[mqT])
            dense_feat(actT, ST, W[:, O_MQ:O_MQ + 512], 512, c_mq)

            def c_xbc(cb_, p):
                fw.op(ACT, lambda h: h.copy(out=xraw[:, 0:3], in_=st.xcarry[:, cb_, :]), reads=[st.xcarry], writes=[xraw])
                fw.op(ACT, lambda h: h.copy(out=xraw[:, 3:ST + 3], in_=p[:, 0:ST]), reads=[p], writes=[xraw])
                fw.op(ACT, lambda h: h.copy(out=st.xcarry[:, cb_, :], in_=xraw[:, ST:ST + 3]), reads=[xraw], writes=[st.xcarry])
                cw = P["cw"]
                fw.op(DVE, lambda h: h.tensor_scalar(out=cacc[:], in0=xraw[:, 0:ST], scalar1=cw[:, cb_, 0:1], scalar2=None, op0=ALU.mult), reads=[xraw, cw], writes=[cacc])
                for j in range(1, 4):
                    fw.op(DVE, lambda h: h.scalar_tensor_tensor(out=cacc[:], in0=xraw[:, j:j + ST], scalar=cw[:, cb_, j:j + 1], in1=cacc[:], op0=ALU.mult, op1=ALU.add),
                          reads=[xraw, cw, cacc], writes=[cacc])
                fw.op(ACT, lambda h: h.activation(out=xcT[:, cb_, :], in_=cacc[:], func=AF.Silu, bias=P["cb"][:, cb_:cb_ + 1]), reads=[cacc, P["cb"]], writes=[xcT])
            dense_feat(actT, ST, W[:, O_XBC:O_XBC + 1536], 1536, c_xbc)
            ck("proj")
            if stn == NST - 1:
                with nc.allow_non_contiguous_dma(reason="tiny conv state out"):
                    for j in range(3):
                        fw.dma(SP, p_conv[l, j, :].rearrange("(b p) -> p b", p=128), st.xcarry[:, :, j], sem_o, reads=[st.xcarry], is_out=True)

            for c in range(NT):
                sl = slice(c * 128, (c + 1) * 128)
                has_prev = not (stn == 0 and c == 0)
                if c == 0:
                    kpf = lambda kv: st.kprev[:, kv, :]; kpb = st.kprev
                    vpf = lambda kv: st.vprev[:, kv, :]; vpb = st.vprev
                else:
                    kpf = (lambda cc_: (lambda kv: kTd[:, kv, (cc_ - 1) * 128:cc_ * 128]))(c); kpb = kTd
                    vpf = (lambda cc_: (lambda kv: vaug[:, cc_ - 1, kv, :]))(c); vpb = vaug
                swa_block(l, lambda j: qT[:, j, sl], qT, lambda kv: kTd[:, kv, sl], kTd, lambda kv: vaug[:, c, kv, :], vaug,
                          kpf, kpb, vpf, vpb, has_prev, mix_tok[:, c, 0:512], big16)
                ck("swa")
                if c == NT - 1:
                    fw.op(ACT, lambda h: h.copy(out=st.kprev[:], in_=kTd[:, :, sl]), reads=[kTd], writes=[st.kprev])
                    fw.op(ACT, lambda h: h.copy(out=st.vprev[:], in_=vaug[:, NT - 1, :, :]), reads=[vaug], writes=[st.vprev])
                for j in range(8):
                    fw.op(PE, lambda h: h.transpose(out=ptb[:, j * 128:(j + 1) * 128], in_=xcT[:, j, sl], identity=ident_b[:]), reads=[xcT, ident_b], writes=[ptb])
                fw.op(ACT, lambda h: h.copy(out=xtok[:], in_=ptb[:, 0:1024]), reads=[ptb], writes=[xtok])
                for j in range(2):
                    fw.op(PE, lambda h: h.transpose(out=ptb[:, j * 128:(j + 1) * 128], in_=xcT[:, 8 + j, sl], identity=ident_b[:]), reads=[xcT, ident_b], writes=[ptb])
                fw.op(ACT, lambda h: h.copy(out=btok[:], in_=ptb[:, 0:256]), reads=[ptb], writes=[btok])
                ssd_chunk(l, st, xtok[:], xtok, btok[:], btok, lambda g: xcT[:, 8 + g, sl], lambda g: xcT[:, 10 + g, sl], xcT,
                          dtt[:, c, :], dtt, ztok[:, c, :], ztok, mix_tok[:, c, 512:1536], big16)
                ck("ssd")
                mlstm_chunk(l, st, lambda hd: mqT[:, hd, sl], lambda hd: mkT[:, hd, sl], [mqT, mkT], mktok[:, c, :], lambda hd: mvaug[:, c, hd, :], [mktok, mvaug],
                            gates[:, c, 0:4], gates[:, c, 4:8], gates, motok[:, c, :], motok, mix_tok[:, c, 1536:2048], big16)
                if DEBUG_STOP[0] == "mlstm":
                    dump("mix", big16, mix_tok[:, c, :], [128, 2048])
                    dump("dtt", dtt, dtt[:, c, :], [128, 16])
                    dump("xtok", xtok, xtok[:], [128, 1024])
                    dump("ST", st.ST, st.ST[:], [128, 1024])
                    dump("Cn", st.Cn, st.Cn[:], [128, 4, 129])
                    dump("mrow", st.mrow, st.mrow[:], [128, 4])
                ck("mlstm")
            if stn == NST - 1:
                emit_state_out(l, st, p_ssm[l], p_C[l], p_n[l], p_m[l])

            for tt in range(NT):
                transpose_to(mix_tok[:, tt, :], big16, 128, D, lambda j: actT[:, j, tt * 128:(tt + 1) * 128], actT)

            def c_res(ti, c0, nb, p):
                fw.op(DVE, lambda h: h.tensor_tensor(out=xres[:, ti, c0:c0 + nb], in0=p[:, 0:nb], in1=xres[:, ti, c0:c0 + nb], op=ALU.add), reads=[p, xres], writes=[xres])
            dense_tok(actT, tts, w_out[l], D, c_res)
            ck("wout")

            norm_to_actT(NT, 128, w_norm_mlp[l, :], lambda tt: xres[:, tt, :])
            for g in range(4):
                def c_up(cb_, p):
                    fw.op(ACT, lambda h: h.activation(out=sq[:, 0:ST], in_=p[:, 0:ST], func=AF.Relu), reads=[p], writes=[sq])
                    fw.op(DVE, lambda h: h.tensor_tensor(out=hTg[:, cb_, :], in0=sq[:, 0:ST], in1=sq[:, 0:ST], op=ALU.mult), reads=[sq], writes=[big16])
                dense_feat(actT, ST, w_up[l][:, g * 2048:(g + 1) * 2048], 2048, c_up)
                dense_tok(HT, tts, w_down[l][g * 2048:(g + 1) * 2048, :], D, c_res)
            ck("layer")

        load_gain(w_norm_final, 0, D)
        for tt in range(NT):
            rmsnorm_to(xres[:, tt, :], xres, 128, gbc[:, :], ostage[:, :], ostage, sq, D, col=tt)
            fw.dma(SP, y_p[t0g + tt * 128:t0g + (tt + 1) * 128, :], ostage[:, :], sem_o, reads=[ostage], is_out=True)
        ck("st")
        ck("st%d" % stn)


    fw.barrier()
    pes.close()
    fw.es_alloc = None
    ck("prompt")
    ses = ExitStack()
    xrs = fw.sb("xrs", [RS, D], F32, ses)
    actS = fw.sb("actS", [128, 16, 128], BF16, ses)
    utS = fw.sb("utS", [RS, D], BF16, ses)
    sqS = fw.sb("sqS", [RS, D], F32, ses)
    sall = fw.sb("sall", [RS, INW], F32, ses)
    mixs = fw.sb("mixs", [RS, D], BF16, ses)
    hsT = fw.sb("hsT", [128, 16, 128], BF16, ses)
    xcs = sqS
    PB = [fw.sb(f"pb{i}", [RS, 8192], F32, ses) for i in range(3)]
    cj = PB[0]
    wj = PB[1]
    pbc = [0]

    def nextpb():
        b_ = PB[pbc[0] % 3]
        pbc[0] += 1
        return b_
    scs = fw.sb("scs", [RS, 8, 129], F32, ses)
    sden = fw.sb("sden", [RS, 8], F32, ses)
    so = fw.sb("so", [RS, 2, 8, 64], F32, ses)
    dec = fw.sb("dec", [RS, 16], F32, ses)
    xdt = Buf("xdt", scs[:].rearrange("p a b -> p (a b)")[:, 0:1024].rearrange("p (a b) -> p a b", a=16), parent=scs)
    yv = Buf("yv", so[:].rearrange("p a h d -> p (a h d)").rearrange("p (a b) -> p a b", a=16), parent=so)
    nst = Buf("nst", sqS[:, 1024:1536].rearrange("p (a b) -> p a b", a=4), parent=sqS)
    ms = fw.sb("ms", [RS, 48], F32, ses)
    mtmp = Buf("mtmp", sqS[:, 0:512].rearrange("p (a b) -> p a b", a=4), parent=sqS)
    kws = Buf("kws", sqS[:, 512:1024].rearrange("p (a b) -> p a b", a=4), parent=sqS)
    qcs = Buf("qcs", so[:].rearrange("p a h d -> p (a h d)").rearrange("p (a h d) -> p a h d", a=2, h=4), parent=so)
    sem_s = None
    sem_m = None

    fw.dma(SP, xrs[:], xsm[:, :], sem_s, writes=[xrs])
    ttS = [(0, 128)]
    fw.op(DVE, lambda h: h.memset(actS[:], 0.0), writes=[actS])
    fw.op(DVE, lambda h: h.memset(hsT[:], 0.0), writes=[hsT])
    new_pass()

    def stage_tok(r, c0, n, dst_ap, dstbuf, scale=None):
        for o in range(0, n, 512):
            m = min(512, n - o)
            fw.op(PE, lambda h: h.matmul(pd[1][:, 0:m], lhsT=oh[:, r, :], rhs=sall[:, c0 + o:c0 + o + m], start=True, stop=True), reads=[oh, sall], writes=[pd[1]])
            fw.op(ACT, lambda h: h.copy(out=dst_ap(o, m), in_=pd[1][:, 0:m]), reads=[pd[1]], writes=[dstbuf])

    def stage_feat(r, srcbuf, src_ap, dst_ap, dstbuf):
        fw.op(PE, lambda h: h.matmul(pd[1][:, 0:128], lhsT=src_ap, rhs=oh[:, r, :], start=True, stop=True), reads=[oh, srcbuf], writes=[pd[1]])
        fw.op(ACT, lambda h: h.copy(out=dst_ap, in_=pd[1][:, 0:128]), reads=[pd[1]], writes=[dstbuf])

    def norm_to_actS(gain_row):
        load_gain(gain_row, 0, D)
        rmsnorm_to(xrs[:, :], xrs, RS, gbc[0:RS, :], utS[:, :], utS, sqS, D, col=0)
        transpose_to(utS[:, :], utS, RS, D, lambda j: actS[:, j, 0:RS], actS)

    def c_res_s(ti, c0, nb, p):
        fw.op(DVE, lambda h: h.tensor_tensor(out=xrs[:, c0:c0 + nb], in0=p[0:RS, 0:nb], in1=xrs[:, c0:c0 + nb], op=ALU.add), reads=[p, xrs], writes=[xrs])

    for l in range(L):
        st = carry[l]
        P = lp[l]
        norm_to_actS(w_norm_mix[l, :])
        load_gain(w_norm_ssm[l, :], 0, 1024)
        load_gain(w_norm_ml[l, :], 1024, 512)

        def c_all(ti, c0, nb, p):
            fw.op(ACT, lambda h: h.copy(out=sall[:, c0:c0 + nb], in_=p[0:RS, 0:nb]), reads=[p], writes=[sall])
        for (o_, n_) in [(O_Q, 768), (O_Z, 1024), (O_DT, 16), (O_MK, 512), (O_MV, 512), (O_MO, 512), (O_MI, 8), (O_MQ, 512), (O_XBC, 1536)]:
            def c_seg(ti, c0, nb, p, o_=o_):
                c_all(ti, o_ + c0, nb, p)
            dense_tok(actS, ttS, w_in[l][:, o_:o_ + n_], n_, c_seg)
        rope(sall, sall[:, 0:640].rearrange("p (a b) -> p a b", b=64), RS, 10, 16)
        fw.op(DVE, lambda h: h.tensor_scalar(out=sall[:, O_MK:O_MK + 512], in0=sall[:, O_MK:O_MK + 512], scalar1=float(128 ** -0.5), scalar2=None, op0=ALU.mult), reads=[sall], writes=[sall])
        fw.op(DVE, lambda h: h.tensor_tensor(out=sall[:, O_DT:O_DT + 16], in0=sall[:, O_DT:O_DT + 16], in1=P["dtb"][0:RS, :], op=ALU.add), reads=[sall, P["dtb"]], writes=[sall])
        softplus(sall, sall[:, O_DT:O_DT + 16])
        fw.op(DVE, lambda h: h.tensor_tensor(out=sall[:, O_MI:O_MI + 8], in0=sall[:, O_MI:O_MI + 8], in1=P["gb"][0:RS, :], op=ALU.add), reads=[sall, P["gb"]], writes=[sall])
        fw.dma(SP, s_k[l, :, 0:127, :], cache_k[l, :, 1:128, :], sem_o, is_out=True)
        fw.dma(SP, s_v[l, :, 0:127, :], cache_v[l, :, 1:128, :], sem_o, is_out=True)
        fw.dma(SP, s_k[l, :, 127, :], sall[:, O_K:O_K + 128], sem_o, reads=[sall], is_out=True)
        fw.dma(SP, s_v[l, :, 127, :], sall[:, O_V:O_V + 128], sem_o, reads=[sall], is_out=True)
        fw.dma(SP, s_conv[l, :, 0:2, :], st_conv[l, :, 1:3, :], sem_o, is_out=True)
        fw.dma(SP, s_conv[l, :, 2, :], sall[:, O_XBC:O_XBC + 1536], sem_o, reads=[sall], is_out=True)
        fw.dma(SP, wj[:, 0:1536], conv_w[l, 3, :].partition_broadcast(RS), sem_s, writes=[wj])
        fw.op(DVE, lambda h: h.tensor_tensor(out=xcs[:, 0:1536], in0=sall[:, O_XBC:O_XBC + 1536], in1=wj[:, 0:1536], op=ALU.mult), reads=[sall, wj], writes=[xcs])
        for j in range(3):
            fw.dma(SP, cj[:, 0:1536], st_conv[l, :, j, :], sem_s, writes=[cj])
            fw.dma(SP, wj[:, 0:1536], conv_w[l, j, :].partition_broadcast(RS), sem_s, writes=[wj])
            fw.op(DVE, lambda h: h.tensor_tensor(out=cj[:, 0:1536], in0=cj[:, 0:1536], in1=wj[:, 0:1536], op=ALU.mult), reads=[cj, wj], writes=[cj])
            fw.op(DVE, lambda h: h.tensor_tensor(out=xcs[:, 0:1536], in0=xcs[:, 0:1536], in1=cj[:, 0:1536], op=ALU.add), reads=[xcs, cj], writes=[xcs])
        fw.dma(SP, wj[:, 0:1536], conv_b[l, :].partition_broadcast(RS), sem_s, writes=[wj])
        fw.op(DVE, lambda h: h.tensor_tensor(out=xcs[:, 0:1536], in0=xcs[:, 0:1536], in1=wj[:, 0:1536], op=ALU.add), reads=[xcs, wj], writes=[xcs])
        fw.op(ACT, lambda h: h.activation(out=sall[:, O_XBC:O_XBC + 1536], in_=xcs[:, 0:1536], func=AF.Silu), reads=[xcs, sall], writes=[sall])

        A_ = P["A"]
        qv = sall[:, 0:512].rearrange("p (h d) -> p h d", h=8)
        knew = sall[:, O_K:O_K + 128].rearrange("p (a d) -> p a d", a=2)
        vnew = sall[:, O_V:O_V + 128].rearrange("p (a d) -> p a d", a=2)
        for kv in range(2):
            Kc, Vc, T = nextpb(), nextpb(), nextpb()
            Kc3 = Kc[:, :].rearrange("p (s d) -> p s d", d=64)
            Vc3 = Vc[:, :].rearrange("p (s d) -> p s d", d=64)
            T3 = T[:, 0:4096].rearrange("p (a b) -> p a b", a=64)
            with nc.allow_non_contiguous_dma(reason="kv cache head slice (256B runs)"):
                fw.dma(SP, Kc3, cache_k[l, :, :, kv * 64:(kv + 1) * 64], None, writes=[Kc])
                fw.dma(SP, Vc3, cache_v[l, :, :, kv * 64:(kv + 1) * 64], None, writes=[Vc])
            hs = slice(kv * 4, kv * 4 + 4)
            for h4 in range(4):
                h_ = kv * 4 + h4
                for ch in range(2):
                    ps_ = slice(ch * 64, (ch + 1) * 64)
                    fw.op(DVE, lambda h: h.tensor_tensor(out=T3, in0=Kc3[:, ps_, :], in1=bc(qv[:, h_:h_ + 1, :], [RS, 64, 64]), op=ALU.mult), reads=[Kc, sall], writes=[T])
                    fw.op(DVE, lambda h: h.tensor_reduce(out=scs[:, h_, ps_], in_=T3, op=ALU.add, axis=AX.X), reads=[T], writes=[scs])
            fw.op(DVE, lambda h: h.tensor_tensor(out=T[:, 0:256].rearrange("p (a b) -> p a b", a=4), in0=qv[:, hs, :], in1=bc(knew[:, kv:kv + 1, :], [RS, 4, 64]), op=ALU.mult), reads=[sall], writes=[T])
            fw.op(DVE, lambda h: h.tensor_reduce(out=scs[:, hs, 128], in_=T[:, 0:256].rearrange("p (a b) -> p a b", a=4), op=ALU.add, axis=AX.X), reads=[T], writes=[scs])
            fw.op(ACT, lambda h: h.activation(out=scs[:, hs, :], in_=scs[:, hs, :], func=AF.Exp, scale=0.125), reads=[scs], writes=[scs])
            fw.op(DVE, lambda h: h.tensor_reduce(out=sden[:, hs], in_=scs[:, hs, :], op=ALU.add, axis=AX.X), reads=[scs], writes=[sden])
            for h4 in range(4):
                h_ = kv * 4 + h4
                for ch in range(2):
                    ps_ = slice(ch * 64, (ch + 1) * 64)
                    fw.op(DVE, lambda h: h.tensor_tensor(out=T3, in0=Vc3[:, ps_, :].rearrange("p s d -> p d s"), in1=bc(scs[:, h_:h_ + 1, ps_], [RS, 64, 64]), op=ALU.mult), reads=[Vc, scs], writes=[T])
                    fw.op(DVE, lambda h: h.tensor_reduce(out=so[:, ch, h_, :], in_=T3, op=ALU.add, axis=AX.X), reads=[T], writes=[so])
                fw.op(DVE, lambda h: h.scalar_tensor_tensor(out=so[:, 0, h_, :], in0=vnew[:, kv, :], scalar=scs[:, h_, 128:129], in1=so[:, 0, h_, :], op0=ALU.mult, op1=ALU.add), reads=[sall, scs, so], writes=[so])
        fw.op(DVE, lambda h: h.tensor_tensor(out=so[:, 0, :, :], in0=so[:, 0, :, :], in1=so[:, 1, :, :], op=ALU.add), reads=[so], writes=[so])
        fw.op(DVE, lambda h: h.tensor_tensor(out=sden[:], in0=sden[:], in1=P["esk"][0:RS, :], op=ALU.add), reads=[sden, P["esk"]], writes=[sden])
        fw.op(DVE, lambda h: h.reciprocal(out=sden[:], in_=sden[:]), reads=[sden], writes=[sden])
        fw.op(DVE, lambda h: h.tensor_tensor(out=mixs[:, 0:512].rearrange("p (a b) -> p a b", a=8), in0=so[:, 0, :, :], in1=bc(sden[:].unsqueeze(2), [RS, 8, 64]), op=ALU.mult), reads=[so, sden], writes=[mixs])

        x16 = sall[:, O_XBC:O_XBC + 1024].rearrange("p (a b) -> p a b", a=16)
        Bm = sall[:, O_XBC + 1024:O_XBC + 1280].rearrange("p (a b) -> p a b", a=2)
        Cm = sall[:, O_XBC + 1280:O_XBC + 1536].rearrange("p (a b) -> p a b", a=2)
        dts = sall[:, O_DT:O_DT + 16]
        fw.op(DVE, lambda h: h.tensor_tensor(out=dec[:], in0=dts, in1=A_[0:RS, :], op=ALU.mult), reads=[sall, A_], writes=[dec])
        fw.op(ACT, lambda h: h.activation(out=dec[:], in_=dec[:], func=AF.Exp), reads=[dec], writes=[dec])
        fw.op(DVE, lambda h: h.tensor_tensor(out=xdt[:], in0=x16, in1=bc(dts.unsqueeze(2), [RS, 16, 64]), op=ALU.mult), reads=[sall], writes=[xdt])
        for hh in range(16):
            g = hh // 8
            Sp, T = nextpb(), nextpb()
            Sp3 = Sp[:, :].rearrange("p (a b) -> p a b", a=64)
            T3 = T[:, :].rearrange("p (a b) -> p a b", a=64)
            fw.dma(SP, Sp3, st_ssm[l, :, hh * 64:(hh + 1) * 64, :], None, writes=[Sp])
            fw.op(POOL, lambda h: h.tensor_tensor(out=T3, in0=bc(xdt[:, hh, :].unsqueeze(2), [RS, 64, 128]), in1=bc(Bm[:, g:g + 1, :], [RS, 64, 128]), op=ALU.mult), reads=[xdt, sall], writes=[T])
            fw.op(DVE, lambda h: h.scalar_tensor_tensor(out=Sp[:, :], in0=Sp[:, :], scalar=dec[:, hh:hh + 1], in1=T[:, :], op0=ALU.mult, op1=ALU.add), reads=[Sp, dec, T], writes=[Sp])
            fw.dma(SP, s_ssm[l, :, hh * 64:(hh + 1) * 64, :], Sp3, None, reads=[Sp], is_out=True)
            fw.op(DVE, lambda h: h.tensor_tensor(out=T3, in0=Sp3, in1=bc(Cm[:, g:g + 1, :], [RS, 64, 128]), op=ALU.mult), reads=[Sp, sall], writes=[T])
            fw.op(DVE, lambda h: h.tensor_reduce(out=yv[:, hh, :], in_=T3, op=ALU.add, axis=AX.X), reads=[T], writes=[yv])
        fw.op(DVE, lambda h: h.tensor_tensor(out=xdt[:], in0=x16, in1=bc(P["dsk"][0:RS, :].unsqueeze(2), [RS, 16, 64]), op=ALU.mult), reads=[sall, P["dsk"]], writes=[xdt])
        fw.op(DVE, lambda h: h.tensor_tensor(out=yv[:], in0=yv[:], in1=xdt[:], op=ALU.add), reads=[yv, xdt], writes=[yv])
        fw.op(ACT, lambda h: h.activation(out=xdt[:].rearrange("p a b -> p (a b)"), in_=sall[:, O_Z:O_Z + 1024], func=AF.Silu), reads=[sall], writes=[xdt])
        fw.op(DVE, lambda h: h.tensor_tensor(out=yv[:], in0=yv[:], in1=xdt[:], op=ALU.mult), reads=[yv, xdt], writes=[yv])
        yvf = yv[:].rearrange("p a b -> p (a b)")
        for g in range(2):
            rmsnorm_to(yvf[:, g * 512:(g + 1) * 512], yv, RS, gbc[0:RS, g * 512:(g + 1) * 512], mixs[:, 512 + g * 512:512 + (g + 1) * 512], mixs, sqS, 512, col=8 + g)

        q4 = sall[:, O_MQ:O_MQ + 512].rearrange("p (a b) -> p a b", a=4)
        k4 = sall[:, O_MK:O_MK + 512].rearrange("p (a b) -> p a b", a=4)
        v4 = sall[:, O_MV:O_MV + 512].rearrange("p (a b) -> p a b", a=4)
        igs = sall[:, O_MI:O_MI + 4]
        fgs = sall[:, O_MF:O_MF + 4]
        fw.dma(SP, nst[:], st_n[l], None, writes=[nst])
        fw.dma(SP, ms[:, 0:4], st_m[l], None, writes=[ms])
        M_, LF, BM, MT, SWS, G_, EMT, QK, QN, SW, DEN, RD = [ms[:, 4 * i:4 * i + 4] for i in range(12)]
        def dv(out, in0, in1, op):
            fw.op(DVE, lambda h: h.tensor_tensor(out=out, in0=in0, in1=in1, op=op), reads=[ms, sall], writes=[ms])
        fw.op(ACT, lambda h: h.activation(out=LF, in_=fgs, func=AF.Exp, scale=-1.0), reads=[sall], writes=[ms])
        fw.op(ACT, lambda h: h.activation(out=LF, in_=LF, func=AF.Ln, bias=1.0), reads=[ms], writes=[ms])
        dv(BM, M_, LF, ALU.subtract)
        dv(MT, BM, igs, ALU.max)
        dv(SWS, igs, MT, ALU.subtract)
        fw.op(ACT, lambda h: h.activation(out=SWS, in_=SWS, func=AF.Exp), reads=[ms], writes=[ms])
        dv(G_, BM, MT, ALU.subtract)
        fw.op(ACT, lambda h: h.activation(out=G_, in_=G_, func=AF.Exp), reads=[ms], writes=[ms])
        fw.op(ACT, lambda h: h.activation(out=EMT, in_=MT, func=AF.Exp, scale=-1.0), reads=[ms], writes=[ms])
        fw.op(DVE, lambda h: h.tensor_tensor(out=mtmp[:], in0=q4, in1=k4, op=ALU.mult), reads=[sall], writes=[mtmp])
        fw.op(DVE, lambda h: h.tensor_reduce(out=QK, in_=mtmp[:], op=ALU.add, axis=AX.X), reads=[mtmp], writes=[ms])
        fw.op(DVE, lambda h: h.tensor_tensor(out=mtmp[:], in0=q4, in1=nst[:], op=ALU.mult), reads=[sall, nst], writes=[mtmp])
        fw.op(DVE, lambda h: h.tensor_reduce(out=QN, in_=mtmp[:], op=ALU.add, axis=AX.X), reads=[mtmp], writes=[ms])
        dv(SW, SWS, QK, ALU.mult)
        dv(DEN, G_, QN, ALU.mult)
        dv(DEN, DEN, SW, ALU.add)
        fw.op(DVE, lambda h: h.tensor_scalar(out=RD, in0=DEN, scalar1=-1.0, scalar2=None, op0=ALU.mult), reads=[ms], writes=[ms])
        dv(RD, RD, DEN, ALU.max)
        dv(RD, RD, EMT, ALU.max)
        fw.op(DVE, lambda h: h.reciprocal(out=RD, in_=RD), reads=[ms], writes=[ms])
        fw.op(DVE, lambda h: h.tensor_tensor(out=kws[:], in0=k4, in1=bc(SWS.unsqueeze(2), [RS, 4, 128]), op=ALU.mult), reads=[sall, ms], writes=[kws])
        fw.op(DVE, lambda h: h.tensor_tensor(out=nst[:], in0=nst[:], in1=bc(G_.unsqueeze(2), [RS, 4, 128]), op=ALU.mult), reads=[nst, ms], writes=[nst])
        fw.op(DVE, lambda h: h.tensor_tensor(out=nst[:], in0=nst[:], in1=kws[:], op=ALU.add), reads=[nst, kws], writes=[nst])
        fw.dma(SP, s_n[l], nst[:], None, reads=[nst], is_out=True)
        fw.dma(SP, s_m[l], MT, None, reads=[ms], is_out=True)
        for hd in range(4):
            for hf in range(2):
                dsl = slice(hf * 64, (hf + 1) * 64)
                Ct, T = nextpb(), nextpb()
                Ct3 = Ct[:, :].rearrange("p (a b) -> p a b", a=64)
                T3 = T[:, :].rearrange("p (a b) -> p a b", a=64)
                Te = T[:, :].rearrange("p (e d) -> p e d", e=128)
                fw.dma(SP, Ct3, st_C[l, :, hd, dsl, :], None, writes=[Ct])
                fw.op(DVE, lambda h: h.tensor_tensor(out=Te, in0=Ct3.rearrange("p d e -> p e d"), in1=bc(q4[:, hd:hd + 1, dsl], [RS, 128, 64]), op=ALU.mult), reads=[Ct, sall], writes=[T])
                fw.op(DVE, lambda h: h.tensor_reduce(out=qcs[:, hf, hd, :], in_=Te, op=ALU.add, axis=AX.X), reads=[T], writes=[qcs])
                fw.op(POOL, lambda h: h.tensor_tensor(out=T3, in0=bc(kws[:, hd, dsl].unsqueeze(2), [RS, 64, 128]), in1=bc(v4[:, hd:hd + 1, :], [RS, 64, 128]), op=ALU.mult), reads=[kws, sall], writes=[T])
                fw.op(DVE, lambda h: h.scalar_tensor_tensor(out=Ct[:, :], in0=Ct[:, :], scalar=G_[:, hd:hd + 1], in1=T[:, :], op0=ALU.mult, op1=ALU.add), reads=[Ct, ms, T], writes=[Ct])
                fw.dma(SP, s_C[l, :, hd, dsl, :], Ct3, None, reads=[Ct], is_out=True)
        fw.op(DVE, lambda h: h.tensor_tensor(out=qcs[:, 0, :, :], in0=qcs[:, 0, :, :], in1=qcs[:, 1, :, :], op=ALU.add), reads=[qcs], writes=[qcs])
        fw.op(DVE, lambda h: h.tensor_tensor(out=mtmp[:], in0=v4, in1=bc(SW.unsqueeze(2), [RS, 4, 128]), op=ALU.mult), reads=[sall, ms], writes=[mtmp])
        fw.op(DVE, lambda h: h.tensor_tensor(out=qcs[:, 0, :, :], in0=qcs[:, 0, :, :], in1=bc(G_.unsqueeze(2), [RS, 4, 128]), op=ALU.mult), reads=[qcs, ms], writes=[qcs])
        fw.op(DVE, lambda h: h.tensor_tensor(out=mtmp[:], in0=mtmp[:], in1=qcs[:, 0, :, :], op=ALU.add), reads=[mtmp, qcs], writes=[mtmp])
        fw.op(DVE, lambda h: h.tensor_tensor(out=mtmp[:], in0=mtmp[:], in1=bc(RD.unsqueeze(2), [RS, 4, 128]), op=ALU.mult), reads=[mtmp, ms], writes=[mtmp])
        fw.op(DVE, lambda h: h.tensor_tensor(out=kws[:], in0=mtmp[:], in1=mtmp[:], op=ALU.mult), reads=[mtmp], writes=[kws])
        fw.op(DVE, lambda h: h.tensor_reduce(out=QK, in_=kws[:], op=ALU.add, axis=AX.X), reads=[kws], writes=[ms])
        fw.op(DVE, lambda h: h.tensor_scalar(out=QK, in0=QK, scalar1=1.0 / 128, scalar2=EPS, op0=ALU.mult, op1=ALU.add), reads=[ms], writes=[ms])
        fw.op(ACT, lambda h: h.activation(out=QK, in_=QK, func=AF.Sqrt), reads=[ms], writes=[ms])
        fw.op(DVE, lambda h: h.reciprocal(out=QK, in_=QK), reads=[ms], writes=[ms])
        fw.op(DVE, lambda h: h.tensor_tensor(out=mtmp[:], in0=mtmp[:], in1=bc(QK.unsqueeze(2), [RS, 4, 128]), op=ALU.mult), reads=[mtmp, ms], writes=[mtmp])
        mtf = mtmp[:].rearrange("p a b -> p (a b)")
        fw.op(DVE, lambda h: h.tensor_tensor(out=mtf, in0=mtf, in1=gbc[0:RS, 1024:1536], op=ALU.mult), reads=[mtmp, gbc], writes=[mtmp])
        fw.op(ACT, lambda h: h.activation(out=kws[:].rearrange("p a b -> p (a b)"), in_=sall[:, O_MO:O_MO + 512], func=AF.Sigmoid), reads=[sall], writes=[kws])
        fw.op(DVE, lambda h: h.tensor_tensor(out=mixs[:, 1536:2048], in0=mtf, in1=kws[:].rearrange("p a b -> p (a b)"), op=ALU.mult), reads=[mtmp, kws], writes=[mixs])

        transpose_to(mixs[:, :], mixs, RS, D, lambda j: actS[:, j, 0:RS], actS)
        dense_tok(actS, ttS, w_out[l], D, c_res_s)
        norm_to_actS(w_norm_mlp[l, :])
        for g in range(4):
            def c_up_s(ti, c0, nb, p):
                fw.op(ACT, lambda h: h.activation(out=sqS[:, c0:c0 + nb], in_=p[0:RS, 0:nb], func=AF.Relu), reads=[p], writes=[sqS])
                fw.op(DVE, lambda h: h.tensor_tensor(out=utS[:, c0:c0 + nb], in0=sqS[:, c0:c0 + nb], in1=sqS[:, c0:c0 + nb], op=ALU.mult), reads=[sqS], writes=[utS])
            dense_tok(actS, ttS, w_up[l][:, g * 2048:(g + 1) * 2048], 2048, c_up_s)
            transpose_to(utS[:, :], utS, RS, D, lambda j: hsT[:, j, 0:RS], hsT)
            dense_tok(hsT, ttS, w_down[l][g * 2048:(g + 1) * 2048, :], D, c_res_s)

    load_gain(w_norm_final, 0, D)
    rmsnorm_to(xrs[:, :], xrs, RS, gbc[0:RS, :], sqS[:, :], sqS, sall, D, col=0)
    fw.dma(SP, y_s[:, :], sqS[:, :], sem_o, reads=[sqS], is_out=True)
    ses.close()


_NC = [None]


def _consts():
    i = np.arange(128)
    c = {}
    c["c_ident"] = np.eye(128, dtype=np.float32)
    c["c_tri"] = (i[:, None] <= i[None, :]).astype(np.float32)
    c["c_triT"] = (i[:, None] >= i[None, :]).astype(np.float32)
    c["c_U"] = (i[:, None] > i[None, :]).astype(np.float32)
    c["c_negqk"] = np.where(i[None, :] > i[:, None], -1e30, 0.0).astype(np.float32)
    c["c_negkq"] = np.where(i[:, None] > i[None, :], -30000.0, 0.0).astype(np.float32)
    e = np.zeros((128, 128), np.float32); e[127, :] = 1.0
    c["c_e127"] = e
    half = 8
    inv = np.power(np.float32(500000.0), -np.arange(half, dtype=np.float32) / half).astype(np.float32)
    pos = np.zeros((128, 17), np.float32)
    for t in range(16):
        pos[:, t] = t * 128 + i
    pos[:, 16] = PAST
    ang = pos[:, :, None].astype(np.float32) * inv[None, None, :]
    c["c_cos"] = np.cos(ang).astype(np.float32)
    c["c_sin"] = np.sin(ang).astype(np.float32)
    oh = np.zeros((RS, RS, 128), np.float32)
    for r in range(RS):
        oh[r, r, 0] = 1.0
    c["c_oh"] = oh
    pad = np.zeros((128, 8), np.float32)
    pad[1:, 0:4] = -1.0e4
    pad[1:, 4:8] = 1.0e4
    c["c_pad"] = pad
    return c


def kernel(x_prompt, x_sample, cache_swa_k, cache_swa_v, state_conv, state_ssm, state_mlstm_C,
           state_mlstm_n, state_mlstm_m, w_norm_mix, w_in, attn_sinks, conv_w, conv_b, dt_bias, a_log,
           d_skip, w_norm_ssm, igate_b, fgate_b, w_norm_mlstm, w_out, w_norm_mlp, w_up, w_down,
           w_norm_final):
    f = lambda a: np.ascontiguousarray(np.asarray(a, dtype=np.float32))
    if _NC[0] is None:
        _NC[0] = build()
    nc = _NC[0]
    cst = _consts()
    shared = {
        "w_norm_mix": f(w_norm_mix), "w_in": f(w_in), "sinks": f(attn_sinks).reshape(L, 8), "conv_w": f(conv_w),
        "conv_b": f(conv_b), "dt_bias": f(dt_bias), "a_log": f(a_log), "d_skip": f(d_skip), "w_norm_ssm": f(w_norm_ssm),
        "igb": f(igate_b), "fgb": f(fgate_b), "w_norm_ml": f(w_norm_mlstm), "w_out": f(w_out), "w_norm_mlp": f(w_norm_mlp),
        "w_up": f(w_up), "w_down": f(w_down), "w_norm_final": f(w_norm_final),
    }
    shared.update(cst)
    in_maps = []
    for c in range(NCORES):
        rs = slice(c * RS, (c + 1) * RS)
        m = dict(shared)
        m["xp"] = f(x_prompt[c])
        m["xsm"] = f(x_sample[rs, 0, :])
        m["cache_k"] = f(np.asarray(cache_swa_k)[:, rs].reshape(L, RS, 128, 128))
        m["cache_v"] = f(np.asarray(cache_swa_v)[:, rs].reshape(L, RS, 128, 128))
        m["st_conv"] = f(np.asarray(state_conv)[:, rs])
        m["st_ssm"] = f(np.asarray(state_ssm)[:, rs].reshape(L, RS, 1024, 128))
        m["st_C"] = f(np.asarray(state_mlstm_C)[:, rs])
        m["st_n"] = f(np.asarray(state_mlstm_n)[:, rs])
        m["st_m"] = f(np.asarray(state_mlstm_m)[:, rs])
        in_maps.append(m)
    res = run_bass_kernel_spmd(nc, in_maps, core_ids=list(range(NCORES)))
    R = res.results
    cat = lambda k, ax: np.concatenate([np.asarray(R[c][k]) for c in range(NCORES)], axis=ax)
    stk = lambda k: np.stack([np.asarray(R[c][k]) for c in range(NCORES)], axis=1)
    y_prompt = np.stack([np.asarray(R[c]["y_p"]) for c in range(NCORES)], axis=0)
    y_sample = cat("y_s", 0).reshape(NCORES * RS, 1, D)
    p_k = stk("p_k").reshape(L, NCORES, 128, 2, 64)
    p_v = stk("p_v").reshape(L, NCORES, 128, 2, 64)
    p_conv = stk("p_conv")
    p_ssm = stk("p_ssm").reshape(L, NCORES, 16, 64, 128)
    p_C = stk("p_C"); p_n = stk("p_n"); p_m = stk("p_m")
    s_k = cat("s_k", 1).reshape(L, NCORES * RS, 128, 2, 64)
    s_v = cat("s_v", 1).reshape(L, NCORES * RS, 128, 2, 64)
    s_conv = cat("s_conv", 1)
    s_ssm = cat("s_ssm", 1).reshape(L, NCORES * RS, 16, 64, 128)
    s_C = cat("s_C", 1); s_n = cat("s_n", 1); s_m = cat("s_m", 1)
    outs = (y_prompt, y_sample, p_k, p_v, p_conv, p_ssm, p_C, p_n, p_m, s_k, s_v, s_conv, s_ssm, s_C, s_n, s_m)
    return tuple(np.ascontiguousarray(o, dtype=np.float32) for o in outs)
```

```python
import numpy as np
import concourse.bass as bass
import concourse.mybir as mybir
from concourse.bass_utils import run_bass_kernel_spmd
from contextlib import ExitStack

F32 = mybir.dt.float32
BF16 = mybir.dt.bfloat16
AF = mybir.ActivationFunctionType
ALU = mybir.AluOpType
AX = mybir.AxisListType

NCORES = 4
D = 2048
SEQ = 2048
ST = 256
NT = ST // 128
NST = SEQ // ST
RS = 32
L = 2
INW = 5400
O_Q, O_K, O_V, O_Z, O_XBC, O_DT, O_MQ, O_MK, O_MV, O_MO, O_MI, O_MF = (
    0, 512, 640, 768, 1792, 3328, 3344, 3856, 4368, 4880, 5392, 5396)
EPS = 1e-6
PAST = 8192
WB = 256
SEM_LIMIT = 8000


class Buf:
    def __init__(self, name, t=None, parent=None):
        self.name = name
        self.t = t
        self.parent = parent
        self.w = None
        self.r = {}

    def root(self):
        return self.parent.root() if self.parent is not None else self

    def __getitem__(self, idx):
        return self.t[idx]


class Eng:
    def __init__(self, fw, name, h):
        self.fw = fw
        self.name = name
        self.h = h
        self.sem = fw.new_sem(name)
        self.own = {id(self.sem)}
        self.cnt = 0
        self.seen = {}

    def _wait(self, ev):
        if ev is None:
            return
        sem, val = ev
        key = id(sem)
        if self.name == "pe" and key in self.own:
            return
        if key in self.fw.dma_sems:
            val = self.fw.dma_sems[key]
        if self.seen.get(key, 0) >= val:
            return
        self.h.wait_ge(sem, val)
        self.seen[key] = val


def _rnd(n):
    return 32 if n <= 32 else (64 if n <= 64 else 128)


class PEProxy:
    def __init__(self, fw):
        self.fw = fw
        self.last = None

    def _mode(self, st_ap, kind):
        shp = tuple(st_ap.shape)
        m = 1
        for v in shp[1:]:
            m *= v
        mode = (_rnd(shp[0]), _rnd(m), str(st_ap.dtype), kind)
        pe = self.fw.pe
        tiled = mode[0] < 128 or mode[1] < 128
        if self.last is not None and (mode != self.last or tiled) and pe.cnt > 0:
            pe.h.wait_ge(pe.sem, pe.cnt)
        self.last = mode

    def matmul(self, out, lhsT=None, rhs=None, **kw):
        self._mode(lhsT, "m")
        return self.fw.pe.h.matmul(out, lhsT=lhsT, rhs=rhs, **kw)

    def transpose(self, out=None, in_=None, identity=None):
        self._mode(in_, "t")
        return self.fw.pe.h.transpose(out=out, in_=in_, identity=identity)


class FW:
    def __init__(self, nc):
        self.nc = nc
        self.es = ExitStack()
        self.nsem = 0
        self.dma_sems = {}
        self.dma_sem_objs = {}
        self.qpool = {}
        self.pe = Eng(self, "pe", nc.tensor)
        self.act = Eng(self, "act", nc.scalar)
        self.dve = Eng(self, "dve", nc.vector)
        self.pool = Eng(self, "pool", nc.gpsimd)
        self.sp = Eng(self, "sp", nc.sync)
        self.engs = [self.pe, self.act, self.dve, self.pool, self.sp]
        self.pe_proxy = PEProxy(self)
        self.out_events = []
        self.all_sems_used = []

    def new_sem(self, name):
        self.nsem += 1
        return self.es.enter_context(self.nc.semaphore(f"s_{name}_{self.nsem}"))

    def sb(self, name, shape, dt, es=None):
        t = (es or getattr(self, "es_alloc", None) or self.es).enter_context(self.nc.sbuf_tensor(name, list(shape), dt))
        return Buf(name, t)

    def ps(self, name, shape, dt=F32):
        t = self.es.enter_context(self.nc.psum_tensor(name, list(shape), dt))
        return Buf(name, t)

    def _deps(self, eng, reads, writes):
        reads = [b.root() for b in reads]
        writes = [b.root() for b in writes]
        for b in reads:
            eng._wait(b.w)
        for b in writes:
            eng._wait(b.w)
            for ev in list(b.r.values()):
                eng._wait(ev)

    def op(self, eng, fn, reads=(), writes=()):
        reads = [b.root() for b in reads]
        writes = [b.root() for b in writes]
        self._deps(eng, reads, writes)
        inst = fn(self.pe_proxy if eng is self.pe else eng.h)
        if eng.cnt >= SEM_LIMIT:
            eng.sem = self.new_sem(eng.name)
            eng.own.add(id(eng.sem))
            eng.cnt = 0
        eng.cnt += 1
        inst.then_inc(eng.sem, 1)
        ev = (eng.sem, eng.cnt)
        for b in writes:
            b.w = ev
            b.r = {}
        for b in reads:
            if b not in writes:
                b.r[id(ev[0])] = ev
        return ev

    def dsem(self, name):
        return [None, name]

    def dma(self, q, out, in_, sem=None, reads=(), writes=(), is_out=False):
        pool = self.qpool.setdefault(q.name, {"sems": [None] * (12 if q.name == "sp" else 4), "i": 0})
        slot = pool["i"] % len(pool["sems"])
        pool["i"] += 1
        sem = pool["sems"][slot]
        if sem is not None:
            prev = self.dma_sems.get(id(sem), 0)
            q._wait((sem, prev))
            if prev >= SEM_LIMIT:
                sem = None
        if sem is None:
            sem = self.new_sem("d" + q.name)
            pool["sems"][slot] = sem
        reads = [b.root() for b in reads]
        writes = [b.root() for b in writes]
        self._deps(q, reads, writes)
        inst = q.h.dma_start(out=out, in_=in_)
        k = id(sem)
        self.dma_sems[k] = self.dma_sems.get(k, 0) + 16
        self.dma_sem_objs[k] = sem
        inst.then_inc(sem, 16)
        ev = (sem, self.dma_sems[k])
        for b in writes:
            b.w = ev
            b.r = {}
        for b in reads:
            b.r[id(ev[0])] = ev
        if is_out:
            self.out_events.append(ev)
        return ev

    def barrier(self):
        evs = [(e.sem, e.cnt) for e in self.engs if e.cnt > 0]
        evs += [(self.dma_sem_objs[k], v) for k, v in self.dma_sems.items()]
        for e in [self.pe, self.act, self.dve, self.pool, self.sp]:
            for ev in evs:
                e._wait(ev)

    def finish(self, close=True):
        for k, v in self.dma_sems.items():
            self.sp._wait((self.dma_sem_objs[k], v))
        if close:
            self.es.close()


def bc(ap, shape):
    return ap.broadcast_to(list(shape))


class _Stop(Exception):
    pass


DEBUG_STOP = [None]


def build():
    nc = bass.Bass("TRN2", target_bir_lowering=False)
    fw = FW(nc)
    stopped = False
    try:
        _build(nc, fw)
    except _Stop:
        stopped = True
    fw.finish(close=not stopped)
    return nc


def _build(nc, fw):
    def ck(name):
        if DEBUG_STOP[0] == name:
            raise _Stop()
    PE, ACT, DVE, POOL, SP = fw.pe, fw.act, fw.dve, fw.pool, fw.sp
    dbg_sem = [None]

    def dump(name, buf, ap, shape):
        if DEBUG_STOP[0] is None:
            return
        if dbg_sem[0] is None:
            dbg_sem[0] = fw.dsem("dbg")
        o = nc.dram_tensor("dbg_" + name, list(shape), F32, kind="ExternalOutput").ap()
        fw.dma(POOL, o, ap, dbg_sem[0], reads=[buf], is_out=True)

    def din(name, shape):
        return nc.dram_tensor(name, list(shape), F32, kind="ExternalInput").ap()

    def dout(name, shape):
        return nc.dram_tensor(name, list(shape), F32, kind="ExternalOutput").ap()

    xp = din("xp", [SEQ, D]); xsm = din("xsm", [RS, D])
    cache_k = din("cache_k", [L, RS, 128, 128]); cache_v = din("cache_v", [L, RS, 128, 128])
    st_conv = din("st_conv", [L, RS, 3, 1536]); st_ssm = din("st_ssm", [L, RS, 1024, 128])
    st_C = din("st_C", [L, RS, 4, 128, 128]); st_n = din("st_n", [L, RS, 4, 128]); st_m = din("st_m", [L, RS, 4])
    w_norm_mix = din("w_norm_mix", [L, D]); w_in = din("w_in", [L, D, INW]); sinks = din("sinks", [L, 8])
    conv_w = din("conv_w", [L, 4, 1536]); conv_b = din("conv_b", [L, 1536]); dt_bias = din("dt_bias", [L, 16])
    a_log = din("a_log", [L, 16]); d_skip = din("d_skip", [L, 16]); w_norm_ssm = din("w_norm_ssm", [L, 1024])
    igb = din("igb", [L, 4]); fgb = din("fgb", [L, 4]); w_norm_ml = din("w_norm_ml", [L, 512])
    w_out = din("w_out", [L, D, D]); w_norm_mlp = din("w_norm_mlp", [L, D]); w_up = din("w_up", [L, D, 4 * D])
    w_down = din("w_down", [L, 4 * D, D]); w_norm_final = din("w_norm_final", [D])
    c_ident = din("c_ident", [128, 128]); c_tri = din("c_tri", [128, 128]); c_triT = din("c_triT", [128, 128])
    c_U = din("c_U", [128, 128]); c_negqk = din("c_negqk", [128, 128]); c_negkq = din("c_negkq", [128, 128])
    c_e127 = din("c_e127", [128, 128]); c_cos = din("c_cos", [128, 17, 8]); c_sin = din("c_sin", [128, 17, 8])
    c_oh = din("c_oh", [RS, RS, 128]); c_pad = din("c_pad", [128, 8])

    y_p = dout("y_p", [SEQ, D]); y_s = dout("y_s", [RS, D])
    p_k = dout("p_k", [L, 128, 128]); p_v = dout("p_v", [L, 128, 128]); p_conv = dout("p_conv", [L, 3, 1536])
    p_ssm = dout("p_ssm", [L, 1024, 128]); p_C = dout("p_C", [L, 4, 128, 128]); p_n = dout("p_n", [L, 4, 128])
    p_m = dout("p_m", [L, 4])
    s_k = dout("s_k", [L, RS, 128, 128]); s_v = dout("s_v", [L, RS, 128, 128]); s_conv = dout("s_conv", [L, RS, 3, 1536])
    s_ssm = dout("s_ssm", [L, RS, 1024, 128]); s_C = dout("s_C", [L, RS, 4, 128, 128]); s_n = dout("s_n", [L, RS, 4, 128])
    s_m = dout("s_m", [L, RS, 4])

    sem_c = fw.dsem("dc")
    sem_o = fw.dsem("do")
    sem_x = fw.dsem("dx")
    sem_g = fw.dsem("dg")
    sem_st = fw.dsem("dst")

    pbig = fw.ps("pbig", [128, 2048], F32)
    pd = [fw.ps(f"pd{i}", [128, 512], F32) for i in range(2)]
    ptb = fw.ps("ptb", [128, 1024], BF16)
    pm = fw.ps("pm", [128, 512], F32)
    pq = [pbig]

    def cload(name, src, shape, dt=F32, cast=None):
        b = fw.sb(name, shape, F32)
        fw.dma(SP, b[:], src, sem_c, writes=[b])
        if cast is not None:
            b2 = fw.sb(name + "_b", shape, cast)
            fw.op(DVE, lambda h: h.tensor_copy(out=b2[:], in_=b[:]), reads=[b], writes=[b2])
            return b, b2
        return b

    ident_f, ident_b = cload("ident", c_ident, [128, 128], cast=BF16)
    tri_f, tri_b = cload("tri", c_tri, [128, 128], cast=BF16)
    triT_f, triT_b = cload("triT", c_triT, [128, 128], cast=BF16)
    U_f = cload("U", c_U, [128, 128])
    negqk = cload("negqk", c_negqk, [128, 128])
    negkq = cload("negkq", c_negkq, [128, 128])
    e127 = cload("e127", c_e127, [128, 128])
    ones_f = fw.sb("ones_f", [128, 128], F32)
    fw.op(DVE, lambda h: h.memset(ones_f[:], 1.0), writes=[ones_f])
    cosT = cload("cosT", c_cos, [128, 17, 8]); sinT = cload("sinT", c_sin, [128, 17, 8])
    padt = cload("padt", c_pad, [128, 8])

    gbc = fw.sb("gbc", [128, 2048], F32)
    lp = {}

    def bload(name, src_row, n):
        b = fw.sb(name, [128, n], F32)
        fw.dma(SP, b[:], src_row.partition_broadcast(128), sem_c, writes=[b])
        return b

    for l in range(L):
        d = {}
        d["dtb"] = bload(f"dtb{l}", dt_bias[l, :], 16)
        al = bload(f"al{l}", a_log[l, :], 16)
        A = fw.sb(f"A{l}", [128, 16], F32)
        fw.op(ACT, lambda h: h.activation(out=A[:], in_=al[:], func=AF.Exp), reads=[al], writes=[A])
        fw.op(DVE, lambda h: h.tensor_scalar(out=A[:], in0=A[:], scalar1=-1.0, scalar2=None, op0=ALU.mult), reads=[A], writes=[A])
        d["A"] = A
        dsk = bload(f"dsk{l}", d_skip[l, :], 16)
        d["dsk"] = dsk
        sk = bload(f"sk{l}", sinks[l, :], 8)
        esk = fw.sb(f"esk{l}", [128, 8], F32)
        fw.op(ACT, lambda h: h.activation(out=esk[:], in_=sk[:], func=AF.Exp), reads=[sk], writes=[esk])
        d["esk"] = esk
        gb = fw.sb(f"gb{l}", [128, 8], F32)
        fw.dma(SP, gb[:, 0:4], igb[l, :].partition_broadcast(128), sem_c, writes=[gb])
        fw.dma(SP, gb[:, 4:8], fgb[l, :].partition_broadcast(128), sem_c, writes=[gb])
        d["gb"] = gb
        cw = fw.sb(f"cw{l}", [128, 12, 4], F32)
        cb = fw.sb(f"cb{l}", [128, 12], F32)
        with nc.allow_non_contiguous_dma(reason="small conv params"):
            for j in range(4):
                fw.dma(SP, cw[:, :, j], conv_w[l, j, :].rearrange("(b p) -> p b", p=128), sem_c, writes=[cw])
            fw.dma(SP, cb[:], conv_b[l, :].rearrange("(b p) -> p b", p=128), sem_c, writes=[cb])
        d["cw"] = cw; d["cb"] = cb
        lp[l] = d

    class S:
        pass
    small = fw.sb("small", [128, 64], F32)
    rtmp = fw.sb("rtmp", [128, 10, 16], F32)
    NWB_ = 4
    wbuf = [fw.sb(f"wb{i}", [128, 16, WB], BF16) for i in range(NWB_)]
    pes = ExitStack()
    _es_orig = fw.es
    carry = []
    fw.es_alloc = pes
    for l in range(L):
        dD = fw.sb(f"dD{l}", [128, 16, 128], BF16)
        dsk = lp[l]["dsk"]
        fw.op(DVE, lambda h: h.tensor_tensor(out=dD[:], in0=bc(ident_f[:].unsqueeze(1), [128, 16, 128]),
                                             in1=bc(dsk[:].unsqueeze(2), [128, 16, 128]), op=ALU.mult),
              reads=[ident_f, dsk], writes=[dD])
        lp[l]["dD"] = dD
    for l in range(L):
        s = S()
        s.ST = fw.sb(f"ST{l}", [128, 1024], F32)
        s.STb = fw.sb(f"STb{l}", [128, 1024], BF16)
        s.Cn = fw.sb(f"Cn{l}", [128, 4, 129], F32)
        s.Cnb = fw.sb(f"Cnb{l}", [128, 4, 129], BF16)
        s.mrow = fw.sb(f"mrow{l}", [128, 4], F32)
        s.kprev = fw.sb(f"kprev{l}", [128, 2, 128], BF16)
        s.vprev = fw.sb(f"vprev{l}", [128, 2, 65], BF16)
        s.xcarry = fw.sb(f"xcar{l}", [128, 12, 3], F32)
        carry.append(s)

    NWB = 4
    wsem = [fw.dsem(f"w{i}") for i in range(NWB)]
    wctr = [0]

    NBLK = 2 * (3 + 4 + 1 + 2 + 2 + 2 + 1 + 2 + 6 + 8 + 4 * 16)
    wcache_t = nc.dram_tensor("wcache", [NBLK, 128, 16, WB], BF16, kind="Internal").ap()
    wcache = Buf("wcache")
    wpass = [0, 0]

    def new_pass():
        assert wpass[0] == 0 or wpass[1] == NBLK, wpass
        wpass[0] += 1
        wpass[1] = 0

    def wload(src2d, ncols):
        i = wctr[0] % NWB
        wctr[0] += 1
        b = wbuf[i]
        blk = wpass[1]
        wpass[1] += 1
        if wpass[0] == 1:
            fw.dma(POOL, b[:, :, 0:ncols], src2d.rearrange("(k p) n -> p k n", p=128), wsem[i], writes=[b])
            fw.dma(SP, wcache_t[blk, :, :, 0:ncols], b[:, :, 0:ncols], None, reads=[b], writes=[wcache])
        else:
            fw.dma(POOL, b[:, :, 0:ncols], wcache_t[blk, :, :, 0:ncols], wsem[i], reads=[wcache], writes=[b])
        return b

    pdc = [0]

    def evac(i, out, in_, p, dstbuf):
        if i % 2 == 0:
            fw.op(ACT, lambda h: h.copy(out=out, in_=in_), reads=[p], writes=[dstbuf])
        else:
            fw.op(DVE, lambda h: h.tensor_copy(out=out, in_=in_), reads=[p], writes=[dstbuf])

    def next_pd():
        p = pd[pdc[0] % 2]
        pdc[0] += 1
        return p

    def dense_tok(actT, tts, W2d, ncols, consume):
        for c0 in range(0, ncols, WB):
            nb = min(WB, ncols - c0)
            wb = wload(W2d[:, c0:c0 + nb], nb)
            for ti, (t0, tsz) in enumerate(tts):
                p = next_pd()
                for k in range(16):
                    fw.op(PE, lambda h: h.matmul(p[0:tsz, 0:nb], lhsT=actT[:, k, t0:t0 + tsz], rhs=wb[:, k, 0:nb],
                                                 start=(k == 0), stop=(k == 15)), reads=[actT, wb], writes=[p])
                consume(ti, c0, nb, p)

    def dense_feat(actT, ntok, W2d, ncols, consume):
        for c0 in range(0, ncols, WB):
            nb = min(WB, ncols - c0)
            wb = wload(W2d[:, c0:c0 + nb], nb)
            for s0 in range(0, nb, 128):
                p = next_pd()
                for k in range(16):
                    fw.op(PE, lambda h: h.matmul(p[:, 0:ntok], lhsT=wb[:, k, s0:s0 + 128], rhs=actT[:, k, 0:ntok],
                                                 start=(k == 0), stop=(k == 15)), reads=[actT, wb], writes=[p])
                consume((c0 + s0) // 128, p)


    def rmsnorm_to(xt_ap, xbuf, np_, gain_ap, out_ap, outbuf, tmpbuf, nfeat, col=0):
        ss = small[0:np_, col:col + 1]
        fw.op(ACT, lambda h: h.activation(out=tmpbuf[0:np_, 0:nfeat], in_=xt_ap, func=AF.Square, accum_out=ss),
              reads=[xbuf], writes=[tmpbuf, small])
        fw.op(DVE, lambda h: h.tensor_scalar(out=ss, in0=ss, scalar1=1.0 / nfeat, scalar2=EPS, op0=ALU.mult, op1=ALU.add),
              reads=[small], writes=[small])
        fw.op(ACT, lambda h: h.activation(out=ss, in_=ss, func=AF.Sqrt), reads=[small], writes=[small])
        fw.op(DVE, lambda h: h.reciprocal(out=ss, in_=ss), reads=[small], writes=[small])
        fw.op(DVE, lambda h: h.scalar_tensor_tensor(out=out_ap, in0=xt_ap, scalar=ss, in1=gain_ap, op0=ALU.mult, op1=ALU.mult),
              reads=[xbuf, small, gbc], writes=[outbuf])

    def transpose_to(src_ap, srcbuf, np_, ncols, dst_fn, dstbuf, dst3=None):
        nblk = ncols // 128
        for g0 in range(0, nblk, 8):
            g1 = min(nblk, g0 + 8)
            for j in range(g0, g1):
                fw.op(PE, lambda h: h.transpose(out=ptb[:, (j - g0) * 128:(j - g0) * 128 + np_],
                                                in_=src_ap[:, j * 128:(j + 1) * 128], identity=ident_b[0:np_, 0:np_]),
                      reads=[srcbuf, ident_b], writes=[ptb])
            if dst3 is not None:
                n_ = g1 - g0
                fw.op(ACT, lambda h: h.copy(out=dst3(g0, g1), in_=ptb[:, 0:n_ * 128].rearrange("p (a b) -> p a b", a=n_)[:, :, 0:np_]),
                      reads=[ptb], writes=[dstbuf])
            else:
                for j in range(g0, g1):
                    fw.op(ACT, lambda h: h.copy(out=dst_fn(j), in_=ptb[:, (j - g0) * 128:(j - g0) * 128 + np_]),
                          reads=[ptb], writes=[dstbuf])

    cs = ExitStack()
    o_f = fw.sb("o_f", [128, 8, 65], F32)
    pT = fw.sb("pT", [128, 2, 512], BF16)
    rden = fw.sb("rden", [128, 8], F32)

    def swa_block(l, qT, qTbuf, kcur, kcurbuf, vcur, vcurbuf, kprev, kprevbuf, vprev, vprevbuf, has_prev, out_ap, outbuf):
        for kv in range(2):
            blocks = ([("p", kprev, kprevbuf, vprev, vprevbuf, triT_b)] if has_prev else []) + [("c", kcur, kcurbuf, vcur, vcurbuf, tri_b)]
            for bi, (nm, kf, kb, vf, vb, msk) in enumerate(blocks):
                for hh in range(4):
                    h_ = kv * 4 + hh
                    half = (h_ % 2) * 64
                    fw.op(PE, lambda h: h.matmul(pbig[:, (bi * 4 + hh) * 128:(bi * 4 + hh + 1) * 128],
                                                 lhsT=kf(kv)[half:half + 64, :], rhs=qT(h_ // 2)[half:half + 64, :],
                                                 start=True, stop=True), reads=[kb, qTbuf], writes=[pbig])
                fw.op(ACT, lambda h: h.activation(out=pT[:, bi, :], in_=pbig[:, bi * 512:(bi + 1) * 512], func=AF.Exp, scale=0.125),
                      reads=[pbig], writes=[pT])
                fw.op(DVE, lambda h: h.tensor_tensor(out=pT[:, bi, :].rearrange("p (a b) -> p a b", a=4),
                                                     in0=pT[:, bi, :].rearrange("p (a b) -> p a b", a=4),
                                                     in1=bc(msk[:].unsqueeze(1), [128, 4, 128]), op=ALU.mult),
                      reads=[pT, msk], writes=[pT])
            for hh in range(4):
                for bi, (nm, kf, kb, vf, vb, msk) in enumerate(blocks):
                    fw.op(PE, lambda h: h.matmul(pm[:, hh * 65:(hh + 1) * 65], lhsT=pT[:, bi, hh * 128:(hh + 1) * 128], rhs=vf(kv),
                                                 start=(bi == 0), stop=(bi == len(blocks) - 1)), reads=[pT, vb], writes=[pm])
            fw.op(ACT, lambda h: h.copy(out=o_f[:, kv * 4:(kv + 1) * 4, :], in_=pm[:, 0:260].rearrange("p (a b) -> p a b", a=4)),
                  reads=[pm], writes=[o_f])
        esk = lp[l]["esk"]
        fw.op(DVE, lambda h: h.tensor_tensor(out=rden[:], in0=o_f[:, :, 64], in1=esk[:], op=ALU.add), reads=[o_f, esk], writes=[rden])
        fw.op(DVE, lambda h: h.reciprocal(out=rden[:], in_=rden[:]), reads=[rden], writes=[rden])
        fw.op(DVE, lambda h: h.tensor_tensor(out=out_ap.rearrange("p (a b) -> p a b", a=8), in0=o_f[:, :, 0:64],
                                             in1=bc(rden[:].unsqueeze(2), [128, 8, 64]), op=ALU.mult),
              reads=[o_f, rden], writes=[outbuf])

    dtA = fw.sb("dtA", [128, 16], F32)
    a_sb = fw.sb("a_sb", [128, 16], F32)
    ea = fw.sb("ea", [128, 16], F32)
    eal = fw.sb("eal", [128, 16], F32)
    wk = fw.sb("wk", [128, 16], F32)
    rseg = fw.sb("rseg", [128, 4, 128], F32)
    LT = fw.sb("LT", [128, 16, 128], BF16)
    cbm = fw.sb("cbm", [128, 2, 128], BF16)
    x_dt = fw.sb("x_dt", [128, 1024], BF16)
    xw = fw.sb("xw", [128, 1024], BF16)
    ytmp = fw.sb("ytmp", [128, 1024], F32)
    yy = fw.sb("yy", [128, 1024], F32)
    zs = ytmp

    def ssd_chunk(l, st, x_tok, x_tokbuf, B_tok, B_tokbuf, BT, CT, BCbuf, dt, dtbuf, z_tok, zbuf, out_ap, outbuf):
        A = lp[l]["A"]; dD = lp[l]["dD"]
        fw.op(DVE, lambda h: h.tensor_tensor(out=dtA[:], in0=dt, in1=A[:], op=ALU.mult), reads=[dtbuf, A], writes=[dtA])
        fw.op(PE, lambda h: h.matmul(pm[:, 0:16], lhsT=tri_f[:], rhs=dtA[:], start=True, stop=True), reads=[tri_f, dtA], writes=[pm])
        fw.op(PE, lambda h: h.matmul(pm[:, 16:32], lhsT=ones_f[:], rhs=dtA[:], start=True, stop=True), reads=[ones_f, dtA], writes=[pm])
        fw.op(ACT, lambda h: h.copy(out=a_sb[:], in_=pm[:, 0:16]), reads=[pm], writes=[a_sb])
        fw.op(ACT, lambda h: h.activation(out=ea[:], in_=pm[:, 0:16], func=AF.Exp), reads=[pm], writes=[ea])
        fw.op(ACT, lambda h: h.activation(out=eal[:], in_=pm[:, 16:32], func=AF.Exp), reads=[pm], writes=[eal])
        fw.op(DVE, lambda h: h.tensor_tensor(out=wk[:], in0=pm[:, 16:32], in1=a_sb[:], op=ALU.subtract), reads=[pm, a_sb], writes=[wk])
        fw.op(ACT, lambda h: h.activation(out=wk[:], in_=wk[:], func=AF.Exp), reads=[wk], writes=[wk])
        fw.op(DVE, lambda h: h.tensor_tensor(out=wk[:], in0=wk[:], in1=dt, op=ALU.mult), reads=[wk, dtbuf], writes=[wk])
        for i in range(4):
            fw.op(DVE, lambda h: h.tensor_tensor(out=rseg[:], in0=bc(tri_f[:].unsqueeze(1), [128, 4, 128]),
                                                 in1=bc(dtA[:, i * 4:(i + 1) * 4].unsqueeze(2), [128, 4, 128]), op=ALU.mult),
                  reads=[tri_f, dtA], writes=[rseg])
            fw.op(PE, lambda h: h.matmul(pbig[:, i * 512:(i + 1) * 512], lhsT=U_f[:],
                                         rhs=rseg[:].rearrange("p a b -> p (a b)"), start=True, stop=True),
                  reads=[U_f, rseg], writes=[pbig])
        fw.op(ACT, lambda h: h.activation(out=LT[:].rearrange("p a b -> p (a b)"), in_=pbig[:], func=AF.Exp), reads=[pbig], writes=[LT])
        for g in range(2):
            fw.op(PE, lambda h: h.matmul(pm[:, 64 + g * 128:64 + (g + 1) * 128], lhsT=BT(g), rhs=CT(g), start=True, stop=True),
                  reads=[BCbuf], writes=[pm])
        fw.op(DVE, lambda h: h.tensor_tensor(out=cbm[:], in0=pm[:, 64:320].rearrange("p (a b) -> p a b", a=2),
                                             in1=bc(tri_f[:].unsqueeze(1), [128, 2, 128]), op=ALU.mult),
              reads=[pm, tri_f], writes=[cbm])
        for g in range(2):
            fw.op(DVE, lambda h: h.tensor_tensor(out=LT[:, g * 8:(g + 1) * 8, :], in0=LT[:, g * 8:(g + 1) * 8, :],
                                                 in1=bc(cbm[:, g:g + 1, :], [128, 8, 128]), op=ALU.mult),
                  reads=[LT, cbm], writes=[LT])
        fw.op(DVE, lambda h: h.tensor_tensor(out=x_dt[:].rearrange("p (a b) -> p a b", a=16), in0=x_tok.rearrange("p (a b) -> p a b", a=16),
                                             in1=bc(dt.unsqueeze(2), [128, 16, 64]), op=ALU.mult),
              reads=[x_tokbuf, dtbuf], writes=[x_dt])
        fw.op(DVE, lambda h: h.tensor_tensor(out=xw[:].rearrange("p (a b) -> p a b", a=16), in0=x_tok.rearrange("p (a b) -> p a b", a=16),
                                             in1=bc(wk[:].unsqueeze(2), [128, 16, 64]), op=ALU.mult),
              reads=[x_tokbuf, wk], writes=[xw])
        for g in range(2):
            fw.op(PE, lambda h: h.matmul(pbig[:, g * 512:(g + 1) * 512], lhsT=CT(g), rhs=st.STb[:, g * 512:(g + 1) * 512], start=True, stop=True),
                  reads=[BCbuf, st.STb], writes=[pbig])
        for hh in range(16):
            fw.op(PE, lambda h: h.matmul(pbig[:, 1024 + hh * 64:1024 + (hh + 1) * 64], lhsT=LT[:, hh, :], rhs=x_dt[:, hh * 64:(hh + 1) * 64],
                                         start=True, stop=False), reads=[LT, x_dt], writes=[pbig])
            fw.op(PE, lambda h: h.matmul(pbig[:, 1024 + hh * 64:1024 + (hh + 1) * 64], lhsT=dD[:, hh, :], rhs=x_tok[:, hh * 64:(hh + 1) * 64],
                                         start=False, stop=True), reads=[dD, x_tokbuf], writes=[pbig])
        fw.op(DVE, lambda h: h.tensor_tensor(out=ytmp[:].rearrange("p (a b) -> p a b", a=16), in0=pbig[:, 0:1024].rearrange("p (a b) -> p a b", a=16),
                                             in1=bc(ea[:].unsqueeze(2), [128, 16, 64]), op=ALU.mult),
              reads=[pbig, ea], writes=[ytmp])
        fw.op(DVE, lambda h: h.tensor_tensor(out=yy[:], in0=pbig[:, 1024:2048], in1=ytmp[:], op=ALU.add), reads=[pbig, ytmp], writes=[yy])
        for g in range(2):
            fw.op(PE, lambda h: h.matmul(pd[g][:, :], lhsT=B_tok[:, g * 128:(g + 1) * 128], rhs=xw[:, g * 512:(g + 1) * 512], start=True, stop=True),
                  reads=[B_tokbuf, xw], writes=[pd[g]])
        fw.op(DVE, lambda h: h.tensor_tensor(out=st.ST[:].rearrange("p (a b) -> p a b", a=16), in0=st.ST[:].rearrange("p (a b) -> p a b", a=16),
                                             in1=bc(eal[:].unsqueeze(2), [128, 16, 64]), op=ALU.mult), reads=[st.ST, eal], writes=[st.ST])
        for g in range(2):
            fw.op(DVE, lambda h: h.tensor_tensor(out=st.ST[:, g * 512:(g + 1) * 512], in0=pd[g][:, :], in1=st.ST[:, g * 512:(g + 1) * 512], op=ALU.add),
                  reads=[pd[g], st.ST], writes=[st.ST])
        fw.op(ACT, lambda h: h.copy(out=st.STb[:], in_=st.ST[:]), reads=[st.ST], writes=[st.STb])
        fw.op(ACT, lambda h: h.activation(out=zs[:], in_=z_tok, func=AF.Silu), reads=[zbuf], writes=[zs])
        fw.op(DVE, lambda h: h.tensor_tensor(out=yy[:], in0=yy[:], in1=zs[:], op=ALU.mult), reads=[yy, zs], writes=[yy])
        for g in range(2):
            rmsnorm_to(yy[:, g * 512:(g + 1) * 512], yy, 128, gbc[:, g * 512:(g + 1) * 512], out_ap[:, g * 512:(g + 1) * 512], outbuf, ytmp, 512, col=8 + g)

    lfn = fw.sb("lfn", [128, 4], F32)
    nb_ = fw.sb("nb_", [128, 4], F32)
    nbl = fw.sb("nbl", [128, 4], F32)
    cc = fw.sb("cc", [128, 4], F32)
    Dc = fw.sb("Dc", [128, 4, 128], F32)
    cmx = fw.sb("cmx", [128, 4, 128], F32)
    cm = fw.sb("cm", [128, 4], F32)
    Mq = fw.sb("Mq", [128, 4], F32)
    negM = fw.sb("negM", [128, 4], F32)
    mt = fw.sb("mt", [128, 4], F32)
    gq = fw.sb("gq", [128, 4], F32)
    emt = fw.sb("emt", [128, 4], F32)
    mnew = fw.sb("mnew", [128, 4], F32)
    gend = fw.sb("gend", [128, 4], F32)
    wkm = fw.sb("wkm", [128, 4], F32)
    swe = fw.sb("swe", [128, 4, 128], F32)
    swT = fw.sb("swT", [128, 4, 128], BF16)
    tot = fw.sb("tot", [128, 4, 129], F32)
    ints = fw.sb("ints", [128, 4, 129], F32)
    hh_ = fw.sb("hh_", [128, 4, 128], F32)
    kwm = fw.sb("kwm", [128, 512], BF16)
    sg = fw.sb("sg", [128, 512], F32)
    negkq4 = fw.sb("negkq4", [128, 4, 128], F32)
    fw.op(DVE, lambda h: h.tensor_copy(out=negkq4[:], in_=bc(negkq[:].unsqueeze(1), [128, 4, 128])), reads=[negkq], writes=[negkq4])

    def mlstm_chunk(l, st, qT, kT, qkbuf, k_tok, v_aug, kvbuf, ig, fg, gbuf, mo, mobuf, out_ap, outbuf):
        fw.op(ACT, lambda h: h.activation(out=lfn[:], in_=fg, func=AF.Exp, scale=-1.0), reads=[gbuf], writes=[lfn])
        fw.op(ACT, lambda h: h.activation(out=lfn[:], in_=lfn[:], func=AF.Ln, bias=1.0), reads=[lfn], writes=[lfn])
        fw.op(PE, lambda h: h.matmul(pm[:, 0:4], lhsT=tri_f[:], rhs=lfn[:], start=True, stop=True), reads=[tri_f, lfn], writes=[pm])
        fw.op(PE, lambda h: h.matmul(pm[:, 4:8], lhsT=ones_f[:], rhs=lfn[:], start=True, stop=True), reads=[ones_f, lfn], writes=[pm])
        fw.op(ACT, lambda h: h.copy(out=nb_[:], in_=pm[:, 0:4]), reads=[pm], writes=[nb_])
        fw.op(ACT, lambda h: h.copy(out=nbl[:], in_=pm[:, 4:8]), reads=[pm], writes=[nbl])
        fw.op(DVE, lambda h: h.tensor_tensor(out=cc[:], in0=ig, in1=nb_[:], op=ALU.add), reads=[gbuf, nb_], writes=[cc])
        fw.op(DVE, lambda h: h.tensor_tensor(out=Dc[:], in0=bc(ident_f[:].unsqueeze(1), [128, 4, 128]), in1=bc(cc[:].unsqueeze(2), [128, 4, 128]), op=ALU.mult),
              reads=[ident_f, cc], writes=[Dc])
        fw.op(PE, lambda h: h.matmul(pd[0][:, :], lhsT=ones_f[:], rhs=Dc[:].rearrange("p a b -> p (a b)"), start=True, stop=True),
              reads=[ones_f, Dc], writes=[pd[0]])
        fw.op(DVE, lambda h: h.tensor_tensor(out=cmx[:], in0=pd[0][:, :].rearrange("p (a b) -> p a b", a=4), in1=bc(negqk[:].unsqueeze(1), [128, 4, 128]), op=ALU.add),
              reads=[pd[0], negqk], writes=[cmx])
        fw.op(DVE, lambda h: h.tensor_reduce(out=cm[:], in_=cmx[:], op=ALU.max, axis=AX.X), reads=[cmx], writes=[cm])
        fw.op(DVE, lambda h: h.tensor_tensor(out=Mq[:], in0=cm[:], in1=st.mrow[:], op=ALU.max), reads=[cm, st.mrow], writes=[Mq])
        fw.op(DVE, lambda h: h.tensor_tensor(out=mt[:], in0=Mq[:], in1=nb_[:], op=ALU.subtract), reads=[Mq, nb_], writes=[mt])
        fw.op(DVE, lambda h: h.tensor_scalar(out=negM[:], in0=Mq[:], scalar1=-1.0, scalar2=None, op0=ALU.mult), reads=[Mq], writes=[negM])
        fw.op(DVE, lambda h: h.tensor_tensor(out=Dc[:], in0=bc(ident_f[:].unsqueeze(1), [128, 4, 128]), in1=bc(negM[:].unsqueeze(2), [128, 4, 128]), op=ALU.mult),
              reads=[ident_f, negM], writes=[Dc])
        fw.op(PE, lambda h: h.matmul(pd[1][:, :], lhsT=ones_f[:], rhs=Dc[:].rearrange("p a b -> p (a b)"), start=True, stop=False),
              reads=[ones_f, Dc], writes=[pd[1]])
        fw.op(PE, lambda h: h.matmul(pd[1][:, :], lhsT=ident_f[:], rhs=negkq4[:].rearrange("p a b -> p (a b)"), start=False, stop=True),
              reads=[ident_f, negkq4], writes=[pd[1]])
        for hd in range(4):
            fw.op(ACT, lambda h: h.activation(out=swe[:, hd, :], in_=pd[1][:, hd * 128:(hd + 1) * 128], func=AF.Exp, bias=cc[:, hd:hd + 1]),
                  reads=[pd[1], cc], writes=[swe])
        for hd in range(4):
            fw.op(PE, lambda h: h.matmul(pd[0][:, hd * 128:(hd + 1) * 128], lhsT=kT(hd), rhs=qT(hd), start=True, stop=True), reads=qkbuf, writes=[pd[0]])
        fw.op(DVE, lambda h: h.tensor_tensor(out=swT[:], in0=pd[0][:, :].rearrange("p (a b) -> p a b", a=4), in1=swe[:], op=ALU.mult),
              reads=[pd[0], swe], writes=[swT])
        for hd in range(4):
            o = (hd // 2) * 512 + (hd % 2) * 129
            fw.op(PE, lambda h: h.matmul(pbig[:, o:o + 129], lhsT=swT[:, hd, :], rhs=v_aug(hd), start=True, stop=True), reads=[swT] + kvbuf, writes=[pbig])
            fw.op(PE, lambda h: h.matmul(pbig[:, 1024 + o:1024 + o + 129], lhsT=qT(hd), rhs=st.Cnb[:, hd, :], start=True, stop=True), reads=qkbuf + [st.Cnb], writes=[pbig])
        fw.op(DVE, lambda h: h.tensor_tensor(out=gq[:], in0=st.mrow[:], in1=Mq[:], op=ALU.subtract), reads=[st.mrow, Mq], writes=[gq])
        fw.op(ACT, lambda h: h.activation(out=gq[:], in_=gq[:], func=AF.Exp), reads=[gq], writes=[gq])
        fw.op(ACT, lambda h: h.activation(out=emt[:], in_=mt[:], func=AF.Exp, scale=-1.0), reads=[mt], writes=[emt])
        for hf in range(2):
            fw.op(DVE, lambda h: h.tensor_tensor(out=ints[:, hf * 2:hf * 2 + 2, :], in0=pbig[:, 1024 + hf * 512:1024 + hf * 512 + 258].rearrange("p (a b) -> p a b", a=2),
                                                 in1=bc(gq[:, hf * 2:hf * 2 + 2].unsqueeze(2), [128, 2, 129]), op=ALU.mult), reads=[pbig, gq], writes=[ints])
            fw.op(DVE, lambda h: h.tensor_tensor(out=tot[:, hf * 2:hf * 2 + 2, :], in0=pbig[:, hf * 512:hf * 512 + 258].rearrange("p (a b) -> p a b", a=2),
                                                 in1=ints[:, hf * 2:hf * 2 + 2, :], op=ALU.add), reads=[pbig, ints], writes=[tot])
        fw.op(DVE, lambda h: h.tensor_scalar(out=cm[:], in0=tot[:, :, 128], scalar1=-1.0, scalar2=None, op0=ALU.mult), reads=[tot], writes=[cm])
        fw.op(DVE, lambda h: h.tensor_tensor(out=cm[:], in0=cm[:], in1=tot[:, :, 128], op=ALU.max), reads=[tot, cm], writes=[cm])
        fw.op(DVE, lambda h: h.tensor_tensor(out=cm[:], in0=cm[:], in1=emt[:], op=ALU.max), reads=[cm, emt], writes=[cm])
        fw.op(DVE, lambda h: h.reciprocal(out=cm[:], in_=cm[:]), reads=[cm], writes=[cm])
        fw.op(DVE, lambda h: h.tensor_tensor(out=hh_[:], in0=tot[:, :, 0:128], in1=bc(cm[:].unsqueeze(2), [128, 4, 128]), op=ALU.mult), reads=[tot, cm], writes=[hh_])
        fw.op(DVE, lambda h: h.tensor_tensor(out=cmx[:], in0=hh_[:], in1=hh_[:], op=ALU.mult), reads=[hh_], writes=[cmx])
        fw.op(DVE, lambda h: h.tensor_reduce(out=cm[:], in_=cmx[:], op=ALU.add, axis=AX.X), reads=[cmx], writes=[cm])
        fw.op(DVE, lambda h: h.tensor_scalar(out=cm[:], in0=cm[:], scalar1=1.0 / 128, scalar2=EPS, op0=ALU.mult, op1=ALU.add), reads=[cm], writes=[cm])
        fw.op(ACT, lambda h: h.activation(out=cm[:], in_=cm[:], func=AF.Sqrt), reads=[cm], writes=[cm])
        fw.op(DVE, lambda h: h.reciprocal(out=cm[:], in_=cm[:]), reads=[cm], writes=[cm])
        fw.op(DVE, lambda h: h.tensor_tensor(out=hh_[:], in0=hh_[:], in1=bc(cm[:].unsqueeze(2), [128, 4, 128]), op=ALU.mult), reads=[hh_, cm], writes=[hh_])
        fw.op(DVE, lambda h: h.tensor_tensor(out=hh_[:].rearrange("p a b -> p (a b)"), in0=hh_[:].rearrange("p a b -> p (a b)"), in1=gbc[:, 1024:1536], op=ALU.mult),
              reads=[hh_, gbc], writes=[hh_])
        fw.op(ACT, lambda h: h.activation(out=sg[:], in_=mo, func=AF.Sigmoid), reads=[mobuf], writes=[sg])
        fw.op(DVE, lambda h: h.tensor_tensor(out=out_ap, in0=hh_[:].rearrange("p a b -> p (a b)"), in1=sg[:], op=ALU.mult), reads=[hh_, sg], writes=[outbuf])
        fw.op(PE, lambda h: h.matmul(pm[:, 8:12], lhsT=e127[:], rhs=mt[:], start=True, stop=True), reads=[e127, mt], writes=[pm])
        fw.op(ACT, lambda h: h.copy(out=mnew[:], in_=pm[:, 8:12]), reads=[pm], writes=[mnew])
        fw.op(DVE, lambda h: h.tensor_tensor(out=wkm[:], in0=cc[:], in1=nbl[:], op=ALU.subtract), reads=[cc, nbl], writes=[wkm])
        fw.op(DVE, lambda h: h.tensor_tensor(out=wkm[:], in0=wkm[:], in1=mnew[:], op=ALU.subtract), reads=[wkm, mnew], writes=[wkm])
        fw.op(ACT, lambda h: h.activation(out=wkm[:], in_=wkm[:], func=AF.Exp), reads=[wkm], writes=[wkm])
        fw.op(DVE, lambda h: h.tensor_tensor(out=gend[:], in0=st.mrow[:], in1=nbl[:], op=ALU.subtract), reads=[st.mrow, nbl], writes=[gend])
        fw.op(DVE, lambda h: h.tensor_tensor(out=gend[:], in0=gend[:], in1=mnew[:], op=ALU.subtract), reads=[gend, mnew], writes=[gend])
        fw.op(ACT, lambda h: h.activation(out=gend[:], in_=gend[:], func=AF.Exp), reads=[gend], writes=[gend])
        fw.op(DVE, lambda h: h.tensor_tensor(out=kwm[:].rearrange("p (a b) -> p a b", a=4), in0=k_tok.rearrange("p (a b) -> p a b", a=4),
                                             in1=bc(wkm[:].unsqueeze(2), [128, 4, 128]), op=ALU.mult), reads=kvbuf + [wkm], writes=[kwm])
        for hd in range(4):
            o = (hd // 2) * 512 + (hd % 2) * 129
            fw.op(PE, lambda h: h.matmul(pbig[:, o:o + 129], lhsT=kwm[:, hd * 128:(hd + 1) * 128], rhs=v_aug(hd), start=True, stop=True), reads=[kwm] + kvbuf, writes=[pbig])
        fw.op(DVE, lambda h: h.tensor_tensor(out=st.Cn[:], in0=st.Cn[:], in1=bc(gend[:].unsqueeze(2), [128, 4, 129]), op=ALU.mult), reads=[st.Cn, gend], writes=[st.Cn])
        for hf in range(2):
            fw.op(DVE, lambda h: h.tensor_tensor(out=st.Cn[:, hf * 2:hf * 2 + 2, :], in0=pbig[:, hf * 512:hf * 512 + 258].rearrange("p (a b) -> p a b", a=2),
                                                 in1=st.Cn[:, hf * 2:hf * 2 + 2, :], op=ALU.add), reads=[pbig, st.Cn], writes=[st.Cn])
        fw.op(ACT, lambda h: h.copy(out=st.Cnb[:], in_=st.Cn[:]), reads=[st.Cn], writes=[st.Cnb])
        fw.op(ACT, lambda h: h.copy(out=st.mrow[:], in_=mnew[:]), reads=[mnew], writes=[st.mrow])


    def rope(buf, ap3, np_, nh, ti):
        x1 = ap3[:, :, 0:8]; x2 = ap3[:, :, 8:16]
        cs_ = bc(cosT[0:np_, ti:ti + 1, :], [np_, nh, 8]); sn_ = bc(sinT[0:np_, ti:ti + 1, :], [np_, nh, 8])
        t = rtmp[0:np_, 0:nh, :]
        fw.op(DVE, lambda h: h.tensor_tensor(out=t[:, :, 0:8], in0=x2, in1=sn_, op=ALU.mult), reads=[buf, sinT], writes=[rtmp])
        fw.op(DVE, lambda h: h.tensor_tensor(out=t[:, :, 8:16], in0=x1, in1=sn_, op=ALU.mult), reads=[buf, sinT], writes=[rtmp])
        fw.op(DVE, lambda h: h.tensor_tensor(out=ap3[:, :, 0:16].rearrange("p a (c d) -> p a c d", c=2),
                                             in0=ap3[:, :, 0:16].rearrange("p a (c d) -> p a c d", c=2),
                                             in1=bc(cosT[0:np_, ti:ti + 1, :].unsqueeze(2), [np_, nh, 2, 8]), op=ALU.mult), reads=[buf, cosT], writes=[buf])
        fw.op(DVE, lambda h: h.tensor_tensor(out=x1, in0=x1, in1=t[:, :, 0:8], op=ALU.subtract), reads=[buf, rtmp], writes=[buf])
        fw.op(DVE, lambda h: h.tensor_tensor(out=x2, in0=x2, in1=t[:, :, 8:16], op=ALU.add), reads=[buf, rtmp], writes=[buf])

    def softplus(buf, ap):
        fw.op(ACT, lambda h: h.activation(out=ap, in_=ap, func=AF.Exp), reads=[buf], writes=[buf])
        fw.op(ACT, lambda h: h.activation(out=ap, in_=ap, func=AF.Ln, bias=1.0), reads=[buf], writes=[buf])

    def load_gain(row_ap, c0, n):
        fw.dma(SP, gbc[:, c0:c0 + n], row_ap.partition_broadcast(128), sem_g, writes=[gbc])


    sst = fw.sb("sst", [128, 8, 128], F32)

    def emit_state_out(l, st, d_ssm, d_C, d_n, d_m):
        for g0 in range(0, 8, 4):
            for j in range(g0, g0 + 4):
                fw.op(PE, lambda h: h.transpose(out=pd[0][:, (j - g0) * 128:(j - g0 + 1) * 128], in_=st.ST[:, j * 128:(j + 1) * 128], identity=ident_f[:]),
                      reads=[st.ST, ident_f], writes=[pd[0]])
            fw.op(ACT, lambda h: h.copy(out=sst[:, g0:g0 + 4, :], in_=pd[0][:, :].rearrange("p (a b) -> p a b", a=4)), reads=[pd[0]], writes=[sst])
        fw.dma(SP, d_ssm.rearrange("(j p) n -> p j n", p=128), sst[:], sem_o, reads=[sst], is_out=True)
        fw.dma(SP, d_C.rearrange("h d e -> d h e"), st.Cn[:, :, 0:128], sem_o, reads=[st.Cn], is_out=True)
        with nc.allow_non_contiguous_dma(reason="tiny state out"):
            fw.dma(SP, d_n.rearrange("h d -> d h"), st.Cn[:, :, 128], sem_o, reads=[st.Cn], is_out=True)
        fw.dma(SP, d_m.unsqueeze(0), st.mrow[0:1, :], sem_o, reads=[st.mrow], is_out=True)

    def load_state(l, st, r):
        fw.dma(SP, sst[:], st_ssm[l, r].rearrange("(j p) n -> p j n", p=128), sem_st, writes=[sst])
        for g0 in range(0, 8, 4):
            for j in range(g0, g0 + 4):
                fw.op(PE, lambda h: h.transpose(out=pd[0][:, (j - g0) * 128:(j - g0 + 1) * 128], in_=sst[:, j, :], identity=ident_f[:]),
                      reads=[sst, ident_f], writes=[pd[0]])
            fw.op(ACT, lambda h: h.copy(out=st.ST[:, g0 * 128:(g0 + 4) * 128], in_=pd[0][:, :]), reads=[pd[0]], writes=[st.ST])
        fw.op(ACT, lambda h: h.copy(out=st.STb[:], in_=st.ST[:]), reads=[st.ST], writes=[st.STb])
        fw.dma(SP, st.Cn[:, :, 0:128], st_C[l, r].rearrange("h d e -> d h e"), sem_st, writes=[st.Cn])
        with nc.allow_non_contiguous_dma(reason="tiny state in"):
            fw.dma(SP, st.Cn[:, :, 128], st_n[l, r].rearrange("h d -> d h"), sem_st, writes=[st.Cn])
        fw.dma(SP, st.mrow[:], st_m[l, r, :].partition_broadcast(128), sem_st, writes=[st.mrow])
        fw.op(ACT, lambda h: h.copy(out=st.Cnb[:], in_=st.Cn[:]), reads=[st.Cn], writes=[st.Cnb])

    xres = fw.sb("xres", [128, NT, D], F32, pes)
    actT = fw.sb("actT", [128, 16, ST], BF16, pes)
    big16 = fw.sb("big16", [128, NT * 2048], BF16, pes)
    utok = fw.sb("utok", [128, D], BF16, pes)
    sq = fw.sb("sq", [128, D], F32, pes)
    qkf = fw.sb("qkf", [128, NT, 640], F32, pes)
    qkb = fw.sb("qkb", [128, 768], BF16, pes)
    qT = fw.sb("qT", [128, 4, ST], BF16, pes)
    kTd = fw.sb("kTd", [128, 2, ST], BF16, pes)
    vaug = fw.sb("vaug", [128, NT, 2, 65], BF16, pes)
    ztok = fw.sb("ztok", [128, NT, 1024], BF16, pes)
    dtt = fw.sb("dtt", [128, NT, 16], F32, pes)
    mktok = fw.sb("mktok", [128, NT, 512], BF16, pes)
    mvaug = fw.sb("mvaug", [128, NT, 4, 129], BF16, pes)
    motok = fw.sb("motok", [128, NT, 512], BF16, pes)
    gates = fw.sb("gates", [128, NT, 8], F32, pes)
    xraw = fw.sb("xraw", [128, ST + 3], F32, pes)
    cacc = fw.sb("cacc", [128, ST], F32, pes)
    xcT = fw.sb("xcT", [128, 12, ST], BF16, pes)
    mqT = fw.sb("mqT", [128, 4, ST], BF16, pes)
    mkT = fw.sb("mkT", [128, 4, ST], BF16, pes)
    xtok = fw.sb("xtok", [128, 1024], BF16, pes)
    btok = fw.sb("btok", [128, 256], BF16, pes)
    ostage = sq

    fw.op(DVE, lambda h: h.memset(vaug[:], 1.0), writes=[vaug])
    fw.op(DVE, lambda h: h.memset(mvaug[:], 1.0), writes=[mvaug])
    for l in range(L):
        s = carry[l]
        fw.op(DVE, lambda h: h.memset(s.ST[:], 0.0), writes=[s.ST])
        fw.op(DVE, lambda h: h.memset(s.STb[:], 0.0), writes=[s.STb])
        fw.op(DVE, lambda h: h.memset(s.Cn[:], 0.0), writes=[s.Cn])
        fw.op(DVE, lambda h: h.memset(s.Cnb[:], 0.0), writes=[s.Cnb])
        fw.op(DVE, lambda h: h.memset(s.mrow[:], 0.0), writes=[s.mrow])
        fw.op(DVE, lambda h: h.memset(s.xcarry[:], 0.0), writes=[s.xcarry])

    mix_tok = big16[:].rearrange("p (a b) -> p a b", a=NT)
    ck("consts")
    hTg = big16[:].rearrange("p (a b) -> p a b", a=16)
    HT = Buf("HT", hTg, parent=big16)

    def norm_to_actT(nt, tsz, gain_row, xfn):
        load_gain(gain_row, 0, D)
        for tt in range(nt):
            rmsnorm_to(xfn(tt), xres, tsz, gbc[0:tsz, :], utok[0:tsz, :], utok, sq, D, col=tt)
            transpose_to(utok[0:tsz, :], utok, tsz, D, lambda j: actT[:, j, tt * 128:tt * 128 + tsz], actT, dst3=lambda a, b: actT[:, a:b, tt * 128:tt * 128 + tsz])

    for stn in range(NST):
        new_pass()
        t0g = stn * ST
        for tt in range(NT):
            fw.dma(SP, xres[:, tt, :], xp[t0g + tt * 128:t0g + (tt + 1) * 128, :], sem_x, writes=[xres])
        tts = [(tt * 128, 128) for tt in range(NT)]
        for l in range(L):
            st = carry[l]
            P = lp[l]
            W = w_in[l]
            norm_to_actT(NT, 128, w_norm_mix[l, :], lambda tt: xres[:, tt, :])
            ck("norm")
            load_gain(w_norm_ssm[l, :], 0, 1024)
            load_gain(w_norm_ml[l, :], 1024, 512)

            def c_qkv(ti, c0, nb, p):
                if c0 < 512:
                    fw.op(ACT, lambda h: h.copy(out=qkf[:, ti, c0:c0 + nb], in_=p[:, 0:nb]), reads=[p], writes=[qkf])
                else:
                    fw.op(ACT, lambda h: h.copy(out=qkf[:, ti, 512:640], in_=p[:, 0:128]), reads=[p], writes=[qkf])
                    fw.op(ACT, lambda h: h.copy(out=vaug[:, ti, :, 0:64], in_=p[:, 128:256].rearrange("p (a b) -> p a b", a=2)), reads=[p], writes=[vaug])
                    rope(qkf, qkf[:, ti, :].rearrange("p (a b) -> p a b", b=64), 128, 10, stn * NT + ti)
                    fw.op(DVE, lambda h: h.tensor_copy(out=qkb[:, 0:512], in_=qkf[:, ti, 0:512]), reads=[qkf], writes=[qkb])
                    fw.op(DVE, lambda h: h.tensor_copy(out=qkb[:, 512:768].rearrange("p (a c b) -> p a c b", a=2, c=2),
                                                       in_=bc(qkf[:, ti, 512:640].rearrange("p (a b) -> p a b", a=2).unsqueeze(2), [128, 2, 2, 64])),
                          reads=[qkf], writes=[qkb])
                    transpose_to(qkb[:, 0:512], qkb, 128, 512, lambda j: qT[:, j, ti * 128:(ti + 1) * 128], qT, dst3=lambda a, b: qT[:, a:b, ti * 128:(ti + 1) * 128])
                    transpose_to(qkb[:, 512:768], qkb, 128, 256, lambda j: kTd[:, j, ti * 128:(ti + 1) * 128], kTd, dst3=lambda a, b: kTd[:, a:b, ti * 128:(ti + 1) * 128])
                    if stn == NST - 1 and ti == NT - 1:
                        fw.op(ACT, lambda h: h.copy(out=ostage[:, 0:128], in_=qkf[:, ti, 512:640]), reads=[qkf], writes=[ostage])
                        fw.op(ACT, lambda h: h.copy(out=ostage[:, 128:256], in_=p[:, 128:256]), reads=[p], writes=[ostage])
                        fw.dma(SP, p_k[l], ostage[:, 0:128], sem_o, reads=[ostage], is_out=True)
                        fw.dma(SP, p_v[l], ostage[:, 128:256], sem_o, reads=[ostage], is_out=True)
            dense_tok(actT, tts, W[:, O_Q:O_Q + 768], 768, c_qkv)
            ck("qkv")

            def c_z(ti, c0, nb, p):
                evac(ti, ztok[:, ti, c0:c0 + nb], p[:, 0:nb], p, ztok)
            dense_tok(actT, tts, W[:, O_Z:O_Z + 1024], 1024, c_z)

            def c_dt(ti, c0, nb, p):
                fw.op(DVE, lambda h: h.tensor_tensor(out=dtt[:, ti, :], in0=p[:, 0:16], in1=P["dtb"][:], op=ALU.add), reads=[p, P["dtb"]], writes=[dtt])
                softplus(dtt, dtt[:, ti, :])
            dense_tok(actT, tts, W[:, O_DT:O_DT + 16], 16, c_dt)

            def c_mk(ti, c0, nb, p):
                fw.op(ACT, lambda h: h.activation(out=mktok[:, ti, c0:c0 + nb], in_=p[:, 0:nb], func=AF.Copy, scale=float(128 ** -0.5)), reads=[p], writes=[mktok])
                if c0 + nb == 512:
                    transpose_to(mktok[:, ti, :], mktok, 128, 512, lambda j: mkT[:, j, ti * 128:(ti + 1) * 128], mkT, dst3=lambda a, b: mkT[:, a:b, ti * 128:(ti + 1) * 128])
            dense_tok(actT, tts, W[:, O_MK:O_MK + 512], 512, c_mk)

            def c_mv(ti, c0, nb, p):
                h0 = c0 // 128
                evac(ti, mvaug[:, ti, h0:h0 + nb // 128, 0:128], p[:, 0:nb].rearrange("p (a b) -> p a b", b=128), p, mvaug)
            dense_tok(actT, tts, W[:, O_MV:O_MV + 512], 512, c_mv)

            def c_mo(ti, c0, nb, p):
                evac(ti, motok[:, ti, c0:c0 + nb], p[:, 0:nb], p, motok)
            dense_tok(actT, tts, W[:, O_MO:O_MO + 512], 512, c_mo)

            def c_g(ti, c0, nb, p):
                fw.op(DVE, lambda h: h.tensor_tensor(out=gates[:, ti, :], in0=p[:, 0:8], in1=P["gb"][:], op=ALU.add), reads=[p, P["gb"]], writes=[gates])
            dense_tok(actT, tts, W[:, O_MI:O_MI + 8], 8, c_g)
            ck("tokproj")

            def c_mq(cb_, p):
                evac(cb_, mqT[:, cb_, :], p[:, 0:ST], p, mqT)
            dense_feat(actT, ST, W[:, O_MQ:O_MQ + 512], 512, c_mq)

            def c_xbc(cb_, p):
                fw.op(ACT, lambda h: h.copy(out=xraw[:, 0:3], in_=st.xcarry[:, cb_, :]), reads=[st.xcarry], writes=[xraw])
                fw.op(ACT, lambda h: h.copy(out=xraw[:, 3:ST + 3], in_=p[:, 0:ST]), reads=[p], writes=[xraw])
                fw.op(ACT, lambda h: h.copy(out=st.xcarry[:, cb_, :], in_=xraw[:, ST:ST + 3]), reads=[xraw], writes=[st.xcarry])
                cw = P["cw"]
                fw.op(DVE, lambda h: h.tensor_scalar(out=cacc[:], in0=xraw[:, 0:ST], scalar1=cw[:, cb_, 0:1], scalar2=None, op0=ALU.mult), reads=[xraw, cw], writes=[cacc])
                for j in range(1, 4):
                    fw.op(DVE, lambda h: h.scalar_tensor_tensor(out=cacc[:], in0=xraw[:, j:j + ST], scalar=cw[:, cb_, j:j + 1], in1=cacc[:], op0=ALU.mult, op1=ALU.add),
                          reads=[xraw, cw, cacc], writes=[cacc])
                fw.op(ACT, lambda h: h.activation(out=xcT[:, cb_, :], in_=cacc[:], func=AF.Silu, bias=P["cb"][:, cb_:cb_ + 1]), reads=[cacc, P["cb"]], writes=[xcT])
            dense_feat(actT, ST, W[:, O_XBC:O_XBC + 1536], 1536, c_xbc)
            ck("proj")
            if stn == NST - 1:
                with nc.allow_non_contiguous_dma(reason="tiny conv state out"):
                    for j in range(3):
                        fw.dma(SP, p_conv[l, j, :].rearrange("(b p) -> p b", p=128), st.xcarry[:, :, j], sem_o, reads=[st.xcarry], is_out=True)

            for c in range(NT):
                sl = slice(c * 128, (c + 1) * 128)
                has_prev = not (stn == 0 and c == 0)
                if c == 0:
                    kpf = lambda kv: st.kprev[:, kv, :]; kpb = st.kprev
                    vpf = lambda kv: st.vprev[:, kv, :]; vpb = st.vprev
                else:
                    kpf = (lambda cc_: (lambda kv: kTd[:, kv, (cc_ - 1) * 128:cc_ * 128]))(c); kpb = kTd
                    vpf = (lambda cc_: (lambda kv: vaug[:, cc_ - 1, kv, :]))(c); vpb = vaug
                swa_block(l, lambda j: qT[:, j, sl], qT, lambda kv: kTd[:, kv, sl], kTd, lambda kv: vaug[:, c, kv, :], vaug,
                          kpf, kpb, vpf, vpb, has_prev, mix_tok[:, c, 0:512], big16)
                ck("swa")
                if c == NT - 1:
                    fw.op(ACT, lambda h: h.copy(out=st.kprev[:], in_=kTd[:, :, sl]), reads=[kTd], writes=[st.kprev])
                    fw.op(ACT, lambda h: h.copy(out=st.vprev[:], in_=vaug[:, NT - 1, :, :]), reads=[vaug], writes=[st.vprev])
                for j in range(8):
                    fw.op(PE, lambda h: h.transpose(out=ptb[:, j * 128:(j + 1) * 128], in_=xcT[:, j, sl], identity=ident_b[:]), reads=[xcT, ident_b], writes=[ptb])
                fw.op(ACT, lambda h: h.copy(out=xtok[:], in_=ptb[:, 0:1024]), reads=[ptb], writes=[xtok])
                for j in range(2):
                    fw.op(PE, lambda h: h.transpose(out=ptb[:, j * 128:(j + 1) * 128], in_=xcT[:, 8 + j, sl], identity=ident_b[:]), reads=[xcT, ident_b], writes=[ptb])
                fw.op(ACT, lambda h: h.copy(out=btok[:], in_=ptb[:, 0:256]), reads=[ptb], writes=[btok])
                ssd_chunk(l, st, xtok[:], xtok, btok[:], btok, lambda g: xcT[:, 8 + g, sl], lambda g: xcT[:, 10 + g, sl], xcT,
                          dtt[:, c, :], dtt, ztok[:, c, :], ztok, mix_tok[:, c, 512:1536], big16)
                ck("ssd")
                mlstm_chunk(l, st, lambda hd: mqT[:, hd, sl], lambda hd: mkT[:, hd, sl], [mqT, mkT], mktok[:, c, :], lambda hd: mvaug[:, c, hd, :], [mktok, mvaug],
                            gates[:, c, 0:4], gates[:, c, 4:8], gates, motok[:, c, :], motok, mix_tok[:, c, 1536:2048], big16)
                if DEBUG_STOP[0] == "mlstm":
                    dump("mix", big16, mix_tok[:, c, :], [128, 2048])
                    dump("dtt", dtt, dtt[:, c, :], [128, 16])
                    dump("xtok", xtok, xtok[:], [128, 1024])
                    dump("ST", st.ST, st.ST[:], [128, 1024])
                    dump("Cn", st.Cn, st.Cn[:], [128, 4, 129])
                    dump("mrow", st.mrow, st.mrow[:], [128, 4])
                ck("mlstm")
            if stn == NST - 1:
                emit_state_out(l, st, p_ssm[l], p_C[l], p_n[l], p_m[l])

            for tt in range(NT):
                transpose_to(mix_tok[:, tt, :], big16, 128, D, lambda j: actT[:, j, tt * 128:(tt + 1) * 128], actT, dst3=lambda a, b: actT[:, a:b, tt * 128:(tt + 1) * 128])

            def c_res(ti, c0, nb, p):
                fw.op(DVE, lambda h: h.tensor_tensor(out=xres[:, ti, c0:c0 + nb], in0=p[:, 0:nb], in1=xres[:, ti, c0:c0 + nb], op=ALU.add), reads=[p, xres], writes=[xres])
            dense_tok(actT, tts, w_out[l], D, c_res)
            ck("wout")

            norm_to_actT(NT, 128, w_norm_mlp[l, :], lambda tt: xres[:, tt, :])
            for g in range(4):
                def c_up(cb_, p):
                    fw.op(ACT, lambda h: h.activation(out=sq[:, 0:ST], in_=p[:, 0:ST], func=AF.Relu), reads=[p], writes=[sq])
                    fw.op(DVE, lambda h: h.tensor_tensor(out=hTg[:, cb_, :], in0=sq[:, 0:ST], in1=sq[:, 0:ST], op=ALU.mult), reads=[sq], writes=[big16])
                dense_feat(actT, ST, w_up[l][:, g * 2048:(g + 1) * 2048], 2048, c_up)
                dense_tok(HT, tts, w_down[l][g * 2048:(g + 1) * 2048, :], D, c_res)
            ck("layer")

        load_gain(w_norm_final, 0, D)
        for tt in range(NT):
            rmsnorm_to(xres[:, tt, :], xres, 128, gbc[:, :], ostage[:, :], ostage, sq, D, col=tt)
            fw.dma(SP, y_p[t0g + tt * 128:t0g + (tt + 1) * 128, :], ostage[:, :], sem_o, reads=[ostage], is_out=True)
        ck("st")
        ck("st%d" % stn)


    fw.barrier()
    pes.close()
    fw.es_alloc = None
    ck("prompt")
    ses = ExitStack()
    xrs = fw.sb("xrs", [RS, D], F32, ses)
    actS = fw.sb("actS", [128, 16, 128], BF16, ses)
    utS = fw.sb("utS", [RS, D], BF16, ses)
    sqS = fw.sb("sqS", [RS, D], F32, ses)
    sall = fw.sb("sall", [RS, INW], F32, ses)
    mixs = fw.sb("mixs", [RS, D], BF16, ses)
    hsT = fw.sb("hsT", [128, 16, 128], BF16, ses)
    xcs = sqS
    PB = [fw.sb(f"pb{i}", [RS, 8192], F32, ses) for i in range(3)]
    cj = PB[0]
    wj = PB[1]
    pbc = [0]

    def nextpb():
        b_ = PB[pbc[0] % 3]
        pbc[0] += 1
        return b_
    scs = fw.sb("scs", [RS, 8, 129], F32, ses)
    sden = fw.sb("sden", [RS, 8], F32, ses)
    so = fw.sb("so", [RS, 2, 8, 64], F32, ses)
    dec = fw.sb("dec", [RS, 16], F32, ses)
    xdt = Buf("xdt", scs[:].rearrange("p a b -> p (a b)")[:, 0:1024].rearrange("p (a b) -> p a b", a=16), parent=scs)
    yv = Buf("yv", so[:].rearrange("p a h d -> p (a h d)").rearrange("p (a b) -> p a b", a=16), parent=so)
    nst = Buf("nst", sqS[:, 1024:1536].rearrange("p (a b) -> p a b", a=4), parent=sqS)
    ms = fw.sb("ms", [RS, 48], F32, ses)
    mtmp = Buf("mtmp", sqS[:, 0:512].rearrange("p (a b) -> p a b", a=4), parent=sqS)
    kws = Buf("kws", sqS[:, 512:1024].rearrange("p (a b) -> p a b", a=4), parent=sqS)
    qcs = Buf("qcs", so[:].rearrange("p a h d -> p (a h d)").rearrange("p (a h d) -> p a h d", a=2, h=4), parent=so)
    sem_s = None
    sem_m = None

    fw.dma(SP, xrs[:], xsm[:, :], sem_s, writes=[xrs])
    ttS = [(0, 128)]
    fw.op(DVE, lambda h: h.memset(actS[:], 0.0), writes=[actS])
    fw.op(DVE, lambda h: h.memset(hsT[:], 0.0), writes=[hsT])
    new_pass()

    def stage_tok(r, c0, n, dst_ap, dstbuf, scale=None):
        for o in range(0, n, 512):
            m = min(512, n - o)
            fw.op(PE, lambda h: h.matmul(pd[1][:, 0:m], lhsT=oh[:, r, :], rhs=sall[:, c0 + o:c0 + o + m], start=True, stop=True), reads=[oh, sall], writes=[pd[1]])
            fw.op(ACT, lambda h: h.copy(out=dst_ap(o, m), in_=pd[1][:, 0:m]), reads=[pd[1]], writes=[dstbuf])

    def stage_feat(r, srcbuf, src_ap, dst_ap, dstbuf):
        fw.op(PE, lambda h: h.matmul(pd[1][:, 0:128], lhsT=src_ap, rhs=oh[:, r, :], start=True, stop=True), reads=[oh, srcbuf], writes=[pd[1]])
        fw.op(ACT, lambda h: h.copy(out=dst_ap, in_=pd[1][:, 0:128]), reads=[pd[1]], writes=[dstbuf])

    def norm_to_actS(gain_row):
        load_gain(gain_row, 0, D)
        rmsnorm_to(xrs[:, :], xrs, RS, gbc[0:RS, :], utS[:, :], utS, sqS, D, col=0)
        transpose_to(utS[:, :], utS, RS, D, lambda j: actS[:, j, 0:RS], actS)

    def c_res_s(ti, c0, nb, p):
        fw.op(DVE, lambda h: h.tensor_tensor(out=xrs[:, c0:c0 + nb], in0=p[0:RS, 0:nb], in1=xrs[:, c0:c0 + nb], op=ALU.add), reads=[p, xrs], writes=[xrs])

    for l in range(L):
        st = carry[l]
        P = lp[l]
        norm_to_actS(w_norm_mix[l, :])
        load_gain(w_norm_ssm[l, :], 0, 1024)
        load_gain(w_norm_ml[l, :], 1024, 512)

        def c_all(ti, c0, nb, p):
            fw.op(ACT, lambda h: h.copy(out=sall[:, c0:c0 + nb], in_=p[0:RS, 0:nb]), reads=[p], writes=[sall])
        for (o_, n_) in [(O_Q, 768), (O_Z, 1024), (O_DT, 16), (O_MK, 512), (O_MV, 512), (O_MO, 512), (O_MI, 8), (O_MQ, 512), (O_XBC, 1536)]:
            def c_seg(ti, c0, nb, p, o_=o_):
                c_all(ti, o_ + c0, nb, p)
            dense_tok(actS, ttS, w_in[l][:, o_:o_ + n_], n_, c_seg)
        rope(sall, sall[:, 0:640].rearrange("p (a b) -> p a b", b=64), RS, 10, 16)
        fw.op(DVE, lambda h: h.tensor_scalar(out=sall[:, O_MK:O_MK + 512], in0=sall[:, O_MK:O_MK + 512], scalar1=float(128 ** -0.5), scalar2=None, op0=ALU.mult), reads=[sall], writes=[sall])
        fw.op(DVE, lambda h: h.tensor_tensor(out=sall[:, O_DT:O_DT + 16], in0=sall[:, O_DT:O_DT + 16], in1=P["dtb"][0:RS, :], op=ALU.add), reads=[sall, P["dtb"]], writes=[sall])
        softplus(sall, sall[:, O_DT:O_DT + 16])
        fw.op(DVE, lambda h: h.tensor_tensor(out=sall[:, O_MI:O_MI + 8], in0=sall[:, O_MI:O_MI + 8], in1=P["gb"][0:RS, :], op=ALU.add), reads=[sall, P["gb"]], writes=[sall])
        fw.dma(SP, s_k[l, :, 0:127, :], cache_k[l, :, 1:128, :], sem_o, is_out=True)
        fw.dma(SP, s_v[l, :, 0:127, :], cache_v[l, :, 1:128, :], sem_o, is_out=True)
        fw.dma(SP, s_k[l, :, 127, :], sall[:, O_K:O_K + 128], sem_o, reads=[sall], is_out=True)
        fw.dma(SP, s_v[l, :, 127, :], sall[:, O_V:O_V + 128], sem_o, reads=[sall], is_out=True)
        fw.dma(SP, s_conv[l, :, 0:2, :], st_conv[l, :, 1:3, :], sem_o, is_out=True)
        fw.dma(SP, s_conv[l, :, 2, :], sall[:, O_XBC:O_XBC + 1536], sem_o, reads=[sall], is_out=True)
        fw.dma(SP, wj[:, 0:1536], conv_w[l, 3, :].partition_broadcast(RS), sem_s, writes=[wj])
        fw.op(DVE, lambda h: h.tensor_tensor(out=xcs[:, 0:1536], in0=sall[:, O_XBC:O_XBC + 1536], in1=wj[:, 0:1536], op=ALU.mult), reads=[sall, wj], writes=[xcs])
        for j in range(3):
            fw.dma(SP, cj[:, 0:1536], st_conv[l, :, j, :], sem_s, writes=[cj])
            fw.dma(SP, wj[:, 0:1536], conv_w[l, j, :].partition_broadcast(RS), sem_s, writes=[wj])
            fw.op(DVE, lambda h: h.tensor_tensor(out=cj[:, 0:1536], in0=cj[:, 0:1536], in1=wj[:, 0:1536], op=ALU.mult), reads=[cj, wj], writes=[cj])
            fw.op(DVE, lambda h: h.tensor_tensor(out=xcs[:, 0:1536], in0=xcs[:, 0:1536], in1=cj[:, 0:1536], op=ALU.add), reads=[xcs, cj], writes=[xcs])
        fw.dma(SP, wj[:, 0:1536], conv_b[l, :].partition_broadcast(RS), sem_s, writes=[wj])
        fw.op(DVE, lambda h: h.tensor_tensor(out=xcs[:, 0:1536], in0=xcs[:, 0:1536], in1=wj[:, 0:1536], op=ALU.add), reads=[xcs, wj], writes=[xcs])
        fw.op(ACT, lambda h: h.activation(out=sall[:, O_XBC:O_XBC + 1536], in_=xcs[:, 0:1536], func=AF.Silu), reads=[xcs, sall], writes=[sall])

        A_ = P["A"]
        qv = sall[:, 0:512].rearrange("p (h d) -> p h d", h=8)
        knew = sall[:, O_K:O_K + 128].rearrange("p (a d) -> p a d", a=2)
        vnew = sall[:, O_V:O_V + 128].rearrange("p (a d) -> p a d", a=2)
        for kv in range(2):
            Kc, Vc, T = nextpb(), nextpb(), nextpb()
            Kc3 = Kc[:, :].rearrange("p (s d) -> p s d", d=64)
            Vc3 = Vc[:, :].rearrange("p (s d) -> p s d", d=64)
            T3 = T[:, 0:4096].rearrange("p (a b) -> p a b", a=64)
            with nc.allow_non_contiguous_dma(reason="kv cache head slice (256B runs)"):
                fw.dma(SP, Kc3, cache_k[l, :, :, kv * 64:(kv + 1) * 64], None, writes=[Kc])
                fw.dma(SP, Vc3, cache_v[l, :, :, kv * 64:(kv + 1) * 64], None, writes=[Vc])
            hs = slice(kv * 4, kv * 4 + 4)
            for h4 in range(4):
                h_ = kv * 4 + h4
                for ch in range(2):
                    ps_ = slice(ch * 64, (ch + 1) * 64)
                    fw.op(DVE, lambda h: h.tensor_tensor(out=T3, in0=Kc3[:, ps_, :], in1=bc(qv[:, h_:h_ + 1, :], [RS, 64, 64]), op=ALU.mult), reads=[Kc, sall], writes=[T])
                    fw.op(DVE, lambda h: h.tensor_reduce(out=scs[:, h_, ps_], in_=T3, op=ALU.add, axis=AX.X), reads=[T], writes=[scs])
            fw.op(DVE, lambda h: h.tensor_tensor(out=T[:, 0:256].rearrange("p (a b) -> p a b", a=4), in0=qv[:, hs, :], in1=bc(knew[:, kv:kv + 1, :], [RS, 4, 64]), op=ALU.mult), reads=[sall], writes=[T])
            fw.op(DVE, lambda h: h.tensor_reduce(out=scs[:, hs, 128], in_=T[:, 0:256].rearrange("p (a b) -> p a b", a=4), op=ALU.add, axis=AX.X), reads=[T], writes=[scs])
            fw.op(ACT, lambda h: h.activation(out=scs[:, hs, :], in_=scs[:, hs, :], func=AF.Exp, scale=0.125), reads=[scs], writes=[scs])
            fw.op(DVE, lambda h: h.tensor_reduce(out=sden[:, hs], in_=scs[:, hs, :], op=ALU.add, axis=AX.X), reads=[scs], writes=[sden])
            for h4 in range(4):
                h_ = kv * 4 + h4
                for ch in range(2):
                    ps_ = slice(ch * 64, (ch + 1) * 64)
                    fw.op(DVE, lambda h: h.tensor_tensor(out=T3, in0=Vc3[:, ps_, :].rearrange("p s d -> p d s"), in1=bc(scs[:, h_:h_ + 1, ps_], [RS, 64, 64]), op=ALU.mult), reads=[Vc, scs], writes=[T])
                    fw.op(DVE, lambda h: h.tensor_reduce(out=so[:, ch, h_, :], in_=T3, op=ALU.add, axis=AX.X), reads=[T], writes=[so])
                fw.op(DVE, lambda h: h.scalar_tensor_tensor(out=so[:, 0, h_, :], in0=vnew[:, kv, :], scalar=scs[:, h_, 128:129], in1=so[:, 0, h_, :], op0=ALU.mult, op1=ALU.add), reads=[sall, scs, so], writes=[so])
        fw.op(DVE, lambda h: h.tensor_tensor(out=so[:, 0, :, :], in0=so[:, 0, :, :], in1=so[:, 1, :, :], op=ALU.add), reads=[so], writes=[so])
        fw.op(DVE, lambda h: h.tensor_tensor(out=sden[:], in0=sden[:], in1=P["esk"][0:RS, :], op=ALU.add), reads=[sden, P["esk"]], writes=[sden])
        fw.op(DVE, lambda h: h.reciprocal(out=sden[:], in_=sden[:]), reads=[sden], writes=[sden])
        fw.op(DVE, lambda h: h.tensor_tensor(out=mixs[:, 0:512].rearrange("p (a b) -> p a b", a=8), in0=so[:, 0, :, :], in1=bc(sden[:].unsqueeze(2), [RS, 8, 64]), op=ALU.mult), reads=[so, sden], writes=[mixs])

        x16 = sall[:, O_XBC:O_XBC + 1024].rearrange("p (a b) -> p a b", a=16)
        Bm = sall[:, O_XBC + 1024:O_XBC + 1280].rearrange("p (a b) -> p a b", a=2)
        Cm = sall[:, O_XBC + 1280:O_XBC + 1536].rearrange("p (a b) -> p a b", a=2)
        dts = sall[:, O_DT:O_DT + 16]
        fw.op(DVE, lambda h: h.tensor_tensor(out=dec[:], in0=dts, in1=A_[0:RS, :], op=ALU.mult), reads=[sall, A_], writes=[dec])
        fw.op(ACT, lambda h: h.activation(out=dec[:], in_=dec[:], func=AF.Exp), reads=[dec], writes=[dec])
        fw.op(DVE, lambda h: h.tensor_tensor(out=xdt[:], in0=x16, in1=bc(dts.unsqueeze(2), [RS, 16, 64]), op=ALU.mult), reads=[sall], writes=[xdt])
        for hh in range(16):
            g = hh // 8
            Sp, T = nextpb(), nextpb()
            Sp3 = Sp[:, :].rearrange("p (a b) -> p a b", a=64)
            T3 = T[:, :].rearrange("p (a b) -> p a b", a=64)
            fw.dma(SP, Sp3, st_ssm[l, :, hh * 64:(hh + 1) * 64, :], None, writes=[Sp])
            fw.op(POOL, lambda h: h.tensor_tensor(out=T3, in0=bc(xdt[:, hh, :].unsqueeze(2), [RS, 64, 128]), in1=bc(Bm[:, g:g + 1, :], [RS, 64, 128]), op=ALU.mult), reads=[xdt, sall], writes=[T])
            fw.op(DVE, lambda h: h.scalar_tensor_tensor(out=Sp[:, :], in0=Sp[:, :], scalar=dec[:, hh:hh + 1], in1=T[:, :], op0=ALU.mult, op1=ALU.add), reads=[Sp, dec, T], writes=[Sp])
            fw.dma(SP, s_ssm[l, :, hh * 64:(hh + 1) * 64, :], Sp3, None, reads=[Sp], is_out=True)
            fw.op(DVE, lambda h: h.tensor_tensor(out=T3, in0=Sp3, in1=bc(Cm[:, g:g + 1, :], [RS, 64, 128]), op=ALU.mult), reads=[Sp, sall], writes=[T])
            fw.op(DVE, lambda h: h.tensor_reduce(out=yv[:, hh, :], in_=T3, op=ALU.add, axis=AX.X), reads=[T], writes=[yv])
        fw.op(DVE, lambda h: h.tensor_tensor(out=xdt[:], in0=x16, in1=bc(P["dsk"][0:RS, :].unsqueeze(2), [RS, 16, 64]), op=ALU.mult), reads=[sall, P["dsk"]], writes=[xdt])
        fw.op(DVE, lambda h: h.tensor_tensor(out=yv[:], in0=yv[:], in1=xdt[:], op=ALU.add), reads=[yv, xdt], writes=[yv])
        fw.op(ACT, lambda h: h.activation(out=xdt[:].rearrange("p a b -> p (a b)"), in_=sall[:, O_Z:O_Z + 1024], func=AF.Silu), reads=[sall], writes=[xdt])
        fw.op(DVE, lambda h: h.tensor_tensor(out=yv[:], in0=yv[:], in1=xdt[:], op=ALU.mult), reads=[yv, xdt], writes=[yv])
        yvf = yv[:].rearrange("p a b -> p (a b)")
        for g in range(2):
            rmsnorm_to(yvf[:, g * 512:(g + 1) * 512], yv, RS, gbc[0:RS, g * 512:(g + 1) * 512], mixs[:, 512 + g * 512:512 + (g + 1) * 512], mixs, sqS, 512, col=8 + g)

        q4 = sall[:, O_MQ:O_MQ + 512].rearrange("p (a b) -> p a b", a=4)
        k4 = sall[:, O_MK:O_MK + 512].rearrange("p (a b) -> p a b", a=4)
        v4 = sall[:, O_MV:O_MV + 512].rearrange("p (a b) -> p a b", a=4)
        igs = sall[:, O_MI:O_MI + 4]
        fgs = sall[:, O_MF:O_MF + 4]
        fw.dma(SP, nst[:], st_n[l], None, writes=[nst])
        fw.dma(SP, ms[:, 0:4], st_m[l], None, writes=[ms])
        M_, LF, BM, MT, SWS, G_, EMT, QK, QN, SW, DEN, RD = [ms[:, 4 * i:4 * i + 4] for i in range(12)]
        def dv(out, in0, in1, op):
            fw.op(DVE, lambda h: h.tensor_tensor(out=out, in0=in0, in1=in1, op=op), reads=[ms, sall], writes=[ms])
        fw.op(ACT, lambda h: h.activation(out=LF, in_=fgs, func=AF.Exp, scale=-1.0), reads=[sall], writes=[ms])
        fw.op(ACT, lambda h: h.activation(out=LF, in_=LF, func=AF.Ln, bias=1.0), reads=[ms], writes=[ms])
        dv(BM, M_, LF, ALU.subtract)
        dv(MT, BM, igs, ALU.max)
        dv(SWS, igs, MT, ALU.subtract)
        fw.op(ACT, lambda h: h.activation(out=SWS, in_=SWS, func=AF.Exp), reads=[ms], writes=[ms])
        dv(G_, BM, MT, ALU.subtract)
        fw.op(ACT, lambda h: h.activation(out=G_, in_=G_, func=AF.Exp), reads=[ms], writes=[ms])
        fw.op(ACT, lambda h: h.activation(out=EMT, in_=MT, func=AF.Exp, scale=-1.0), reads=[ms], writes=[ms])
        fw.op(DVE, lambda h: h.tensor_tensor(out=mtmp[:], in0=q4, in1=k4, op=ALU.mult), reads=[sall], writes=[mtmp])
        fw.op(DVE, lambda h: h.tensor_reduce(out=QK, in_=mtmp[:], op=ALU.add, axis=AX.X), reads=[mtmp], writes=[ms])
        fw.op(DVE, lambda h: h.tensor_tensor(out=mtmp[:], in0=q4, in1=nst[:], op=ALU.mult), reads=[sall, nst], writes=[mtmp])
        fw.op(DVE, lambda h: h.tensor_reduce(out=QN, in_=mtmp[:], op=ALU.add, axis=AX.X), reads=[mtmp], writes=[ms])
        dv(SW, SWS, QK, ALU.mult)
        dv(DEN, G_, QN, ALU.mult)
        dv(DEN, DEN, SW, ALU.add)
        fw.op(DVE, lambda h: h.tensor_scalar(out=RD, in0=DEN, scalar1=-1.0, scalar2=None, op0=ALU.mult), reads=[ms], writes=[ms])
        dv(RD, RD, DEN, ALU.max)
        dv(RD, RD, EMT, ALU.max)
        fw.op(DVE, lambda h: h.reciprocal(out=RD, in_=RD), reads=[ms], writes=[ms])
        fw.op(DVE, lambda h: h.tensor_tensor(out=kws[:], in0=k4, in1=bc(SWS.unsqueeze(2), [RS, 4, 128]), op=ALU.mult), reads=[sall, ms], writes=[kws])
        fw.op(DVE, lambda h: h.tensor_tensor(out=nst[:], in0=nst[:], in1=bc(G_.unsqueeze(2), [RS, 4, 128]), op=ALU.mult), reads=[nst, ms], writes=[nst])
        fw.op(DVE, lambda h: h.tensor_tensor(out=nst[:], in0=nst[:], in1=kws[:], op=ALU.add), reads=[nst, kws], writes=[nst])
        fw.dma(SP, s_n[l], nst[:], None, reads=[nst], is_out=True)
        fw.dma(SP, s_m[l], MT, None, reads=[ms], is_out=True)
        for hd in range(4):
            for hf in range(2):
                dsl = slice(hf * 64, (hf + 1) * 64)
                Ct, T = nextpb(), nextpb()
                Ct3 = Ct[:, :].rearrange("p (a b) -> p a b", a=64)
                T3 = T[:, :].rearrange("p (a b) -> p a b", a=64)
                Te = T[:, :].rearrange("p (e d) -> p e d", e=128)
                fw.dma(SP, Ct3, st_C[l, :, hd, dsl, :], None, writes=[Ct])
                fw.op(DVE, lambda h: h.tensor_tensor(out=Te, in0=Ct3.rearrange("p d e -> p e d"), in1=bc(q4[:, hd:hd + 1, dsl], [RS, 128, 64]), op=ALU.mult), reads=[Ct, sall], writes=[T])
                fw.op(DVE, lambda h: h.tensor_reduce(out=qcs[:, hf, hd, :], in_=Te, op=ALU.add, axis=AX.X), reads=[T], writes=[qcs])
                fw.op(POOL, lambda h: h.tensor_tensor(out=T3, in0=bc(kws[:, hd, dsl].unsqueeze(2), [RS, 64, 128]), in1=bc(v4[:, hd:hd + 1, :], [RS, 64, 128]), op=ALU.mult), reads=[kws, sall], writes=[T])
                fw.op(DVE, lambda h: h.scalar_tensor_tensor(out=Ct[:, :], in0=Ct[:, :], scalar=G_[:, hd:hd + 1], in1=T[:, :], op0=ALU.mult, op1=ALU.add), reads=[Ct, ms, T], writes=[Ct])
                fw.dma(SP, s_C[l, :, hd, dsl, :], Ct3, None, reads=[Ct], is_out=True)
        fw.op(DVE, lambda h: h.tensor_tensor(out=qcs[:, 0, :, :], in0=qcs[:, 0, :, :], in1=qcs[:, 1, :, :], op=ALU.add), reads=[qcs], writes=[qcs])
        fw.op(DVE, lambda h: h.tensor_tensor(out=mtmp[:], in0=v4, in1=bc(SW.unsqueeze(2), [RS, 4, 128]), op=ALU.mult), reads=[sall, ms], writes=[mtmp])
        fw.op(DVE, lambda h: h.tensor_tensor(out=qcs[:, 0, :, :], in0=qcs[:, 0, :, :], in1=bc(G_.unsqueeze(2), [RS, 4, 128]), op=ALU.mult), reads=[qcs, ms], writes=[qcs])
        fw.op(DVE, lambda h: h.tensor_tensor(out=mtmp[:], in0=mtmp[:], in1=qcs[:, 0, :, :], op=ALU.add), reads=[mtmp, qcs], writes=[mtmp])
        fw.op(DVE, lambda h: h.tensor_tensor(out=mtmp[:], in0=mtmp[:], in1=bc(RD.unsqueeze(2), [RS, 4, 128]), op=ALU.mult), reads=[mtmp, ms], writes=[mtmp])
        fw.op(DVE, lambda h: h.tensor_tensor(out=kws[:], in0=mtmp[:], in1=mtmp[:], op=ALU.mult), reads=[mtmp], writes=[kws])
        fw.op(DVE, lambda h: h.tensor_reduce(out=QK, in_=kws[:], op=ALU.add, axis=AX.X), reads=[kws], writes=[ms])
        fw.op(DVE, lambda h: h.tensor_scalar(out=QK, in0=QK, scalar1=1.0 / 128, scalar2=EPS, op0=ALU.mult, op1=ALU.add), reads=[ms], writes=[ms])
        fw.op(ACT, lambda h: h.activation(out=QK, in_=QK, func=AF.Sqrt), reads=[ms], writes=[ms])
        fw.op(DVE, lambda h: h.reciprocal(out=QK, in_=QK), reads=[ms], writes=[ms])
        fw.op(DVE, lambda h: h.tensor_tensor(out=mtmp[:], in0=mtmp[:], in1=bc(QK.unsqueeze(2), [RS, 4, 128]), op=ALU.mult), reads=[mtmp, ms], writes=[mtmp])
        mtf = mtmp[:].rearrange("p a b -> p (a b)")
        fw.op(DVE, lambda h: h.tensor_tensor(out=mtf, in0=mtf, in1=gbc[0:RS, 1024:1536], op=ALU.mult), reads=[mtmp, gbc], writes=[mtmp])
        fw.op(ACT, lambda h: h.activation(out=kws[:].rearrange("p a b -> p (a b)"), in_=sall[:, O_MO:O_MO + 512], func=AF.Sigmoid), reads=[sall], writes=[kws])
        fw.op(DVE, lambda h: h.tensor_tensor(out=mixs[:, 1536:2048], in0=mtf, in1=kws[:].rearrange("p a b -> p (a b)"), op=ALU.mult), reads=[mtmp, kws], writes=[mixs])

        transpose_to(mixs[:, :], mixs, RS, D, lambda j: actS[:, j, 0:RS], actS)
        dense_tok(actS, ttS, w_out[l], D, c_res_s)
        norm_to_actS(w_norm_mlp[l, :])
        for g in range(4):
            def c_up_s(ti, c0, nb, p):
                fw.op(ACT, lambda h: h.activation(out=sqS[:, c0:c0 + nb], in_=p[0:RS, 0:nb], func=AF.Relu), reads=[p], writes=[sqS])
                fw.op(DVE, lambda h: h.tensor_tensor(out=utS[:, c0:c0 + nb], in0=sqS[:, c0:c0 + nb], in1=sqS[:, c0:c0 + nb], op=ALU.mult), reads=[sqS], writes=[utS])
            dense_tok(actS, ttS, w_up[l][:, g * 2048:(g + 1) * 2048], 2048, c_up_s)
            transpose_to(utS[:, :], utS, RS, D, lambda j: hsT[:, j, 0:RS], hsT)
            dense_tok(hsT, ttS, w_down[l][g * 2048:(g + 1) * 2048, :], D, c_res_s)

    load_gain(w_norm_final, 0, D)
    rmsnorm_to(xrs[:, :], xrs, RS, gbc[0:RS, :], sqS[:, :], sqS, sall, D, col=0)
    fw.dma(SP, y_s[:, :], sqS[:, :], sem_o, reads=[sqS], is_out=True)
    ses.close()


_NC = [None]


def _consts():
    i = np.arange(128)
    c = {}
    c["c_ident"] = np.eye(128, dtype=np.float32)
    c["c_tri"] = (i[:, None] <= i[None, :]).astype(np.float32)
    c["c_triT"] = (i[:, None] >= i[None, :]).astype(np.float32)
    c["c_U"] = (i[:, None] > i[None, :]).astype(np.float32)
    c["c_negqk"] = np.where(i[None, :] > i[:, None], -1e30, 0.0).astype(np.float32)
    c["c_negkq"] = np.where(i[:, None] > i[None, :], -30000.0, 0.0).astype(np.float32)
    e = np.zeros((128, 128), np.float32); e[127, :] = 1.0
    c["c_e127"] = e
    half = 8
    inv = np.power(np.float32(500000.0), -np.arange(half, dtype=np.float32) / half).astype(np.float32)
    pos = np.zeros((128, 17), np.float32)
    for t in range(16):
        pos[:, t] = t * 128 + i
    pos[:, 16] = PAST
    ang = pos[:, :, None].astype(np.float32) * inv[None, None, :]
    c["c_cos"] = np.cos(ang).astype(np.float32)
    c["c_sin"] = np.sin(ang).astype(np.float32)
    oh = np.zeros((RS, RS, 128), np.float32)
    for r in range(RS):
        oh[r, r, 0] = 1.0
    c["c_oh"] = oh
    pad = np.zeros((128, 8), np.float32)
    pad[1:, 0:4] = -1.0e4
    pad[1:, 4:8] = 1.0e4
    c["c_pad"] = pad
    return c


def kernel(x_prompt, x_sample, cache_swa_k, cache_swa_v, state_conv, state_ssm, state_mlstm_C,
           state_mlstm_n, state_mlstm_m, w_norm_mix, w_in, attn_sinks, conv_w, conv_b, dt_bias, a_log,
           d_skip, w_norm_ssm, igate_b, fgate_b, w_norm_mlstm, w_out, w_norm_mlp, w_up, w_down,
           w_norm_final):
    f = lambda a: np.ascontiguousarray(np.asarray(a, dtype=np.float32))
    if _NC[0] is None:
        _NC[0] = build()
    nc = _NC[0]
    cst = _consts()
    shared = {
        "w_norm_mix": f(w_norm_mix), "w_in": f(w_in), "sinks": f(attn_sinks).reshape(L, 8), "conv_w": f(conv_w),
        "conv_b": f(conv_b), "dt_bias": f(dt_bias), "a_log": f(a_log), "d_skip": f(d_skip), "w_norm_ssm": f(w_norm_ssm),
        "igb": f(igate_b), "fgb": f(fgate_b), "w_norm_ml": f(w_norm_mlstm), "w_out": f(w_out), "w_norm_mlp": f(w_norm_mlp),
        "w_up": f(w_up), "w_down": f(w_down), "w_norm_final": f(w_norm_final),
    }
    shared.update(cst)
    in_maps = []
    for c in range(NCORES):
        rs = slice(c * RS, (c + 1) * RS)
        m = dict(shared)
        m["xp"] = f(x_prompt[c])
        m["xsm"] = f(x_sample[rs, 0, :])
        m["cache_k"] = f(np.asarray(cache_swa_k)[:, rs].reshape(L, RS, 128, 128))
        m["cache_v"] = f(np.asarray(cache_swa_v)[:, rs].reshape(L, RS, 128, 128))
        m["st_conv"] = f(np.asarray(state_conv)[:, rs])
        m["st_ssm"] = f(np.asarray(state_ssm)[:, rs].reshape(L, RS, 1024, 128))
        m["st_C"] = f(np.asarray(state_mlstm_C)[:, rs])
        m["st_n"] = f(np.asarray(state_mlstm_n)[:, rs])
        m["st_m"] = f(np.asarray(state_mlstm_m)[:, rs])
        in_maps.append(m)
    res = run_bass_kernel_spmd(nc, in_maps, core_ids=list(range(NCORES)))
    R = res.results
    cat = lambda k, ax: np.concatenate([np.asarray(R[c][k]) for c in range(NCORES)], axis=ax)
    stk = lambda k: np.stack([np.asarray(R[c][k]) for c in range(NCORES)], axis=1)
    y_prompt = np.stack([np.asarray(R[c]["y_p"]) for c in range(NCORES)], axis=0)
    y_sample = cat("y_s", 0).reshape(NCORES * RS, 1, D)
    p_k = stk("p_k").reshape(L, NCORES, 128, 2, 64)
    p_v = stk("p_v").reshape(L, NCORES, 128, 2, 64)
    p_conv = stk("p_conv")
    p_ssm = stk("p_ssm").reshape(L, NCORES, 16, 64, 128)
    p_C = stk("p_C"); p_n = stk("p_n"); p_m = stk("p_m")
    s_k = cat("s_k", 1).reshape(L, NCORES * RS, 128, 2, 64)
    s_v = cat("s_v", 1).reshape(L, NCORES * RS, 128, 2, 64)
    s_conv = cat("s_conv", 1)
    s_ssm = cat("s_ssm", 1).reshape(L, NCORES * RS, 16, 64, 128)
    s_C = cat("s_C", 1); s_n = cat("s_n", 1); s_m = cat("s_m", 1)
    outs = (y_prompt, y_sample, p_k, p_v, p_conv, p_ssm, p_C, p_n, p_m, s_k, s_v, s_conv, s_ssm, s_C, s_n, s_m)
    return tuple(np.ascontiguousarray(o, dtype=np.float32) for o in outs)
```
